# Optimizing a Trainium2 kernel written in Bass

```python
import jax, jax.numpy as jnp
from jax import lax
import numpy as np

D_MODEL = 2048
BATCH = 2
SEQ = 16384
DEPTH = 1

GRID_W = 64
CTX_LEN = 256
LRU_WIDTH = D_MODEL // 2
LRU_HEADS = 8
LRU_HEAD_DIM = LRU_WIDTH // LRU_HEADS
LRU_C = 8.0
CONV_W = 4
CONV_LEFT = 2
SGU_WIDTH = D_MODEL // 2
SGU_GROUPS = 8
SGU_GROUP_DIM = SGU_WIDTH // SGU_GROUPS
CHUNK = 128
ROWS_PER_CHUNK = CHUNK // GRID_W
MIX_WIDTH = LRU_WIDTH + SGU_WIDTH
IN_WIDTH = 2 * LRU_WIDTH + 2 * SGU_WIDTH
PEER_HEADS = 8
PEER_NKEYS = 128
PEER_EXPERTS = PEER_NKEYS * PEER_NKEYS
PEER_QDIM = 256
PEER_HALF = PEER_QDIM // 2
PEER_TOPK = 16
PEER_BLOCK = 128
N_MOD = 6
EPS = 1e-6

kernel_name = "hybrid_rglru_sgu_peer_prefix_dit"


def rms_norm(x, g):
    xf = x.astype(jnp.float32)
    y = xf * lax.rsqrt(jnp.mean(xf * xf, axis=-1, keepdims=True) + EPS)
    return (y * g.astype(jnp.float32)).astype(x.dtype)


def layer_norm_noaffine(x):
    xf = x.astype(jnp.float32)
    mu = jnp.mean(xf, axis=-1, keepdims=True)
    var = jnp.mean(jnp.square(xf - mu), axis=-1, keepdims=True)
    return ((xf - mu) * lax.rsqrt(var + EPS)).astype(x.dtype)


def adaln(s, w_mod, b_mod, k):
    lo, hi = k * D_MODEL, (k + 1) * D_MODEL
    return (s @ w_mod[:, lo:hi] + b_mod[lo:hi])[..., None, :]


def modulate(h, shift, scale):
    return h * (1.0 + scale) + shift


def centred_conv(x, w, b):
    L = x.shape[1]
    xp = jnp.pad(x, ((0, 0), (CONV_LEFT, CONV_W - 1 - CONV_LEFT), (0, 0)))
    y = b
    for k in range(CONV_W):
        y = y + xp[:, k:k + L, :] * w[k]
    return y


def block_diag(x, w, b):
    B, L, _ = x.shape
    xh = x.reshape(B, L, LRU_HEADS, LRU_HEAD_DIM)
    return jnp.einsum('blhi,hij->blhj', xh, w).reshape(B, L, LRU_WIDTH) + b


def lru_coeffs(xc, wa, ba, wx, bx, lam):
    xf = xc.astype(jnp.float32)
    r = jax.nn.sigmoid(block_diag(xc, wa, ba).astype(jnp.float32))
    i = jax.nn.sigmoid(block_diag(xc, wx, bx).astype(jnp.float32))
    log_a = -LRU_C * r * jax.nn.softplus(-lam.astype(jnp.float32))
    a = jnp.exp(log_a)
    b = jnp.sqrt(-jnp.expm1(2.0 * log_a)) * (i * xf)
    return a, b


def linear_scan(a, b, h0, reverse):
    def step(h, ab):
        h = ab[0] * h + ab[1]
        return h, h
    h_fin, hs = lax.scan(step, h0, (jnp.swapaxes(a, 0, 1), jnp.swapaxes(b, 0, 1)), reverse=reverse)
    return h_fin, jnp.swapaxes(hs, 0, 1)


def scan_final_state(a, b, h0, reverse):
    def step(h, ab):
        return ab[0] * h + ab[1], None
    h_fin, _ = lax.scan(step, h0, (jnp.swapaxes(a, 0, 1), jnp.swapaxes(b, 0, 1)), reverse=reverse)
    return h_fin


def spatial_gating(u, v, w_s, b_s, n_chunks):
    B, L, _ = u.shape
    shp = (B, n_chunks, CHUNK, SGU_GROUPS, SGU_GROUP_DIM)
    ug = jax.nn.gelu(u).reshape(shp)
    vg = layer_norm_noaffine(jax.nn.gelu(v).reshape(shp))
    mixed = jnp.einsum('gpq,bnqgc->bnpgc', w_s, vg) + b_s.T[:, :, None]
    return (ug * mixed).reshape(B, L, SGU_WIDTH)


def mixer_output(proj, y_scan, sgu_w, sgu_b, w_out, n_chunks):
    gate = proj[..., LRU_WIDTH:2 * LRU_WIDTH]
    u = proj[..., 2 * LRU_WIDTH:2 * LRU_WIDTH + SGU_WIDTH]
    v = proj[..., 2 * LRU_WIDTH + SGU_WIDTH:]
    y_lru = jax.nn.gelu(gate) * y_scan.astype(proj.dtype)
    y_sgu = spatial_gating(u, v, sgu_w, sgu_b, n_chunks)
    return jnp.concatenate([y_lru, y_sgu], axis=-1) @ w_out


def peer_ffn(h, w_q, k1, k2, u_tab, v_tab):
    B, L, D = h.shape
    blocks = h.reshape((B * L) // PEER_BLOCK, PEER_BLOCK, D)

    def block_fn(hb):
        q = (hb @ w_q).reshape(PEER_BLOCK, PEER_HEADS, 2, PEER_HALF)
        s1 = jnp.einsum('thd,hkd->thk', q[:, :, 0], k1).astype(jnp.float32)
        s2 = jnp.einsum('thd,hkd->thk', q[:, :, 1], k2).astype(jnp.float32)
        v1, i1 = lax.top_k(s1, PEER_TOPK)
        v2, i2 = lax.top_k(s2, PEER_TOPK)
        cand = (v1[..., :, None] + v2[..., None, :]).reshape(PEER_BLOCK, PEER_HEADS, PEER_TOPK * PEER_TOPK)
        sc, flat = lax.top_k(cand, PEER_TOPK)
        e1 = jnp.take_along_axis(i1, flat // PEER_TOPK, axis=-1)
        e2 = jnp.take_along_axis(i2, flat % PEER_TOPK, axis=-1)
        idx = e1 * PEER_NKEYS + e2
        g = jax.nn.softmax(sc, axis=-1)
        u_sel = jnp.take(u_tab, idx, axis=0)
        act = jax.nn.gelu(jnp.einsum('td,thkd->thk', hb, u_sel))
        v_sel = jnp.take(v_tab, idx, axis=0)
        return jnp.einsum('thk,thkd->td', (g * act.astype(jnp.float32)).astype(hb.dtype), v_sel)

    return lax.map(block_fn, blocks).reshape(B, L, D)


def setup_inputs(seed: int = 0) -> dict:
    key = jax.random.key(seed)
    ks = jax.random.split(key, 32)
    f32 = jnp.float32
    D = D_MODEL

    def nrm(k, shape, scale):
        return jax.random.normal(k, shape, f32) * scale

    u_a = jax.random.uniform(ks[14], (DEPTH, 2, LRU_WIDTH), f32, 0.9, 0.999)
    s_a = u_a ** (1.0 / LRU_C)
    lru_lambda = jnp.log(s_a) - jnp.log1p(-s_a)
    return {
        "x": nrm(ks[0], (BATCH, SEQ, D), 1.0),
        "c": nrm(ks[1], (BATCH, D), 1.0),
        "ctx": nrm(ks[2], (BATCH, CTX_LEN, D), 1.0),
        "c_ctx": nrm(ks[3], (D,), 1.0),
        "w_mod": nrm(ks[4], (DEPTH, D, N_MOD * D), D ** -0.5),
        "b_mod": nrm(ks[5], (DEPTH, N_MOD * D), 0.02),
        "norm1_g": 1.0 + nrm(ks[6], (DEPTH, D), 0.02),
        "norm2_g": 1.0 + nrm(ks[7], (DEPTH, D), 0.02),
        "w_in": nrm(ks[8], (DEPTH, D, IN_WIDTH), D ** -0.5),
        "conv_w": nrm(ks[9], (DEPTH, CONV_W, LRU_WIDTH), CONV_W ** -0.5),
        "conv_b": nrm(ks[10], (DEPTH, LRU_WIDTH), 0.02),
        "lru_wa": nrm(ks[11], (DEPTH, 2, LRU_HEADS, LRU_HEAD_DIM, LRU_HEAD_DIM), LRU_HEAD_DIM ** -0.5),
        "lru_ba": nrm(ks[12], (DEPTH, 2, LRU_WIDTH), 0.02),
        "lru_wx": nrm(ks[13], (DEPTH, 2, LRU_HEADS, LRU_HEAD_DIM, LRU_HEAD_DIM), LRU_HEAD_DIM ** -0.5),
        "lru_bx": nrm(ks[15], (DEPTH, 2, LRU_WIDTH), 0.02),
        "lru_lambda": lru_lambda,
        "sgu_w": nrm(ks[16], (DEPTH, SGU_GROUPS, CHUNK, CHUNK), CHUNK ** -0.5),
        "sgu_b": nrm(ks[17], (DEPTH, SGU_GROUPS, CHUNK), 0.02),
        "w_out": nrm(ks[18], (DEPTH, MIX_WIDTH, D), MIX_WIDTH ** -0.5),
        "peer_wq": nrm(ks[19], (DEPTH, D, PEER_HEADS * PEER_QDIM), D ** -0.5),
        "peer_k1": nrm(ks[20], (DEPTH, PEER_HEADS, PEER_NKEYS, PEER_HALF), PEER_HALF ** -0.5),
        "peer_k2": nrm(ks[21], (DEPTH, PEER_HEADS, PEER_NKEYS, PEER_HALF), PEER_HALF ** -0.5),
        "peer_u": nrm(ks[22], (DEPTH, PEER_EXPERTS, D), D ** -0.5),
        "peer_v": nrm(ks[23], (DEPTH, PEER_EXPERTS, D), PEER_HEADS ** -0.5),
        "final_g": 1.0 + nrm(ks[24], (D,), 0.02),
    }


def reference(x, c, ctx, c_ctx, w_mod, b_mod, norm1_g, norm2_g, w_in, conv_w, conv_b,
              lru_wa, lru_ba, lru_wx, lru_bx, lru_lambda, sgu_w, sgu_b, w_out,
              peer_wq, peer_k1, peer_k2, peer_u, peer_v, final_g):
    B, L, _ = x.shape
    rows = L // GRID_W
    lat_chunks = rows // ROWS_PER_CHUNK
    ctx_chunks = ctx.shape[1] // CHUNK
    s_lat = jax.nn.silu(c)
    s_ctx = jax.nn.silu(c_ctx)
    x_lat, x_ctx = x, ctx
    for l in range(DEPTH):
        last = l == DEPTH - 1
        wm, bm = w_mod[l], b_mod[l]
        hn_lat = modulate(rms_norm(x_lat, norm1_g[l]), adaln(s_lat, wm, bm, 0), adaln(s_lat, wm, bm, 1))
        hn_ctx = modulate(rms_norm(x_ctx, norm1_g[l]), adaln(s_ctx, wm, bm, 0), adaln(s_ctx, wm, bm, 1))
        proj_lat = hn_lat @ w_in[l]
        if last:
            proj_ctx_x = hn_ctx @ w_in[l][:, :LRU_WIDTH]
        else:
            proj_ctx = hn_ctx @ w_in[l]
            proj_ctx_x = proj_ctx[..., :LRU_WIDTH]
        xc_lat = centred_conv(proj_lat[..., :LRU_WIDTH], conv_w[l], conv_b[l])
        xc_ctx = centred_conv(proj_ctx_x, conv_w[l], conv_b[l])
        y_scan_lat = jnp.zeros((B, L, LRU_WIDTH), jnp.float32)
        y_scan_ctx = jnp.zeros((B, x_ctx.shape[1], LRU_WIDTH), jnp.float32)
        for d, rev in enumerate((False, True)):
            prm = (lru_wa[l, d], lru_ba[l, d], lru_wx[l, d], lru_bx[l, d], lru_lambda[l, d])
            a_c, b_c = lru_coeffs(xc_ctx, *prm)
            h0 = jnp.zeros((B, LRU_WIDTH), jnp.float32)
            if last:
                h_ctx = scan_final_state(a_c, b_c, h0, rev)
            else:
                h_ctx, ys_c = linear_scan(a_c, b_c, h0, rev)
                y_scan_ctx = y_scan_ctx + ys_c
            a_l, b_l = lru_coeffs(xc_lat, *prm)
            _, ys_l = linear_scan(a_l, b_l, h_ctx, rev)
            y_scan_lat = y_scan_lat + ys_l
        y_lat = mixer_output(proj_lat, y_scan_lat, sgu_w[l], sgu_b[l], w_out[l], lat_chunks)
        x_lat = x_lat + adaln(s_lat, wm, bm, 2) * y_lat
        hn2_lat = modulate(rms_norm(x_lat, norm2_g[l]), adaln(s_lat, wm, bm, 3), adaln(s_lat, wm, bm, 4))
        x_lat = x_lat + adaln(s_lat, wm, bm, 5) * peer_ffn(hn2_lat, peer_wq[l], peer_k1[l], peer_k2[l], peer_u[l], peer_v[l])
        if not last:
            y_ctx = mixer_output(proj_ctx, y_scan_ctx, sgu_w[l], sgu_b[l], w_out[l], ctx_chunks)
            x_ctx = x_ctx + adaln(s_ctx, wm, bm, 2) * y_ctx
            hn2_ctx = modulate(rms_norm(x_ctx, norm2_g[l]), adaln(s_ctx, wm, bm, 3), adaln(s_ctx, wm, bm, 4))
            x_ctx = x_ctx + adaln(s_ctx, wm, bm, 5) * peer_ffn(hn2_ctx, peer_wq[l], peer_k1[l], peer_k2[l], peer_u[l], peer_v[l])
    return rms_norm(x_lat, final_g)
```

```python
import numpy as np
from contextlib import ExitStack
import concourse.bass as bass
import concourse.mybir as mybir
from concourse.bass_utils import run_bass_kernel_spmd

F32 = mybir.dt.float32
BF16 = mybir.dt.bfloat16
AF = mybir.ActivationFunctionType
ALU = mybir.AluOpType
AX = mybir.AxisListType

P = 128
D = 2048
KC = 16
T = 512
EPS = 1e-6
NEG = -1.0e30


class Buf:
    __slots__ = ("name", "last_w", "readers", "dsem", "dcount")

    def __init__(self, name):
        self.name = name
        self.last_w = None
        self.readers = {}
        self.dsem = None
        self.dcount = 0


class Sched:
    ENG = ("pe", "act", "dve", "pool", "sp")
    COMPUTE = ("pe", "act", "dve", "pool")

    def __init__(self, nc, stack):
        self.nc = nc
        self.stack = stack
        self.e = {}
        for n in self.ENG:
            sem = stack.enter_context(nc.semaphore("s_" + n))
            self.e[n] = dict(sem=sem, count=0, ops=[], waited={})
        self.nb = 0

    def buf(self, name, dma=False):
        self.nb += 1
        b = Buf("%s_%d" % (name, self.nb))
        if dma:
            b.dsem = self.stack.enter_context(self.nc.semaphore("d%d" % self.nb))
        return b

    def _collect(self, eng, reads, writes):
        need = {}

        def add(ev, raw):
            if ev is None:
                return
            key, sem, val = ev
            if key == eng and (eng == "pe" or not raw):
                return
            if key not in need or need[key][1] < val:
                need[key] = (sem, val)

        for b in reads:
            if b.last_w:
                for ev in b.last_w.values():
                    add(ev, True)
        for b in writes:
            if b.last_w:
                for ev in b.last_w.values():
                    add(ev, False)
            for ev in b.readers.values():
                add(ev, False)
        E = self.e[eng]
        waits = []
        for key, (sem, val) in need.items():
            if E["waited"].get(key, 0) >= val:
                continue
            E["waited"][key] = val
            waits.append((key, sem, val))
        return waits

    def _update(self, ev, reads, writes):
        key = ev[0]
        for b in reads:
            old = b.readers.get(key)
            if old is None or old[2] < ev[2]:
                b.readers[key] = ev
        for b in writes:
            if b.last_w is None:
                b.last_w = {}
            b.last_w[key] = ev
            b.readers = {}

    def op(self, eng, fn, reads=(), writes=(), attach=True):
        E = self.e[eng]
        waits = self._collect(eng, reads, writes)
        E["count"] += 1
        ev = (eng, E["sem"], E["count"])
        E["ops"].append(dict(waits=waits, fn=fn, kind="op", idx=E["count"], attach=attach and eng != "pe"))
        self._update(ev, reads, writes)
        return ev

    def dma(self, fn, owner, reads=(), writes=(), queue="sp"):
        Q = self.e[queue]
        waits = self._collect(queue, reads, writes)
        owner.dcount += 16
        ev = ("d_" + owner.name, owner.dsem, owner.dcount)
        Q["ops"].append(dict(waits=waits, fn=fn, kind="dma", inc=(owner.dsem, 16), attach=True))
        self._update(ev, reads, writes)
        return ev

    def final_wait(self, eng, bufs):
        waits = self._collect(eng, bufs, bufs)
        self.e[eng]["ops"].append(dict(waits=waits, fn=None, kind="wait", attach=False))

    def emit(self):
        nc = self.nc
        waited = {n: set() for n in self.COMPUTE}
        for n in self.ENG:
            for o in self.e[n]["ops"]:
                for key, sem, val in o["waits"]:
                    if key in waited:
                        waited[key].add(val)
        rank = {}
        for n in self.COMPUTE:
            rank[n] = {v: i + 1 for i, v in enumerate(sorted(waited[n]))}
        with nc.Block() as block:
            def run(name):
                def body(e):
                    for o in self.e[name]["ops"]:
                        ws = []
                        for key, sem, val in o["waits"]:
                            ws.append((sem, rank[key][val] if key in rank else val))
                        fn = o["fn"]
                        att = None
                        if fn is not None and o["attach"] and ws:
                            att = ws.pop()
                        for sem, val in ws:
                            e.wait_ge(sem, val)
                        if fn is None:
                            continue
                        r = fn(e)
                        first, last = r if isinstance(r, tuple) else (r, r)
                        if att is not None:
                            first._wait_ge(att[0], att[1])
                        if o["kind"] == "dma":
                            last.then_inc(o["inc"][0], o["inc"][1])
                        elif o["idx"] in waited[name]:
                            last.then_inc(self.e[name]["sem"], 1)
                return body
            block.tensor(run("pe"))
            block.scalar(run("act"))
            block.vector(run("dve"))
            block.gpsimd(run("pool"))
            block.sync(run("sp"))


class K:
    def __init__(self, NT, dbg=None):
        self.NT = NT
        self.NTL = NT // T
        self.dbg = dbg

    def act(self, out, in_, func, reads, writes, bias=None, scale=None, accum=None):
        kw = {}
        if bias is not None:
            kw["bias"] = bias
        if scale is not None:
            kw["scale"] = scale
        if accum is not None:
            kw["accum_out"] = accum
        return self.S.op("act", lambda e: e.activation(out=out, in_=in_, func=func, **kw), reads, writes,
                         attach=(accum is None))

    def tt(self, eng, out, in0, in1, op, reads, writes):
        return self.S.op(eng, lambda e: e.tensor_tensor(out=out, in0=in0, in1=in1, op=op), reads, writes)

    def ts(self, eng, out, in0, s1, s2, op0, op1, reads, writes):
        if op1 is None:
            return self.S.op(eng, lambda e: e.tensor_scalar(out=out, in0=in0, scalar1=s1, scalar2=None, op0=op0), reads, writes)
        return self.S.op(eng, lambda e: e.tensor_scalar(out=out, in0=in0, scalar1=s1, scalar2=s2, op0=op0, op1=op1), reads, writes)

    def stt(self, out, in0, scalar, in1, op0, op1, reads, writes):
        return self.S.op("dve", lambda e: e.scalar_tensor_tensor(out=out, in0=in0, scalar=scalar, in1=in1, op0=op0, op1=op1), reads, writes)

    def cp(self, eng, out, in_, reads, writes):
        if eng == "act":
            return self.S.op("act", lambda e: e.activation(out=out, in_=in_, func=AF.Copy), reads, writes)
        return self.S.op(eng, lambda e: e.tensor_copy(out=out, in_=in_), reads, writes)

    def mm(self, out, pairs, reads, writes):
        def fn(e):
            n = len(pairs)
            ins = None
            for i, (l, r) in enumerate(pairs):
                ins = e.matmul(out, l, r, start=(i == 0), stop=(i == n - 1))
            return ins
        return self.S.op("pe", fn, reads, writes)

    def mm_multi(self, groups, reads, writes):
        def fn(e):
            ins = None
            for out, pairs in groups:
                n = len(pairs)
                for i, (l, r) in enumerate(pairs):
                    ins = e.matmul(out, l, r, start=(i == 0), stop=(i == n - 1))
            return ins
        return self.S.op("pe", fn, reads, writes)

    def trs(self, items, reads, writes):
        ident = self.IDENT

        def fn(e):
            ins = None
            for out, in_ in items:
                ins = e.transpose(out=out, in_=in_, identity=ident[:])
            return ins
        return self.S.op("pe", fn, reads, writes)

    def load(self, out, in_, owner, reads=(), writes=None):
        if writes is None:
            writes = [owner]
        return self.S.dma(lambda e: e.dma_start(out=out, in_=in_, allow_slow_non_contiguous=True), owner, reads, writes)

    def store(self, out, in_, owner, reads, writes=()):
        return self.S.dma(lambda e: e.dma_start(out=out, in_=in_, allow_slow_non_contiguous=True), owner, reads, writes)

    def build(self):
        NT, NTL = self.NT, self.NTL
        nc = bass.Bass("TRN2", target_bir_lowering=False)
        self.nc = nc

        def din(name, shape, dt=F32):
            return nc.dram_tensor(name, list(shape), dt, kind="ExternalInput").ap()

        def dscr(name, shape, dt):
            return nc.dram_tensor(name, list(shape), dt, kind="Internal").ap()

        I = {}
        I["x_own"] = din("x_own", [NT, D])
        I["x_b"] = din("x_b", [4 * NT, D])
        I["x_halo"] = din("x_halo", [P, D])
        I["ctx_b"] = din("ctx_b", [256, D])
        I["s_in"] = din("s_in", [P, 32])
        I["sel"] = din("sel", [P, 12])
        I["w_mod_l"] = din("w_mod_l", [24, P, 8192])
        I["bmodF"] = din("bmodF", [P, 96])
        I["bmodR"] = din("bmodR", [P, 4096])
        I["gF"] = din("gF", [P, 32])
        I["fgR"] = din("fgR", [P, D])
        I["w_in_l"] = din("w_in_l", [8, P, 8192])
        I["w_out_l"] = din("w_out_l", [4, P, 8192])
        I["wq_l"] = din("wq_l", [4, P, 8192])
        I["uT_l"] = din("uT_l", [32, P, 8192])
        I["v_l"] = din("v_l", [64, P, 4096])
        I["conv_l"] = din("conv_l", [P, 40])
        I["lru_w"] = din("lru_w", [P, 4096])
        I["lru_v"] = din("lru_v", [P, 48])
        I["wsT_l"] = din("wsT_l", [P, 1024])
        I["bsR"] = din("bsR", [P, 1024])
        I["kT_l"] = din("kT_l", [P, 2048])
        self.I = I
        OUT = nc.dram_tensor("out", [NT, D], F32, kind="ExternalOutput").ap()
        self.OUT = OUT
        if self.dbg:
            self.DBG = nc.dram_tensor("dbg", [P, 4096], F32, kind="ExternalOutput").ap()
        Sc = {}
        Sc["w_in_b"] = dscr("w_in_b", [8, P, 8192], BF16)
        Sc["w_out_b"] = dscr("w_out_b", [4, P, 8192], BF16)
        Sc["wq_b"] = dscr("wq_b", [4, P, 8192], BF16)
        Sc["uT_b"] = dscr("uT_b", [32, P, 8192], BF16)
        Sc["v_b"] = dscr("v_b", [64, P, 4096], BF16)
        Sc["xc_st"] = dscr("xc_st", [P, 8, NT], F32)
        Sc["hf_st"] = dscr("hf_st", [P, 8, NT], F32)
        Sc["x1_st"] = dscr("x1_st", [NT, D], F32)
        Sc["g1row"] = dscr("g1row", [P, D], F32)
        Sc["g2row"] = dscr("g2row", [P, D], F32)
        self.Sc = Sc

        with ExitStack() as st:
            S = Sched(nc, st)
            self.S = S

            def sb(name, shape, dt):
                return st.enter_context(nc.sbuf_tensor(name, list(shape), dt))

            self.FSA = sb("FSA", [P, 10 * 2048], F32)
            self.fsb = [S.buf("fs%d" % i, dma=True) for i in range(10)]
            self.HNT = sb("HNT", [P, KC, T], BF16)
            self.hntb = [S.buf("hnt%d" % i) for i in range(4)]
            self.YQ = sb("YQ", [P, KC, T], BF16)
            self.yqb = S.buf("yq")
            self.UG = sb("UG", [P, 8, T], BF16)
            self.ugb = S.buf("ug")
            self.VGN = sb("VGN", [P, 4, 1024], BF16)
            self.vgnb = S.buf("vgn")
            self.XCB = sb("XCB", [P, 4, T], BF16)
            self.xcbb = S.buf("xcb")
            self.XN = sb("XN", [P, 2048], BF16)
            self.xnb = S.buf("xn")
            self.WB = sb("WB", [P, 3, 8192], BF16)
            self.wbb = [S.buf("wb%d" % i, dma=True) for i in range(3)]
            self.vhb = [S.buf("vh0", dma=True), S.buf("vh1", dma=True)]
            self.GT = sb("GT", [P, 2, T], BF16)
            self.gtb = [S.buf("gt0"), S.buf("gt1")]
            self.IDENT = sb("IDENT", [P, P], BF16)
            self.SMW = sb("SMW", [P, 4096], BF16)
            self.smwb = S.buf("smw")
            self.WST = sb("WST", [P, 1024], BF16)
            self.CST = sb("CST", [P, 512], F32)
            self.cstb = S.buf("cst", dma=True)
            self.SMALL = sb("SMALL", [P, 512], F32)
            self.smallb = S.buf("small")
            self.EPSC = sb("EPSC", [P, 2], F32)
            self.STT = sb("STT", [P, 128], F32)
            self.sttb = S.buf("stt")
            self.PF = [st.enter_context(nc.psum_tensor("PF%d" % i, [P, 512], F32)) for i in range(6)]
            self.pfb = [S.buf("pf%d" % i) for i in range(6)]
            self.PB = [st.enter_context(nc.psum_tensor("PB%d" % i, [P, 1024], BF16)) for i in range(2)]
            self.pbb = [S.buf("pb%d" % i) for i in range(2)]
            self.pfi = 0
            self.pbi = 0
            self.outb = S.buf("outd", dma=True)
            self.scrb = {k: S.buf("scr_" + k, dma=True) for k in Sc}

            self.phase_consts()
            self.phase_mods()
            self.phase_convert()
            self.phase_chains()
            self.phase_own_f()
            self.phase_own_r()

            allb = self.fsb + self.wbb + [self.outb, self.cstb] + list(self.scrb.values())
            S.final_wait("sp", allb)
            S.emit()
        return nc

    def FS(self, i, n=2048, off=0):
        return self.FSA[:, i * 2048 + off: i * 2048 + off + n]

    def FS3(self, i, a, b, off=0):
        return self.FSA[:, i * 2048 + off: i * 2048 + off + a * b].rearrange("p (a b) -> p a b", a=a)

    def next_pf(self):
        i = self.pfi
        self.pfi = (self.pfi + 1) % 6
        return self.PF[i], self.pfb[i]

    def next_pb(self):
        i = self.pbi
        self.pbi = (self.pbi + 1) % 2
        return self.PB[i], self.pbb[i]

    def wb3(self, s):
        return self.WB[:, s, :].rearrange("p (k n) -> p k n", k=KC)

    C_CONV = 0
    C_LV = 40
    C_CL = 88
    C_CL2 = 104
    C_SEL = 120
    C_GF = 132
    C_SIN = 164
    C_MODF = 196
    C_A1 = 324
    C_B1 = 356
    C_A2 = 388
    C_B2 = 404
    C_BMF = 420
    C_END = 484

    def C(self, off, n):
        return self.CST[:, off:off + n]

    def phase_consts(self):
        S, I = self.S, self.I
        cb = self.cstb
        S.op("dve", lambda e: e.memset(self.EPSC[:, :], EPS), [], [self.smallb])
        idf = self.FS(0, 128)
        S.op("pool", lambda e: e.iota(idf, pattern=[[1, 128]], base=0, channel_multiplier=-1,
                                      allow_small_or_imprecise_dtypes=True), [], [self.fsb[0]])
        S.op("dve", lambda e: e.tensor_single_scalar(out=self.IDENT[:], in_=idf, scalar=0.0, op=ALU.is_equal),
             [self.fsb[0]], [self.smwb])
        self.load(self.C(self.C_CONV, 40), I["conv_l"][:, :], cb)
        self.load(self.C(self.C_LV, 48), I["lru_v"][:, :], cb)
        self.load(self.C(self.C_SEL, 12), I["sel"][:, :], cb)
        self.load(self.C(self.C_GF, 32), I["gF"][:, :], cb)
        self.load(self.C(self.C_SIN, 32), I["s_in"][:, :], cb)
        for jj, j in enumerate((0, 1, 3, 4)):
            self.load(self.C(self.C_BMF + 16 * jj, 16), I["bmodF"][:, 16 * j:16 * j + 16], cb)
        lam = self.C(self.C_LV + 32, 16)
        cl = self.C(self.C_CL, 16)
        cl2 = self.C(self.C_CL2, 16)
        self.act(cl, lam, AF.Exp, [cb], [cb], scale=-1.0)
        self.act(cl, cl, AF.Ln, [cb], [cb], bias=1.0)
        self.ts("dve", cl2, cl, -16.0, None, ALU.mult, None, [cb], [cb])
        self.ts("dve", cl, cl, -8.0, None, ALU.mult, None, [cb], [cb])
        sin = self.C(self.C_SIN, 32)
        self.act(sin, sin, AF.Silu, [cb], [cb])
        self.load(self.FS(0, 4096), I["lru_w"][:, :], self.fsb[0], writes=[self.fsb[0], self.fsb[1]])
        self.cp("dve", self.SMW[:, :], self.FS(0, 4096), [self.fsb[0], self.fsb[1]], [self.smwb])
        self.load(self.FS(2, 1024), I["wsT_l"][:, :], self.fsb[2])
        self.cp("dve", self.WST[:, :], self.FS(2, 1024), [self.fsb[2]], [self.smwb])

    def phase_mods(self):
        S, I, Sc = self.S, self.I, self.Sc
        cb = self.cstb
        sin3 = self.C(self.C_SIN, 32).rearrange("p (k j) -> p k j", k=KC)
        srep = self.FS3(8, KC, 128)
        self.cp("dve", srep, sin3[:, :, 0:1].to_broadcast([P, KC, 128]), [cb], [self.fsb[8]])
        modf = self.C(self.C_MODF, 128).rearrange("p (j k c) -> p j k c", j=4, k=KC)
        bmf = self.C(self.C_BMF, 64).rearrange("p (j k) -> p j k", j=4)
        fm = {0: 0, 1: 1, 3: 2, 4: 3}
        rowdst = {2: ("g1row", Sc["g1row"]), 5: ("g2row", Sc["g2row"])}
        rb = 0
        for pi in range(24):
            j = pi // 4
            s0 = 4 * (pi % 2)
            bufs = self.fsb[s0:s0 + 4]
            pan = self.FSA[:, s0 * 2048:(s0 + 4) * 2048].rearrange("p (k n) -> p k n", k=KC)
            self.load(self.FSA[:, s0 * 2048:(s0 + 4) * 2048], I["w_mod_l"][pi], bufs[0], writes=bufs)
            if j in fm:
                pf, pfb = self.next_pf()
                groups = []
                for cc in range(4):
                    groups.append((pf[:, 2 * cc:2 * cc + 2],
                                   [(pan[:, kc, cc * 128:(cc + 1) * 128], sin3[:, kc, :]) for kc in range(KC)]))
                self.mm_multi(groups, bufs + [cb], [pfb])
                jj = fm[j]
                kc0 = 4 * (pi % 4)
                self.tt("dve", modf[:, jj, kc0:kc0 + 4, :], pf[:, 0:8].rearrange("p (c j) -> p c j", c=4),
                        bmf[:, jj, kc0:kc0 + 4].unsqueeze(2).to_broadcast([P, 4, 2]), ALU.add, [pfb, cb], [cb])
            else:
                pf, pfb = self.next_pf()
                self.mm(pf[:, :], [(srep[:, kc, :], pan[:, kc, :]) for kc in range(KC)], bufs + [self.fsb[8]], [pfb])
                ds = pi % 4
                which = 0 if j == 2 else 1
                bm = self.FS(9, 512, off=512 * (rb % 2))
                rt = self.FS(9, 512, off=1024 + 512 * (rb % 2))
                rb += 1
                self.load(bm, I["bmodR"][:, which * 2048 + ds * 512: which * 2048 + ds * 512 + 512], self.fsb[9])
                self.tt("dve", rt, pf[:, :], bm, ALU.add, [pfb, self.fsb[9]], [self.fsb[9]])
                nm, dst = rowdst[j]
                self.store(dst[:, ds * 512:(ds + 1) * 512], rt, self.scrb[nm], [self.fsb[9]], [self.scrb[nm]])
        gf = self.C(self.C_GF, 32)
        a1 = self.C(self.C_A1, 32).rearrange("p (k c) -> p k c", k=KC)
        b1 = self.C(self.C_B1, 32).rearrange("p (k c) -> p k c", k=KC)
        self.ts("dve", a1, modf[:, 1, :, :], 1.0, None, ALU.add, None, [cb], [cb])
        self.tt("dve", a1, a1, gf[:, 0:16].unsqueeze(2).to_broadcast([P, KC, 2]), ALU.mult, [cb], [cb])
        self.cp("dve", b1, modf[:, 0, :, :], [cb], [cb])
        a2 = self.C(self.C_A2, 16)
        b2 = self.C(self.C_B2, 16)
        self.ts("dve", a2, modf[:, 3, :, 0], 1.0, None, ALU.add, None, [cb], [cb])
        self.tt("dve", a2, a2, gf[:, 16:32], ALU.mult, [cb], [cb])
        self.cp("dve", b2, modf[:, 2, :, 0], [cb], [cb])

    def phase_convert(self):
        I, Sc = self.I, self.Sc
        jobs = []
        for nm_s, nm_d, n, F in (("w_in_l", "w_in_b", 8, 8192), ("w_out_l", "w_out_b", 4, 8192),
                                 ("wq_l", "wq_b", 4, 8192), ("uT_l", "uT_b", 32, 8192), ("v_l", "v_b", 64, 4096)):
            for i in range(n):
                for h in range(F // 4096):
                    jobs.append((I[nm_s][i][:, h * 4096:(h + 1) * 4096], Sc[nm_d][i][:, h * 4096:(h + 1) * 4096], nm_d))
        for ji, (src, dst, nm) in enumerate(jobs):
            s0 = 2 * (ji % 3)
            bufs = self.fsb[s0:s0 + 2]
            w = ji % 3
            half = self.WB[:, w, 0:4096]
            self.load(self.FSA[:, s0 * 2048:(s0 + 2) * 2048], src, bufs[0], writes=bufs)
            eng = "dve" if ji % 2 == 0 else "act"
            self.cp(eng, half, self.FSA[:, s0 * 2048:(s0 + 2) * 2048], bufs, [self.wbb[w]])
            self.store(dst, half, self.wbb[w], [self.wbb[w]], [self.scrb[nm]])

    def norm_tt(self, xs, xsb, tt, A, B, areads, hn=None):
        HN, hnb = hn if hn is not None else (self.HNT, self.hntb)
        sm = self.smallb
        ssq = self.SMALL[:, 0:1]
        rstd = self.SMALL[:, 1:2]
        self.act(self.XN[:, :], xs, AF.Square, [xsb], [self.xnb, sm], accum=ssq)
        self.act(rstd, ssq, AF.Ln, [sm], [sm], scale=1.0 / D, bias=self.EPSC[:, 0:1])
        self.act(rstd, rstd, AF.Exp, [sm], [sm], scale=-0.5)
        self.act(self.XN[:, :], xs, AF.Copy, [xsb, sm], [self.xnb], scale=rstd)
        for half in range(2):
            pb, pbb = self.next_pb()
            self.trs([(pb[:, i * 128:(i + 1) * 128], self.XN[:, (half * 8 + i) * 128:(half * 8 + i + 1) * 128])
                      for i in range(8)], [self.xnb], [pbb])
            dst = HN[:, half * 8:half * 8 + 8, tt * 128:(tt + 1) * 128]
            pv = pb[:, :].rearrange("p (k n) -> p k n", k=8)
            self.tt("dve", dst, pv, A[:, half * 8:half * 8 + 8].unsqueeze(2).to_broadcast([P, 8, 128]), ALU.mult,
                    [pbb] + areads, [hnb[tt]])
            self.tt("pool", dst, dst, B[:, half * 8:half * 8 + 8].unsqueeze(2).to_broadcast([P, 8, 128]), ALU.add,
                    [hnb[tt]] + areads, [hnb[tt]])

    def norm_rows(self, rows_fn, ntt, A, B, hn=None, tts=None):
        for tt in (range(ntt) if tts is None else tts):
            s = tt % 2
            self.load(self.FS(s), rows_fn(tt), self.fsb[s])
            self.norm_tt(self.FS(s), self.fsb[s], tt, A, B, [self.cstb], hn=hn)

    def xb_half(self, hc, ncur, wxs, mode, d, scan_rng, snap_col=None, snap_idx=None, stash=None, from_stash=None,
                hn=None, filler=None):
        cb, sb_ = self.cstb, self.sttb
        nwin = ncur
        XC = self.FS3(2, 4, T)
        R = self.FS3(3, 4, T)
        Ii = self.FS3(4, 4, T)
        TM = self.FS3(5, 4, T)
        H = self.FS3(6, 4, T)
        fb = self.fsb
        conv = self.C(self.C_CONV, 40)
        if mode is not None:
            EXT = self.FSA[:, 7 * 2048: 7 * 2048 + 4 * 516].rearrange("p (c n) -> p c n", c=4)
            eb = [fb[7], fb[8]]
            sav = self.STT[:, 80:104].rearrange("p (c n) -> p c n", c=8)
            co = 3 if mode == "f" else 0
            for c in range(4):
                oc = 4 * hc + c
                pf, pfb = self.next_pf()
                wt = self.wb3(wxs[oc // 4])
                HN, hnb = hn if hn is not None else (self.HNT, self.hntb)
                self.mm(pf[:, 0:ncur], [(wt[:, kc, (oc % 4) * 128:(oc % 4 + 1) * 128], HN[:, kc, 0:ncur])
                                        for kc in range(KC)], hnb + [self.wbb[wxs[oc // 4]]], [pfb])
                self.cp("act", EXT[:, c, co:co + ncur], pf[:, 0:ncur], [pfb], eb)
            if mode == "f":
                self.cp("dve", EXT[:, :, 0:3], sav[:, 4 * hc:4 * hc + 4, :], [sb_], eb)
                self.cp("dve", sav[:, 4 * hc:4 * hc + 4, :], EXT[:, :, ncur:ncur + 3], eb, [sb_])
            else:
                self.cp("dve", EXT[:, :, ncur:ncur + 3], sav[:, 4 * hc:4 * hc + 4, :], [sb_], eb)
                self.cp("dve", sav[:, 4 * hc:4 * hc + 4, :], EXT[:, :, 0:3], eb, [sb_])
            for c in range(4):
                oc = 4 * hc + c
                w = lambda k: conv[:, oc * 5 + k: oc * 5 + k + 1]
                self.ts("pool", XC[:, c, 0:nwin], EXT[:, c, 0:nwin], w(0), w(4), ALU.mult, ALU.add, eb + [cb], [fb[2]])
                for k in range(1, 4):
                    self.stt(XC[:, c, 0:nwin], EXT[:, c, k:k + nwin], w(k), XC[:, c, 0:nwin], ALU.mult, ALU.add,
                             eb + [cb, fb[2]], [fb[2]])
        else:
            self.load(XC[:, :, :], from_stash, fb[2], reads=[self.scrb["xc_st"]])
        if filler is not None:
            filler()
        return self._xb_gates(hc, nwin, d, scan_rng, snap_col, snap_idx)

    def load_wx(self, tiles, slots, src):
        for tno, s in zip(tiles, slots):
            self.load(self.WB[:, s, :], src[tno], self.wbb[s], reads=[self.scrb["w_in_b"]])

    def zero_sav(self):
        self.S.op("dve", lambda e: e.memset(self.STT[:, 80:104], 0.0), [], [self.sttb])

    def phase_chains(self):
        NT, NTL = self.NT, self.NTL
        I, Sc = self.I, self.Sc
        a1 = self.C(self.C_A1, 32).rearrange("p (k c) -> p k c", k=KC)
        b1 = self.C(self.C_B1, 32).rearrange("p (k c) -> p k c", k=KC)
        self.load_wx([0, 1], [0, 1], Sc["w_in_b"])
        self.S.op("dve", lambda e: e.memset(self.STT[:, :], 0.0), [], [self.sttb])
        self.S.op("pool", lambda e: e.memset(self.HNT[:, :, 256:384], 0.0), [], self.hntb)
        for d in (0, 1):
            self.zero_sav()
            self.norm_rows(lambda tt: I["ctx_b"][tt * 128:(tt + 1) * 128, :], 2, a1[:, :, 1], b1[:, :, 1])
            self.S.op("pool", lambda e: e.memset(self.HNT[:, :, 256:257], 0.0), [], self.hntb)
            for hc in (0, 1):
                self.xb_half(hc, 257, [0, 1], "f", d, (1, 257))
            base = 16 if d == 0 else 48
            idx = 0 if d == 0 else 3
            self.cp("dve", self.STT[:, base + 8 * idx: base + 8 * idx + 8], self.STT[:, 8 * d:8 * d + 8],
                    [self.sttb], [self.sttb])
        sel = self.C(self.C_SEL, 12)
        self.norm_rows(lambda tt: I["x_halo"][:, :], 1, a1[:, :, 0], b1[:, :, 0])
        sav = self.STT[:, 80:104].rearrange("p (c n) -> p c n", c=8)
        hal = self.SMALL[:, 16:48].rearrange("p (c n) -> p c n", c=8)
        for oc in range(8):
            pf, pfb = self.next_pf()
            wt = self.wb3(oc // 4)
            self.mm(pf[:, 0:128], [(wt[:, kc, (oc % 4) * 128:(oc % 4 + 1) * 128], self.HNT[:, kc, 0:128])
                                   for kc in range(KC)], self.hntb + [self.wbb[oc // 4]], [pfb])
            self.tt("dve", hal[:, oc, :], pf[:, 0:4], sel[:, 8:12], ALU.mult, [pfb, self.cstb], [self.smallb])

        items = []
        for j in range(3 * NTL + 1):
            if j == 0:
                rng = (1, T)
            elif j == 3 * NTL:
                rng = (0, 1)
            else:
                rng = (0, T)
            snap = j // NTL if (j > 0 and j % NTL == 0) else None
            items.append(dict(rows=(lambda tt, j=j: I["x_b"][j * T + tt * 128: j * T + (tt + 1) * 128, :]),
                              mode="f", d=0, rng=rng, snap_col=0, snap_idx=snap,
                              pre=(self.zero_sav if j == 0 else None), post=None))
        for j in range(4 * NTL - 1, NTL - 2, -1):
            if j == 4 * NTL - 1:
                rng = (0, T - 2)
            elif j == NTL - 1:
                rng = (T - 2, T)
            else:
                rng = (0, T)
            snap = None
            if (j + 1) % NTL == 0 and j != 4 * NTL - 1:
                snap = (j + 1) // NTL - 1
            items.append(dict(rows=(lambda tt, j=j: I["x_b"][j * T + tt * 128: j * T + (tt + 1) * 128, :]),
                              mode="b", d=1, rng=rng, snap_col=T - 2, snap_idx=snap,
                              pre=(self.zero_sav if j == 4 * NTL - 1 else None), post=None))

        def own_pre():
            for d in (0, 1):
                base = 16 if d == 0 else 48
                stt = self.STT[:, 8 * d:8 * d + 8]
                self.ts("dve", stt, self.STT[:, base:base + 8], sel[:, 4 * d:4 * d + 1], None, ALU.mult, None,
                        [self.sttb, self.cstb], [self.sttb])
                for i in range(1, 4):
                    self.stt(stt, self.STT[:, base + 8 * i: base + 8 * i + 8], sel[:, 4 * d + i:4 * d + i + 1], stt,
                             ALU.mult, ALU.add, [self.sttb, self.cstb], [self.sttb])
            self.S.op("dve", lambda e: e.memset(self.STT[:, 80:104], 0.0), [], [self.sttb])
            self.cp("dve", sav[:, :, 1:3], hal[:, :, 0:2], [self.smallb], [self.sttb])

        def own_post(j):
            def f(hc, XC, H, rng):
                lo, hi = rng
                p0 = j * T - 1
                self.store(Sc["xc_st"][:, 4 * hc:4 * hc + 4, p0 + lo:p0 + hi], XC[:, :, lo:hi], self.fsb[2],
                           [self.fsb[2]], [self.scrb["xc_st"]])
                self.store(Sc["hf_st"][:, 4 * hc:4 * hc + 4, p0 + lo:p0 + hi], H[:, :, lo:hi], self.fsb[6],
                           [self.fsb[6]], [self.scrb["hf_st"]])
            return f

        for j in range(NTL):
            rng = (1, T) if j == 0 else (0, T)
            items.append(dict(rows=(lambda tt, j=j: I["x_own"][j * T + tt * 128: j * T + (tt + 1) * 128, :]),
                              mode="f", d=0, rng=rng, snap_col=None, snap_idx=None,
                              pre=(own_pre if j == 0 else None), post=own_post(j)))

        hntb2 = [self.S.buf("hn2_%d" % i) for i in range(4)]
        self.S.op("dve", lambda e: e.memset(self.EPSC[:, 1:2], 0.0), [], [self.yqb] + hntb2)
        HNS = [(self.HNT, self.hntb), (self.YQ, hntb2)]
        al, bl = a1[:, :, 0], b1[:, :, 0]
        self.norm_rows(items[0]["rows"], 4, al, bl, hn=HNS[0])
        for k, it in enumerate(items):
            hn = HNS[k % 2]
            nxt = items[k + 1] if k + 1 < len(items) else None
            if it["pre"] is not None:
                it["pre"]()
            for hc in (0, 1):
                filler = None
                if nxt is not None:
                    filler = (lambda nxt=nxt, hc=hc, k=k: self.norm_rows(
                        nxt["rows"], 4, al, bl, hn=HNS[(k + 1) % 2], tts=(2 * hc, 2 * hc + 1)))
                XC, H = self.xb_half(hc, T, [0, 1], it["mode"], it["d"], it["rng"], snap_col=it["snap_col"],
                                     snap_idx=it["snap_idx"], hn=hn, filler=filler)
                if it["post"] is not None:
                    it["post"](hc, XC, H, it["rng"])
        for hc in (0, 1):
            XC, H = self.xb_half_virtual(hc, hal, (0, 1))
            own_post(NTL)(hc, XC, H, (0, 1))
        self.S.op("dve", lambda e: e.memset(self.EPSC[:, 1:2], 0.0), [], [self.yqb] + hntb2)

    def phase_own_f(self):
        return

    def xb_half_virtual(self, hc, hal, rng):
        cb, sb_ = self.cstb, self.sttb
        fb = self.fsb
        EXT = self.FSA[:, 7 * 2048: 7 * 2048 + 4 * 516].rearrange("p (c n) -> p c n", c=4)
        eb = [fb[7], fb[8]]
        sav = self.STT[:, 80:104].rearrange("p (c n) -> p c n", c=8)
        XC = self.FS3(2, 4, T)
        conv = self.C(self.C_CONV, 40)
        n = 8
        self.S.op("dve", lambda e: e.memset(EXT[:, :, 0:16], 0.0), [], eb)
        self.cp("dve", EXT[:, :, 0:3], sav[:, 4 * hc:4 * hc + 4, :], [sb_], eb)
        self.cp("dve", EXT[:, :, 3:4], hal[:, 4 * hc:4 * hc + 4, 2:3], [self.smallb], eb)
        for c in range(4):
            oc = 4 * hc + c
            w = lambda k: conv[:, oc * 5 + k: oc * 5 + k + 1]
            self.ts("pool", XC[:, c, 0:n], EXT[:, c, 0:n], w(0), w(4), ALU.mult, ALU.add, eb + [cb], [fb[2]])
            for k in range(1, 4):
                self.stt(XC[:, c, 0:n], EXT[:, c, k:k + n], w(k), XC[:, c, 0:n], ALU.mult, ALU.add,
                         eb + [cb, fb[2]], [fb[2]])
        return self.xb_tail(hc, n, 0, rng)

    def xb_tail(self, hc, nwin, d, scan_rng):
        return self._xb_gates(hc, nwin, d, scan_rng)

    def _xb_gates(self, hc, nwin, d, scan_rng, snap_col=None, snap_idx=None):
        cb, sb_ = self.cstb, self.sttb
        fb = self.fsb
        XC = self.FS3(2, 4, T)
        R = self.FS3(3, 4, T)
        Ii = self.FS3(4, 4, T)
        TM = self.FS3(5, 4, T)
        H = self.FS3(6, 4, T)
        self.cp("pool", self.XCB[:, :, 0:nwin], XC[:, :, 0:nwin], [fb[2]], [self.xcbb])
        lw = self.SMW[:, :].rearrange("p (m d h j) -> p m d h j", m=2, d=2, h=8)
        lv = self.C(self.C_LV, 48).rearrange("p (m d h) -> p m d h", m=3, d=2)
        for m, dstt, dbuf in ((0, R, fb[3]), (1, Ii, fb[4])):
            for c in range(4):
                oc = 4 * hc + c
                pf, pfb = self.next_pf()
                self.mm(pf[:, 0:nwin], [(lw[:, m, d, oc, :], self.XCB[:, c, 0:nwin])], [self.xcbb, self.smwb], [pfb])
                self.act(dstt[:, c, 0:nwin], pf[:, 0:nwin], AF.Sigmoid, [pfb, cb], [dbuf], bias=lv[:, m, d, oc:oc + 1])
        cl = self.C(self.C_CL, 16).rearrange("p (d h) -> p d h", d=2)
        cl2 = self.C(self.C_CL2, 16).rearrange("p (d h) -> p d h", d=2)
        for c in range(4):
            oc = 4 * hc + c
            self.act(TM[:, c, 0:nwin], R[:, c, 0:nwin], AF.Exp, [fb[3], cb], [fb[5]], scale=cl2[:, d, oc:oc + 1])
            self.act(R[:, c, 0:nwin], R[:, c, 0:nwin], AF.Exp, [fb[3], cb], [fb[3]], scale=cl[:, d, oc:oc + 1])
        self.act(TM[:, :, 0:nwin], TM[:, :, 0:nwin], AF.Ln, [fb[5]], [fb[5]], scale=-1.0, bias=1.0)
        self.act(TM[:, :, 0:nwin], TM[:, :, 0:nwin], AF.Exp, [fb[5]], [fb[5]], scale=0.5)
        self.tt("dve", Ii[:, :, 0:nwin], Ii[:, :, 0:nwin], TM[:, :, 0:nwin], ALU.mult, [fb[4], fb[5]], [fb[4]])
        self.tt("dve", Ii[:, :, 0:nwin], Ii[:, :, 0:nwin], XC[:, :, 0:nwin], ALU.mult, [fb[4], fb[2]], [fb[4]])
        lo, hi = scan_rng
        stt = self.STT[:, 8 * d:8 * d + 8]
        for c in range(4):
            oc = 4 * hc + c
            if d == 0:
                o, a, b = H[:, c, lo:hi], R[:, c, lo:hi], Ii[:, c, lo:hi]
            else:
                o = H[:, c, lo:hi][:, ::-1]
                a = R[:, c, lo:hi][:, ::-1]
                b = Ii[:, c, lo:hi][:, ::-1]
            ini = stt[:, oc:oc + 1]
            self.S.op("dve", (lambda o=o, a=a, b=b, ini=ini: (lambda e: e.tensor_tensor_scan(
                out=o, data0=a, data1=b, initial=ini, op0=ALU.mult, op1=ALU.add)))(),
                [fb[3], fb[4], sb_], [fb[6]])
        last = hi - 1 if d == 0 else lo
        self.cp("dve", stt[:, 4 * hc:4 * hc + 4], H[:, :, last], [fb[6]], [sb_])
        if snap_idx is not None:
            base = 16 if d == 0 else 48
            sn = self.STT[:, base + 8 * snap_idx + 4 * hc: base + 8 * snap_idx + 4 * hc + 4]
            self.cp("dve", sn, H[:, :, snap_col], [fb[6]], [sb_])
        return XC, H

    def wstream(self, srcs):
        st = {"i": 0, "n": len(srcs), "srcs": srcs, "slot": getattr(self, "_wslot", 0)}
        return st

    def wload(self, src, scr_name, half=False):
        s = getattr(self, "_wslot", 0)
        self._wslot = (s + 1) % 3
        if half:
            self.load(self.WB[:, s, 0:4096], src, self.wbb[s], reads=[self.scrb[scr_name]])
        else:
            self.load(self.WB[:, s, :], src, self.wbb[s], reads=[self.scrb[scr_name]])
        return s

    def phase_own_r(self):
        NT, NTL = self.NT, self.NTL
        I, Sc = self.I, self.Sc
        cb = self.cstb
        fb = self.fsb
        a1 = self.C(self.C_A1, 32).rearrange("p (k c) -> p k c", k=KC)
        b1 = self.C(self.C_B1, 32).rearrange("p (k c) -> p k c", k=KC)
        a2 = self.C(self.C_A2, 16)
        b2 = self.C(self.C_B2, 16)
        self._wslot = 0
        for j in range(NTL - 1, -1, -1):
            t0 = j * T
            self.load(self.FS(0, 4096), I["lru_w"][:, :], fb[0], writes=[fb[0], fb[1]])
            self.cp("dve", self.SMW[:, :], self.FS(0, 4096), [fb[0], fb[1]], [self.smwb])
            self.norm_rows(lambda tt: I["x_own"][t0 + tt * 128: t0 + (tt + 1) * 128, :], 4, a1[:, :, 0], b1[:, :, 0])
            for hc in (0, 1):
                s = self.wload(Sc["w_in_b"][2 + hc], "w_in_b")
                XC = self.FS3(2, 4, T)
                self.load(XC[:, :, :], Sc["xc_st"][:, 4 * hc:4 * hc + 4, t0:t0 + T], fb[2], reads=[self.scrb["xc_st"]])
                XC, H = self._xb_gates(hc, T, 1, (0, T))
                HF = self.FS3(2, 4, T)
                self.load(HF[:, :, :], Sc["hf_st"][:, 4 * hc:4 * hc + 4, t0:t0 + T], fb[2], reads=[self.scrb["hf_st"]])
                self.tt("pool", H[:, :, :], H[:, :, :], HF[:, :, :], ALU.add, [fb[6], fb[2]], [fb[6]])
                wt = self.wb3(s)
                for c in range(4):
                    oc = 4 * hc + c
                    pf, pfb = self.next_pf()
                    self.mm(pf[:, :], [(wt[:, kc, c * 128:(c + 1) * 128], self.HNT[:, kc, :]) for kc in range(KC)],
                            self.hntb + [self.wbb[s]], [pfb])
                    g = c % 2
                    self.act(self.GT[:, g, :], pf[:, :], AF.Gelu_apprx_tanh, [pfb], [self.gtb[g]])
                    self.tt("dve", self.YQ[:, oc, :], self.GT[:, g, :], H[:, c, :], ALU.mult, [self.gtb[g], fb[6]], [self.yqb])
            for hc in (0, 1):
                s = self.wload(Sc["w_in_b"][4 + hc], "w_in_b")
                wt = self.wb3(s)
                for c in range(4):
                    pf, pfb = self.next_pf()
                    self.mm(pf[:, :], [(wt[:, kc, c * 128:(c + 1) * 128], self.HNT[:, kc, :]) for kc in range(KC)],
                            self.hntb + [self.wbb[s]], [pfb])
                    self.act(self.UG[:, 4 * hc + c, :], pf[:, :], AF.Gelu_apprx_tanh, [pfb], [self.ugb])
            sv = [self.wload(Sc["w_in_b"][6], "w_in_b"), self.wload(Sc["w_in_b"][7], "w_in_b")]
            VG = self.FS3(8, 2, 1024)
            SQ = self.FS(5, 1024)
            sm = self.smallb
            for tt in range(4):
                vg = VG[:, tt % 2, :]
                for cs in range(2):
                    wt = self.wb3(sv[cs])
                    pf, pfb = self.next_pf()
                    self.mm(pf[:, :], [(self.HNT[:, kc, tt * 128:(tt + 1) * 128], wt[:, kc, :]) for kc in range(KC)],
                            self.hntb + [self.wbb[sv[cs]]], [pfb])
                    self.act(vg[:, cs * 512:(cs + 1) * 512], pf[:, :], AF.Gelu_apprx_tanh, [pfb], [fb[8]])
                vg3 = vg.rearrange("p (g c) -> p g c", g=8)
                sq3 = SQ.rearrange("p (g c) -> p g c", g=8)
                su = self.SMALL[:, 64:72]
                ss = self.SMALL[:, 72:80]
                mean = self.SMALL[:, 80:88]
                var = self.SMALL[:, 88:96]
                nmr = self.SMALL[:, 96:104]
                self.S.op("dve", (lambda vg3=vg3: lambda e: e.tensor_reduce(out=su, in_=vg3, axis=AX.X, op=ALU.add))(),
                          [fb[8]], [sm])
                self.tt("pool", SQ, vg, vg, ALU.mult, [fb[8]], [fb[5]])
                self.S.op("dve", lambda e: e.tensor_reduce(out=ss, in_=sq3, axis=AX.X, op=ALU.add), [fb[5]], [sm])
                self.ts("dve", mean, su, 1.0 / 128, None, ALU.mult, None, [sm], [sm])
                self.tt("dve", var, mean, mean, ALU.mult, [sm], [sm])
                self.stt(var, ss, 1.0 / 128, var, ALU.mult, ALU.subtract, [sm], [sm])
                self.act(var, var, AF.Ln, [sm], [sm], bias=self.EPSC[:, 0:1])
                self.act(var, var, AF.Exp, [sm], [sm], scale=-0.5)
                self.tt("dve", nmr, mean, var, ALU.mult, [sm], [sm])
                self.tt("dve", sq3, vg3, var.unsqueeze(2).to_broadcast([P, 8, 128]), ALU.mult, [fb[8], sm], [fb[5]])
                self.tt("dve", self.VGN[:, tt, :].rearrange("p (g c) -> p g c", g=8), sq3,
                        nmr.unsqueeze(2).to_broadcast([P, 8, 128]), ALU.subtract, [fb[5], sm], [self.vgnb])
            self.load(self.FS(9, 1024), I["bsR"][:, :], fb[9])
            for g in range(8):
                pf, pfb = self.next_pf()
                groups = [(pf[:, tt * 128:(tt + 1) * 128],
                           [(self.VGN[:, tt, g * 128:(g + 1) * 128], self.WST[:, g * 128:(g + 1) * 128])])
                          for tt in range(4)]
                self.mm_multi(groups, [self.vgnb, self.smwb], [pfb])
                tmp = self.FS(5, 512, off=1024)
                self.tt("dve", tmp.rearrange("p (a b) -> p a b", a=4), pf[:, :].rearrange("p (a b) -> p a b", a=4),
                        self.FS(9, 128, off=g * 128).unsqueeze(1).to_broadcast([P, 4, 128]), ALU.add,
                        [pfb, fb[9]], [fb[5]])
                self.tt("dve", self.YQ[:, 8 + g, :], tmp, self.UG[:, g, :], ALU.mult, [fb[5], self.ugb], [self.yqb])
            self.load(self.FS(7), Sc["g1row"][:, :], fb[7], reads=[self.scrb["g1row"]])
            for tt in range(4):
                s = tt % 2
                xs = self.FS(s)
                self.load(xs, I["x_own"][t0 + tt * 128: t0 + (tt + 1) * 128, :], fb[s])
                for ds in range(4):
                    ws = self.wload(Sc["w_out_b"][ds], "w_out_b")
                    wt = self.wb3(ws)
                    pf, pfb = self.next_pf()
                    self.mm(pf[:, :], [(self.YQ[:, kc, tt * 128:(tt + 1) * 128], wt[:, kc, :]) for kc in range(KC)],
                            [self.yqb, self.wbb[ws]], [pfb])
                    tmp = self.FS(5, 512, off=1536)
                    self.tt("dve", tmp, pf[:, :], self.FS(7, 512, off=ds * 512), ALU.mult, [pfb, fb[7]], [fb[5]])
                    self.tt("pool", xs[:, ds * 512:(ds + 1) * 512], xs[:, ds * 512:(ds + 1) * 512], tmp, ALU.add,
                            [fb[5], fb[s]], [fb[s]])
                self.store(Sc["x1_st"][t0 + tt * 128: t0 + (tt + 1) * 128, :], xs, fb[s], [fb[s]], [self.scrb["x1_st"]])
                self.norm_tt(xs, fb[s], tt, a2, b2, [cb])
            if self.dbg == "x1":
                continue
            self.peer(j)
            self.final(j)
        if self.dbg == "x1":
            self.S.dma(lambda e: e.dma_start(out=self.OUT[:, :], in_=Sc["x1_st"][:, :]), self.outb,
                       [self.scrb["x1_st"]], [self.outb])

    def peer(self, j):
        I, Sc = self.I, self.Sc
        fb = self.fsb
        cb = self.cstb
        sm = self.smallb
        self.load(self.FS(8), I["kT_l"][:, :], fb[8])
        self.cp("dve", self.SMW[:, 0:2048], self.FS(8), [fb[8]], [self.smwb])
        kT = self.SMW[:, 0:2048].rearrange("p (a h e) -> p a h e", a=2, h=8)
        for qt in range(4):
            ws = self.wload(Sc["wq_b"][qt], "wq_b")
            wt = self.wb3(ws)
            for c in range(4):
                pf, pfb = self.next_pf()
                self.mm(pf[:, :], [(wt[:, kc, c * 128:(c + 1) * 128], self.HNT[:, kc, :]) for kc in range(KC)],
                        self.hntb + [self.wbb[ws]], [pfb])
                self.cp("act", self.YQ[:, 4 * qt + c, :], pf[:, :], [pfb], [self.yqb])
        for tt in range(4):
            ssb = self.FSA[:, (4 + tt) * 2048:(5 + tt) * 2048].rearrange("p (h a e) -> p h a e", h=8, a=2)
            for h2 in range(4):
                pf, pfb = self.next_pf()
                groups = []
                for hh in range(2):
                    h = 2 * h2 + hh
                    for a in range(2):
                        groups.append((pf[:, (2 * hh + a) * 128:(2 * hh + a + 1) * 128],
                                       [(self.YQ[:, 2 * h + a, tt * 128:(tt + 1) * 128], kT[:, a, h, :])]))
                self.mm_multi(groups, [self.yqb, self.smwb], [pfb])
                self.cp("act", ssb[:, 2 * h2:2 * h2 + 2, :, :],
                        pf[:, :].rearrange("p (h a e) -> p h a e", h=2, a=2), [pfb], [fb[4 + tt]])
        V1 = self.SMALL[:, 256:272]
        V2 = self.SMALL[:, 272:288]
        C16 = self.SMALL[:, 288:304]
        TMPK = self.FS(9, 256, off=3072 // 2)
        CAND = self.FS(9, 256, off=1792)
        for tt in range(4):
            ssb = self.FSA[:, (4 + tt) * 2048:(5 + tt) * 2048].rearrange("p (h a e) -> p h a e", h=8, a=2)
            for h in range(8):
                for a, V in ((0, V1), (1, V2)):
                    src = ssb[:, h, a, :]
                    self.S.op("dve", (lambda V=V, src=src: lambda e: e.max(out=V[:, 0:8], in_=src))(), [fb[4 + tt]], [sm])
                    self.S.op("dve", (lambda V=V, src=src: lambda e: e.match_replace(
                        out=TMPK[:, 0:128], in_to_replace=V[:, 0:8], in_values=src, imm_value=NEG))(),
                        [fb[4 + tt], sm], [fb[9]], attach=False)
                    self.S.op("dve", (lambda V=V: lambda e: e.max(out=V[:, 8:16], in_=TMPK[:, 0:128]))(), [fb[9]], [sm])
                self.tt("pool", CAND.rearrange("p (a b) -> p a b", a=16), V1.unsqueeze(2).to_broadcast([P, 16, 16]),
                        V2.unsqueeze(1).to_broadcast([P, 16, 16]), ALU.add, [sm], [fb[9]])
                self.S.op("dve", lambda e: e.max(out=C16[:, 0:8], in_=CAND), [fb[9]], [sm])
                self.S.op("dve", lambda e: e.match_replace(out=TMPK, in_to_replace=C16[:, 0:8], in_values=CAND,
                                                           imm_value=NEG), [fb[9], sm], [fb[9]], attach=False)
                self.S.op("dve", lambda e: e.max(out=C16[:, 8:16], in_=TMPK), [fb[9]], [sm])
                tau = self.SMALL[:, 128 + tt * 8 + h: 128 + tt * 8 + h + 1]
                nb = self.SMALL[:, 160 + tt * 8 + h: 160 + tt * 8 + h + 1]
                nm = self.SMALL[:, 304:305]
                zs = self.SMALL[:, 305:306]
                ex = self.SMALL[:, 312:328]
                self.cp("dve", tau, C16[:, 15:16], [sm], [sm])
                self.ts("dve", nm, C16[:, 0:1], -1.0, None, ALU.mult, None, [sm], [sm])
                self.act(ex, C16, AF.Exp, [sm], [sm], bias=nm, accum=zs)
                self.act(zs, zs, AF.Ln, [sm], [sm])
                self.tt("dve", nb, nm, zs, ALU.subtract, [sm], [sm])
        tau_all = self.SMALL[:, 128:160]
        nb_all = self.SMALL[:, 160:192]
        bias2 = self.SMALL[:, 192:224]
        self.ts("dve", tau_all, tau_all, -2.0e-5, None, ALU.add, None, [sm], [sm])
        self.tt("dve", bias2, tau_all, nb_all, ALU.add, [sm], [sm])
        for tt in range(4):
            ssb = self.FSA[:, (4 + tt) * 2048:(5 + tt) * 2048].rearrange("p (h a e) -> p h a e", h=8, a=2)
            self.tt("dve", ssb[:, :, 0, :], ssb[:, :, 0, :],
                    tau_all[:, tt * 8:(tt + 1) * 8].unsqueeze(2).to_broadcast([P, 8, 128]), ALU.subtract,
                    [fb[4 + tt], sm], [fb[4 + tt]])
        ZB = [self.FS(8), self.FS(9)]
        zbuf = [fb[8], fb[9]]
        yq2 = self.YQ[:, :, :].rearrange("p a n -> p (a n)")
        EB = [yq2[:, 0:2048], yq2[:, 2048:4096]]
        W2 = [yq2[:, 4096:6144], yq2[:, 6144:8192]]
        ebuf = [self.S.buf("eb0"), self.S.buf("eb1")]
        w2buf = [self.S.buf("w2b0"), self.S.buf("w2b1")]
        G2 = self.XCB[:, :, :].rearrange("p a n -> p (a n)")
        g2b = self.xcbb
        self.S.op("dve", lambda e: e.memset(yq2[:, 4096:4100], 0.0), [], [self.yqb, self.wbb[2]] + ebuf + w2buf + self.vhb)
        GA = self.XN[:, :].rearrange("p (a n) -> p a n", a=4)
        gab = self.xnb
        WACT = [self.UG, self.VGN[:, :, :].rearrange("p a n -> p (a n)").rearrange("p (c t) -> p c t", c=8)]
        wactb = [self.ugb, self.vgnb]
        OUTS = self.FSA[:, 0:4 * 2048].rearrange("p (t d) -> p t d", t=4)
        zi = 0
        vhalf = [0]

        def vstep(gv, ds, tt, vt, vtb):
            WTv = WACT[gv % 2]
            wtbv = wactb[gv % 2]
            pf, pfb = self.next_pf()
            self.mm(pf[:, :], [(WTv[:, c, tt * 128:(tt + 1) * 128], vt[:, c, :]) for c in range(8)],
                    [wtbv, vtb], [pfb])
            dst = OUTS[:, tt, ds * 512:(ds + 1) * 512]
            if gv == 0:
                self.cp("dve", dst, pf[:, :], [pfb], [fb[tt]])
            else:
                self.tt("dve", dst, dst, pf[:, :], ALU.add, [pfb, fb[tt]], [fb[tt]])

        def vload(gv, ds):
            hh = vhalf[0] % 2
            vhalf[0] += 1
            dstv = self.WB[:, 2, hh * 4096:(hh + 1) * 4096]
            self.load(dstv, Sc["v_b"][4 * gv + ds], self.vhb[hh], reads=[self.scrb["v_b"]])
            return dstv.rearrange("p (c n) -> p c n", c=8), self.vhb[hh]

        for g in range(17):
            pend = []
            if g > 0:
                for ds in range(4):
                    for tt in range(4):
                        pend.append((g - 1, ds, tt))
            vcur = {}
            if g == 16:
                for (gv, ds, tt) in pend:
                    if ds not in vcur:
                        vcur[ds] = vload(gv, ds)
                    vstep(gv, ds, tt, *vcur[ds])
                break
            WT_ = WACT[g % 2]
            wtb = wactb[g % 2]
            us = []
            for sl in range(2):
                self.load(self.WB[:, sl, :], Sc["uT_b"][2 * g + sl], self.wbb[sl], reads=[self.scrb["uT_b"]])
                us.append(sl)
            for tt in range(4):
                ssb = self.FSA[:, (4 + tt) * 2048:(5 + tt) * 2048].rearrange("p (h a e) -> p h a e", h=8, a=2)
                w2 = W2[tt % 2]
                w2b = w2buf[tt % 2]
                for hb in range(4):
                    z = ZB[zi % 2]
                    zb = zbuf[zi % 2]
                    eb = EB[zi % 2]
                    ebb = ebuf[zi % 2]
                    zi += 1
                    self.tt("dve", z.rearrange("p (j a b) -> p j a b", j=2, a=8),
                            ssb[:, 2 * hb:2 * hb + 2, 0, 8 * g:8 * g + 8].unsqueeze(3).to_broadcast([P, 2, 8, 128]),
                            ssb[:, 2 * hb:2 * hb + 2, 1, :].unsqueeze(2).to_broadcast([P, 2, 8, 128]), ALU.add,
                            [fb[4 + tt]], [zb])
                    for jh in range(2):
                        h = 2 * hb + jh
                        self.act(eb[:, jh * 1024:(jh + 1) * 1024], z[:, jh * 1024:(jh + 1) * 1024], AF.Exp,
                                 [zb, sm], [ebb], bias=bias2[:, tt * 8 + h: tt * 8 + h + 1])
                    if hb == 0:
                        self.stt(w2, z, 0.0, eb, ALU.is_ge, ALU.mult, [zb, ebb], [w2b])
                    else:
                        self.stt(G2, z, 0.0, eb, ALU.is_ge, ALU.mult, [zb, ebb], [g2b])
                        self.tt("dve", w2, w2, G2, ALU.add, [g2b, w2b], [w2b])
                    if pend:
                        gv, ds, ttv = pend.pop(0)
                        if ds not in vcur:
                            vcur[ds] = vload(gv, ds)
                        vstep(gv, ds, ttv, *vcur[ds])
                wcur = w2[:, 0:1024]
                self.tt("dve", wcur, wcur, w2[:, 1024:2048], ALU.add, [w2b], [w2b])
                for sl in range(2):
                    ut = self.wb3(us[sl])
                    pf, pfb = self.next_pf()
                    self.mm(pf[:, :], [(self.HNT[:, kc, tt * 128:(tt + 1) * 128], ut[:, kc, :]) for kc in range(KC)],
                            self.hntb + [self.wbb[us[sl]]], [pfb])
                    ga = GA[:, sl, :]
                    wa = GA[:, 2 + sl, :]
                    self.act(ga, pf[:, :], AF.Gelu_apprx_tanh, [pfb], [gab])
                    self.tt("dve", wa, ga, wcur[:, sl * 512:(sl + 1) * 512], ALU.mult, [gab, w2b], [gab])
                    pb, pbb = self.next_pb()
                    self.trs([(pb[:, c * 128:(c + 1) * 128], wa[:, c * 128:(c + 1) * 128]) for c in range(4)],
                             [gab], [pbb])
                    self.cp("act", WT_[:, 4 * sl:4 * sl + 4, tt * 128:(tt + 1) * 128],
                            pb[:, 0:512].rearrange("p (c t) -> p c t", c=4), [pbb], [wtb])
        self._wslot = 0
        self.S.op("dve", lambda e: e.memset(yq2[:, 4096:4100], 0.0), [], [self.yqb, self.wbb[2]] + ebuf + w2buf + self.vhb)

    def final(self, j):
        I, Sc = self.I, self.Sc
        fb = self.fsb
        sm = self.smallb
        t0 = j * T
        self.load(self.FS(5), Sc["g2row"][:, :], fb[5], reads=[self.scrb["g2row"]])
        self.load(self.FS(6), I["fgR"][:, :], fb[6])
        for tt in range(4):
            x1 = self.FS(4)
            self.load(x1, Sc["x1_st"][t0 + tt * 128: t0 + (tt + 1) * 128, :], fb[4], reads=[self.scrb["x1_st"]])
            o = self.FS(tt)
            self.tt("dve", o, o, self.FS(5), ALU.mult, [fb[tt], fb[5]], [fb[tt]])
            self.tt("pool", o, o, x1, ALU.add, [fb[tt], fb[4]], [fb[tt]])
            ssq = self.SMALL[:, 8:9]
            rstd = self.SMALL[:, 9:10]
            self.act(self.XN[:, :], o, AF.Square, [fb[tt]], [self.xnb, sm], accum=ssq)
            self.act(rstd, ssq, AF.Ln, [sm], [sm], scale=1.0 / D, bias=self.EPSC[:, 0:1])
            self.act(rstd, rstd, AF.Exp, [sm], [sm], scale=-0.5)
            self.stt(o, o, rstd, self.FS(6), ALU.mult, ALU.mult, [fb[tt], fb[6], sm], [fb[tt]])
            self.store(self.OUT[t0 + tt * 128: t0 + (tt + 1) * 128, :], o, fb[tt], [fb[tt]], [self.outb])


def _wtiles(w, ncols):
    K_, C = w.shape
    t = w.reshape(KC, P, C // ncols, ncols).transpose(2, 1, 0, 3)
    return np.ascontiguousarray(t).reshape(C // ncols, P, KC * ncols)


def _fm(v, n):
    return np.ascontiguousarray(v.reshape(n, P).T)


def prepare_inputs(inp, NT):
    f = lambda a: np.asarray(a, dtype=np.float32)
    x, c, ctx, c_ctx = f(inp["x"]), f(inp["c"]), f(inp["ctx"]), f(inp["c_ctx"])
    w_mod, b_mod = f(inp["w_mod"])[0], f(inp["b_mod"])[0]
    shared = {}
    shared["w_mod_l"] = _wtiles(w_mod, 512)
    shared["bmodF"] = _fm(b_mod, 96)
    shared["bmodR"] = np.ascontiguousarray(np.broadcast_to(
        np.concatenate([b_mod[2 * D:3 * D], b_mod[5 * D:6 * D]])[None, :], (P, 2 * D)))
    shared["gF"] = np.ascontiguousarray(np.concatenate([_fm(f(inp["norm1_g"])[0], 16), _fm(f(inp["norm2_g"])[0], 16)], axis=1))
    shared["fgR"] = np.ascontiguousarray(np.broadcast_to(f(inp["final_g"])[None, :], (P, D)))
    shared["w_in_l"] = _wtiles(f(inp["w_in"])[0], 512)
    shared["w_out_l"] = _wtiles(f(inp["w_out"])[0], 512)
    shared["wq_l"] = _wtiles(f(inp["peer_wq"])[0], 512)
    shared["uT_l"] = _wtiles(np.ascontiguousarray(f(inp["peer_u"])[0].T), 512)
    v = f(inp["peer_v"])[0]
    shared["v_l"] = np.ascontiguousarray(v.reshape(16, 8, P, 4, 512).transpose(0, 3, 2, 1, 4)).reshape(64, P, 4096)
    cw, cbias = f(inp["conv_w"])[0], f(inp["conv_b"])[0]
    conv = np.concatenate([cw.reshape(4, 8, P), cbias.reshape(1, 8, P)], axis=0)
    shared["conv_l"] = np.ascontiguousarray(conv.transpose(2, 1, 0)).reshape(P, 40)
    wa, wx = f(inp["lru_wa"])[0], f(inp["lru_wx"])[0]
    lw = np.stack([wa, wx], axis=0)
    shared["lru_w"] = np.ascontiguousarray(lw.transpose(3, 0, 1, 2, 4)).reshape(P, 4096)
    lv = np.stack([f(inp["lru_ba"])[0], f(inp["lru_bx"])[0], f(inp["lru_lambda"])[0]], axis=0)
    shared["lru_v"] = np.ascontiguousarray(lv.reshape(3, 2, 8, P).transpose(3, 0, 1, 2)).reshape(P, 48)
    sw = f(inp["sgu_w"])[0]
    shared["wsT_l"] = np.ascontiguousarray(sw.transpose(2, 0, 1)).reshape(P, 1024)
    shared["bsR"] = np.ascontiguousarray(np.broadcast_to(f(inp["sgu_b"])[0].reshape(1, 1024), (P, 1024)))
    k = np.stack([f(inp["peer_k1"])[0], f(inp["peer_k2"])[0]], axis=0)
    shared["kT_l"] = np.ascontiguousarray(k.transpose(3, 0, 1, 2)).reshape(P, 2048)
    maps = []
    L = 4 * NT
    for core in range(8):
        b, q = core // 4, core % 4
        m = dict(shared)
        m["x_own"] = np.ascontiguousarray(x[b, q * NT:(q + 1) * NT])
        m["x_b"] = np.ascontiguousarray(x[b])
        halo = np.zeros((P, D), np.float32)
        if q > 0:
            halo[0:2] = x[b, q * NT - 2:q * NT]
        if q < 3:
            halo[2] = x[b, (q + 1) * NT]
        m["x_halo"] = halo
        m["ctx_b"] = np.ascontiguousarray(ctx[b])
        m["s_in"] = np.ascontiguousarray(np.stack([_fm(c[b], 16), _fm(c_ctx, 16)], axis=2)).reshape(P, 32)
        sel = np.zeros((P, 12), np.float32)
        sel[:, q] = 1.0
        sel[:, 4 + q] = 1.0
        sel[:, 8] = sel[:, 9] = 1.0 if q > 0 else 0.0
        sel[:, 10] = 1.0 if q < 3 else 0.0
        m["sel"] = sel
        maps.append(m)
    return maps


_NC_CACHE = {}


def run(inp, NT, dbg=None):
    key = (NT, dbg)
    if key not in _NC_CACHE:
        _NC_CACHE[key] = K(NT, dbg).build()
    nc = _NC_CACHE[key]
    maps = prepare_inputs(inp, NT)
    res = run_bass_kernel_spmd(nc, maps, core_ids=list(range(8)))
    B = 2
    out = np.zeros((B, 4 * NT, D), np.float32)
    for core in range(8):
        b, q = core // 4, core % 4
        out[b, q * NT:(q + 1) * NT] = res.results[core]["out"]
    return out, res


def kernel(**inputs):
    import os
    NT = np.asarray(inputs["x"]).shape[1] // 4
    if os.environ.get("KPROBE_NT"):
        NT2 = int(os.environ["KPROBE_NT"])
        inp = dict(inputs)
        inp["x"] = np.ascontiguousarray(np.asarray(inputs["x"])[:, :4 * NT2])
        o, _ = run(inp, NT2)
        out = np.zeros((2, 4 * NT, D), np.float32)
        out[:, :4 * NT2] = o
        return out
    out, _ = run(inputs, NT)
    return out
```

```python
import numpy as np
from contextlib import ExitStack
import concourse.bass as bass
import concourse.mybir as mybir
from concourse.bass_utils import run_bass_kernel_spmd

F32 = mybir.dt.float32
BF16 = mybir.dt.bfloat16
AF = mybir.ActivationFunctionType
ALU = mybir.AluOpType
AX = mybir.AxisListType

P = 128
D = 2048
KC = 16
T = 512
EPS = 1e-6
NEG = -1.0e30


class Buf:
    __slots__ = ("name", "last_w", "readers", "dsem", "dcount")

    def __init__(self, name):
        self.name = name
        self.last_w = None
        self.readers = {}
        self.dsem = None
        self.dcount = 0


class Sched:
    ENG = ("pe", "act", "dve", "pool", "sp")
    COMPUTE = ("pe", "act", "dve", "pool")

    def __init__(self, nc, stack):
        self.nc = nc
        self.stack = stack
        self.e = {}
        for n in self.ENG:
            sem = stack.enter_context(nc.semaphore("s_" + n))
            self.e[n] = dict(sem=sem, count=0, ops=[], waited={})
        self.nb = 0

    def buf(self, name, dma=False):
        self.nb += 1
        b = Buf("%s_%d" % (name, self.nb))
        if dma:
            b.dsem = self.stack.enter_context(self.nc.semaphore("d%d" % self.nb))
        return b

    def _collect(self, eng, reads, writes):
        need = {}

        def add(ev, raw):
            if ev is None:
                return
            key, sem, val = ev
            if key == eng and (eng == "pe" or not raw):
                return
            if key not in need or need[key][1] < val:
                need[key] = (sem, val)

        for b in reads:
            if b.last_w:
                for ev in b.last_w.values():
                    add(ev, True)
        for b in writes:
            if b.last_w:
                for ev in b.last_w.values():
                    add(ev, False)
            for ev in b.readers.values():
                add(ev, False)
        E = self.e[eng]
        waits = []
        for key, (sem, val) in need.items():
            if E["waited"].get(key, 0) >= val:
                continue
            E["waited"][key] = val
            waits.append((key, sem, val))
        return waits

    def _update(self, ev, reads, writes):
        key = ev[0]
        for b in reads:
            old = b.readers.get(key)
            if old is None or old[2] < ev[2]:
                b.readers[key] = ev
        for b in writes:
            if b.last_w is None:
                b.last_w = {}
            b.last_w[key] = ev
            b.readers = {}

    def op(self, eng, fn, reads=(), writes=(), attach=True):
        E = self.e[eng]
        waits = self._collect(eng, reads, writes)
        E["count"] += 1
        ev = (eng, E["sem"], E["count"])
        E["ops"].append(dict(waits=waits, fn=fn, kind="op", idx=E["count"], attach=attach and eng != "pe"))
        self._update(ev, reads, writes)
        return ev

    def dma(self, fn, owner, reads=(), writes=(), queue="sp"):
        Q = self.e[queue]
        waits = self._collect(queue, reads, writes)
        owner.dcount += 16
        ev = ("d_" + owner.name, owner.dsem, owner.dcount)
        Q["ops"].append(dict(waits=waits, fn=fn, kind="dma", inc=(owner.dsem, 16), attach=True))
        self._update(ev, reads, writes)
        return ev

    def final_wait(self, eng, bufs):
        waits = self._collect(eng, bufs, bufs)
        self.e[eng]["ops"].append(dict(waits=waits, fn=None, kind="wait", attach=False))

    def emit(self):
        nc = self.nc
        waited = {n: set() for n in self.COMPUTE}
        for n in self.ENG:
            for o in self.e[n]["ops"]:
                for key, sem, val in o["waits"]:
                    if key in waited:
                        waited[key].add(val)
        rank = {}
        for n in self.COMPUTE:
            rank[n] = {v: i + 1 for i, v in enumerate(sorted(waited[n]))}
        with nc.Block() as block:
            def run(name):
                def body(e):
                    for o in self.e[name]["ops"]:
                        ws = []
                        for key, sem, val in o["waits"]:
                            ws.append((sem, rank[key][val] if key in rank else val))
                        fn = o["fn"]
                        att = None
                        if fn is not None and o["attach"] and ws:
                            att = ws.pop()
                        for sem, val in ws:
                            e.wait_ge(sem, val)
                        if fn is None:
                            continue
                        r = fn(e)
                        first, last = r if isinstance(r, tuple) else (r, r)
                        if att is not None:
                            first._wait_ge(att[0], att[1])
                        if o["kind"] == "dma":
                            last.then_inc(o["inc"][0], o["inc"][1])
                        elif o["idx"] in waited[name]:
                            last.then_inc(self.e[name]["sem"], 1)
                return body
            block.tensor(run("pe"))
            block.scalar(run("act"))
            block.vector(run("dve"))
            block.gpsimd(run("pool"))
            block.sync(run("sp"))


class K:
    def __init__(self, NT, dbg=None):
        self.NT = NT
        self.NTL = NT // T
        self.dbg = dbg

    def act(self, out, in_, func, reads, writes, bias=None, scale=None, accum=None):
        kw = {}
        if bias is not None:
            kw["bias"] = bias
        if scale is not None:
            kw["scale"] = scale
        if accum is not None:
            kw["accum_out"] = accum
        return self.S.op("act", lambda e: e.activation(out=out, in_=in_, func=func, **kw), reads, writes,
                         attach=(accum is None))

    def tt(self, eng, out, in0, in1, op, reads, writes):
        return self.S.op(eng, lambda e: e.tensor_tensor(out=out, in0=in0, in1=in1, op=op), reads, writes)

    def ts(self, eng, out, in0, s1, s2, op0, op1, reads, writes):
        if op1 is None:
            return self.S.op(eng, lambda e: e.tensor_scalar(out=out, in0=in0, scalar1=s1, scalar2=None, op0=op0), reads, writes)
        return self.S.op(eng, lambda e: e.tensor_scalar(out=out, in0=in0, scalar1=s1, scalar2=s2, op0=op0, op1=op1), reads, writes)

    def stt(self, out, in0, scalar, in1, op0, op1, reads, writes):
        return self.S.op("dve", lambda e: e.scalar_tensor_tensor(out=out, in0=in0, scalar=scalar, in1=in1, op0=op0, op1=op1), reads, writes)

    def cp(self, eng, out, in_, reads, writes):
        if eng == "act":
            return self.S.op("act", lambda e: e.activation(out=out, in_=in_, func=AF.Copy), reads, writes)
        return self.S.op(eng, lambda e: e.tensor_copy(out=out, in_=in_), reads, writes)

    def mm(self, out, pairs, reads, writes):
        def fn(e):
            n = len(pairs)
            ins = None
            for i, (l, r) in enumerate(pairs):
                ins = e.matmul(out, l, r, start=(i == 0), stop=(i == n - 1))
            return ins
        return self.S.op("pe", fn, reads, writes)

    def mm_multi(self, groups, reads, writes):
        def fn(e):
            ins = None
            for out, pairs in groups:
                n = len(pairs)
                for i, (l, r) in enumerate(pairs):
                    ins = e.matmul(out, l, r, start=(i == 0), stop=(i == n - 1))
            return ins
        return self.S.op("pe", fn, reads, writes)

    def trs(self, items, reads, writes):
        ident = self.IDENT

        def fn(e):
            ins = None
            for out, in_ in items:
                ins = e.transpose(out=out, in_=in_, identity=ident[:])
            return ins
        return self.S.op("pe", fn, reads, writes)

    def load(self, out, in_, owner, reads=(), writes=None):
        if writes is None:
            writes = [owner]
        return self.S.dma(lambda e: e.dma_start(out=out, in_=in_, allow_slow_non_contiguous=True), owner, reads, writes)

    def store(self, out, in_, owner, reads, writes=()):
        return self.S.dma(lambda e: e.dma_start(out=out, in_=in_, allow_slow_non_contiguous=True), owner, reads, writes)

    def build(self):
        NT, NTL = self.NT, self.NTL
        nc = bass.Bass("TRN2", target_bir_lowering=False)
        self.nc = nc

        def din(name, shape, dt=F32):
            return nc.dram_tensor(name, list(shape), dt, kind="ExternalInput").ap()

        def dscr(name, shape, dt):
            return nc.dram_tensor(name, list(shape), dt, kind="Internal").ap()

        I = {}
        I["x_own"] = din("x_own", [NT, D])
        I["x_b"] = din("x_b", [4 * NT, D])
        I["x_halo"] = din("x_halo", [P, D])
        I["ctx_b"] = din("ctx_b", [256, D])
        I["s_in"] = din("s_in", [P, 32])
        I["sel"] = din("sel", [P, 12])
        I["w_mod_l"] = din("w_mod_l", [24, P, 8192])
        I["bmodF"] = din("bmodF", [P, 96])
        I["bmodR"] = din("bmodR", [P, 4096])
        I["gF"] = din("gF", [P, 32])
        I["fgR"] = din("fgR", [P, D])
        I["w_in_l"] = din("w_in_l", [8, P, 8192])
        I["w_out_l"] = din("w_out_l", [4, P, 8192])
        I["wq_l"] = din("wq_l", [4, P, 8192])
        I["uT_l"] = din("uT_l", [32, P, 8192])
        I["v_l"] = din("v_l", [64, P, 4096])
        I["conv_l"] = din("conv_l", [P, 40])
        I["lru_w"] = din("lru_w", [P, 4096])
        I["lru_v"] = din("lru_v", [P, 48])
        I["wsT_l"] = din("wsT_l", [P, 1024])
        I["bsR"] = din("bsR", [P, 1024])
        I["kT_l"] = din("kT_l", [P, 2048])
        self.I = I
        OUT = nc.dram_tensor("out", [NT, D], F32, kind="ExternalOutput").ap()
        self.OUT = OUT
        if self.dbg:
            self.DBG = nc.dram_tensor("dbg", [P, 4096], F32, kind="ExternalOutput").ap()
        Sc = {}
        Sc["w_in_b"] = dscr("w_in_b", [8, P, 8192], BF16)
        Sc["w_out_b"] = dscr("w_out_b", [4, P, 8192], BF16)
        Sc["wq_b"] = dscr("wq_b", [4, P, 8192], BF16)
        Sc["uT_b"] = dscr("uT_b", [32, P, 8192], BF16)
        Sc["v_b"] = dscr("v_b", [64, P, 4096], BF16)
        Sc["xc_st"] = dscr("xc_st", [P, 8, NT], F32)
        Sc["hf_st"] = dscr("hf_st", [P, 8, NT], F32)
        Sc["x1_st"] = dscr("x1_st", [NT, D], F32)
        Sc["g1row"] = dscr("g1row", [P, D], F32)
        Sc["g2row"] = dscr("g2row", [P, D], F32)
        self.Sc = Sc

        with ExitStack() as st:
            S = Sched(nc, st)
            self.S = S

            def sb(name, shape, dt):
                return st.enter_context(nc.sbuf_tensor(name, list(shape), dt))

            self.FSA = sb("FSA", [P, 10 * 2048], F32)
            self.fsb = [S.buf("fs%d" % i, dma=True) for i in range(10)]
            self.HNT = sb("HNT", [P, KC, T], BF16)
            self.hntb = [S.buf("hnt%d" % i) for i in range(4)]
            self.YQ = sb("YQ", [P, KC, T], BF16)
            self.yqb = S.buf("yq")
            self.UG = sb("UG", [P, 8, T], BF16)
            self.ugb = S.buf("ug")
            self.VGN = sb("VGN", [P, 4, 1024], BF16)
            self.vgnb = S.buf("vgn")
            self.XCB = sb("XCB", [P, 4, T], BF16)
            self.xcbb = S.buf("xcb")
            self.XN = sb("XN", [P, 2048], BF16)
            self.xnb = S.buf("xn")
            self.WB = sb("WB", [P, 3, 8192], BF16)
            self.wbb = [S.buf("wb%d" % i, dma=True) for i in range(3)]
            self.vhb = [S.buf("vh0", dma=True), S.buf("vh1", dma=True)]
            self.cva = [S.buf("cva0", dma=True), S.buf("cva1", dma=True)]
            self.cvb = [S.buf("cvb%d" % i, dma=True) for i in range(4)]
            self.GT = sb("GT", [P, 2, T], BF16)
            self.gtb = [S.buf("gt0"), S.buf("gt1")]
            self.IDENT = sb("IDENT", [P, P], BF16)
            self.SMW = sb("SMW", [P, 4096], BF16)
            self.smwb = S.buf("smw")
            self.WST = sb("WST", [P, 1024], BF16)
            self.CST = sb("CST", [P, 512], F32)
            self.cstb = S.buf("cst", dma=True)
            self.SMALL = sb("SMALL", [P, 640], F32)
            self.smallb = S.buf("small")
            self.EPSC = sb("EPSC", [P, 2], F32)
            self.STT = sb("STT", [P, 128], F32)
            self.sttb = S.buf("stt")
            self.PF = [st.enter_context(nc.psum_tensor("PF%d" % i, [P, 512], F32)) for i in range(6)]
            self.pfb = [S.buf("pf%d" % i) for i in range(6)]
            self.PB = [st.enter_context(nc.psum_tensor("PB%d" % i, [P, 1024], BF16)) for i in range(2)]
            self.pbb = [S.buf("pb%d" % i) for i in range(2)]
            self.pfi = 0
            self.pbi = 0
            self.outb = S.buf("outd", dma=True)
            self.scrb = {k: S.buf("scr_" + k, dma=True) for k in Sc}

            self.phase_consts()
            self.phase_mods()
            self.phase_convert()
            self.phase_chains()
            self.phase_own_f()
            self.phase_own_r()

            allb = self.fsb + self.wbb + [self.outb, self.cstb] + list(self.scrb.values())
            S.final_wait("sp", allb)
            S.emit()
        return nc

    def FS(self, i, n=2048, off=0):
        return self.FSA[:, i * 2048 + off: i * 2048 + off + n]

    def FS3(self, i, a, b, off=0):
        return self.FSA[:, i * 2048 + off: i * 2048 + off + a * b].rearrange("p (a b) -> p a b", a=a)

    def next_pf(self):
        i = self.pfi
        self.pfi = (self.pfi + 1) % 6
        return self.PF[i], self.pfb[i]

    def next_pb(self):
        i = self.pbi
        self.pbi = (self.pbi + 1) % 2
        return self.PB[i], self.pbb[i]

    def wb3(self, s):
        return self.WB[:, s, :].rearrange("p (k n) -> p k n", k=KC)

    C_CONV = 0
    C_LV = 40
    C_CL = 88
    C_CL2 = 104
    C_SEL = 120
    C_GF = 132
    C_SIN = 164
    C_MODF = 196
    C_A1 = 324
    C_B1 = 356
    C_A2 = 388
    C_B2 = 404
    C_BMF = 420
    C_END = 484

    def C(self, off, n):
        return self.CST[:, off:off + n]

    def phase_consts(self):
        S, I = self.S, self.I
        cb = self.cstb
        S.op("dve", lambda e: e.memset(self.EPSC[:, :], EPS), [], [self.smallb])
        idf = self.FS(0, 128)
        S.op("pool", lambda e: e.iota(idf, pattern=[[1, 128]], base=0, channel_multiplier=-1,
                                      allow_small_or_imprecise_dtypes=True), [], [self.fsb[0]])
        S.op("dve", lambda e: e.tensor_single_scalar(out=self.IDENT[:], in_=idf, scalar=0.0, op=ALU.is_equal),
             [self.fsb[0]], [self.smwb])
        self.load(self.C(self.C_CONV, 40), I["conv_l"][:, :], cb)
        self.load(self.C(self.C_LV, 48), I["lru_v"][:, :], cb)
        self.load(self.C(self.C_SEL, 12), I["sel"][:, :], cb)
        self.load(self.C(self.C_GF, 32), I["gF"][:, :], cb)
        self.load(self.C(self.C_SIN, 32), I["s_in"][:, :], cb)
        for jj, j in enumerate((0, 1, 3, 4)):
            self.load(self.C(self.C_BMF + 16 * jj, 16), I["bmodF"][:, 16 * j:16 * j + 16], cb)
        lam = self.C(self.C_LV + 32, 16)
        cl = self.C(self.C_CL, 16)
        cl2 = self.C(self.C_CL2, 16)
        self.act(cl, lam, AF.Exp, [cb], [cb], scale=-1.0)
        self.act(cl, cl, AF.Ln, [cb], [cb], bias=1.0)
        self.ts("dve", cl2, cl, -16.0, None, ALU.mult, None, [cb], [cb])
        self.ts("dve", cl, cl, -8.0, None, ALU.mult, None, [cb], [cb])
        sin = self.C(self.C_SIN, 32)
        self.act(sin, sin, AF.Silu, [cb], [cb])
        self.load(self.FS(0, 4096), I["lru_w"][:, :], self.fsb[0], writes=[self.fsb[0], self.fsb[1]])
        self.cp("dve", self.SMW[:, :], self.FS(0, 4096), [self.fsb[0], self.fsb[1]], [self.smwb])
        self.load(self.FS(2, 1024), I["wsT_l"][:, :], self.fsb[2])
        self.cp("dve", self.WST[:, :], self.FS(2, 1024), [self.fsb[2]], [self.smwb])

    def phase_mods(self):
        S, I, Sc = self.S, self.I, self.Sc
        cb = self.cstb
        sin3 = self.C(self.C_SIN, 32).rearrange("p (k j) -> p k j", k=KC)
        srep = self.FS3(8, KC, 128)
        self.cp("dve", srep, sin3[:, :, 0:1].to_broadcast([P, KC, 128]), [cb], [self.fsb[8]])
        modf = self.C(self.C_MODF, 128).rearrange("p (j k c) -> p j k c", j=4, k=KC)
        bmf = self.C(self.C_BMF, 64).rearrange("p (j k) -> p j k", j=4)
        fm = {0: 0, 1: 1, 3: 2, 4: 3}
        rowdst = {2: ("g1row", Sc["g1row"]), 5: ("g2row", Sc["g2row"])}
        rb = 0
        for pi in range(24):
            j = pi // 4
            s0 = 4 * (pi % 2)
            bufs = self.fsb[s0:s0 + 4]
            pan = self.FSA[:, s0 * 2048:(s0 + 4) * 2048].rearrange("p (k n) -> p k n", k=KC)
            self.load(self.FSA[:, s0 * 2048:(s0 + 4) * 2048], I["w_mod_l"][pi], bufs[0], writes=bufs)
            if j in fm:
                pf, pfb = self.next_pf()
                groups = []
                for cc in range(4):
                    groups.append((pf[:, 2 * cc:2 * cc + 2],
                                   [(pan[:, kc, cc * 128:(cc + 1) * 128], sin3[:, kc, :]) for kc in range(KC)]))
                self.mm_multi(groups, bufs + [cb], [pfb])
                jj = fm[j]
                kc0 = 4 * (pi % 4)
                self.tt("dve", modf[:, jj, kc0:kc0 + 4, :], pf[:, 0:8].rearrange("p (c j) -> p c j", c=4),
                        bmf[:, jj, kc0:kc0 + 4].unsqueeze(2).to_broadcast([P, 4, 2]), ALU.add, [pfb, cb], [cb])
            else:
                pf, pfb = self.next_pf()
                self.mm(pf[:, :], [(srep[:, kc, :], pan[:, kc, :]) for kc in range(KC)], bufs + [self.fsb[8]], [pfb])
                ds = pi % 4
                which = 0 if j == 2 else 1
                bm = self.FS(9, 512, off=512 * (rb % 2))
                rt = self.FS(9, 512, off=1024 + 512 * (rb % 2))
                rb += 1
                self.load(bm, I["bmodR"][:, which * 2048 + ds * 512: which * 2048 + ds * 512 + 512], self.fsb[9])
                self.tt("dve", rt, pf[:, :], bm, ALU.add, [pfb, self.fsb[9]], [self.fsb[9]])
                nm, dst = rowdst[j]
                self.store(dst[:, ds * 512:(ds + 1) * 512], rt, self.scrb[nm], [self.fsb[9]], [self.scrb[nm]])
        gf = self.C(self.C_GF, 32)
        a1 = self.C(self.C_A1, 32).rearrange("p (k c) -> p k c", k=KC)
        b1 = self.C(self.C_B1, 32).rearrange("p (k c) -> p k c", k=KC)
        self.ts("dve", a1, modf[:, 1, :, :], 1.0, None, ALU.add, None, [cb], [cb])
        self.tt("dve", a1, a1, gf[:, 0:16].unsqueeze(2).to_broadcast([P, KC, 2]), ALU.mult, [cb], [cb])
        self.cp("dve", b1, modf[:, 0, :, :], [cb], [cb])
        a2 = self.C(self.C_A2, 16)
        b2 = self.C(self.C_B2, 16)
        self.ts("dve", a2, modf[:, 3, :, 0], 1.0, None, ALU.add, None, [cb], [cb])
        self.tt("dve", a2, a2, gf[:, 16:32], ALU.mult, [cb], [cb])
        self.cp("dve", b2, modf[:, 2, :, 0], [cb], [cb])

    def phase_convert(self):
        I, Sc = self.I, self.Sc
        jobs = []
        self.cvjobs = []
        for nm_s, nm_d, n, F in (("uT_l", "uT_b", 32, 8192), ("v_l", "v_b", 64, 4096)):
            for i in range(n):
                for h in range(F // 1024):
                    self.cvjobs.append((I[nm_s][i][:, h * 1024:(h + 1) * 1024], Sc[nm_d][i][:, h * 1024:(h + 1) * 1024], nm_d))
        self.cvk = 0
        for nm_s, nm_d, n, F in (("w_in_l", "w_in_b", 8, 8192), ("w_out_l", "w_out_b", 4, 8192),
                                 ("wq_l", "wq_b", 4, 8192)):
            for i in range(n):
                for h in range(F // 4096):
                    jobs.append((I[nm_s][i][:, h * 4096:(h + 1) * 4096], Sc[nm_d][i][:, h * 4096:(h + 1) * 4096], nm_d))
        for ji, (src, dst, nm) in enumerate(jobs):
            s0 = 2 * (ji % 3)
            bufs = self.fsb[s0:s0 + 2]
            w = ji % 3
            half = self.WB[:, w, 0:4096]
            self.load(self.FSA[:, s0 * 2048:(s0 + 2) * 2048], src, bufs[0], writes=bufs)
            eng = "dve" if ji % 2 == 0 else "act"
            self.cp(eng, half, self.FSA[:, s0 * 2048:(s0 + 2) * 2048], bufs, [self.wbb[w]])
            self.store(dst, half, self.wbb[w], [self.wbb[w]], [self.scrb[nm]])

    def cv_emit(self, n):
        ugf = self.UG[:, :, :].rearrange("p a n -> p (a n)")
        for _ in range(n):
            if self.cvk >= len(self.cvjobs):
                return
            k = self.cvk
            self.cvk += 1
            src_, dst_, nm = self.cvjobs[k]
            a = k % 2
            b = k % 4
            st = self.FS(9, 1024, off=1024 * a)
            ob = ugf[:, b * 1024:(b + 1) * 1024]
            self.load(st, src_, self.cva[a])
            self.cp("pool", ob, st, [self.cva[a]], [self.cvb[b]])
            self.store(dst_, ob, self.cvb[b], [self.cvb[b]], [self.scrb[nm]])

    def norm_tt(self, xs, xsb, tt, A, B, areads, hn=None):
        HN, hnb = hn if hn is not None else (self.HNT, self.hntb)
        sm = self.smallb
        ssq = self.SMALL[:, 0:1]
        rstd = self.SMALL[:, 1:2]
        self.act(self.XN[:, :], xs, AF.Square, [xsb], [self.xnb, sm], accum=ssq)
        self.act(rstd, ssq, AF.Ln, [sm], [sm], scale=1.0 / D, bias=self.EPSC[:, 0:1])
        self.act(rstd, rstd, AF.Exp, [sm], [sm], scale=-0.5)
        self.act(self.XN[:, :], xs, AF.Copy, [xsb, sm], [self.xnb], scale=rstd)
        for half in range(2):
            pb, pbb = self.next_pb()
            self.trs([(pb[:, i * 128:(i + 1) * 128], self.XN[:, (half * 8 + i) * 128:(half * 8 + i + 1) * 128])
                      for i in range(8)], [self.xnb], [pbb])
            dst = HN[:, half * 8:half * 8 + 8, tt * 128:(tt + 1) * 128]
            pv = pb[:, :].rearrange("p (k n) -> p k n", k=8)
            self.tt("dve", dst, pv, A[:, half * 8:half * 8 + 8].unsqueeze(2).to_broadcast([P, 8, 128]), ALU.mult,
                    [pbb] + areads, [hnb[tt]])
            self.tt("pool", dst, dst, B[:, half * 8:half * 8 + 8].unsqueeze(2).to_broadcast([P, 8, 128]), ALU.add,
                    [hnb[tt]] + areads, [hnb[tt]])

    def norm_rows(self, rows_fn, ntt, A, B, hn=None, tts=None):
        for tt in (range(ntt) if tts is None else tts):
            s = tt % 2
            self.load(self.FS(s), rows_fn(tt), self.fsb[s])
            self.norm_tt(self.FS(s), self.fsb[s], tt, A, B, [self.cstb], hn=hn)

    def xb_half(self, hc, ncur, wxs, mode, d, scan_rng, snap_col=None, snap_idx=None, stash=None, from_stash=None,
                hn=None, filler=None):
        cb, sb_ = self.cstb, self.sttb
        nwin = ncur
        XC = self.FS3(2, 4, T)
        R = self.FS3(3, 4, T)
        Ii = self.FS3(4, 4, T)
        TM = self.FS3(5, 4, T)
        H = self.FS3(6, 4, T)
        fb = self.fsb
        conv = self.C(self.C_CONV, 40)
        if mode is not None:
            EXT = self.FSA[:, 7 * 2048: 7 * 2048 + 4 * 516].rearrange("p (c n) -> p c n", c=4)
            eb = [fb[7], fb[8]]
            sav = self.STT[:, 80:104].rearrange("p (c n) -> p c n", c=8)
            co = 3 if mode == "f" else 0
            for c in range(4):
                oc = 4 * hc + c
                pf, pfb = self.next_pf()
                wt = self.wb3(wxs[oc // 4])
                HN, hnb = hn if hn is not None else (self.HNT, self.hntb)
                self.mm(pf[:, 0:ncur], [(wt[:, kc, (oc % 4) * 128:(oc % 4 + 1) * 128], HN[:, kc, 0:ncur])
                                        for kc in range(KC)], hnb + [self.wbb[wxs[oc // 4]]], [pfb])
                self.cp("act", EXT[:, c, co:co + ncur], pf[:, 0:ncur], [pfb], eb)
            if mode == "f":
                self.cp("dve", EXT[:, :, 0:3], sav[:, 4 * hc:4 * hc + 4, :], [sb_], eb)
                self.cp("dve", sav[:, 4 * hc:4 * hc + 4, :], EXT[:, :, ncur:ncur + 3], eb, [sb_])
            else:
                self.cp("dve", EXT[:, :, ncur:ncur + 3], sav[:, 4 * hc:4 * hc + 4, :], [sb_], eb)
                self.cp("dve", sav[:, 4 * hc:4 * hc + 4, :], EXT[:, :, 0:3], eb, [sb_])
            for c in range(4):
                oc = 4 * hc + c
                w = lambda k: conv[:, oc * 5 + k: oc * 5 + k + 1]
                self.ts("pool", XC[:, c, 0:nwin], EXT[:, c, 0:nwin], w(0), w(4), ALU.mult, ALU.add, eb + [cb], [fb[2]])
                for k in range(1, 4):
                    self.stt(XC[:, c, 0:nwin], EXT[:, c, k:k + nwin], w(k), XC[:, c, 0:nwin], ALU.mult, ALU.add,
                             eb + [cb, fb[2]], [fb[2]])
        else:
            self.load(XC[:, :, :], from_stash, fb[2], reads=[self.scrb["xc_st"]])
        if filler is not None:
            filler()
        return self._xb_gates(hc, nwin, d, scan_rng, snap_col, snap_idx)

    def load_wx(self, tiles, slots, src):
        for tno, s in zip(tiles, slots):
            self.load(self.WB[:, s, :], src[tno], self.wbb[s], reads=[self.scrb["w_in_b"]])

    def zero_sav(self):
        self.S.op("dve", lambda e: e.memset(self.STT[:, 80:104], 0.0), [], [self.sttb])

    def phase_chains(self):
        NT, NTL = self.NT, self.NTL
        I, Sc = self.I, self.Sc
        a1 = self.C(self.C_A1, 32).rearrange("p (k c) -> p k c", k=KC)
        b1 = self.C(self.C_B1, 32).rearrange("p (k c) -> p k c", k=KC)
        self.load_wx([0, 1], [0, 1], Sc["w_in_b"])
        self.S.op("dve", lambda e: e.memset(self.STT[:, :], 0.0), [], [self.sttb])
        self.S.op("pool", lambda e: e.memset(self.HNT[:, :, 256:384], 0.0), [], self.hntb)
        for d in (0, 1):
            self.zero_sav()
            self.norm_rows(lambda tt: I["ctx_b"][tt * 128:(tt + 1) * 128, :], 2, a1[:, :, 1], b1[:, :, 1])
            self.S.op("pool", lambda e: e.memset(self.HNT[:, :, 256:257], 0.0), [], self.hntb)
            for hc in (0, 1):
                self.xb_half(hc, 257, [0, 1], "f", d, (1, 257))
            base = 16 if d == 0 else 48
            idx = 0 if d == 0 else 3
            self.cp("dve", self.STT[:, base + 8 * idx: base + 8 * idx + 8], self.STT[:, 8 * d:8 * d + 8],
                    [self.sttb], [self.sttb])
        sel = self.C(self.C_SEL, 12)
        self.norm_rows(lambda tt: I["x_halo"][:, :], 1, a1[:, :, 0], b1[:, :, 0])
        sav = self.STT[:, 80:104].rearrange("p (c n) -> p c n", c=8)
        hal = self.SMALL[:, 16:48].rearrange("p (c n) -> p c n", c=8)
        for oc in range(8):
            pf, pfb = self.next_pf()
            wt = self.wb3(oc // 4)
            self.mm(pf[:, 0:128], [(wt[:, kc, (oc % 4) * 128:(oc % 4 + 1) * 128], self.HNT[:, kc, 0:128])
                                   for kc in range(KC)], self.hntb + [self.wbb[oc // 4]], [pfb])
            self.tt("dve", hal[:, oc, :], pf[:, 0:4], sel[:, 8:12], ALU.mult, [pfb, self.cstb], [self.smallb])

        items = []
        for j in range(3 * NTL + 1):
            if j == 0:
                rng = (1, T)
            elif j == 3 * NTL:
                rng = (0, 1)
            else:
                rng = (0, T)
            snap = j // NTL if (j > 0 and j % NTL == 0) else None
            items.append(dict(rows=(lambda tt, j=j: I["x_b"][j * T + tt * 128: j * T + (tt + 1) * 128, :]),
                              mode="f", d=0, rng=rng, snap_col=0, snap_idx=snap,
                              pre=(self.zero_sav if j == 0 else None), post=None))
        for j in range(4 * NTL - 1, NTL - 2, -1):
            if j == 4 * NTL - 1:
                rng = (0, T - 2)
            elif j == NTL - 1:
                rng = (T - 2, T)
            else:
                rng = (0, T)
            snap = None
            if (j + 1) % NTL == 0 and j != 4 * NTL - 1:
                snap = (j + 1) // NTL - 1
            items.append(dict(rows=(lambda tt, j=j: I["x_b"][j * T + tt * 128: j * T + (tt + 1) * 128, :]),
                              mode="b", d=1, rng=rng, snap_col=T - 2, snap_idx=snap,
                              pre=(self.zero_sav if j == 4 * NTL - 1 else None), post=None))

        def own_pre():
            for d in (0, 1):
                base = 16 if d == 0 else 48
                stt = self.STT[:, 8 * d:8 * d + 8]
                self.ts("dve", stt, self.STT[:, base:base + 8], sel[:, 4 * d:4 * d + 1], None, ALU.mult, None,
                        [self.sttb, self.cstb], [self.sttb])
                for i in range(1, 4):
                    self.stt(stt, self.STT[:, base + 8 * i: base + 8 * i + 8], sel[:, 4 * d + i:4 * d + i + 1], stt,
                             ALU.mult, ALU.add, [self.sttb, self.cstb], [self.sttb])
            self.S.op("dve", lambda e: e.memset(self.STT[:, 80:104], 0.0), [], [self.sttb])
            self.cp("dve", sav[:, :, 1:3], hal[:, :, 0:2], [self.smallb], [self.sttb])

        def own_post(j):
            def f(hc, XC, H, rng):
                lo, hi = rng
                p0 = j * T - 1
                self.store(Sc["xc_st"][:, 4 * hc:4 * hc + 4, p0 + lo:p0 + hi], XC[:, :, lo:hi], self.fsb[2],
                           [self.fsb[2]], [self.scrb["xc_st"]])
                self.store(Sc["hf_st"][:, 4 * hc:4 * hc + 4, p0 + lo:p0 + hi], H[:, :, lo:hi], self.fsb[6],
                           [self.fsb[6]], [self.scrb["hf_st"]])
            return f

        for j in range(NTL):
            rng = (1, T) if j == 0 else (0, T)
            items.append(dict(rows=(lambda tt, j=j: I["x_own"][j * T + tt * 128: j * T + (tt + 1) * 128, :]),
                              mode="f", d=0, rng=rng, snap_col=None, snap_idx=None,
                              pre=(own_pre if j == 0 else None), post=own_post(j)))

        hntb2 = [self.S.buf("hn2_%d" % i) for i in range(4)]
        self.S.op("dve", lambda e: e.memset(self.EPSC[:, 1:2], 0.0), [], [self.yqb] + hntb2)
        self.S.op("pool", lambda e: e.memset(self.SMALL[:, 600:601], 0.0), [], [self.fsb[9], self.ugb] + self.cva + self.cvb)
        nhalf = 2 * (len(items))
        per = (len(self.cvjobs) + nhalf - 3) // (nhalf - 2) + 1
        HNS = [(self.HNT, self.hntb), (self.YQ, hntb2)]
        al, bl = a1[:, :, 0], b1[:, :, 0]
        self.norm_rows(items[0]["rows"], 4, al, bl, hn=HNS[0])
        for k, it in enumerate(items):
            hn = HNS[k % 2]
            nxt = items[k + 1] if k + 1 < len(items) else None
            if it["pre"] is not None:
                it["pre"]()
            for hc in (0, 1):
                filler = None
                if nxt is not None:
                    filler = (lambda nxt=nxt, hc=hc, k=k: (self.norm_rows(
                        nxt["rows"], 4, al, bl, hn=HNS[(k + 1) % 2], tts=(2 * hc, 2 * hc + 1)), self.cv_emit(per)))
                XC, H = self.xb_half(hc, T, [0, 1], it["mode"], it["d"], it["rng"], snap_col=it["snap_col"],
                                     snap_idx=it["snap_idx"], hn=hn, filler=filler)
                if it["post"] is not None:
                    it["post"](hc, XC, H, it["rng"])
        for hc in (0, 1):
            XC, H = self.xb_half_virtual(hc, hal, (0, 1))
            own_post(NTL)(hc, XC, H, (0, 1))
        self.cv_emit(len(self.cvjobs))
        self.S.op("dve", lambda e: e.memset(self.EPSC[:, 1:2], 0.0), [], [self.yqb] + hntb2)
        self.S.op("pool", lambda e: e.memset(self.SMALL[:, 600:601], 0.0), [], [self.fsb[9], self.ugb] + self.cva + self.cvb)

    def phase_own_f(self):
        return

    def xb_half_virtual(self, hc, hal, rng):
        cb, sb_ = self.cstb, self.sttb
        fb = self.fsb
        EXT = self.FSA[:, 7 * 2048: 7 * 2048 + 4 * 516].rearrange("p (c n) -> p c n", c=4)
        eb = [fb[7], fb[8]]
        sav = self.STT[:, 80:104].rearrange("p (c n) -> p c n", c=8)
        XC = self.FS3(2, 4, T)
        conv = self.C(self.C_CONV, 40)
        n = 8
        self.S.op("dve", lambda e: e.memset(EXT[:, :, 0:16], 0.0), [], eb)
        self.cp("dve", EXT[:, :, 0:3], sav[:, 4 * hc:4 * hc + 4, :], [sb_], eb)
        self.cp("dve", EXT[:, :, 3:4], hal[:, 4 * hc:4 * hc + 4, 2:3], [self.smallb], eb)
        for c in range(4):
            oc = 4 * hc + c
            w = lambda k: conv[:, oc * 5 + k: oc * 5 + k + 1]
            self.ts("pool", XC[:, c, 0:n], EXT[:, c, 0:n], w(0), w(4), ALU.mult, ALU.add, eb + [cb], [fb[2]])
            for k in range(1, 4):
                self.stt(XC[:, c, 0:n], EXT[:, c, k:k + n], w(k), XC[:, c, 0:n], ALU.mult, ALU.add,
                         eb + [cb, fb[2]], [fb[2]])
        return self.xb_tail(hc, n, 0, rng)

    def xb_tail(self, hc, nwin, d, scan_rng):
        return self._xb_gates(hc, nwin, d, scan_rng)

    def _xb_gates(self, hc, nwin, d, scan_rng, snap_col=None, snap_idx=None):
        cb, sb_ = self.cstb, self.sttb
        fb = self.fsb
        XC = self.FS3(2, 4, T)
        R = self.FS3(3, 4, T)
        Ii = self.FS3(4, 4, T)
        TM = self.FS3(5, 4, T)
        H = self.FS3(6, 4, T)
        self.cp("act", self.XCB[:, :, 0:nwin], XC[:, :, 0:nwin], [fb[2]], [self.xcbb])
        lw = self.SMW[:, :].rearrange("p (m d h j) -> p m d h j", m=2, d=2, h=8)
        lv = self.C(self.C_LV, 48).rearrange("p (m d h) -> p m d h", m=3, d=2)
        for m, dstt, dbuf in ((0, R, fb[3]), (1, Ii, fb[4])):
            for c in range(4):
                oc = 4 * hc + c
                pf, pfb = self.next_pf()
                self.mm(pf[:, 0:nwin], [(lw[:, m, d, oc, :], self.XCB[:, c, 0:nwin])], [self.xcbb, self.smwb], [pfb])
                self.act(dstt[:, c, 0:nwin], pf[:, 0:nwin], AF.Sigmoid, [pfb, cb], [dbuf], bias=lv[:, m, d, oc:oc + 1])
        cl = self.C(self.C_CL, 16).rearrange("p (d h) -> p d h", d=2)
        cl2 = self.C(self.C_CL2, 16).rearrange("p (d h) -> p d h", d=2)
        for c in range(4):
            oc = 4 * hc + c
            self.act(TM[:, c, 0:nwin], R[:, c, 0:nwin], AF.Exp, [fb[3], cb], [fb[5]], scale=cl2[:, d, oc:oc + 1])
            self.act(R[:, c, 0:nwin], R[:, c, 0:nwin], AF.Exp, [fb[3], cb], [fb[3]], scale=cl[:, d, oc:oc + 1])
        self.act(TM[:, :, 0:nwin], TM[:, :, 0:nwin], AF.Ln, [fb[5]], [fb[5]], scale=-1.0, bias=1.0)
        self.act(TM[:, :, 0:nwin], TM[:, :, 0:nwin], AF.Exp, [fb[5]], [fb[5]], scale=0.5)
        self.tt("dve", Ii[:, :, 0:nwin], Ii[:, :, 0:nwin], TM[:, :, 0:nwin], ALU.mult, [fb[4], fb[5]], [fb[4]])
        self.tt("dve", Ii[:, :, 0:nwin], Ii[:, :, 0:nwin], XC[:, :, 0:nwin], ALU.mult, [fb[4], fb[2]], [fb[4]])
        lo, hi = scan_rng
        stt = self.STT[:, 8 * d:8 * d + 8]
        for c in range(4):
            oc = 4 * hc + c
            if d == 0:
                o, a, b = H[:, c, lo:hi], R[:, c, lo:hi], Ii[:, c, lo:hi]
            else:
                o = H[:, c, lo:hi][:, ::-1]
                a = R[:, c, lo:hi][:, ::-1]
                b = Ii[:, c, lo:hi][:, ::-1]
            ini = stt[:, oc:oc + 1]
            self.S.op("dve", (lambda o=o, a=a, b=b, ini=ini: (lambda e: e.tensor_tensor_scan(
                out=o, data0=a, data1=b, initial=ini, op0=ALU.mult, op1=ALU.add)))(),
                [fb[3], fb[4], sb_], [fb[6]])
        last = hi - 1 if d == 0 else lo
        self.cp("dve", stt[:, 4 * hc:4 * hc + 4], H[:, :, last], [fb[6]], [sb_])
        if snap_idx is not None:
            base = 16 if d == 0 else 48
            sn = self.STT[:, base + 8 * snap_idx + 4 * hc: base + 8 * snap_idx + 4 * hc + 4]
            self.cp("dve", sn, H[:, :, snap_col], [fb[6]], [sb_])
        return XC, H

    def wstream(self, srcs):
        st = {"i": 0, "n": len(srcs), "srcs": srcs, "slot": getattr(self, "_wslot", 0)}
        return st

    def wload(self, src, scr_name, half=False):
        s = getattr(self, "_wslot", 0)
        self._wslot = (s + 1) % 3
        if half:
            self.load(self.WB[:, s, 0:4096], src, self.wbb[s], reads=[self.scrb[scr_name]])
        else:
            self.load(self.WB[:, s, :], src, self.wbb[s], reads=[self.scrb[scr_name]])
        return s

    def phase_own_r(self):
        NT, NTL = self.NT, self.NTL
        I, Sc = self.I, self.Sc
        cb = self.cstb
        fb = self.fsb
        a1 = self.C(self.C_A1, 32).rearrange("p (k c) -> p k c", k=KC)
        b1 = self.C(self.C_B1, 32).rearrange("p (k c) -> p k c", k=KC)
        a2 = self.C(self.C_A2, 16)
        b2 = self.C(self.C_B2, 16)
        self._wslot = 0
        for j in range(NTL - 1, -1, -1):
            t0 = j * T
            self.load(self.FS(0, 4096), I["lru_w"][:, :], fb[0], writes=[fb[0], fb[1]])
            self.cp("dve", self.SMW[:, :], self.FS(0, 4096), [fb[0], fb[1]], [self.smwb])
            self.norm_rows(lambda tt: I["x_own"][t0 + tt * 128: t0 + (tt + 1) * 128, :], 4, a1[:, :, 0], b1[:, :, 0])
            for hc in (0, 1):
                s = self.wload(Sc["w_in_b"][2 + hc], "w_in_b")
                XC = self.FS3(2, 4, T)
                self.load(XC[:, :, :], Sc["xc_st"][:, 4 * hc:4 * hc + 4, t0:t0 + T], fb[2], reads=[self.scrb["xc_st"]])
                XC, H = self._xb_gates(hc, T, 1, (0, T))
                HF = self.FS3(2, 4, T)
                self.load(HF[:, :, :], Sc["hf_st"][:, 4 * hc:4 * hc + 4, t0:t0 + T], fb[2], reads=[self.scrb["hf_st"]])
                self.tt("pool", H[:, :, :], H[:, :, :], HF[:, :, :], ALU.add, [fb[6], fb[2]], [fb[6]])
                wt = self.wb3(s)
                for c in range(4):
                    oc = 4 * hc + c
                    pf, pfb = self.next_pf()
                    self.mm(pf[:, :], [(wt[:, kc, c * 128:(c + 1) * 128], self.HNT[:, kc, :]) for kc in range(KC)],
                            self.hntb + [self.wbb[s]], [pfb])
                    g = c % 2
                    self.act(self.GT[:, g, :], pf[:, :], AF.Gelu_apprx_tanh, [pfb], [self.gtb[g]])
                    self.tt("dve", self.YQ[:, oc, :], self.GT[:, g, :], H[:, c, :], ALU.mult, [self.gtb[g], fb[6]], [self.yqb])
            for hc in (0, 1):
                s = self.wload(Sc["w_in_b"][4 + hc], "w_in_b")
                wt = self.wb3(s)
                for c in range(4):
                    pf, pfb = self.next_pf()
                    self.mm(pf[:, :], [(wt[:, kc, c * 128:(c + 1) * 128], self.HNT[:, kc, :]) for kc in range(KC)],
                            self.hntb + [self.wbb[s]], [pfb])
                    self.act(self.UG[:, 4 * hc + c, :], pf[:, :], AF.Gelu_apprx_tanh, [pfb], [self.ugb])
            sv = [self.wload(Sc["w_in_b"][6], "w_in_b"), self.wload(Sc["w_in_b"][7], "w_in_b")]
            VG = self.FS3(8, 2, 1024)
            SQ = self.FS(5, 1024)
            sm = self.smallb
            for tt in range(4):
                vg = VG[:, tt % 2, :]
                for cs in range(2):
                    wt = self.wb3(sv[cs])
                    pf, pfb = self.next_pf()
                    self.mm(pf[:, :], [(self.HNT[:, kc, tt * 128:(tt + 1) * 128], wt[:, kc, :]) for kc in range(KC)],
                            self.hntb + [self.wbb[sv[cs]]], [pfb])
                    self.act(vg[:, cs * 512:(cs + 1) * 512], pf[:, :], AF.Gelu_apprx_tanh, [pfb], [fb[8]])
                vg3 = vg.rearrange("p (g c) -> p g c", g=8)
                sq3 = SQ.rearrange("p (g c) -> p g c", g=8)
                su = self.SMALL[:, 64:72]
                ss = self.SMALL[:, 72:80]
                mean = self.SMALL[:, 80:88]
                var = self.SMALL[:, 88:96]
                nmr = self.SMALL[:, 96:104]
                self.S.op("dve", (lambda vg3=vg3: lambda e: e.tensor_reduce(out=su, in_=vg3, axis=AX.X, op=ALU.add))(),
                          [fb[8]], [sm])
                self.tt("pool", SQ, vg, vg, ALU.mult, [fb[8]], [fb[5]])
                self.S.op("dve", lambda e: e.tensor_reduce(out=ss, in_=sq3, axis=AX.X, op=ALU.add), [fb[5]], [sm])
                self.ts("dve", mean, su, 1.0 / 128, None, ALU.mult, None, [sm], [sm])
                self.tt("dve", var, mean, mean, ALU.mult, [sm], [sm])
                self.stt(var, ss, 1.0 / 128, var, ALU.mult, ALU.subtract, [sm], [sm])
                self.act(var, var, AF.Ln, [sm], [sm], bias=self.EPSC[:, 0:1])
                self.act(var, var, AF.Exp, [sm], [sm], scale=-0.5)
                self.tt("dve", nmr, mean, var, ALU.mult, [sm], [sm])
                self.tt("dve", sq3, vg3, var.unsqueeze(2).to_broadcast([P, 8, 128]), ALU.mult, [fb[8], sm], [fb[5]])
                self.tt("dve", self.VGN[:, tt, :].rearrange("p (g c) -> p g c", g=8), sq3,
                        nmr.unsqueeze(2).to_broadcast([P, 8, 128]), ALU.subtract, [fb[5], sm], [self.vgnb])
            self.load(self.FS(9, 1024), I["bsR"][:, :], fb[9])
            for g in range(8):
                pf, pfb = self.next_pf()
                groups = [(pf[:, tt * 128:(tt + 1) * 128],
                           [(self.VGN[:, tt, g * 128:(g + 1) * 128], self.WST[:, g * 128:(g + 1) * 128])])
                          for tt in range(4)]
                self.mm_multi(groups, [self.vgnb, self.smwb], [pfb])
                tmp = self.FS(5, 512, off=1024)
                self.tt("dve", tmp.rearrange("p (a b) -> p a b", a=4), pf[:, :].rearrange("p (a b) -> p a b", a=4),
                        self.FS(9, 128, off=g * 128).unsqueeze(1).to_broadcast([P, 4, 128]), ALU.add,
                        [pfb, fb[9]], [fb[5]])
                self.tt("dve", self.YQ[:, 8 + g, :], tmp, self.UG[:, g, :], ALU.mult, [fb[5], self.ugb], [self.yqb])
            self.load(self.FS(7), Sc["g1row"][:, :], fb[7], reads=[self.scrb["g1row"]])
            for tt in range(4):
                s = tt % 2
                xs = self.FS(s)
                self.load(xs, I["x_own"][t0 + tt * 128: t0 + (tt + 1) * 128, :], fb[s])
                for ds in range(4):
                    ws = self.wload(Sc["w_out_b"][ds], "w_out_b")
                    wt = self.wb3(ws)
                    pf, pfb = self.next_pf()
                    self.mm(pf[:, :], [(self.YQ[:, kc, tt * 128:(tt + 1) * 128], wt[:, kc, :]) for kc in range(KC)],
                            [self.yqb, self.wbb[ws]], [pfb])
                    tmp = self.FS(5, 512, off=1536)
                    self.tt("dve", tmp, pf[:, :], self.FS(7, 512, off=ds * 512), ALU.mult, [pfb, fb[7]], [fb[5]])
                    self.tt("pool", xs[:, ds * 512:(ds + 1) * 512], xs[:, ds * 512:(ds + 1) * 512], tmp, ALU.add,
                            [fb[5], fb[s]], [fb[s]])
                self.store(Sc["x1_st"][t0 + tt * 128: t0 + (tt + 1) * 128, :], xs, fb[s], [fb[s]], [self.scrb["x1_st"]])
                self.norm_tt(xs, fb[s], tt, a2, b2, [cb])
            if self.dbg == "x1":
                continue
            self.peer(j)
            self.final(j)
        if self.dbg == "x1":
            self.S.dma(lambda e: e.dma_start(out=self.OUT[:, :], in_=Sc["x1_st"][:, :]), self.outb,
                       [self.scrb["x1_st"]], [self.outb])

    def peer(self, j):
        I, Sc = self.I, self.Sc
        fb = self.fsb
        cb = self.cstb
        sm = self.smallb
        self.load(self.FS(8), I["kT_l"][:, :], fb[8])
        self.cp("dve", self.SMW[:, 0:2048], self.FS(8), [fb[8]], [self.smwb])
        kT = self.SMW[:, 0:2048].rearrange("p (a h e) -> p a h e", a=2, h=8)
        for qt in range(4):
            ws = self.wload(Sc["wq_b"][qt], "wq_b")
            wt = self.wb3(ws)
            for c in range(4):
                pf, pfb = self.next_pf()
                self.mm(pf[:, :], [(wt[:, kc, c * 128:(c + 1) * 128], self.HNT[:, kc, :]) for kc in range(KC)],
                        self.hntb + [self.wbb[ws]], [pfb])
                self.cp("act", self.YQ[:, 4 * qt + c, :], pf[:, :], [pfb], [self.yqb])
        for tt in range(4):
            ssb = self.FSA[:, (4 + tt) * 2048:(5 + tt) * 2048].rearrange("p (h a e) -> p h a e", h=8, a=2)
            for h2 in range(4):
                pf, pfb = self.next_pf()
                groups = []
                for hh in range(2):
                    h = 2 * h2 + hh
                    for a in range(2):
                        groups.append((pf[:, (2 * hh + a) * 128:(2 * hh + a + 1) * 128],
                                       [(self.YQ[:, 2 * h + a, tt * 128:(tt + 1) * 128], kT[:, a, h, :])]))
                self.mm_multi(groups, [self.yqb, self.smwb], [pfb])
                self.cp("act", ssb[:, 2 * h2:2 * h2 + 2, :, :],
                        pf[:, :].rearrange("p (h a e) -> p h a e", h=2, a=2), [pfb], [fb[4 + tt]])
        V1 = self.SMALL[:, 256:272]
        V2 = self.SMALL[:, 272:288]
        C16 = self.SMALL[:, 288:304]
        CALL = self.SMALL[:, 328:456].rearrange("p (h k) -> p h k", h=8)
        CEX = self.SMALL[:, 456:584].rearrange("p (h k) -> p h k", h=8)
        TMPK = self.FS(9, 256, off=3072 // 2)
        CAND = self.FS(9, 256, off=1792)
        for tt in range(4):
            ssb = self.FSA[:, (4 + tt) * 2048:(5 + tt) * 2048].rearrange("p (h a e) -> p h a e", h=8, a=2)
            for h in range(8):
                for a, V in ((0, V1), (1, V2)):
                    src = ssb[:, h, a, :]
                    self.S.op("dve", (lambda V=V, src=src: lambda e: e.max(out=V[:, 0:8], in_=src))(), [fb[4 + tt]], [sm])
                    self.S.op("dve", (lambda V=V, src=src: lambda e: e.match_replace(
                        out=TMPK[:, 0:128], in_to_replace=V[:, 0:8], in_values=src, imm_value=NEG))(),
                        [fb[4 + tt], sm], [fb[9]], attach=False)
                    self.S.op("dve", (lambda V=V: lambda e: e.max(out=V[:, 8:16], in_=TMPK[:, 0:128]))(), [fb[9]], [sm])
                self.tt("pool", CAND.rearrange("p (a b) -> p a b", a=16), V1.unsqueeze(2).to_broadcast([P, 16, 16]),
                        V2.unsqueeze(1).to_broadcast([P, 16, 16]), ALU.add, [sm], [fb[9]])
                self.S.op("dve", lambda e: e.max(out=C16[:, 0:8], in_=CAND), [fb[9]], [sm])
                self.S.op("dve", lambda e: e.match_replace(out=TMPK, in_to_replace=C16[:, 0:8], in_values=CAND,
                                                           imm_value=NEG), [fb[9], sm], [fb[9]], attach=False)
                self.S.op("dve", lambda e: e.max(out=C16[:, 8:16], in_=TMPK), [fb[9]], [sm])
                self.cp("dve", CALL[:, h, :], C16, [sm], [sm])
            tau8 = self.SMALL[:, 128 + tt * 8: 128 + tt * 8 + 8]
            nb8 = self.SMALL[:, 160 + tt * 8: 160 + tt * 8 + 8]
            zs8 = self.SMALL[:, 304:312]
            self.cp("dve", tau8, CALL[:, :, 15], [sm], [sm])
            self.tt("dve", CEX, CALL, CALL[:, :, 0:1].to_broadcast([P, 8, 16]), ALU.subtract, [sm], [sm])
            self.act(CEX, CEX, AF.Exp, [sm], [sm])
            self.S.op("dve", lambda e: e.tensor_reduce(out=zs8, in_=CEX, axis=AX.X, op=ALU.add), [sm], [sm])
            self.act(zs8, zs8, AF.Ln, [sm], [sm])
            self.tt("dve", nb8, zs8, CALL[:, :, 0], ALU.add, [sm], [sm])
            self.ts("dve", nb8, nb8, -1.0, None, ALU.mult, None, [sm], [sm])
        tau_all = self.SMALL[:, 128:160]
        nb_all = self.SMALL[:, 160:192]
        bias2 = self.SMALL[:, 192:224]
        self.ts("dve", tau_all, tau_all, -2.0e-5, None, ALU.add, None, [sm], [sm])
        self.tt("dve", bias2, tau_all, nb_all, ALU.add, [sm], [sm])
        for tt in range(4):
            ssb = self.FSA[:, (4 + tt) * 2048:(5 + tt) * 2048].rearrange("p (h a e) -> p h a e", h=8, a=2)
            self.tt("dve", ssb[:, :, 0, :], ssb[:, :, 0, :],
                    tau_all[:, tt * 8:(tt + 1) * 8].unsqueeze(2).to_broadcast([P, 8, 128]), ALU.subtract,
                    [fb[4 + tt], sm], [fb[4 + tt]])
        ZB = [self.FS(8), self.FS(9)]
        zbuf = [fb[8], fb[9]]
        yq2 = self.YQ[:, :, :].rearrange("p a n -> p (a n)")
        EB = [yq2[:, 0:2048], yq2[:, 2048:4096]]
        W2 = [yq2[:, 4096:6144], yq2[:, 6144:8192]]
        ebuf = [self.S.buf("eb0"), self.S.buf("eb1")]
        w2buf = [self.S.buf("w2b0"), self.S.buf("w2b1")]
        G2 = self.XCB[:, :, :].rearrange("p a n -> p (a n)")
        g2b = self.xcbb
        self.S.op("dve", lambda e: e.memset(yq2[:, 4096:4100], 0.0), [], [self.yqb, self.wbb[2]] + ebuf + w2buf + self.vhb)
        GA = self.XN[:, :].rearrange("p (a n) -> p a n", a=4)
        gab = self.xnb
        WACT = [self.UG, self.VGN[:, :, :].rearrange("p a n -> p (a n)").rearrange("p (c t) -> p c t", c=8)]
        wactb = [self.ugb, self.vgnb]
        OUTS = self.FSA[:, 0:4 * 2048].rearrange("p (t d) -> p t d", t=4)
        zi = 0
        vhalf = [0]

        def vstep(gv, ds, tt, vt, vtb):
            WTv = WACT[gv % 2]
            wtbv = wactb[gv % 2]
            pf, pfb = self.next_pf()
            self.mm(pf[:, :], [(WTv[:, c, tt * 128:(tt + 1) * 128], vt[:, c, :]) for c in range(8)],
                    [wtbv, vtb], [pfb])
            dst = OUTS[:, tt, ds * 512:(ds + 1) * 512]
            if gv == 0:
                self.cp("dve", dst, pf[:, :], [pfb], [fb[tt]])
            else:
                self.tt("dve", dst, dst, pf[:, :], ALU.add, [pfb, fb[tt]], [fb[tt]])

        def vload(gv, ds):
            hh = vhalf[0] % 2
            vhalf[0] += 1
            dstv = self.WB[:, 2, hh * 4096:(hh + 1) * 4096]
            self.load(dstv, Sc["v_b"][4 * gv + ds], self.vhb[hh], reads=[self.scrb["v_b"]])
            return dstv.rearrange("p (c n) -> p c n", c=8), self.vhb[hh]

        for g in range(17):
            pend = []
            if g > 0:
                for ds in range(4):
                    for tt in range(4):
                        pend.append((g - 1, ds, tt))
            vcur = {}
            if g == 16:
                for (gv, ds, tt) in pend:
                    if ds not in vcur:
                        vcur[ds] = vload(gv, ds)
                    vstep(gv, ds, tt, *vcur[ds])
                break
            WT_ = WACT[g % 2]
            wtb = wactb[g % 2]
            us = []
            for sl in range(2):
                self.load(self.WB[:, sl, :], Sc["uT_b"][2 * g + sl], self.wbb[sl], reads=[self.scrb["uT_b"]])
                us.append(sl)
            for tt in range(4):
                ssb = self.FSA[:, (4 + tt) * 2048:(5 + tt) * 2048].rearrange("p (h a e) -> p h a e", h=8, a=2)
                w2 = W2[tt % 2]
                w2b = w2buf[tt % 2]
                for hb in range(4):
                    z = ZB[zi % 2]
                    zb = zbuf[zi % 2]
                    eb = EB[zi % 2]
                    ebb = ebuf[zi % 2]
                    zi += 1
                    self.tt("pool", z.rearrange("p (j a b) -> p j a b", j=2, a=8),
                            ssb[:, 2 * hb:2 * hb + 2, 0, 8 * g:8 * g + 8].unsqueeze(3).to_broadcast([P, 2, 8, 128]),
                            ssb[:, 2 * hb:2 * hb + 2, 1, :].unsqueeze(2).to_broadcast([P, 2, 8, 128]), ALU.add,
                            [fb[4 + tt]], [zb])
                    for jh in range(2):
                        h = 2 * hb + jh
                        self.act(eb[:, jh * 1024:(jh + 1) * 1024], z[:, jh * 1024:(jh + 1) * 1024], AF.Exp,
                                 [zb, sm], [ebb], bias=bias2[:, tt * 8 + h: tt * 8 + h + 1])
                    if hb == 0:
                        self.stt(w2, z, 0.0, eb, ALU.is_ge, ALU.mult, [zb, ebb], [w2b])
                    else:
                        self.stt(G2, z, 0.0, eb, ALU.is_ge, ALU.mult, [zb, ebb], [g2b])
                        self.tt("dve", w2, w2, G2, ALU.add, [g2b, w2b], [w2b])
                    if pend:
                        gv, ds, ttv = pend.pop(0)
                        if ds not in vcur:
                            vcur[ds] = vload(gv, ds)
                        vstep(gv, ds, ttv, *vcur[ds])
                wcur = w2[:, 0:1024]
                self.tt("dve", wcur, wcur, w2[:, 1024:2048], ALU.add, [w2b], [w2b])
                for sl in range(2):
                    ut = self.wb3(us[sl])
                    pf, pfb = self.next_pf()
                    self.mm(pf[:, :], [(self.HNT[:, kc, tt * 128:(tt + 1) * 128], ut[:, kc, :]) for kc in range(KC)],
                            self.hntb + [self.wbb[us[sl]]], [pfb])
                    ga = GA[:, sl, :]
                    wa = GA[:, 2 + sl, :]
                    self.act(ga, pf[:, :], AF.Gelu_apprx_tanh, [pfb], [gab])
                    self.tt("dve", wa, ga, wcur[:, sl * 512:(sl + 1) * 512], ALU.mult, [gab, w2b], [gab])
                    pb, pbb = self.next_pb()
                    self.trs([(pb[:, c * 128:(c + 1) * 128], wa[:, c * 128:(c + 1) * 128]) for c in range(4)],
                             [gab], [pbb])
                    self.cp("act", WT_[:, 4 * sl:4 * sl + 4, tt * 128:(tt + 1) * 128],
                            pb[:, 0:512].rearrange("p (c t) -> p c t", c=4), [pbb], [wtb])
        self._wslot = 0
        self.S.op("dve", lambda e: e.memset(yq2[:, 4096:4100], 0.0), [], [self.yqb, self.wbb[2]] + ebuf + w2buf + self.vhb)

    def final(self, j):
        I, Sc = self.I, self.Sc
        fb = self.fsb
        sm = self.smallb
        t0 = j * T
        self.load(self.FS(5), Sc["g2row"][:, :], fb[5], reads=[self.scrb["g2row"]])
        self.load(self.FS(6), I["fgR"][:, :], fb[6])
        for tt in range(4):
            x1 = self.FS(4)
            self.load(x1, Sc["x1_st"][t0 + tt * 128: t0 + (tt + 1) * 128, :], fb[4], reads=[self.scrb["x1_st"]])
            o = self.FS(tt)
            self.tt("dve", o, o, self.FS(5), ALU.mult, [fb[tt], fb[5]], [fb[tt]])
            self.tt("pool", o, o, x1, ALU.add, [fb[tt], fb[4]], [fb[tt]])
            ssq = self.SMALL[:, 8:9]
            rstd = self.SMALL[:, 9:10]
            self.act(self.XN[:, :], o, AF.Square, [fb[tt]], [self.xnb, sm], accum=ssq)
            self.act(rstd, ssq, AF.Ln, [sm], [sm], scale=1.0 / D, bias=self.EPSC[:, 0:1])
            self.act(rstd, rstd, AF.Exp, [sm], [sm], scale=-0.5)
            self.stt(o, o, rstd, self.FS(6), ALU.mult, ALU.mult, [fb[tt], fb[6], sm], [fb[tt]])
            self.store(self.OUT[t0 + tt * 128: t0 + (tt + 1) * 128, :], o, fb[tt], [fb[tt]], [self.outb])


def _wtiles(w, ncols):
    K_, C = w.shape
    t = w.reshape(KC, P, C // ncols, ncols).transpose(2, 1, 0, 3)
    return np.ascontiguousarray(t).reshape(C // ncols, P, KC * ncols)


def _fm(v, n):
    return np.ascontiguousarray(v.reshape(n, P).T)


def prepare_inputs(inp, NT):
    f = lambda a: np.asarray(a, dtype=np.float32)
    x, c, ctx, c_ctx = f(inp["x"]), f(inp["c"]), f(inp["ctx"]), f(inp["c_ctx"])
    w_mod, b_mod = f(inp["w_mod"])[0], f(inp["b_mod"])[0]
    shared = {}
    shared["w_mod_l"] = _wtiles(w_mod, 512)
    shared["bmodF"] = _fm(b_mod, 96)
    shared["bmodR"] = np.ascontiguousarray(np.broadcast_to(
        np.concatenate([b_mod[2 * D:3 * D], b_mod[5 * D:6 * D]])[None, :], (P, 2 * D)))
    shared["gF"] = np.ascontiguousarray(np.concatenate([_fm(f(inp["norm1_g"])[0], 16), _fm(f(inp["norm2_g"])[0], 16)], axis=1))
    shared["fgR"] = np.ascontiguousarray(np.broadcast_to(f(inp["final_g"])[None, :], (P, D)))
    shared["w_in_l"] = _wtiles(f(inp["w_in"])[0], 512)
    shared["w_out_l"] = _wtiles(f(inp["w_out"])[0], 512)
    shared["wq_l"] = _wtiles(f(inp["peer_wq"])[0], 512)
    shared["uT_l"] = _wtiles(np.ascontiguousarray(f(inp["peer_u"])[0].T), 512)
    v = f(inp["peer_v"])[0]
    shared["v_l"] = np.ascontiguousarray(v.reshape(16, 8, P, 4, 512).transpose(0, 3, 2, 1, 4)).reshape(64, P, 4096)
    cw, cbias = f(inp["conv_w"])[0], f(inp["conv_b"])[0]
    conv = np.concatenate([cw.reshape(4, 8, P), cbias.reshape(1, 8, P)], axis=0)
    shared["conv_l"] = np.ascontiguousarray(conv.transpose(2, 1, 0)).reshape(P, 40)
    wa, wx = f(inp["lru_wa"])[0], f(inp["lru_wx"])[0]
    lw = np.stack([wa, wx], axis=0)
    shared["lru_w"] = np.ascontiguousarray(lw.transpose(3, 0, 1, 2, 4)).reshape(P, 4096)
    lv = np.stack([f(inp["lru_ba"])[0], f(inp["lru_bx"])[0], f(inp["lru_lambda"])[0]], axis=0)
    shared["lru_v"] = np.ascontiguousarray(lv.reshape(3, 2, 8, P).transpose(3, 0, 1, 2)).reshape(P, 48)
    sw = f(inp["sgu_w"])[0]
    shared["wsT_l"] = np.ascontiguousarray(sw.transpose(2, 0, 1)).reshape(P, 1024)
    shared["bsR"] = np.ascontiguousarray(np.broadcast_to(f(inp["sgu_b"])[0].reshape(1, 1024), (P, 1024)))
    k = np.stack([f(inp["peer_k1"])[0], f(inp["peer_k2"])[0]], axis=0)
    shared["kT_l"] = np.ascontiguousarray(k.transpose(3, 0, 1, 2)).reshape(P, 2048)
    maps = []
    L = 4 * NT
    for core in range(8):
        b, q = core // 4, core % 4
        m = dict(shared)
        m["x_own"] = np.ascontiguousarray(x[b, q * NT:(q + 1) * NT])
        m["x_b"] = np.ascontiguousarray(x[b])
        halo = np.zeros((P, D), np.float32)
        if q > 0:
            halo[0:2] = x[b, q * NT - 2:q * NT]
        if q < 3:
            halo[2] = x[b, (q + 1) * NT]
        m["x_halo"] = halo
        m["ctx_b"] = np.ascontiguousarray(ctx[b])
        m["s_in"] = np.ascontiguousarray(np.stack([_fm(c[b], 16), _fm(c_ctx, 16)], axis=2)).reshape(P, 32)
        sel = np.zeros((P, 12), np.float32)
        sel[:, q] = 1.0
        sel[:, 4 + q] = 1.0
        sel[:, 8] = sel[:, 9] = 1.0 if q > 0 else 0.0
        sel[:, 10] = 1.0 if q < 3 else 0.0
        m["sel"] = sel
        maps.append(m)
    return maps


_NC_CACHE = {}


def run(inp, NT, dbg=None):
    key = (NT, dbg)
    if key not in _NC_CACHE:
        _NC_CACHE[key] = K(NT, dbg).build()
    nc = _NC_CACHE[key]
    maps = prepare_inputs(inp, NT)
    res = run_bass_kernel_spmd(nc, maps, core_ids=list(range(8)))
    B = 2
    out = np.zeros((B, 4 * NT, D), np.float32)
    for core in range(8):
        b, q = core // 4, core % 4
        out[b, q * NT:(q + 1) * NT] = res.results[core]["out"]
    return out, res


def kernel(**inputs):
    import os
    NT = np.asarray(inputs["x"]).shape[1] // 4
    if os.environ.get("KPROBE_NT"):
        NT2 = int(os.environ["KPROBE_NT"])
        inp = dict(inputs)
        inp["x"] = np.ascontiguousarray(np.asarray(inputs["x"])[:, :4 * NT2])
        o, _ = run(inp, NT2)
        out = np.zeros((2, 4 * NT, D), np.float32)
        out[:, :4 * NT2] = o
        return out
    out, _ = run(inputs, NT)
    return out
```

```python
import numpy as np
from contextlib import ExitStack
import concourse.bass as bass
import concourse.mybir as mybir
from concourse.bass_utils import run_bass_kernel_spmd

F32 = mybir.dt.float32
BF16 = mybir.dt.bfloat16
AF = mybir.ActivationFunctionType
ALU = mybir.AluOpType
AX = mybir.AxisListType

P = 128
D = 2048
KC = 16
T = 512
EPS = 1e-6
NEG = -1.0e30


class Buf:
    __slots__ = ("name", "last_w", "readers", "dsem", "dcount")

    def __init__(self, name):
        self.name = name
        self.last_w = None
        self.readers = {}
        self.dsem = None
        self.dcount = 0


class Sched:
    ENG = ("pe", "act", "dve", "pool", "sp")
    COMPUTE = ("pe", "act", "dve", "pool")

    def __init__(self, nc, stack):
        self.nc = nc
        self.stack = stack
        self.e = {}
        for n in self.ENG:
            sem = stack.enter_context(nc.semaphore("s_" + n))
            self.e[n] = dict(sem=sem, count=0, ops=[], waited={})
        self.nb = 0

    def buf(self, name, dma=False):
        self.nb += 1
        b = Buf("%s_%d" % (name, self.nb))
        if dma:
            b.dsem = self.stack.enter_context(self.nc.semaphore("d%d" % self.nb))
        return b

    def _collect(self, eng, reads, writes):
        need = {}

        def add(ev, raw):
            if ev is None:
                return
            key, sem, val = ev
            if key == eng and (eng == "pe" or not raw):
                return
            if key not in need or need[key][1] < val:
                need[key] = (sem, val)

        for b in reads:
            if b.last_w:
                for ev in b.last_w.values():
                    add(ev, True)
        for b in writes:
            if b.last_w:
                for ev in b.last_w.values():
                    add(ev, False)
            for ev in b.readers.values():
                add(ev, False)
        E = self.e[eng]
        waits = []
        for key, (sem, val) in need.items():
            if E["waited"].get(key, 0) >= val:
                continue
            E["waited"][key] = val
            waits.append((key, sem, val))
        return waits

    def _update(self, ev, reads, writes):
        key = ev[0]
        for b in reads:
            old = b.readers.get(key)
            if old is None or old[2] < ev[2]:
                b.readers[key] = ev
        for b in writes:
            if b.last_w is None:
                b.last_w = {}
            b.last_w[key] = ev
            b.readers = {}

    def op(self, eng, fn, reads=(), writes=(), attach=True):
        E = self.e[eng]
        waits = self._collect(eng, reads, writes)
        E["count"] += 1
        ev = (eng, E["sem"], E["count"])
        E["ops"].append(dict(waits=waits, fn=fn, kind="op", idx=E["count"], attach=attach and eng != "pe"))
        self._update(ev, reads, writes)
        return ev

    def dma(self, fn, owner, reads=(), writes=(), queue="sp"):
        Q = self.e[queue]
        waits = self._collect(queue, reads, writes)
        owner.dcount += 16
        ev = ("d_" + owner.name, owner.dsem, owner.dcount)
        Q["ops"].append(dict(waits=waits, fn=fn, kind="dma", inc=(owner.dsem, 16), attach=True))
        self._update(ev, reads, writes)
        return ev

    def final_wait(self, eng, bufs):
        waits = self._collect(eng, bufs, bufs)
        self.e[eng]["ops"].append(dict(waits=waits, fn=None, kind="wait", attach=False))

    def emit(self):
        nc = self.nc
        waited = {n: set() for n in self.COMPUTE}
        for n in self.ENG:
            for o in self.e[n]["ops"]:
                for key, sem, val in o["waits"]:
                    if key in waited:
                        waited[key].add(val)
        rank = {}
        for n in self.COMPUTE:
            rank[n] = {v: i + 1 for i, v in enumerate(sorted(waited[n]))}
        with nc.Block() as block:
            def run(name):
                def body(e):
                    for o in self.e[name]["ops"]:
                        ws = []
                        for key, sem, val in o["waits"]:
                            ws.append((sem, rank[key][val] if key in rank else val))
                        fn = o["fn"]
                        att = None
                        if fn is not None and o["attach"] and ws:
                            att = ws.pop()
                        for sem, val in ws:
                            e.wait_ge(sem, val)
                        if fn is None:
                            continue
                        r = fn(e)
                        first, last = r if isinstance(r, tuple) else (r, r)
                        if att is not None:
                            first._wait_ge(att[0], att[1])
                        if o["kind"] == "dma":
                            last.then_inc(o["inc"][0], o["inc"][1])
                        elif o["idx"] in waited[name]:
                            last.then_inc(self.e[name]["sem"], 1)
                return body
            block.tensor(run("pe"))
            block.scalar(run("act"))
            block.vector(run("dve"))
            block.gpsimd(run("pool"))
            block.sync(run("sp"))


class K:
    def __init__(self, NT, dbg=None):
        self.NT = NT
        self.NTL = NT // T
        self.dbg = dbg

    def act(self, out, in_, func, reads, writes, bias=None, scale=None, accum=None):
        kw = {}
        if bias is not None:
            kw["bias"] = bias
        if scale is not None:
            kw["scale"] = scale
        if accum is not None:
            kw["accum_out"] = accum
        return self.S.op("act", lambda e: e.activation(out=out, in_=in_, func=func, **kw), reads, writes,
                         attach=(accum is None))

    def tt(self, eng, out, in0, in1, op, reads, writes):
        return self.S.op(eng, lambda e: e.tensor_tensor(out=out, in0=in0, in1=in1, op=op), reads, writes)

    def ts(self, eng, out, in0, s1, s2, op0, op1, reads, writes):
        if op1 is None:
            return self.S.op(eng, lambda e: e.tensor_scalar(out=out, in0=in0, scalar1=s1, scalar2=None, op0=op0), reads, writes)
        return self.S.op(eng, lambda e: e.tensor_scalar(out=out, in0=in0, scalar1=s1, scalar2=s2, op0=op0, op1=op1), reads, writes)

    def stt(self, out, in0, scalar, in1, op0, op1, reads, writes):
        return self.S.op("dve", lambda e: e.scalar_tensor_tensor(out=out, in0=in0, scalar=scalar, in1=in1, op0=op0, op1=op1), reads, writes)

    def cp(self, eng, out, in_, reads, writes):
        if eng == "act":
            return self.S.op("act", lambda e: e.activation(out=out, in_=in_, func=AF.Copy), reads, writes)
        return self.S.op(eng, lambda e: e.tensor_copy(out=out, in_=in_), reads, writes)

    def mm(self, out, pairs, reads, writes):
        def fn(e):
            n = len(pairs)
            ins = None
            for i, (l, r) in enumerate(pairs):
                ins = e.matmul(out, l, r, start=(i == 0), stop=(i == n - 1))
            return ins
        return self.S.op("pe", fn, reads, writes)

    def mm_multi(self, groups, reads, writes):
        def fn(e):
            ins = None
            for out, pairs in groups:
                n = len(pairs)
                for i, (l, r) in enumerate(pairs):
                    ins = e.matmul(out, l, r, start=(i == 0), stop=(i == n - 1))
            return ins
        return self.S.op("pe", fn, reads, writes)

    def trs(self, items, reads, writes):
        ident = self.IDENT

        def fn(e):
            ins = None
            for out, in_ in items:
                ins = e.transpose(out=out, in_=in_, identity=ident[:])
            return ins
        return self.S.op("pe", fn, reads, writes)

    def load(self, out, in_, owner, reads=(), writes=None):
        if writes is None:
            writes = [owner]
        return self.S.dma(lambda e: e.dma_start(out=out, in_=in_, allow_slow_non_contiguous=True), owner, reads, writes)

    def store(self, out, in_, owner, reads, writes=()):
        return self.S.dma(lambda e: e.dma_start(out=out, in_=in_, allow_slow_non_contiguous=True), owner, reads, writes)

    def build(self):
        NT, NTL = self.NT, self.NTL
        nc = bass.Bass("TRN2", target_bir_lowering=False)
        self.nc = nc

        def din(name, shape, dt=F32):
            return nc.dram_tensor(name, list(shape), dt, kind="ExternalInput").ap()

        def dscr(name, shape, dt):
            return nc.dram_tensor(name, list(shape), dt, kind="Internal").ap()

        I = {}
        I["x_own"] = din("x_own", [NT, D])
        I["x_b"] = din("x_b", [4 * NT, D])
        I["x_halo"] = din("x_halo", [P, D])
        I["ctx_b"] = din("ctx_b", [256, D])
        I["s_in"] = din("s_in", [P, 32])
        I["sel"] = din("sel", [P, 12])
        I["w_mod_l"] = din("w_mod_l", [24, P, 8192])
        I["bmodF"] = din("bmodF", [P, 96])
        I["bmodR"] = din("bmodR", [P, 4096])
        I["gF"] = din("gF", [P, 32])
        I["fgR"] = din("fgR", [P, D])
        I["w_in_l"] = din("w_in_l", [8, P, 8192])
        I["w_out_l"] = din("w_out_l", [4, P, 8192])
        I["wq_l"] = din("wq_l", [4, P, 8192])
        I["uT_l"] = din("uT_l", [32, P, 8192])
        I["v_l"] = din("v_l", [64, P, 4096])
        I["conv_l"] = din("conv_l", [P, 40])
        I["lru_w"] = din("lru_w", [P, 4096])
        I["lru_v"] = din("lru_v", [P, 48])
        I["wsT_l"] = din("wsT_l", [P, 1024])
        I["bsR"] = din("bsR", [P, 1024])
        I["kT_l"] = din("kT_l", [P, 2048])
        self.I = I
        OUT = nc.dram_tensor("out", [NT, D], F32, kind="ExternalOutput").ap()
        self.OUT = OUT
        if self.dbg:
            self.DBG = nc.dram_tensor("dbg", [P, 4096], F32, kind="ExternalOutput").ap()
        Sc = {}
        Sc["w_in_b"] = dscr("w_in_b", [8, P, 8192], BF16)
        Sc["w_out_b"] = dscr("w_out_b", [4, P, 8192], BF16)
        Sc["wq_b"] = dscr("wq_b", [4, P, 8192], BF16)
        Sc["uT_b"] = dscr("uT_b", [32, P, 8192], BF16)
        Sc["v_b"] = dscr("v_b", [64, P, 4096], BF16)
        Sc["xc_st"] = dscr("xc_st", [P, 8, NT], F32)
        Sc["hf_st"] = dscr("hf_st", [P, 8, NT], F32)
        Sc["x1_st"] = dscr("x1_st", [NT, D], F32)
        Sc["g1row"] = dscr("g1row", [P, D], F32)
        Sc["g2row"] = dscr("g2row", [P, D], F32)
        self.Sc = Sc

        with ExitStack() as st:
            S = Sched(nc, st)
            self.S = S

            def sb(name, shape, dt):
                return st.enter_context(nc.sbuf_tensor(name, list(shape), dt))

            self.FSA = sb("FSA", [P, 10 * 2048], F32)
            self.fsb = [S.buf("fs%d" % i, dma=True) for i in range(10)]
            self.HNT = sb("HNT", [P, KC, T], BF16)
            self.hntb = [S.buf("hnt%d" % i) for i in range(4)]
            self.YQ = sb("YQ", [P, KC, T], BF16)
            self.yqb = S.buf("yq")
            self.UG = sb("UG", [P, 8, T], BF16)
            self.ugb = S.buf("ug")
            self.VGN = sb("VGN", [P, 4, 1024], BF16)
            self.vgnb = S.buf("vgn")
            self.XCB = sb("XCB", [P, 4, T], BF16)
            self.xcbb = S.buf("xcb")
            self.XN = sb("XN", [P, 2048], BF16)
            self.xnb = S.buf("xn")
            self.WB = sb("WB", [P, 3, 8192], BF16)
            self.wbb = [S.buf("wb%d" % i, dma=True) for i in range(3)]
            self.vhb = [S.buf("vh0", dma=True), S.buf("vh1", dma=True)]
            self.cva = [S.buf("cva0", dma=True), S.buf("cva1", dma=True)]
            self.cvb = [S.buf("cvb%d" % i, dma=True) for i in range(4)]
            self.GT = sb("GT", [P, 2, T], BF16)
            self.gtb = [S.buf("gt0"), S.buf("gt1")]
            self.IDENT = sb("IDENT", [P, P], BF16)
            self.SMW = sb("SMW", [P, 4096], BF16)
            self.smwb = S.buf("smw")
            self.WST = sb("WST", [P, 1024], BF16)
            self.CST = sb("CST", [P, 512], F32)
            self.cstb = S.buf("cst", dma=True)
            self.SMALL = sb("SMALL", [P, 640], F32)
            self.smallb = S.buf("small")
            self.EPSC = sb("EPSC", [P, 2], F32)
            self.STT = sb("STT", [P, 128], F32)
            self.sttb = S.buf("stt")
            self.PF = [st.enter_context(nc.psum_tensor("PF%d" % i, [P, 512], F32)) for i in range(6)]
            self.pfb = [S.buf("pf%d" % i) for i in range(6)]
            self.PB = [st.enter_context(nc.psum_tensor("PB%d" % i, [P, 1024], BF16)) for i in range(2)]
            self.pbb = [S.buf("pb%d" % i) for i in range(2)]
            self.pfi = 0
            self.pbi = 0
            self.outb = S.buf("outd", dma=True)
            self.scrb = {k: S.buf("scr_" + k, dma=True) for k in Sc}

            self.phase_consts()
            self.phase_mods()
            self.phase_convert()
            self.phase_chains()
            self.phase_own_f()
            self.phase_own_r()

            allb = self.fsb + self.wbb + [self.outb, self.cstb] + list(self.scrb.values())
            S.final_wait("sp", allb)
            S.emit()
        return nc

    def FS(self, i, n=2048, off=0):
        return self.FSA[:, i * 2048 + off: i * 2048 + off + n]

    def FS3(self, i, a, b, off=0):
        return self.FSA[:, i * 2048 + off: i * 2048 + off + a * b].rearrange("p (a b) -> p a b", a=a)

    def next_pf(self):
        i = self.pfi
        self.pfi = (self.pfi + 1) % 6
        return self.PF[i], self.pfb[i]

    def next_pb(self):
        i = self.pbi
        self.pbi = (self.pbi + 1) % 2
        return self.PB[i], self.pbb[i]

    def wb3(self, s):
        return self.WB[:, s, :].rearrange("p (k n) -> p k n", k=KC)

    C_CONV = 0
    C_LV = 40
    C_CL = 88
    C_CL2 = 104
    C_SEL = 120
    C_GF = 132
    C_SIN = 164
    C_MODF = 196
    C_A1 = 324
    C_B1 = 356
    C_A2 = 388
    C_B2 = 404
    C_BMF = 420
    C_END = 484

    def C(self, off, n):
        return self.CST[:, off:off + n]

    def phase_consts(self):
        S, I = self.S, self.I
        cb = self.cstb
        S.op("dve", lambda e: e.memset(self.EPSC[:, :], EPS), [], [self.smallb])
        idf = self.FS(0, 128)
        S.op("pool", lambda e: e.iota(idf, pattern=[[1, 128]], base=0, channel_multiplier=-1,
                                      allow_small_or_imprecise_dtypes=True), [], [self.fsb[0]])
        S.op("dve", lambda e: e.tensor_single_scalar(out=self.IDENT[:], in_=idf, scalar=0.0, op=ALU.is_equal),
             [self.fsb[0]], [self.smwb])
        self.load(self.C(self.C_CONV, 40), I["conv_l"][:, :], cb)
        self.load(self.C(self.C_LV, 48), I["lru_v"][:, :], cb)
        self.load(self.C(self.C_SEL, 12), I["sel"][:, :], cb)
        self.load(self.C(self.C_GF, 32), I["gF"][:, :], cb)
        self.load(self.C(self.C_SIN, 32), I["s_in"][:, :], cb)
        for jj, j in enumerate((0, 1, 3, 4)):
            self.load(self.C(self.C_BMF + 16 * jj, 16), I["bmodF"][:, 16 * j:16 * j + 16], cb)
        lam = self.C(self.C_LV + 32, 16)
        cl = self.C(self.C_CL, 16)
        cl2 = self.C(self.C_CL2, 16)
        self.act(cl, lam, AF.Exp, [cb], [cb], scale=-1.0)
        self.act(cl, cl, AF.Ln, [cb], [cb], bias=1.0)
        self.ts("dve", cl2, cl, -16.0, None, ALU.mult, None, [cb], [cb])
        self.ts("dve", cl, cl, -8.0, None, ALU.mult, None, [cb], [cb])
        sin = self.C(self.C_SIN, 32)
        self.act(sin, sin, AF.Silu, [cb], [cb])
        self.load(self.FS(0, 4096), I["lru_w"][:, :], self.fsb[0], writes=[self.fsb[0], self.fsb[1]])
        self.cp("dve", self.SMW[:, :], self.FS(0, 4096), [self.fsb[0], self.fsb[1]], [self.smwb])
        self.load(self.FS(2, 1024), I["wsT_l"][:, :], self.fsb[2])
        self.cp("dve", self.WST[:, :], self.FS(2, 1024), [self.fsb[2]], [self.smwb])

    def phase_mods(self):
        S, I, Sc = self.S, self.I, self.Sc
        cb = self.cstb
        sin3 = self.C(self.C_SIN, 32).rearrange("p (k j) -> p k j", k=KC)
        srep = self.FS3(8, KC, 128)
        self.cp("dve", srep, sin3[:, :, 0:1].to_broadcast([P, KC, 128]), [cb], [self.fsb[8]])
        modf = self.C(self.C_MODF, 128).rearrange("p (j k c) -> p j k c", j=4, k=KC)
        bmf = self.C(self.C_BMF, 64).rearrange("p (j k) -> p j k", j=4)
        fm = {0: 0, 1: 1, 3: 2, 4: 3}
        rowdst = {2: ("g1row", Sc["g1row"]), 5: ("g2row", Sc["g2row"])}
        rb = 0
        for pi in range(24):
            j = pi // 4
            s0 = 4 * (pi % 2)
            bufs = self.fsb[s0:s0 + 4]
            pan = self.FSA[:, s0 * 2048:(s0 + 4) * 2048].rearrange("p (k n) -> p k n", k=KC)
            self.load(self.FSA[:, s0 * 2048:(s0 + 4) * 2048], I["w_mod_l"][pi], bufs[0], writes=bufs)
            if j in fm:
                pf, pfb = self.next_pf()
                groups = []
                for cc in range(4):
                    groups.append((pf[:, 2 * cc:2 * cc + 2],
                                   [(pan[:, kc, cc * 128:(cc + 1) * 128], sin3[:, kc, :]) for kc in range(KC)]))
                self.mm_multi(groups, bufs + [cb], [pfb])
                jj = fm[j]
                kc0 = 4 * (pi % 4)
                self.tt("dve", modf[:, jj, kc0:kc0 + 4, :], pf[:, 0:8].rearrange("p (c j) -> p c j", c=4),
                        bmf[:, jj, kc0:kc0 + 4].unsqueeze(2).to_broadcast([P, 4, 2]), ALU.add, [pfb, cb], [cb])
            else:
                pf, pfb = self.next_pf()
                self.mm(pf[:, :], [(srep[:, kc, :], pan[:, kc, :]) for kc in range(KC)], bufs + [self.fsb[8]], [pfb])
                ds = pi % 4
                which = 0 if j == 2 else 1
                bm = self.FS(9, 512, off=512 * (rb % 2))
                rt = self.FS(9, 512, off=1024 + 512 * (rb % 2))
                rb += 1
                self.load(bm, I["bmodR"][:, which * 2048 + ds * 512: which * 2048 + ds * 512 + 512], self.fsb[9])
                self.tt("dve", rt, pf[:, :], bm, ALU.add, [pfb, self.fsb[9]], [self.fsb[9]])
                nm, dst = rowdst[j]
                self.store(dst[:, ds * 512:(ds + 1) * 512], rt, self.scrb[nm], [self.fsb[9]], [self.scrb[nm]])
        gf = self.C(self.C_GF, 32)
        a1 = self.C(self.C_A1, 32).rearrange("p (k c) -> p k c", k=KC)
        b1 = self.C(self.C_B1, 32).rearrange("p (k c) -> p k c", k=KC)
        self.ts("dve", a1, modf[:, 1, :, :], 1.0, None, ALU.add, None, [cb], [cb])
        self.tt("dve", a1, a1, gf[:, 0:16].unsqueeze(2).to_broadcast([P, KC, 2]), ALU.mult, [cb], [cb])
        self.cp("dve", b1, modf[:, 0, :, :], [cb], [cb])
        a2 = self.C(self.C_A2, 16)
        b2 = self.C(self.C_B2, 16)
        self.ts("dve", a2, modf[:, 3, :, 0], 1.0, None, ALU.add, None, [cb], [cb])
        self.tt("dve", a2, a2, gf[:, 16:32], ALU.mult, [cb], [cb])
        self.cp("dve", b2, modf[:, 2, :, 0], [cb], [cb])

    def phase_convert(self):
        I, Sc = self.I, self.Sc
        jobs = []
        self.cvjobs = []
        for nm_s, nm_d, n, F in (("uT_l", "uT_b", 32, 8192), ("v_l", "v_b", 64, 4096)):
            for i in range(n):
                for h in range(F // 1024):
                    self.cvjobs.append((I[nm_s][i][:, h * 1024:(h + 1) * 1024], Sc[nm_d][i][:, h * 1024:(h + 1) * 1024], nm_d))
        self.cvk = 0
        self.cvjobs = []
        for nm_s, nm_d, n, F in (("w_in_l", "w_in_b", 8, 8192), ("w_out_l", "w_out_b", 4, 8192),
                                 ("wq_l", "wq_b", 4, 8192), ("uT_l", "uT_b", 32, 8192), ("v_l", "v_b", 64, 4096)):
            for i in range(n):
                for h in range(F // 4096):
                    jobs.append((I[nm_s][i][:, h * 4096:(h + 1) * 4096], Sc[nm_d][i][:, h * 4096:(h + 1) * 4096], nm_d))
        for ji, (src, dst, nm) in enumerate(jobs):
            s0 = 2 * (ji % 3)
            bufs = self.fsb[s0:s0 + 2]
            w = ji % 3
            half = self.WB[:, w, 0:4096]
            self.load(self.FSA[:, s0 * 2048:(s0 + 2) * 2048], src, bufs[0], writes=bufs)
            eng = "dve" if ji % 2 == 0 else "act"
            self.cp(eng, half, self.FSA[:, s0 * 2048:(s0 + 2) * 2048], bufs, [self.wbb[w]])
            self.store(dst, half, self.wbb[w], [self.wbb[w]], [self.scrb[nm]])

    def cv_emit(self, n):
        ugf = self.UG[:, :, :].rearrange("p a n -> p (a n)")
        for _ in range(n):
            if self.cvk >= len(self.cvjobs):
                return
            k = self.cvk
            self.cvk += 1
            src_, dst_, nm = self.cvjobs[k]
            a = k % 2
            b = k % 4
            st = self.FS(9, 1024, off=1024 * a)
            ob = ugf[:, b * 1024:(b + 1) * 1024]
            self.load(st, src_, self.cva[a])
            self.cp("pool", ob, st, [self.cva[a]], [self.cvb[b]])
            self.store(dst_, ob, self.cvb[b], [self.cvb[b]], [self.scrb[nm]])

    def norm_tt(self, xs, xsb, tt, A, B, areads, hn=None):
        HN, hnb = hn if hn is not None else (self.HNT, self.hntb)
        sm = self.smallb
        ssq = self.SMALL[:, 0:1]
        rstd = self.SMALL[:, 1:2]
        self.act(self.XN[:, :], xs, AF.Square, [xsb], [self.xnb, sm], accum=ssq)
        self.act(rstd, ssq, AF.Ln, [sm], [sm], scale=1.0 / D, bias=self.EPSC[:, 0:1])
        self.act(rstd, rstd, AF.Exp, [sm], [sm], scale=-0.5)
        self.act(self.XN[:, :], xs, AF.Copy, [xsb, sm], [self.xnb], scale=rstd)
        for half in range(2):
            pb, pbb = self.next_pb()
            self.trs([(pb[:, i * 128:(i + 1) * 128], self.XN[:, (half * 8 + i) * 128:(half * 8 + i + 1) * 128])
                      for i in range(8)], [self.xnb], [pbb])
            dst = HN[:, half * 8:half * 8 + 8, tt * 128:(tt + 1) * 128]
            pv = pb[:, :].rearrange("p (k n) -> p k n", k=8)
            self.tt("dve", dst, pv, A[:, half * 8:half * 8 + 8].unsqueeze(2).to_broadcast([P, 8, 128]), ALU.mult,
                    [pbb] + areads, [hnb[tt]])
            self.tt("pool", dst, dst, B[:, half * 8:half * 8 + 8].unsqueeze(2).to_broadcast([P, 8, 128]), ALU.add,
                    [hnb[tt]] + areads, [hnb[tt]])

    def norm_rows(self, rows_fn, ntt, A, B, hn=None, tts=None):
        for tt in (range(ntt) if tts is None else tts):
            s = tt % 2
            self.load(self.FS(s), rows_fn(tt), self.fsb[s])
            self.norm_tt(self.FS(s), self.fsb[s], tt, A, B, [self.cstb], hn=hn)

    def xb_half(self, hc, ncur, wxs, mode, d, scan_rng, snap_col=None, snap_idx=None, stash=None, from_stash=None,
                hn=None, filler=None):
        cb, sb_ = self.cstb, self.sttb
        nwin = ncur
        XC = self.FS3(2, 4, T)
        R = self.FS3(3, 4, T)
        Ii = self.FS3(4, 4, T)
        TM = self.FS3(5, 4, T)
        H = self.FS3(6, 4, T)
        fb = self.fsb
        conv = self.C(self.C_CONV, 40)
        if mode is not None:
            EXT = self.FSA[:, 7 * 2048: 7 * 2048 + 4 * 516].rearrange("p (c n) -> p c n", c=4)
            eb = [fb[7], fb[8]]
            sav = self.STT[:, 80:104].rearrange("p (c n) -> p c n", c=8)
            co = 3 if mode == "f" else 0
            for c in range(4):
                oc = 4 * hc + c
                pf, pfb = self.next_pf()
                wt = self.wb3(wxs[oc // 4])
                HN, hnb = hn if hn is not None else (self.HNT, self.hntb)
                self.mm(pf[:, 0:ncur], [(wt[:, kc, (oc % 4) * 128:(oc % 4 + 1) * 128], HN[:, kc, 0:ncur])
                                        for kc in range(KC)], hnb + [self.wbb[wxs[oc // 4]]], [pfb])
                self.cp("act" if c % 2 == 0 else "dve", EXT[:, c, co:co + ncur], pf[:, 0:ncur], [pfb], eb)
            if mode == "f":
                self.cp("dve", EXT[:, :, 0:3], sav[:, 4 * hc:4 * hc + 4, :], [sb_], eb)
                self.cp("dve", sav[:, 4 * hc:4 * hc + 4, :], EXT[:, :, ncur:ncur + 3], eb, [sb_])
            else:
                self.cp("dve", EXT[:, :, ncur:ncur + 3], sav[:, 4 * hc:4 * hc + 4, :], [sb_], eb)
                self.cp("dve", sav[:, 4 * hc:4 * hc + 4, :], EXT[:, :, 0:3], eb, [sb_])
            for c in range(4):
                oc = 4 * hc + c
                w = lambda k: conv[:, oc * 5 + k: oc * 5 + k + 1]
                self.ts("pool", XC[:, c, 0:nwin], EXT[:, c, 0:nwin], w(0), w(4), ALU.mult, ALU.add, eb + [cb], [fb[2]])
                for k in range(1, 4):
                    self.stt(XC[:, c, 0:nwin], EXT[:, c, k:k + nwin], w(k), XC[:, c, 0:nwin], ALU.mult, ALU.add,
                             eb + [cb, fb[2]], [fb[2]])
        else:
            self.load(XC[:, :, :], from_stash, fb[2], reads=[self.scrb["xc_st"]])
        if filler is not None:
            filler()
        return self._xb_gates(hc, nwin, d, scan_rng, snap_col, snap_idx)

    def load_wx(self, tiles, slots, src):
        for tno, s in zip(tiles, slots):
            self.load(self.WB[:, s, :], src[tno], self.wbb[s], reads=[self.scrb["w_in_b"]])

    def zero_sav(self):
        self.S.op("dve", lambda e: e.memset(self.STT[:, 80:104], 0.0), [], [self.sttb])

    def phase_chains(self):
        NT, NTL = self.NT, self.NTL
        I, Sc = self.I, self.Sc
        a1 = self.C(self.C_A1, 32).rearrange("p (k c) -> p k c", k=KC)
        b1 = self.C(self.C_B1, 32).rearrange("p (k c) -> p k c", k=KC)
        self.load_wx([0, 1], [0, 1], Sc["w_in_b"])
        self.S.op("dve", lambda e: e.memset(self.STT[:, :], 0.0), [], [self.sttb])
        self.S.op("pool", lambda e: e.memset(self.HNT[:, :, 256:384], 0.0), [], self.hntb)
        for d in (0, 1):
            self.zero_sav()
            self.norm_rows(lambda tt: I["ctx_b"][tt * 128:(tt + 1) * 128, :], 2, a1[:, :, 1], b1[:, :, 1])
            self.S.op("pool", lambda e: e.memset(self.HNT[:, :, 256:257], 0.0), [], self.hntb)
            for hc in (0, 1):
                self.xb_half(hc, 257, [0, 1], "f", d, (1, 257))
            base = 16 if d == 0 else 48
            idx = 0 if d == 0 else 3
            self.cp("dve", self.STT[:, base + 8 * idx: base + 8 * idx + 8], self.STT[:, 8 * d:8 * d + 8],
                    [self.sttb], [self.sttb])
        sel = self.C(self.C_SEL, 12)
        self.norm_rows(lambda tt: I["x_halo"][:, :], 1, a1[:, :, 0], b1[:, :, 0])
        sav = self.STT[:, 80:104].rearrange("p (c n) -> p c n", c=8)
        hal = self.SMALL[:, 16:48].rearrange("p (c n) -> p c n", c=8)
        for oc in range(8):
            pf, pfb = self.next_pf()
            wt = self.wb3(oc // 4)
            self.mm(pf[:, 0:128], [(wt[:, kc, (oc % 4) * 128:(oc % 4 + 1) * 128], self.HNT[:, kc, 0:128])
                                   for kc in range(KC)], self.hntb + [self.wbb[oc // 4]], [pfb])
            self.tt("dve", hal[:, oc, :], pf[:, 0:4], sel[:, 8:12], ALU.mult, [pfb, self.cstb], [self.smallb])

        items = []
        for j in range(3 * NTL + 1):
            if j == 0:
                rng = (1, T)
            elif j == 3 * NTL:
                rng = (0, 1)
            else:
                rng = (0, T)
            snap = j // NTL if (j > 0 and j % NTL == 0) else None
            items.append(dict(rows=(lambda tt, j=j: I["x_b"][j * T + tt * 128: j * T + (tt + 1) * 128, :]),
                              mode="f", d=0, rng=rng, snap_col=0, snap_idx=snap,
                              pre=(self.zero_sav if j == 0 else None), post=None))
        for j in range(4 * NTL - 1, NTL - 2, -1):
            if j == 4 * NTL - 1:
                rng = (0, T - 2)
            elif j == NTL - 1:
                rng = (T - 2, T)
            else:
                rng = (0, T)
            snap = None
            if (j + 1) % NTL == 0 and j != 4 * NTL - 1:
                snap = (j + 1) // NTL - 1
            items.append(dict(rows=(lambda tt, j=j: I["x_b"][j * T + tt * 128: j * T + (tt + 1) * 128, :]),
                              mode="b", d=1, rng=rng, snap_col=T - 2, snap_idx=snap,
                              pre=(self.zero_sav if j == 4 * NTL - 1 else None), post=None))

        def own_pre():
            for d in (0, 1):
                base = 16 if d == 0 else 48
                stt = self.STT[:, 8 * d:8 * d + 8]
                self.ts("dve", stt, self.STT[:, base:base + 8], sel[:, 4 * d:4 * d + 1], None, ALU.mult, None,
                        [self.sttb, self.cstb], [self.sttb])
                for i in range(1, 4):
                    self.stt(stt, self.STT[:, base + 8 * i: base + 8 * i + 8], sel[:, 4 * d + i:4 * d + i + 1], stt,
                             ALU.mult, ALU.add, [self.sttb, self.cstb], [self.sttb])
            self.S.op("dve", lambda e: e.memset(self.STT[:, 80:104], 0.0), [], [self.sttb])
            self.cp("dve", sav[:, :, 1:3], hal[:, :, 0:2], [self.smallb], [self.sttb])

        def own_post(j):
            def f(hc, XC, H, rng):
                lo, hi = rng
                p0 = j * T - 1
                self.store(Sc["xc_st"][:, 4 * hc:4 * hc + 4, p0 + lo:p0 + hi], XC[:, :, lo:hi], self.fsb[2],
                           [self.fsb[2]], [self.scrb["xc_st"]])
                self.store(Sc["hf_st"][:, 4 * hc:4 * hc + 4, p0 + lo:p0 + hi], H[:, :, lo:hi], self.fsb[6],
                           [self.fsb[6]], [self.scrb["hf_st"]])
            return f

        for j in range(NTL):
            rng = (1, T) if j == 0 else (0, T)
            items.append(dict(rows=(lambda tt, j=j: I["x_own"][j * T + tt * 128: j * T + (tt + 1) * 128, :]),
                              mode="f", d=0, rng=rng, snap_col=None, snap_idx=None,
                              pre=(own_pre if j == 0 else None), post=own_post(j)))

        hntb2 = [self.S.buf("hn2_%d" % i) for i in range(4)]
        self.S.op("dve", lambda e: e.memset(self.EPSC[:, 1:2], 0.0), [], [self.yqb] + hntb2)
        self.S.op("pool", lambda e: e.memset(self.SMALL[:, 600:601], 0.0), [], [self.fsb[9], self.ugb] + self.cva + self.cvb)
        nhalf = 2 * (len(items))
        per = (len(self.cvjobs) + nhalf - 3) // (nhalf - 2) + 1
        HNS = [(self.HNT, self.hntb), (self.YQ, hntb2)]
        al, bl = a1[:, :, 0], b1[:, :, 0]
        self.norm_rows(items[0]["rows"], 4, al, bl, hn=HNS[0])
        for k, it in enumerate(items):
            hn = HNS[k % 2]
            nxt = items[k + 1] if k + 1 < len(items) else None
            if it["pre"] is not None:
                it["pre"]()
            for hc in (0, 1):
                filler = None
                if nxt is not None:
                    filler = (lambda nxt=nxt, hc=hc, k=k: (self.norm_rows(
                        nxt["rows"], 4, al, bl, hn=HNS[(k + 1) % 2], tts=(2 * hc, 2 * hc + 1)), self.cv_emit(per)))
                XC, H = self.xb_half(hc, T, [0, 1], it["mode"], it["d"], it["rng"], snap_col=it["snap_col"],
                                     snap_idx=it["snap_idx"], hn=hn, filler=filler)
                if it["post"] is not None:
                    it["post"](hc, XC, H, it["rng"])
        for hc in (0, 1):
            XC, H = self.xb_half_virtual(hc, hal, (0, 1))
            own_post(NTL)(hc, XC, H, (0, 1))
        self.cv_emit(len(self.cvjobs))
        self.S.op("dve", lambda e: e.memset(self.EPSC[:, 1:2], 0.0), [], [self.yqb] + hntb2)
        self.S.op("pool", lambda e: e.memset(self.SMALL[:, 600:601], 0.0), [], [self.fsb[9], self.ugb] + self.cva + self.cvb)

    def phase_own_f(self):
        return

    def xb_half_virtual(self, hc, hal, rng):
        cb, sb_ = self.cstb, self.sttb
        fb = self.fsb
        EXT = self.FSA[:, 7 * 2048: 7 * 2048 + 4 * 516].rearrange("p (c n) -> p c n", c=4)
        eb = [fb[7], fb[8]]
        sav = self.STT[:, 80:104].rearrange("p (c n) -> p c n", c=8)
        XC = self.FS3(2, 4, T)
        conv = self.C(self.C_CONV, 40)
        n = 8
        self.S.op("dve", lambda e: e.memset(EXT[:, :, 0:16], 0.0), [], eb)
        self.cp("dve", EXT[:, :, 0:3], sav[:, 4 * hc:4 * hc + 4, :], [sb_], eb)
        self.cp("dve", EXT[:, :, 3:4], hal[:, 4 * hc:4 * hc + 4, 2:3], [self.smallb], eb)
        for c in range(4):
            oc = 4 * hc + c
            w = lambda k: conv[:, oc * 5 + k: oc * 5 + k + 1]
            self.ts("pool", XC[:, c, 0:n], EXT[:, c, 0:n], w(0), w(4), ALU.mult, ALU.add, eb + [cb], [fb[2]])
            for k in range(1, 4):
                self.stt(XC[:, c, 0:n], EXT[:, c, k:k + n], w(k), XC[:, c, 0:n], ALU.mult, ALU.add,
                         eb + [cb, fb[2]], [fb[2]])
        return self.xb_tail(hc, n, 0, rng)

    def xb_tail(self, hc, nwin, d, scan_rng):
        return self._xb_gates(hc, nwin, d, scan_rng)

    def _xb_gates(self, hc, nwin, d, scan_rng, snap_col=None, snap_idx=None):
        cb, sb_ = self.cstb, self.sttb
        fb = self.fsb
        XC = self.FS3(2, 4, T)
        R = self.FS3(3, 4, T)
        Ii = self.FS3(4, 4, T)
        TM = self.FS3(5, 4, T)
        H = self.FS3(6, 4, T)
        self.cp("act", self.XCB[:, :, 0:nwin], XC[:, :, 0:nwin], [fb[2]], [self.xcbb])
        lw = self.SMW[:, :].rearrange("p (m d h j) -> p m d h j", m=2, d=2, h=8)
        lv = self.C(self.C_LV, 48).rearrange("p (m d h) -> p m d h", m=3, d=2)
        for m, dstt, dbuf in ((0, R, fb[3]), (1, Ii, fb[4])):
            for c in range(4):
                oc = 4 * hc + c
                pf, pfb = self.next_pf()
                self.mm(pf[:, 0:nwin], [(lw[:, m, d, oc, :], self.XCB[:, c, 0:nwin])], [self.xcbb, self.smwb], [pfb])
                self.act(dstt[:, c, 0:nwin], pf[:, 0:nwin], AF.Sigmoid, [pfb, cb], [dbuf], bias=lv[:, m, d, oc:oc + 1])
        cl = self.C(self.C_CL, 16).rearrange("p (d h) -> p d h", d=2)
        cl2 = self.C(self.C_CL2, 16).rearrange("p (d h) -> p d h", d=2)
        for c in range(4):
            oc = 4 * hc + c
            self.act(TM[:, c, 0:nwin], R[:, c, 0:nwin], AF.Exp, [fb[3], cb], [fb[5]], scale=cl2[:, d, oc:oc + 1])
            self.act(R[:, c, 0:nwin], R[:, c, 0:nwin], AF.Exp, [fb[3], cb], [fb[3]], scale=cl[:, d, oc:oc + 1])
        self.act(TM[:, :, 0:nwin], TM[:, :, 0:nwin], AF.Ln, [fb[5]], [fb[5]], scale=-1.0, bias=1.0)
        self.act(TM[:, :, 0:nwin], TM[:, :, 0:nwin], AF.Exp, [fb[5]], [fb[5]], scale=0.5)
        self.tt("dve", Ii[:, :, 0:nwin], Ii[:, :, 0:nwin], TM[:, :, 0:nwin], ALU.mult, [fb[4], fb[5]], [fb[4]])
        self.tt("dve", Ii[:, :, 0:nwin], Ii[:, :, 0:nwin], XC[:, :, 0:nwin], ALU.mult, [fb[4], fb[2]], [fb[4]])
        lo, hi = scan_rng
        stt = self.STT[:, 8 * d:8 * d + 8]
        for c in range(4):
            oc = 4 * hc + c
            if d == 0:
                o, a, b = H[:, c, lo:hi], R[:, c, lo:hi], Ii[:, c, lo:hi]
            else:
                o = H[:, c, lo:hi][:, ::-1]
                a = R[:, c, lo:hi][:, ::-1]
                b = Ii[:, c, lo:hi][:, ::-1]
            ini = stt[:, oc:oc + 1]
            self.S.op("dve", (lambda o=o, a=a, b=b, ini=ini: (lambda e: e.tensor_tensor_scan(
                out=o, data0=a, data1=b, initial=ini, op0=ALU.mult, op1=ALU.add)))(),
                [fb[3], fb[4], sb_], [fb[6]])
        last = hi - 1 if d == 0 else lo
        self.cp("dve", stt[:, 4 * hc:4 * hc + 4], H[:, :, last], [fb[6]], [sb_])
        if snap_idx is not None:
            base = 16 if d == 0 else 48
            sn = self.STT[:, base + 8 * snap_idx + 4 * hc: base + 8 * snap_idx + 4 * hc + 4]
            self.cp("dve", sn, H[:, :, snap_col], [fb[6]], [sb_])
        return XC, H

    def wstream(self, srcs):
        st = {"i": 0, "n": len(srcs), "srcs": srcs, "slot": getattr(self, "_wslot", 0)}
        return st

    def wload(self, src, scr_name, half=False):
        s = getattr(self, "_wslot", 0)
        self._wslot = (s + 1) % 3
        if half:
            self.load(self.WB[:, s, 0:4096], src, self.wbb[s], reads=[self.scrb[scr_name]])
        else:
            self.load(self.WB[:, s, :], src, self.wbb[s], reads=[self.scrb[scr_name]])
        return s

    def phase_own_r(self):
        NT, NTL = self.NT, self.NTL
        I, Sc = self.I, self.Sc
        cb = self.cstb
        fb = self.fsb
        a1 = self.C(self.C_A1, 32).rearrange("p (k c) -> p k c", k=KC)
        b1 = self.C(self.C_B1, 32).rearrange("p (k c) -> p k c", k=KC)
        a2 = self.C(self.C_A2, 16)
        b2 = self.C(self.C_B2, 16)
        self._wslot = 0
        for j in range(NTL - 1, -1, -1):
            t0 = j * T
            self.load(self.FS(0, 4096), I["lru_w"][:, :], fb[0], writes=[fb[0], fb[1]])
            self.cp("dve", self.SMW[:, :], self.FS(0, 4096), [fb[0], fb[1]], [self.smwb])
            self.norm_rows(lambda tt: I["x_own"][t0 + tt * 128: t0 + (tt + 1) * 128, :], 4, a1[:, :, 0], b1[:, :, 0])
            for hc in (0, 1):
                s = self.wload(Sc["w_in_b"][2 + hc], "w_in_b")
                XC = self.FS3(2, 4, T)
                self.load(XC[:, :, :], Sc["xc_st"][:, 4 * hc:4 * hc + 4, t0:t0 + T], fb[2], reads=[self.scrb["xc_st"]])
                XC, H = self._xb_gates(hc, T, 1, (0, T))
                HF = self.FS3(2, 4, T)
                self.load(HF[:, :, :], Sc["hf_st"][:, 4 * hc:4 * hc + 4, t0:t0 + T], fb[2], reads=[self.scrb["hf_st"]])
                self.tt("pool", H[:, :, :], H[:, :, :], HF[:, :, :], ALU.add, [fb[6], fb[2]], [fb[6]])
                wt = self.wb3(s)
                for c in range(4):
                    oc = 4 * hc + c
                    pf, pfb = self.next_pf()
                    self.mm(pf[:, :], [(wt[:, kc, c * 128:(c + 1) * 128], self.HNT[:, kc, :]) for kc in range(KC)],
                            self.hntb + [self.wbb[s]], [pfb])
                    g = c % 2
                    self.act(self.GT[:, g, :], pf[:, :], AF.Gelu_apprx_tanh, [pfb], [self.gtb[g]])
                    self.tt("dve", self.YQ[:, oc, :], self.GT[:, g, :], H[:, c, :], ALU.mult, [self.gtb[g], fb[6]], [self.yqb])
            for hc in (0, 1):
                s = self.wload(Sc["w_in_b"][4 + hc], "w_in_b")
                wt = self.wb3(s)
                for c in range(4):
                    pf, pfb = self.next_pf()
                    self.mm(pf[:, :], [(wt[:, kc, c * 128:(c + 1) * 128], self.HNT[:, kc, :]) for kc in range(KC)],
                            self.hntb + [self.wbb[s]], [pfb])
                    self.act(self.UG[:, 4 * hc + c, :], pf[:, :], AF.Gelu_apprx_tanh, [pfb], [self.ugb])
            sv = [self.wload(Sc["w_in_b"][6], "w_in_b"), self.wload(Sc["w_in_b"][7], "w_in_b")]
            VG = self.FS3(8, 2, 1024)
            SQ = self.FS(5, 1024)
            sm = self.smallb
            for tt in range(4):
                vg = VG[:, tt % 2, :]
                for cs in range(2):
                    wt = self.wb3(sv[cs])
                    pf, pfb = self.next_pf()
                    self.mm(pf[:, :], [(self.HNT[:, kc, tt * 128:(tt + 1) * 128], wt[:, kc, :]) for kc in range(KC)],
                            self.hntb + [self.wbb[sv[cs]]], [pfb])
                    self.act(vg[:, cs * 512:(cs + 1) * 512], pf[:, :], AF.Gelu_apprx_tanh, [pfb], [fb[8]])
                vg3 = vg.rearrange("p (g c) -> p g c", g=8)
                sq3 = SQ.rearrange("p (g c) -> p g c", g=8)
                su = self.SMALL[:, 64:72]
                ss = self.SMALL[:, 72:80]
                mean = self.SMALL[:, 80:88]
                var = self.SMALL[:, 88:96]
                nmr = self.SMALL[:, 96:104]
                self.S.op("dve", (lambda vg3=vg3: lambda e: e.tensor_reduce(out=su, in_=vg3, axis=AX.X, op=ALU.add))(),
                          [fb[8]], [sm])
                self.tt("pool", SQ, vg, vg, ALU.mult, [fb[8]], [fb[5]])
                self.S.op("dve", lambda e: e.tensor_reduce(out=ss, in_=sq3, axis=AX.X, op=ALU.add), [fb[5]], [sm])
                self.ts("dve", mean, su, 1.0 / 128, None, ALU.mult, None, [sm], [sm])
                self.tt("dve", var, mean, mean, ALU.mult, [sm], [sm])
                self.stt(var, ss, 1.0 / 128, var, ALU.mult, ALU.subtract, [sm], [sm])
                self.act(var, var, AF.Ln, [sm], [sm], bias=self.EPSC[:, 0:1])
                self.act(var, var, AF.Exp, [sm], [sm], scale=-0.5)
                self.tt("dve", nmr, mean, var, ALU.mult, [sm], [sm])
                self.tt("dve", sq3, vg3, var.unsqueeze(2).to_broadcast([P, 8, 128]), ALU.mult, [fb[8], sm], [fb[5]])
                self.tt("dve", self.VGN[:, tt, :].rearrange("p (g c) -> p g c", g=8), sq3,
                        nmr.unsqueeze(2).to_broadcast([P, 8, 128]), ALU.subtract, [fb[5], sm], [self.vgnb])
            self.load(self.FS(9, 1024), I["bsR"][:, :], fb[9])
            for g in range(8):
                pf, pfb = self.next_pf()
                groups = [(pf[:, tt * 128:(tt + 1) * 128],
                           [(self.VGN[:, tt, g * 128:(g + 1) * 128], self.WST[:, g * 128:(g + 1) * 128])])
                          for tt in range(4)]
                self.mm_multi(groups, [self.vgnb, self.smwb], [pfb])
                tmp = self.FS(5, 512, off=1024)
                self.tt("dve", tmp.rearrange("p (a b) -> p a b", a=4), pf[:, :].rearrange("p (a b) -> p a b", a=4),
                        self.FS(9, 128, off=g * 128).unsqueeze(1).to_broadcast([P, 4, 128]), ALU.add,
                        [pfb, fb[9]], [fb[5]])
                self.tt("dve", self.YQ[:, 8 + g, :], tmp, self.UG[:, g, :], ALU.mult, [fb[5], self.ugb], [self.yqb])
            self.load(self.FS(7), Sc["g1row"][:, :], fb[7], reads=[self.scrb["g1row"]])
            for tt in range(4):
                s = tt % 2
                xs = self.FS(s)
                self.load(xs, I["x_own"][t0 + tt * 128: t0 + (tt + 1) * 128, :], fb[s])
                for ds in range(4):
                    ws = self.wload(Sc["w_out_b"][ds], "w_out_b")
                    wt = self.wb3(ws)
                    pf, pfb = self.next_pf()
                    self.mm(pf[:, :], [(self.YQ[:, kc, tt * 128:(tt + 1) * 128], wt[:, kc, :]) for kc in range(KC)],
                            [self.yqb, self.wbb[ws]], [pfb])
                    tmp = self.FS(5, 512, off=1536)
                    self.tt("dve", tmp, pf[:, :], self.FS(7, 512, off=ds * 512), ALU.mult, [pfb, fb[7]], [fb[5]])
                    self.tt("pool", xs[:, ds * 512:(ds + 1) * 512], xs[:, ds * 512:(ds + 1) * 512], tmp, ALU.add,
                            [fb[5], fb[s]], [fb[s]])
                self.store(Sc["x1_st"][t0 + tt * 128: t0 + (tt + 1) * 128, :], xs, fb[s], [fb[s]], [self.scrb["x1_st"]])
                self.norm_tt(xs, fb[s], tt, a2, b2, [cb])
            if self.dbg == "x1":
                continue
            self.peer(j)
            self.final(j)
        if self.dbg == "x1":
            self.S.dma(lambda e: e.dma_start(out=self.OUT[:, :], in_=Sc["x1_st"][:, :]), self.outb,
                       [self.scrb["x1_st"]], [self.outb])

    def peer(self, j):
        I, Sc = self.I, self.Sc
        fb = self.fsb
        cb = self.cstb
        sm = self.smallb
        self.load(self.FS(8), I["kT_l"][:, :], fb[8])
        self.cp("dve", self.SMW[:, 0:2048], self.FS(8), [fb[8]], [self.smwb])
        kT = self.SMW[:, 0:2048].rearrange("p (a h e) -> p a h e", a=2, h=8)
        for qt in range(4):
            ws = self.wload(Sc["wq_b"][qt], "wq_b")
            wt = self.wb3(ws)
            for c in range(4):
                pf, pfb = self.next_pf()
                self.mm(pf[:, :], [(wt[:, kc, c * 128:(c + 1) * 128], self.HNT[:, kc, :]) for kc in range(KC)],
                        self.hntb + [self.wbb[ws]], [pfb])
                self.cp("act", self.YQ[:, 4 * qt + c, :], pf[:, :], [pfb], [self.yqb])
        for tt in range(4):
            ssb = self.FSA[:, (4 + tt) * 2048:(5 + tt) * 2048].rearrange("p (h a e) -> p h a e", h=8, a=2)
            for h2 in range(4):
                pf, pfb = self.next_pf()
                groups = []
                for hh in range(2):
                    h = 2 * h2 + hh
                    for a in range(2):
                        groups.append((pf[:, (2 * hh + a) * 128:(2 * hh + a + 1) * 128],
                                       [(self.YQ[:, 2 * h + a, tt * 128:(tt + 1) * 128], kT[:, a, h, :])]))
                self.mm_multi(groups, [self.yqb, self.smwb], [pfb])
                self.cp("act", ssb[:, 2 * h2:2 * h2 + 2, :, :],
                        pf[:, :].rearrange("p (h a e) -> p h a e", h=2, a=2), [pfb], [fb[4 + tt]])
        V1 = self.SMALL[:, 256:272]
        V2 = self.SMALL[:, 272:288]
        C16 = self.SMALL[:, 288:304]
        CALL = self.SMALL[:, 328:456].rearrange("p (h k) -> p h k", h=8)
        CEX = self.SMALL[:, 456:584].rearrange("p (h k) -> p h k", h=8)
        TMPK = self.FS(9, 256, off=3072 // 2)
        CAND = self.FS(9, 256, off=1792)
        for tt in range(4):
            ssb = self.FSA[:, (4 + tt) * 2048:(5 + tt) * 2048].rearrange("p (h a e) -> p h a e", h=8, a=2)
            for h in range(8):
                for a, V in ((0, V1), (1, V2)):
                    src = ssb[:, h, a, :]
                    self.S.op("dve", (lambda V=V, src=src: lambda e: e.max(out=V[:, 0:8], in_=src))(), [fb[4 + tt]], [sm])
                    self.S.op("dve", (lambda V=V, src=src: lambda e: e.match_replace(
                        out=TMPK[:, 0:128], in_to_replace=V[:, 0:8], in_values=src, imm_value=NEG))(),
                        [fb[4 + tt], sm], [fb[9]], attach=False)
                    self.S.op("dve", (lambda V=V: lambda e: e.max(out=V[:, 8:16], in_=TMPK[:, 0:128]))(), [fb[9]], [sm])
                self.tt("pool", CAND.rearrange("p (a b) -> p a b", a=16), V1.unsqueeze(2).to_broadcast([P, 16, 16]),
                        V2.unsqueeze(1).to_broadcast([P, 16, 16]), ALU.add, [sm], [fb[9]])
                self.S.op("dve", lambda e: e.max(out=C16[:, 0:8], in_=CAND), [fb[9]], [sm])
                self.S.op("dve", lambda e: e.match_replace(out=TMPK, in_to_replace=C16[:, 0:8], in_values=CAND,
                                                           imm_value=NEG), [fb[9], sm], [fb[9]], attach=False)
                self.S.op("dve", lambda e: e.max(out=C16[:, 8:16], in_=TMPK), [fb[9]], [sm])
                self.cp("dve", CALL[:, h, :], C16, [sm], [sm])
            tau8 = self.SMALL[:, 128 + tt * 8: 128 + tt * 8 + 8]
            nb8 = self.SMALL[:, 160 + tt * 8: 160 + tt * 8 + 8]
            zs8 = self.SMALL[:, 304:312]
            self.cp("dve", tau8, CALL[:, :, 15], [sm], [sm])
            self.tt("dve", CEX, CALL, CALL[:, :, 0:1].to_broadcast([P, 8, 16]), ALU.subtract, [sm], [sm])
            self.act(CEX, CEX, AF.Exp, [sm], [sm])
            self.S.op("dve", lambda e: e.tensor_reduce(out=zs8, in_=CEX, axis=AX.X, op=ALU.add), [sm], [sm])
            self.act(zs8, zs8, AF.Ln, [sm], [sm])
            self.tt("dve", nb8, zs8, CALL[:, :, 0], ALU.add, [sm], [sm])
            self.ts("dve", nb8, nb8, -1.0, None, ALU.mult, None, [sm], [sm])
        tau_all = self.SMALL[:, 128:160]
        nb_all = self.SMALL[:, 160:192]
        bias2 = self.SMALL[:, 192:224]
        self.ts("dve", tau_all, tau_all, -2.0e-5, None, ALU.add, None, [sm], [sm])
        self.tt("dve", bias2, tau_all, nb_all, ALU.add, [sm], [sm])
        for tt in range(4):
            ssb = self.FSA[:, (4 + tt) * 2048:(5 + tt) * 2048].rearrange("p (h a e) -> p h a e", h=8, a=2)
            self.tt("dve", ssb[:, :, 0, :], ssb[:, :, 0, :],
                    tau_all[:, tt * 8:(tt + 1) * 8].unsqueeze(2).to_broadcast([P, 8, 128]), ALU.subtract,
                    [fb[4 + tt], sm], [fb[4 + tt]])
        ZB = [self.FS(8), self.FS(9)]
        zbuf = [fb[8], fb[9]]
        yq2 = self.YQ[:, :, :].rearrange("p a n -> p (a n)")
        EB = [yq2[:, 0:2048], yq2[:, 2048:4096]]
        W2 = [yq2[:, 4096:6144], yq2[:, 6144:8192]]
        ebuf = [self.S.buf("eb0"), self.S.buf("eb1")]
        w2buf = [self.S.buf("w2b0"), self.S.buf("w2b1")]
        G2 = self.XCB[:, :, :].rearrange("p a n -> p (a n)")
        g2b = self.xcbb
        self.S.op("dve", lambda e: e.memset(yq2[:, 4096:4100], 0.0), [], [self.yqb, self.wbb[2]] + ebuf + w2buf + self.vhb)
        GA = self.XN[:, :].rearrange("p (a n) -> p a n", a=4)
        gab = self.xnb
        WACT = [self.UG, self.VGN[:, :, :].rearrange("p a n -> p (a n)").rearrange("p (c t) -> p c t", c=8)]
        wactb = [self.ugb, self.vgnb]
        OUTS = self.FSA[:, 0:4 * 2048].rearrange("p (t d) -> p t d", t=4)
        zi = 0
        vhalf = [0]

        def vstep(gv, ds, tt, vt, vtb):
            WTv = WACT[gv % 2]
            wtbv = wactb[gv % 2]
            pf, pfb = self.next_pf()
            self.mm(pf[:, :], [(WTv[:, c, tt * 128:(tt + 1) * 128], vt[:, c, :]) for c in range(8)],
                    [wtbv, vtb], [pfb])
            dst = OUTS[:, tt, ds * 512:(ds + 1) * 512]
            if gv == 0:
                self.cp("dve", dst, pf[:, :], [pfb], [fb[tt]])
            else:
                self.tt("dve", dst, dst, pf[:, :], ALU.add, [pfb, fb[tt]], [fb[tt]])

        def vload(gv, ds):
            hh = vhalf[0] % 2
            vhalf[0] += 1
            dstv = self.WB[:, 2, hh * 4096:(hh + 1) * 4096]
            self.load(dstv, Sc["v_b"][4 * gv + ds], self.vhb[hh], reads=[self.scrb["v_b"]])
            return dstv.rearrange("p (c n) -> p c n", c=8), self.vhb[hh]

        for g in range(17):
            pend = []
            if g > 0:
                for ds in range(4):
                    for tt in range(4):
                        pend.append((g - 1, ds, tt))
            vcur = {}
            if g == 16:
                for (gv, ds, tt) in pend:
                    if ds not in vcur:
                        vcur[ds] = vload(gv, ds)
                    vstep(gv, ds, tt, *vcur[ds])
                break
            WT_ = WACT[g % 2]
            wtb = wactb[g % 2]
            us = []
            for sl in range(2):
                self.load(self.WB[:, sl, :], Sc["uT_b"][2 * g + sl], self.wbb[sl], reads=[self.scrb["uT_b"]])
                us.append(sl)
            for tt in range(4):
                ssb = self.FSA[:, (4 + tt) * 2048:(5 + tt) * 2048].rearrange("p (h a e) -> p h a e", h=8, a=2)
                w2 = W2[tt % 2]
                w2b = w2buf[tt % 2]
                for hb in range(4):
                    z = ZB[zi % 2]
                    zb = zbuf[zi % 2]
                    eb = EB[zi % 2]
                    ebb = ebuf[zi % 2]
                    zi += 1
                    self.tt("pool", z.rearrange("p (j a b) -> p j a b", j=2, a=8),
                            ssb[:, 2 * hb:2 * hb + 2, 0, 8 * g:8 * g + 8].unsqueeze(3).to_broadcast([P, 2, 8, 128]),
                            ssb[:, 2 * hb:2 * hb + 2, 1, :].unsqueeze(2).to_broadcast([P, 2, 8, 128]), ALU.add,
                            [fb[4 + tt]], [zb])
                    for jh in range(2):
                        h = 2 * hb + jh
                        self.act(eb[:, jh * 1024:(jh + 1) * 1024], z[:, jh * 1024:(jh + 1) * 1024], AF.Exp,
                                 [zb, sm], [ebb], bias=bias2[:, tt * 8 + h: tt * 8 + h + 1])
                    if hb == 0:
                        self.stt(w2, z, 0.0, eb, ALU.is_ge, ALU.mult, [zb, ebb], [w2b])
                    else:
                        self.stt(G2, z, 0.0, eb, ALU.is_ge, ALU.mult, [zb, ebb], [g2b])
                        self.tt("dve", w2, w2, G2, ALU.add, [g2b, w2b], [w2b])
                    if pend:
                        gv, ds, ttv = pend.pop(0)
                        if ds not in vcur:
                            vcur[ds] = vload(gv, ds)
                        vstep(gv, ds, ttv, *vcur[ds])
                wcur = w2[:, 0:1024]
                self.tt("dve", wcur, wcur, w2[:, 1024:2048], ALU.add, [w2b], [w2b])
                for sl in range(2):
                    ut = self.wb3(us[sl])
                    pf, pfb = self.next_pf()
                    self.mm(pf[:, :], [(self.HNT[:, kc, tt * 128:(tt + 1) * 128], ut[:, kc, :]) for kc in range(KC)],
                            self.hntb + [self.wbb[us[sl]]], [pfb])
                    ga = GA[:, sl, :]
                    wa = GA[:, 2 + sl, :]
                    self.act(ga, pf[:, :], AF.Gelu_apprx_tanh, [pfb], [gab])
                    self.tt("dve", wa, ga, wcur[:, sl * 512:(sl + 1) * 512], ALU.mult, [gab, w2b], [gab])
                    pb, pbb = self.next_pb()
                    self.trs([(pb[:, c * 128:(c + 1) * 128], wa[:, c * 128:(c + 1) * 128]) for c in range(4)],
                             [gab], [pbb])
                    self.cp("act", WT_[:, 4 * sl:4 * sl + 4, tt * 128:(tt + 1) * 128],
                            pb[:, 0:512].rearrange("p (c t) -> p c t", c=4), [pbb], [wtb])
        self._wslot = 0
        self.S.op("dve", lambda e: e.memset(yq2[:, 4096:4100], 0.0), [], [self.yqb, self.wbb[2]] + ebuf + w2buf + self.vhb)

    def final(self, j):
        I, Sc = self.I, self.Sc
        fb = self.fsb
        sm = self.smallb
        t0 = j * T
        self.load(self.FS(5), Sc["g2row"][:, :], fb[5], reads=[self.scrb["g2row"]])
        self.load(self.FS(6), I["fgR"][:, :], fb[6])
        for tt in range(4):
            x1 = self.FS(4)
            self.load(x1, Sc["x1_st"][t0 + tt * 128: t0 + (tt + 1) * 128, :], fb[4], reads=[self.scrb["x1_st"]])
            o = self.FS(tt)
            self.tt("dve", o, o, self.FS(5), ALU.mult, [fb[tt], fb[5]], [fb[tt]])
            self.tt("pool", o, o, x1, ALU.add, [fb[tt], fb[4]], [fb[tt]])
            ssq = self.SMALL[:, 8:9]
            rstd = self.SMALL[:, 9:10]
            self.act(self.XN[:, :], o, AF.Square, [fb[tt]], [self.xnb, sm], accum=ssq)
            self.act(rstd, ssq, AF.Ln, [sm], [sm], scale=1.0 / D, bias=self.EPSC[:, 0:1])
            self.act(rstd, rstd, AF.Exp, [sm], [sm], scale=-0.5)
            self.stt(o, o, rstd, self.FS(6), ALU.mult, ALU.mult, [fb[tt], fb[6], sm], [fb[tt]])
            self.store(self.OUT[t0 + tt * 128: t0 + (tt + 1) * 128, :], o, fb[tt], [fb[tt]], [self.outb])


def _wtiles(w, ncols):
    K_, C = w.shape
    t = w.reshape(KC, P, C // ncols, ncols).transpose(2, 1, 0, 3)
    return np.ascontiguousarray(t).reshape(C // ncols, P, KC * ncols)


def _fm(v, n):
    return np.ascontiguousarray(v.reshape(n, P).T)


def prepare_inputs(inp, NT):
    f = lambda a: np.asarray(a, dtype=np.float32)
    x, c, ctx, c_ctx = f(inp["x"]), f(inp["c"]), f(inp["ctx"]), f(inp["c_ctx"])
    w_mod, b_mod = f(inp["w_mod"])[0], f(inp["b_mod"])[0]
    shared = {}
    shared["w_mod_l"] = _wtiles(w_mod, 512)
    shared["bmodF"] = _fm(b_mod, 96)
    shared["bmodR"] = np.ascontiguousarray(np.broadcast_to(
        np.concatenate([b_mod[2 * D:3 * D], b_mod[5 * D:6 * D]])[None, :], (P, 2 * D)))
    shared["gF"] = np.ascontiguousarray(np.concatenate([_fm(f(inp["norm1_g"])[0], 16), _fm(f(inp["norm2_g"])[0], 16)], axis=1))
    shared["fgR"] = np.ascontiguousarray(np.broadcast_to(f(inp["final_g"])[None, :], (P, D)))
    shared["w_in_l"] = _wtiles(f(inp["w_in"])[0], 512)
    shared["w_out_l"] = _wtiles(f(inp["w_out"])[0], 512)
    shared["wq_l"] = _wtiles(f(inp["peer_wq"])[0], 512)
    shared["uT_l"] = _wtiles(np.ascontiguousarray(f(inp["peer_u"])[0].T), 512)
    v = f(inp["peer_v"])[0]
    shared["v_l"] = np.ascontiguousarray(v.reshape(16, 8, P, 4, 512).transpose(0, 3, 2, 1, 4)).reshape(64, P, 4096)
    cw, cbias = f(inp["conv_w"])[0], f(inp["conv_b"])[0]
    conv = np.concatenate([cw.reshape(4, 8, P), cbias.reshape(1, 8, P)], axis=0)
    shared["conv_l"] = np.ascontiguousarray(conv.transpose(2, 1, 0)).reshape(P, 40)
    wa, wx = f(inp["lru_wa"])[0], f(inp["lru_wx"])[0]
    lw = np.stack([wa, wx], axis=0)
    shared["lru_w"] = np.ascontiguousarray(lw.transpose(3, 0, 1, 2, 4)).reshape(P, 4096)
    lv = np.stack([f(inp["lru_ba"])[0], f(inp["lru_bx"])[0], f(inp["lru_lambda"])[0]], axis=0)
    shared["lru_v"] = np.ascontiguousarray(lv.reshape(3, 2, 8, P).transpose(3, 0, 1, 2)).reshape(P, 48)
    sw = f(inp["sgu_w"])[0]
    shared["wsT_l"] = np.ascontiguousarray(sw.transpose(2, 0, 1)).reshape(P, 1024)
    shared["bsR"] = np.ascontiguousarray(np.broadcast_to(f(inp["sgu_b"])[0].reshape(1, 1024), (P, 1024)))
    k = np.stack([f(inp["peer_k1"])[0], f(inp["peer_k2"])[0]], axis=0)
    shared["kT_l"] = np.ascontiguousarray(k.transpose(3, 0, 1, 2)).reshape(P, 2048)
    maps = []
    L = 4 * NT
    for core in range(8):
        b, q = core // 4, core % 4
        m = dict(shared)
        m["x_own"] = np.ascontiguousarray(x[b, q * NT:(q + 1) * NT])
        m["x_b"] = np.ascontiguousarray(x[b])
        halo = np.zeros((P, D), np.float32)
        if q > 0:
            halo[0:2] = x[b, q * NT - 2:q * NT]
        if q < 3:
            halo[2] = x[b, (q + 1) * NT]
        m["x_halo"] = halo
        m["ctx_b"] = np.ascontiguousarray(ctx[b])
        m["s_in"] = np.ascontiguousarray(np.stack([_fm(c[b], 16), _fm(c_ctx, 16)], axis=2)).reshape(P, 32)
        sel = np.zeros((P, 12), np.float32)
        sel[:, q] = 1.0
        sel[:, 4 + q] = 1.0
        sel[:, 8] = sel[:, 9] = 1.0 if q > 0 else 0.0
        sel[:, 10] = 1.0 if q < 3 else 0.0
        m["sel"] = sel
        maps.append(m)
    return maps


_NC_CACHE = {}


def run(inp, NT, dbg=None):
    key = (NT, dbg)
    if key not in _NC_CACHE:
        _NC_CACHE[key] = K(NT, dbg).build()
    nc = _NC_CACHE[key]
    maps = prepare_inputs(inp, NT)
    res = run_bass_kernel_spmd(nc, maps, core_ids=list(range(8)))
    B = 2
    out = np.zeros((B, 4 * NT, D), np.float32)
    for core in range(8):
        b, q = core // 4, core % 4
        out[b, q * NT:(q + 1) * NT] = res.results[core]["out"]
    return out, res


def kernel(**inputs):
    import os
    NT = np.asarray(inputs["x"]).shape[1] // 4
    if os.environ.get("KPROBE_NT"):
        NT2 = int(os.environ["KPROBE_NT"])
        inp = dict(inputs)
        inp["x"] = np.ascontiguousarray(np.asarray(inputs["x"])[:, :4 * NT2])
        o, _ = run(inp, NT2)
        out = np.zeros((2, 4 * NT, D), np.float32)
        out[:, :4 * NT2] = o
        return out
    out, _ = run(inputs, NT)
    return out
```

```python
import numpy as np
from contextlib import ExitStack
import concourse.bass as bass
import concourse.mybir as mybir
from concourse.bass_utils import run_bass_kernel_spmd

F32 = mybir.dt.float32
BF16 = mybir.dt.bfloat16
AF = mybir.ActivationFunctionType
ALU = mybir.AluOpType
AX = mybir.AxisListType

P = 128
D = 2048
KC = 16
T = 512
EPS = 1e-6
NEG = -1.0e30


class Buf:
    __slots__ = ("name", "last_w", "readers", "dsem", "dcount")

    def __init__(self, name):
        self.name = name
        self.last_w = None
        self.readers = {}
        self.dsem = None
        self.dcount = 0


class Sched:
    ENG = ("pe", "act", "dve", "pool", "sp")
    COMPUTE = ("pe", "act", "dve", "pool")

    def __init__(self, nc, stack):
        self.nc = nc
        self.stack = stack
        self.e = {}
        for n in self.ENG:
            sem = stack.enter_context(nc.semaphore("s_" + n))
            self.e[n] = dict(sem=sem, count=0, ops=[], waited={})
        self.nb = 0

    def buf(self, name, dma=False):
        self.nb += 1
        b = Buf("%s_%d" % (name, self.nb))
        if dma:
            b.dsem = self.stack.enter_context(self.nc.semaphore("d%d" % self.nb))
        return b

    def _collect(self, eng, reads, writes):
        need = {}

        def add(ev, raw):
            if ev is None:
                return
            key, sem, val = ev
            if key == eng and (eng == "pe" or not raw):
                return
            if key not in need or need[key][1] < val:
                need[key] = (sem, val)

        for b in reads:
            if b.last_w:
                for ev in b.last_w.values():
                    add(ev, True)
        for b in writes:
            if b.last_w:
                for ev in b.last_w.values():
                    add(ev, False)
            for ev in b.readers.values():
                add(ev, False)
        E = self.e[eng]
        waits = []
        for key, (sem, val) in need.items():
            if E["waited"].get(key, 0) >= val:
                continue
            E["waited"][key] = val
            waits.append((key, sem, val))
        return waits

    def _update(self, ev, reads, writes):
        key = ev[0]
        for b in reads:
            old = b.readers.get(key)
            if old is None or old[2] < ev[2]:
                b.readers[key] = ev
        for b in writes:
            if b.last_w is None:
                b.last_w = {}
            b.last_w[key] = ev
            b.readers = {}

    def op(self, eng, fn, reads=(), writes=(), attach=True):
        E = self.e[eng]
        waits = self._collect(eng, reads, writes)
        E["count"] += 1
        ev = (eng, E["sem"], E["count"])
        E["ops"].append(dict(waits=waits, fn=fn, kind="op", idx=E["count"], attach=attach and eng != "pe"))
        self._update(ev, reads, writes)
        return ev

    def dma(self, fn, owner, reads=(), writes=(), queue="sp"):
        Q = self.e[queue]
        waits = self._collect(queue, reads, writes)
        owner.dcount += 16
        ev = ("d_" + owner.name, owner.dsem, owner.dcount)
        Q["ops"].append(dict(waits=waits, fn=fn, kind="dma", inc=(owner.dsem, 16), attach=True))
        self._update(ev, reads, writes)
        return ev

    def final_wait(self, eng, bufs):
        waits = self._collect(eng, bufs, bufs)
        self.e[eng]["ops"].append(dict(waits=waits, fn=None, kind="wait", attach=False))

    def emit(self):
        nc = self.nc
        waited = {n: set() for n in self.COMPUTE}
        for n in self.ENG:
            for o in self.e[n]["ops"]:
                for key, sem, val in o["waits"]:
                    if key in waited:
                        waited[key].add(val)
        rank = {}
        for n in self.COMPUTE:
            rank[n] = {v: i + 1 for i, v in enumerate(sorted(waited[n]))}
        with nc.Block() as block:
            def run(name):
                def body(e):
                    for o in self.e[name]["ops"]:
                        ws = []
                        for key, sem, val in o["waits"]:
                            ws.append((sem, rank[key][val] if key in rank else val))
                        fn = o["fn"]
                        att = None
                        if fn is not None and o["attach"] and ws:
                            att = ws.pop()
                        for sem, val in ws:
                            e.wait_ge(sem, val)
                        if fn is None:
                            continue
                        r = fn(e)
                        first, last = r if isinstance(r, tuple) else (r, r)
                        if att is not None:
                            first._wait_ge(att[0], att[1])
                        if o["kind"] == "dma":
                            last.then_inc(o["inc"][0], o["inc"][1])
                        elif o["idx"] in waited[name]:
                            last.then_inc(self.e[name]["sem"], 1)
                return body
            block.tensor(run("pe"))
            block.scalar(run("act"))
            block.vector(run("dve"))
            block.gpsimd(run("pool"))
            block.sync(run("sp"))


class K:
    def __init__(self, NT, dbg=None):
        self.NT = NT
        self.NTL = NT // T
        self.dbg = dbg

    def act(self, out, in_, func, reads, writes, bias=None, scale=None, accum=None):
        kw = {}
        if bias is not None:
            kw["bias"] = bias
        if scale is not None:
            kw["scale"] = scale
        if accum is not None:
            kw["accum_out"] = accum
        return self.S.op("act", lambda e: e.activation(out=out, in_=in_, func=func, **kw), reads, writes,
                         attach=(accum is None))

    def tt(self, eng, out, in0, in1, op, reads, writes):
        return self.S.op(eng, lambda e: e.tensor_tensor(out=out, in0=in0, in1=in1, op=op), reads, writes)

    def ts(self, eng, out, in0, s1, s2, op0, op1, reads, writes):
        if op1 is None:
            return self.S.op(eng, lambda e: e.tensor_scalar(out=out, in0=in0, scalar1=s1, scalar2=None, op0=op0), reads, writes)
        return self.S.op(eng, lambda e: e.tensor_scalar(out=out, in0=in0, scalar1=s1, scalar2=s2, op0=op0, op1=op1), reads, writes)

    def stt(self, out, in0, scalar, in1, op0, op1, reads, writes):
        return self.S.op("dve", lambda e: e.scalar_tensor_tensor(out=out, in0=in0, scalar=scalar, in1=in1, op0=op0, op1=op1), reads, writes)

    def cp(self, eng, out, in_, reads, writes):
        if eng == "act":
            return self.S.op("act", lambda e: e.activation(out=out, in_=in_, func=AF.Copy), reads, writes)
        return self.S.op(eng, lambda e: e.tensor_copy(out=out, in_=in_), reads, writes)

    def mm(self, out, pairs, reads, writes):
        def fn(e):
            n = len(pairs)
            ins = None
            for i, (l, r) in enumerate(pairs):
                ins = e.matmul(out, l, r, start=(i == 0), stop=(i == n - 1))
            return ins
        return self.S.op("pe", fn, reads, writes)

    def mm_multi(self, groups, reads, writes):
        def fn(e):
            ins = None
            for out, pairs in groups:
                n = len(pairs)
                for i, (l, r) in enumerate(pairs):
                    ins = e.matmul(out, l, r, start=(i == 0), stop=(i == n - 1))
            return ins
        return self.S.op("pe", fn, reads, writes)

    def trs(self, items, reads, writes):
        ident = self.IDENT

        def fn(e):
            ins = None
            for out, in_ in items:
                ins = e.transpose(out=out, in_=in_, identity=ident[:])
            return ins
        return self.S.op("pe", fn, reads, writes)

    def load(self, out, in_, owner, reads=(), writes=None):
        if writes is None:
            writes = [owner]
        return self.S.dma(lambda e: e.dma_start(out=out, in_=in_, allow_slow_non_contiguous=True), owner, reads, writes)

    def store(self, out, in_, owner, reads, writes=()):
        return self.S.dma(lambda e: e.dma_start(out=out, in_=in_, allow_slow_non_contiguous=True), owner, reads, writes)

    def build(self):
        NT, NTL = self.NT, self.NTL
        nc = bass.Bass("TRN2", target_bir_lowering=False)
        self.nc = nc

        def din(name, shape, dt=F32):
            return nc.dram_tensor(name, list(shape), dt, kind="ExternalInput").ap()

        def dscr(name, shape, dt):
            return nc.dram_tensor(name, list(shape), dt, kind="Internal").ap()

        I = {}
        I["x_own"] = din("x_own", [NT, D])
        I["x_b"] = din("x_b", [4 * NT, D])
        I["x_halo"] = din("x_halo", [P, D])
        I["ctx_b"] = din("ctx_b", [256, D])
        I["s_in"] = din("s_in", [P, 32])
        I["sel"] = din("sel", [P, 12])
        I["w_mod_l"] = din("w_mod_l", [24, P, 8192])
        I["bmodF"] = din("bmodF", [P, 96])
        I["bmodR"] = din("bmodR", [P, 4096])
        I["gF"] = din("gF", [P, 32])
        I["fgR"] = din("fgR", [P, D])
        I["w_in_l"] = din("w_in_l", [8, P, 8192])
        I["w_out_l"] = din("w_out_l", [4, P, 8192])
        I["wq_l"] = din("wq_l", [4, P, 8192])
        I["uT_l"] = din("uT_l", [32, P, 8192])
        I["v_l"] = din("v_l", [64, P, 4096])
        I["conv_l"] = din("conv_l", [P, 40])
        I["lru_w"] = din("lru_w", [P, 4096])
        I["lru_v"] = din("lru_v", [P, 48])
        I["wsT_l"] = din("wsT_l", [P, 1024])
        I["bsR"] = din("bsR", [P, 1024])
        I["kT_l"] = din("kT_l", [P, 2048])
        self.I = I
        OUT = nc.dram_tensor("out", [NT, D], F32, kind="ExternalOutput").ap()
        self.OUT = OUT
        if self.dbg:
            self.DBG = nc.dram_tensor("dbg", [P, 4096], F32, kind="ExternalOutput").ap()
        Sc = {}
        Sc["w_in_b"] = dscr("w_in_b", [8, P, 8192], BF16)
        Sc["w_out_b"] = dscr("w_out_b", [4, P, 8192], BF16)
        Sc["wq_b"] = dscr("wq_b", [4, P, 8192], BF16)
        Sc["uT_b"] = dscr("uT_b", [32, P, 8192], BF16)
        Sc["v_b"] = dscr("v_b", [64, P, 4096], BF16)
        Sc["xc_st"] = dscr("xc_st", [P, 8, NT], F32)
        Sc["hf_st"] = dscr("hf_st", [P, 8, NT], F32)
        Sc["x1_st"] = dscr("x1_st", [NT, D], F32)
        Sc["g1row"] = dscr("g1row", [P, D], F32)
        Sc["g2row"] = dscr("g2row", [P, D], F32)
        self.Sc = Sc

        with ExitStack() as st:
            S = Sched(nc, st)
            self.S = S

            def sb(name, shape, dt):
                return st.enter_context(nc.sbuf_tensor(name, list(shape), dt))

            self.FSA = sb("FSA", [P, 10 * 2048], F32)
            self.fsb = [S.buf("fs%d" % i, dma=True) for i in range(10)]
            self.HNT = sb("HNT", [P, KC, T], BF16)
            self.hntb = [S.buf("hnt%d" % i) for i in range(4)]
            self.YQ = sb("YQ", [P, KC, T], BF16)
            self.yqb = S.buf("yq")
            self.UG = sb("UG", [P, 8, T], BF16)
            self.ugb = S.buf("ug")
            self.VGN = sb("VGN", [P, 4, 1024], BF16)
            self.vgnb = S.buf("vgn")
            self.XCB = sb("XCB", [P, 4, T], BF16)
            self.xcbb = S.buf("xcb")
            self.XN = sb("XN", [P, 2048], BF16)
            self.xnb = S.buf("xn")
            self.WB = sb("WB", [P, 3, 8192], BF16)
            self.wbb = [S.buf("wb%d" % i, dma=True) for i in range(3)]
            self.vhb = [S.buf("vh0", dma=True), S.buf("vh1", dma=True)]
            self.cva = [S.buf("cva0", dma=True), S.buf("cva1", dma=True)]
            self.cvb = [S.buf("cvb%d" % i, dma=True) for i in range(4)]
            self.GT = sb("GT", [P, 2, T], BF16)
            self.gtb = [S.buf("gt0"), S.buf("gt1")]
            self.IDENT = sb("IDENT", [P, P], BF16)
            self.SMW = sb("SMW", [P, 4096], BF16)
            self.smwb = S.buf("smw")
            self.WST = sb("WST", [P, 1024], BF16)
            self.CST = sb("CST", [P, 512], F32)
            self.cstb = S.buf("cst", dma=True)
            self.SMALL = sb("SMALL", [P, 640], F32)
            self.smallb = S.buf("small")
            self.EPSC = sb("EPSC", [P, 2], F32)
            self.STT = sb("STT", [P, 128], F32)
            self.sttb = S.buf("stt")
            self.PF = [st.enter_context(nc.psum_tensor("PF%d" % i, [P, 512], F32)) for i in range(6)]
            self.pfb = [S.buf("pf%d" % i) for i in range(6)]
            self.PB = [st.enter_context(nc.psum_tensor("PB%d" % i, [P, 1024], BF16)) for i in range(2)]
            self.pbb = [S.buf("pb%d" % i) for i in range(2)]
            self.pfi = 0
            self.pbi = 0
            self.outb = S.buf("outd", dma=True)
            self.scrb = {k: S.buf("scr_" + k, dma=True) for k in Sc}

            self.phase_consts()
            self.phase_mods()
            self.phase_convert()
            self.phase_chains()
            self.phase_own_f()
            self.phase_own_r()

            allb = self.fsb + self.wbb + [self.outb, self.cstb] + list(self.scrb.values())
            S.final_wait("sp", allb)
            S.emit()
        return nc

    def FS(self, i, n=2048, off=0):
        return self.FSA[:, i * 2048 + off: i * 2048 + off + n]

    def FS3(self, i, a, b, off=0):
        return self.FSA[:, i * 2048 + off: i * 2048 + off + a * b].rearrange("p (a b) -> p a b", a=a)

    def next_pf(self):
        i = self.pfi
        self.pfi = (self.pfi + 1) % 6
        return self.PF[i], self.pfb[i]

    def next_pb(self):
        i = self.pbi
        self.pbi = (self.pbi + 1) % 2
        return self.PB[i], self.pbb[i]

    def wb3(self, s):
        return self.WB[:, s, :].rearrange("p (k n) -> p k n", k=KC)

    C_CONV = 0
    C_LV = 40
    C_CL = 88
    C_CL2 = 104
    C_SEL = 120
    C_GF = 132
    C_SIN = 164
    C_MODF = 196
    C_A1 = 324
    C_B1 = 356
    C_A2 = 388
    C_B2 = 404
    C_BMF = 420
    C_END = 484

    def C(self, off, n):
        return self.CST[:, off:off + n]

    def phase_consts(self):
        S, I = self.S, self.I
        cb = self.cstb
        S.op("dve", lambda e: e.memset(self.EPSC[:, :], EPS), [], [self.smallb])
        idf = self.FS(0, 128)
        S.op("pool", lambda e: e.iota(idf, pattern=[[1, 128]], base=0, channel_multiplier=-1,
                                      allow_small_or_imprecise_dtypes=True), [], [self.fsb[0]])
        S.op("dve", lambda e: e.tensor_single_scalar(out=self.IDENT[:], in_=idf, scalar=0.0, op=ALU.is_equal),
             [self.fsb[0]], [self.smwb])
        self.load(self.C(self.C_CONV, 40), I["conv_l"][:, :], cb)
        self.load(self.C(self.C_LV, 48), I["lru_v"][:, :], cb)
        self.load(self.C(self.C_SEL, 12), I["sel"][:, :], cb)
        self.load(self.C(self.C_GF, 32), I["gF"][:, :], cb)
        self.load(self.C(self.C_SIN, 32), I["s_in"][:, :], cb)
        for jj, j in enumerate((0, 1, 3, 4)):
            self.load(self.C(self.C_BMF + 16 * jj, 16), I["bmodF"][:, 16 * j:16 * j + 16], cb)
        lam = self.C(self.C_LV + 32, 16)
        cl = self.C(self.C_CL, 16)
        cl2 = self.C(self.C_CL2, 16)
        self.act(cl, lam, AF.Exp, [cb], [cb], scale=-1.0)
        self.act(cl, cl, AF.Ln, [cb], [cb], bias=1.0)
        self.ts("dve", cl2, cl, -16.0, None, ALU.mult, None, [cb], [cb])
        self.ts("dve", cl, cl, -8.0, None, ALU.mult, None, [cb], [cb])
        sin = self.C(self.C_SIN, 32)
        self.act(sin, sin, AF.Silu, [cb], [cb])
        self.load(self.FS(0, 4096), I["lru_w"][:, :], self.fsb[0], writes=[self.fsb[0], self.fsb[1]])
        self.cp("dve", self.SMW[:, :], self.FS(0, 4096), [self.fsb[0], self.fsb[1]], [self.smwb])
        self.load(self.FS(2, 1024), I["wsT_l"][:, :], self.fsb[2])
        self.cp("dve", self.WST[:, :], self.FS(2, 1024), [self.fsb[2]], [self.smwb])

    def phase_mods(self):
        S, I, Sc = self.S, self.I, self.Sc
        cb = self.cstb
        sin3 = self.C(self.C_SIN, 32).rearrange("p (k j) -> p k j", k=KC)
        srep = self.FS3(8, KC, 128)
        self.cp("dve", srep, sin3[:, :, 0:1].to_broadcast([P, KC, 128]), [cb], [self.fsb[8]])
        modf = self.C(self.C_MODF, 128).rearrange("p (j k c) -> p j k c", j=4, k=KC)
        bmf = self.C(self.C_BMF, 64).rearrange("p (j k) -> p j k", j=4)
        fm = {0: 0, 1: 1, 3: 2, 4: 3}
        rowdst = {2: ("g1row", Sc["g1row"]), 5: ("g2row", Sc["g2row"])}
        rb = 0
        for pi in range(24):
            j = pi // 4
            s0 = 4 * (pi % 2)
            bufs = self.fsb[s0:s0 + 4]
            pan = self.FSA[:, s0 * 2048:(s0 + 4) * 2048].rearrange("p (k n) -> p k n", k=KC)
            self.load(self.FSA[:, s0 * 2048:(s0 + 4) * 2048], I["w_mod_l"][pi], bufs[0], writes=bufs)
            if j in fm:
                pf, pfb = self.next_pf()
                groups = []
                for cc in range(4):
                    groups.append((pf[:, 2 * cc:2 * cc + 2],
                                   [(pan[:, kc, cc * 128:(cc + 1) * 128], sin3[:, kc, :]) for kc in range(KC)]))
                self.mm_multi(groups, bufs + [cb], [pfb])
                jj = fm[j]
                kc0 = 4 * (pi % 4)
                self.tt("dve", modf[:, jj, kc0:kc0 + 4, :], pf[:, 0:8].rearrange("p (c j) -> p c j", c=4),
                        bmf[:, jj, kc0:kc0 + 4].unsqueeze(2).to_broadcast([P, 4, 2]), ALU.add, [pfb, cb], [cb])
            else:
                pf, pfb = self.next_pf()
                self.mm(pf[:, :], [(srep[:, kc, :], pan[:, kc, :]) for kc in range(KC)], bufs + [self.fsb[8]], [pfb])
                ds = pi % 4
                which = 0 if j == 2 else 1
                bm = self.FS(9, 512, off=512 * (rb % 2))
                rt = self.FS(9, 512, off=1024 + 512 * (rb % 2))
                rb += 1
                self.load(bm, I["bmodR"][:, which * 2048 + ds * 512: which * 2048 + ds * 512 + 512], self.fsb[9])
                self.tt("dve", rt, pf[:, :], bm, ALU.add, [pfb, self.fsb[9]], [self.fsb[9]])
                nm, dst = rowdst[j]
                self.store(dst[:, ds * 512:(ds + 1) * 512], rt, self.scrb[nm], [self.fsb[9]], [self.scrb[nm]])
        gf = self.C(self.C_GF, 32)
        a1 = self.C(self.C_A1, 32).rearrange("p (k c) -> p k c", k=KC)
        b1 = self.C(self.C_B1, 32).rearrange("p (k c) -> p k c", k=KC)
        self.ts("dve", a1, modf[:, 1, :, :], 1.0, None, ALU.add, None, [cb], [cb])
        self.tt("dve", a1, a1, gf[:, 0:16].unsqueeze(2).to_broadcast([P, KC, 2]), ALU.mult, [cb], [cb])
        self.cp("dve", b1, modf[:, 0, :, :], [cb], [cb])
        a2 = self.C(self.C_A2, 16)
        b2 = self.C(self.C_B2, 16)
        self.ts("dve", a2, modf[:, 3, :, 0], 1.0, None, ALU.add, None, [cb], [cb])
        self.tt("dve", a2, a2, gf[:, 16:32], ALU.mult, [cb], [cb])
        self.cp("dve", b2, modf[:, 2, :, 0], [cb], [cb])

    def phase_convert(self):
        I, Sc = self.I, self.Sc
        jobs = []
        self.cvjobs = []
        for nm_s, nm_d, n, F in (("uT_l", "uT_b", 32, 8192), ("v_l", "v_b", 64, 4096)):
            for i in range(n):
                for h in range(F // 1024):
                    self.cvjobs.append((I[nm_s][i][:, h * 1024:(h + 1) * 1024], Sc[nm_d][i][:, h * 1024:(h + 1) * 1024], nm_d))
        self.cvk = 0
        self.cvjobs = []
        for nm_s, nm_d, n, F in (("w_in_l", "w_in_b", 8, 8192), ("w_out_l", "w_out_b", 4, 8192),
                                 ("wq_l", "wq_b", 4, 8192), ("uT_l", "uT_b", 32, 8192), ("v_l", "v_b", 64, 4096)):
            for i in range(n):
                for h in range(F // 4096):
                    jobs.append((I[nm_s][i][:, h * 4096:(h + 1) * 4096], Sc[nm_d][i][:, h * 4096:(h + 1) * 4096], nm_d))
        for ji, (src, dst, nm) in enumerate(jobs):
            s0 = 2 * (ji % 3)
            bufs = self.fsb[s0:s0 + 2]
            w = ji % 3
            half = self.WB[:, w, 0:4096]
            self.load(self.FSA[:, s0 * 2048:(s0 + 2) * 2048], src, bufs[0], writes=bufs)
            eng = "dve" if ji % 2 == 0 else "act"
            self.cp(eng, half, self.FSA[:, s0 * 2048:(s0 + 2) * 2048], bufs, [self.wbb[w]])
            self.store(dst, half, self.wbb[w], [self.wbb[w]], [self.scrb[nm]])

    def cv_emit(self, n):
        ugf = self.UG[:, :, :].rearrange("p a n -> p (a n)")
        for _ in range(n):
            if self.cvk >= len(self.cvjobs):
                return
            k = self.cvk
            self.cvk += 1
            src_, dst_, nm = self.cvjobs[k]
            a = k % 2
            b = k % 4
            st = self.FS(9, 1024, off=1024 * a)
            ob = ugf[:, b * 1024:(b + 1) * 1024]
            self.load(st, src_, self.cva[a])
            self.cp("pool", ob, st, [self.cva[a]], [self.cvb[b]])
            self.store(dst_, ob, self.cvb[b], [self.cvb[b]], [self.scrb[nm]])

    def norm_tt(self, xs, xsb, tt, A, B, areads, hn=None):
        HN, hnb = hn if hn is not None else (self.HNT, self.hntb)
        sm = self.smallb
        ssq = self.SMALL[:, 0:1]
        rstd = self.SMALL[:, 1:2]
        self.act(self.XN[:, :], xs, AF.Square, [xsb], [self.xnb, sm], accum=ssq)
        self.act(rstd, ssq, AF.Ln, [sm], [sm], scale=1.0 / D, bias=self.EPSC[:, 0:1])
        self.act(rstd, rstd, AF.Exp, [sm], [sm], scale=-0.5)
        self.act(self.XN[:, :], xs, AF.Copy, [xsb, sm], [self.xnb], scale=rstd)
        for half in range(2):
            pb, pbb = self.next_pb()
            self.trs([(pb[:, i * 128:(i + 1) * 128], self.XN[:, (half * 8 + i) * 128:(half * 8 + i + 1) * 128])
                      for i in range(8)], [self.xnb], [pbb])
            dst = HN[:, half * 8:half * 8 + 8, tt * 128:(tt + 1) * 128]
            pv = pb[:, :].rearrange("p (k n) -> p k n", k=8)
            self.tt("dve", dst, pv, A[:, half * 8:half * 8 + 8].unsqueeze(2).to_broadcast([P, 8, 128]), ALU.mult,
                    [pbb] + areads, [hnb[tt]])
            self.tt("pool", dst, dst, B[:, half * 8:half * 8 + 8].unsqueeze(2).to_broadcast([P, 8, 128]), ALU.add,
                    [hnb[tt]] + areads, [hnb[tt]])

    def norm_rows(self, rows_fn, ntt, A, B, hn=None, tts=None):
        for tt in (range(ntt) if tts is None else tts):
            s = tt % 2
            self.load(self.FS(s), rows_fn(tt), self.fsb[s])
            self.norm_tt(self.FS(s), self.fsb[s], tt, A, B, [self.cstb], hn=hn)

    def xb_half(self, hc, ncur, wxs, mode, d, scan_rng, snap_col=None, snap_idx=None, stash=None, from_stash=None,
                hn=None, filler=None):
        cb, sb_ = self.cstb, self.sttb
        nwin = ncur
        XC = self.FS3(2, 4, T)
        R = self.FS3(3, 4, T)
        Ii = self.FS3(4, 4, T)
        TM = self.FS3(5, 4, T)
        H = self.FS3(6, 4, T)
        fb = self.fsb
        conv = self.C(self.C_CONV, 40)
        if mode is not None:
            EXT = self.FSA[:, 7 * 2048: 7 * 2048 + 4 * 516].rearrange("p (c n) -> p c n", c=4)
            eb = [fb[7], fb[8]]
            sav = self.STT[:, 80:104].rearrange("p (c n) -> p c n", c=8)
            co = 3 if mode == "f" else 0
            for c in range(4):
                oc = 4 * hc + c
                pf, pfb = self.next_pf()
                wt = self.wb3(wxs[oc // 4])
                HN, hnb = hn if hn is not None else (self.HNT, self.hntb)
                self.mm(pf[:, 0:ncur], [(wt[:, kc, (oc % 4) * 128:(oc % 4 + 1) * 128], HN[:, kc, 0:ncur])
                                        for kc in range(KC)], hnb + [self.wbb[wxs[oc // 4]]], [pfb])
                self.cp("act", EXT[:, c, co:co + ncur], pf[:, 0:ncur], [pfb], eb)
            if mode == "f":
                self.cp("dve", EXT[:, :, 0:3], sav[:, 4 * hc:4 * hc + 4, :], [sb_], eb)
                self.cp("dve", sav[:, 4 * hc:4 * hc + 4, :], EXT[:, :, ncur:ncur + 3], eb, [sb_])
            else:
                self.cp("dve", EXT[:, :, ncur:ncur + 3], sav[:, 4 * hc:4 * hc + 4, :], [sb_], eb)
                self.cp("dve", sav[:, 4 * hc:4 * hc + 4, :], EXT[:, :, 0:3], eb, [sb_])
            for c in range(4):
                oc = 4 * hc + c
                w = lambda k: conv[:, oc * 5 + k: oc * 5 + k + 1]
                self.ts("pool", XC[:, c, 0:nwin], EXT[:, c, 0:nwin], w(0), w(4), ALU.mult, ALU.add, eb + [cb], [fb[2]])
                for k in range(1, 4):
                    self.stt(XC[:, c, 0:nwin], EXT[:, c, k:k + nwin], w(k), XC[:, c, 0:nwin], ALU.mult, ALU.add,
                             eb + [cb, fb[2]], [fb[2]])
        else:
            self.load(XC[:, :, :], from_stash, fb[2], reads=[self.scrb["xc_st"]])
        if filler is not None:
            filler()
        return self._xb_gates(hc, nwin, d, scan_rng, snap_col, snap_idx)

    def load_wx(self, tiles, slots, src):
        for tno, s in zip(tiles, slots):
            self.load(self.WB[:, s, :], src[tno], self.wbb[s], reads=[self.scrb["w_in_b"]])

    def zero_sav(self):
        self.S.op("dve", lambda e: e.memset(self.STT[:, 80:104], 0.0), [], [self.sttb])

    def phase_chains(self):
        NT, NTL = self.NT, self.NTL
        I, Sc = self.I, self.Sc
        a1 = self.C(self.C_A1, 32).rearrange("p (k c) -> p k c", k=KC)
        b1 = self.C(self.C_B1, 32).rearrange("p (k c) -> p k c", k=KC)
        self.load_wx([0, 1], [0, 1], Sc["w_in_b"])
        self.S.op("dve", lambda e: e.memset(self.STT[:, :], 0.0), [], [self.sttb])
        self.S.op("pool", lambda e: e.memset(self.HNT[:, :, 256:384], 0.0), [], self.hntb)
        for d in (0, 1):
            self.zero_sav()
            self.norm_rows(lambda tt: I["ctx_b"][tt * 128:(tt + 1) * 128, :], 2, a1[:, :, 1], b1[:, :, 1])
            self.S.op("pool", lambda e: e.memset(self.HNT[:, :, 256:257], 0.0), [], self.hntb)
            for hc in (0, 1):
                self.xb_half(hc, 257, [0, 1], "f", d, (1, 257))
            base = 16 if d == 0 else 48
            idx = 0 if d == 0 else 3
            self.cp("dve", self.STT[:, base + 8 * idx: base + 8 * idx + 8], self.STT[:, 8 * d:8 * d + 8],
                    [self.sttb], [self.sttb])
        sel = self.C(self.C_SEL, 12)
        self.norm_rows(lambda tt: I["x_halo"][:, :], 1, a1[:, :, 0], b1[:, :, 0])
        sav = self.STT[:, 80:104].rearrange("p (c n) -> p c n", c=8)
        hal = self.SMALL[:, 16:48].rearrange("p (c n) -> p c n", c=8)
        for oc in range(8):
            pf, pfb = self.next_pf()
            wt = self.wb3(oc // 4)
            self.mm(pf[:, 0:128], [(wt[:, kc, (oc % 4) * 128:(oc % 4 + 1) * 128], self.HNT[:, kc, 0:128])
                                   for kc in range(KC)], self.hntb + [self.wbb[oc // 4]], [pfb])
            self.tt("dve", hal[:, oc, :], pf[:, 0:4], sel[:, 8:12], ALU.mult, [pfb, self.cstb], [self.smallb])

        items = []
        for j in range(3 * NTL + 1):
            if j == 0:
                rng = (1, T)
            elif j == 3 * NTL:
                rng = (0, 1)
            else:
                rng = (0, T)
            snap = j // NTL if (j > 0 and j % NTL == 0) else None
            items.append(dict(rows=(lambda tt, j=j: I["x_b"][j * T + tt * 128: j * T + (tt + 1) * 128, :]),
                              mode="f", d=0, rng=rng, snap_col=0, snap_idx=snap,
                              pre=(self.zero_sav if j == 0 else None), post=None))
        for j in range(4 * NTL - 1, NTL - 2, -1):
            if j == 4 * NTL - 1:
                rng = (0, T - 2)
            elif j == NTL - 1:
                rng = (T - 2, T)
            else:
                rng = (0, T)
            snap = None
            if (j + 1) % NTL == 0 and j != 4 * NTL - 1:
                snap = (j + 1) // NTL - 1
            items.append(dict(rows=(lambda tt, j=j: I["x_b"][j * T + tt * 128: j * T + (tt + 1) * 128, :]),
                              mode="b", d=1, rng=rng, snap_col=T - 2, snap_idx=snap,
                              pre=(self.zero_sav if j == 4 * NTL - 1 else None), post=None))

        def own_pre():
            for d in (0, 1):
                base = 16 if d == 0 else 48
                stt = self.STT[:, 8 * d:8 * d + 8]
                self.ts("dve", stt, self.STT[:, base:base + 8], sel[:, 4 * d:4 * d + 1], None, ALU.mult, None,
                        [self.sttb, self.cstb], [self.sttb])
                for i in range(1, 4):
                    self.stt(stt, self.STT[:, base + 8 * i: base + 8 * i + 8], sel[:, 4 * d + i:4 * d + i + 1], stt,
                             ALU.mult, ALU.add, [self.sttb, self.cstb], [self.sttb])
            self.S.op("dve", lambda e: e.memset(self.STT[:, 80:104], 0.0), [], [self.sttb])
            self.cp("dve", sav[:, :, 1:3], hal[:, :, 0:2], [self.smallb], [self.sttb])

        def own_post(j):
            def f(hc, XC, H, rng):
                lo, hi = rng
                p0 = j * T - 1
                self.store(Sc["xc_st"][:, 4 * hc:4 * hc + 4, p0 + lo:p0 + hi], XC[:, :, lo:hi], self.fsb[2],
                           [self.fsb[2]], [self.scrb["xc_st"]])
                self.store(Sc["hf_st"][:, 4 * hc:4 * hc + 4, p0 + lo:p0 + hi], H[:, :, lo:hi], self.fsb[6],
                           [self.fsb[6]], [self.scrb["hf_st"]])
            return f

        for j in range(NTL):
            rng = (1, T) if j == 0 else (0, T)
            items.append(dict(rows=(lambda tt, j=j: I["x_own"][j * T + tt * 128: j * T + (tt + 1) * 128, :]),
                              mode="f", d=0, rng=rng, snap_col=None, snap_idx=None,
                              pre=(own_pre if j == 0 else None), post=own_post(j)))

        hntb2 = [self.S.buf("hn2_%d" % i) for i in range(4)]
        self.S.op("dve", lambda e: e.memset(self.EPSC[:, 1:2], 0.0), [], [self.yqb] + hntb2)
        self.S.op("pool", lambda e: e.memset(self.SMALL[:, 600:601], 0.0), [], [self.fsb[9], self.ugb] + self.cva + self.cvb)
        nhalf = 2 * (len(items))
        per = (len(self.cvjobs) + nhalf - 3) // (nhalf - 2) + 1
        HNS = [(self.HNT, self.hntb), (self.YQ, hntb2)]
        al, bl = a1[:, :, 0], b1[:, :, 0]
        self.norm_rows(items[0]["rows"], 4, al, bl, hn=HNS[0])
        for k, it in enumerate(items):
            hn = HNS[k % 2]
            nxt = items[k + 1] if k + 1 < len(items) else None
            if it["pre"] is not None:
                it["pre"]()
            for hc in (0, 1):
                filler = None
                if nxt is not None:
                    filler = (lambda nxt=nxt, hc=hc, k=k: (self.norm_rows(
                        nxt["rows"], 4, al, bl, hn=HNS[(k + 1) % 2], tts=(2 * hc, 2 * hc + 1)), self.cv_emit(per)))
                XC, H = self.xb_half(hc, T, [0, 1], it["mode"], it["d"], it["rng"], snap_col=it["snap_col"],
                                     snap_idx=it["snap_idx"], hn=hn, filler=filler)
                if it["post"] is not None:
                    it["post"](hc, XC, H, it["rng"])
        for hc in (0, 1):
            XC, H = self.xb_half_virtual(hc, hal, (0, 1))
            own_post(NTL)(hc, XC, H, (0, 1))
        self.cv_emit(len(self.cvjobs))
        self.S.op("dve", lambda e: e.memset(self.EPSC[:, 1:2], 0.0), [], [self.yqb] + hntb2)
        self.S.op("pool", lambda e: e.memset(self.SMALL[:, 600:601], 0.0), [], [self.fsb[9], self.ugb] + self.cva + self.cvb)

    def phase_own_f(self):
        return

    def xb_half_virtual(self, hc, hal, rng):
        cb, sb_ = self.cstb, self.sttb
        fb = self.fsb
        EXT = self.FSA[:, 7 * 2048: 7 * 2048 + 4 * 516].rearrange("p (c n) -> p c n", c=4)
        eb = [fb[7], fb[8]]
        sav = self.STT[:, 80:104].rearrange("p (c n) -> p c n", c=8)
        XC = self.FS3(2, 4, T)
        conv = self.C(self.C_CONV, 40)
        n = 8
        self.S.op("dve", lambda e: e.memset(EXT[:, :, 0:16], 0.0), [], eb)
        self.cp("dve", EXT[:, :, 0:3], sav[:, 4 * hc:4 * hc + 4, :], [sb_], eb)
        self.cp("dve", EXT[:, :, 3:4], hal[:, 4 * hc:4 * hc + 4, 2:3], [self.smallb], eb)
        for c in range(4):
            oc = 4 * hc + c
            w = lambda k: conv[:, oc * 5 + k: oc * 5 + k + 1]
            self.ts("pool", XC[:, c, 0:n], EXT[:, c, 0:n], w(0), w(4), ALU.mult, ALU.add, eb + [cb], [fb[2]])
            for k in range(1, 4):
                self.stt(XC[:, c, 0:n], EXT[:, c, k:k + n], w(k), XC[:, c, 0:n], ALU.mult, ALU.add,
                         eb + [cb, fb[2]], [fb[2]])
        return self.xb_tail(hc, n, 0, rng)

    def xb_tail(self, hc, nwin, d, scan_rng):
        return self._xb_gates(hc, nwin, d, scan_rng)

    def _xb_gates(self, hc, nwin, d, scan_rng, snap_col=None, snap_idx=None):
        cb, sb_ = self.cstb, self.sttb
        fb = self.fsb
        XC = self.FS3(2, 4, T)
        R = self.FS3(3, 4, T)
        Ii = self.FS3(4, 4, T)
        TM = self.FS3(5, 4, T)
        H = self.FS3(6, 4, T)
        self.cp("act", self.XCB[:, :, 0:nwin], XC[:, :, 0:nwin], [fb[2]], [self.xcbb])
        lw = self.SMW[:, :].rearrange("p (m d h j) -> p m d h j", m=2, d=2, h=8)
        lv = self.C(self.C_LV, 48).rearrange("p (m d h) -> p m d h", m=3, d=2)
        for m, dstt, dbuf in ((0, R, fb[3]), (1, Ii, fb[4])):
            for c in range(4):
                oc = 4 * hc + c
                pf, pfb = self.next_pf()
                self.mm(pf[:, 0:nwin], [(lw[:, m, d, oc, :], self.XCB[:, c, 0:nwin])], [self.xcbb, self.smwb], [pfb])
                self.act(dstt[:, c, 0:nwin], pf[:, 0:nwin], AF.Sigmoid, [pfb, cb], [dbuf], bias=lv[:, m, d, oc:oc + 1])
        cl = self.C(self.C_CL, 16).rearrange("p (d h) -> p d h", d=2)
        cl2 = self.C(self.C_CL2, 16).rearrange("p (d h) -> p d h", d=2)
        for c in range(4):
            oc = 4 * hc + c
            self.act(TM[:, c, 0:nwin], R[:, c, 0:nwin], AF.Exp, [fb[3], cb], [fb[5]], scale=cl2[:, d, oc:oc + 1])
            self.act(R[:, c, 0:nwin], R[:, c, 0:nwin], AF.Exp, [fb[3], cb], [fb[3]], scale=cl[:, d, oc:oc + 1])
        self.act(TM[:, :, 0:nwin], TM[:, :, 0:nwin], AF.Ln, [fb[5]], [fb[5]], scale=-1.0, bias=1.0)
        self.act(TM[:, :, 0:nwin], TM[:, :, 0:nwin], AF.Exp, [fb[5]], [fb[5]], scale=0.5)
        self.tt("dve", Ii[:, :, 0:nwin], Ii[:, :, 0:nwin], TM[:, :, 0:nwin], ALU.mult, [fb[4], fb[5]], [fb[4]])
        self.tt("dve", Ii[:, :, 0:nwin], Ii[:, :, 0:nwin], XC[:, :, 0:nwin], ALU.mult, [fb[4], fb[2]], [fb[4]])
        lo, hi = scan_rng
        stt = self.STT[:, 8 * d:8 * d + 8]
        for c in range(4):
            oc = 4 * hc + c
            if d == 0:
                o, a, b = H[:, c, lo:hi], R[:, c, lo:hi], Ii[:, c, lo:hi]
            else:
                o = H[:, c, lo:hi][:, ::-1]
                a = R[:, c, lo:hi][:, ::-1]
                b = Ii[:, c, lo:hi][:, ::-1]
            ini = stt[:, oc:oc + 1]
            self.S.op("dve", (lambda o=o, a=a, b=b, ini=ini: (lambda e: e.tensor_tensor_scan(
                out=o, data0=a, data1=b, initial=ini, op0=ALU.mult, op1=ALU.add)))(),
                [fb[3], fb[4], sb_], [fb[6]])
        last = hi - 1 if d == 0 else lo
        self.cp("dve", stt[:, 4 * hc:4 * hc + 4], H[:, :, last], [fb[6]], [sb_])
        if snap_idx is not None:
            base = 16 if d == 0 else 48
            sn = self.STT[:, base + 8 * snap_idx + 4 * hc: base + 8 * snap_idx + 4 * hc + 4]
            self.cp("dve", sn, H[:, :, snap_col], [fb[6]], [sb_])
        return XC, H

    def wstream(self, srcs):
        st = {"i": 0, "n": len(srcs), "srcs": srcs, "slot": getattr(self, "_wslot", 0)}
        return st

    def wload(self, src, scr_name, half=False):
        s = getattr(self, "_wslot", 0)
        self._wslot = (s + 1) % 3
        if half:
            self.load(self.WB[:, s, 0:4096], src, self.wbb[s], reads=[self.scrb[scr_name]])
        else:
            self.load(self.WB[:, s, :], src, self.wbb[s], reads=[self.scrb[scr_name]])
        return s

    def phase_own_r(self):
        NT, NTL = self.NT, self.NTL
        I, Sc = self.I, self.Sc
        cb = self.cstb
        fb = self.fsb
        a1 = self.C(self.C_A1, 32).rearrange("p (k c) -> p k c", k=KC)
        b1 = self.C(self.C_B1, 32).rearrange("p (k c) -> p k c", k=KC)
        a2 = self.C(self.C_A2, 16)
        b2 = self.C(self.C_B2, 16)
        self._wslot = 0
        for j in range(NTL - 1, -1, -1):
            t0 = j * T
            self.load(self.FS(0, 4096), I["lru_w"][:, :], fb[0], writes=[fb[0], fb[1]])
            self.cp("dve", self.SMW[:, :], self.FS(0, 4096), [fb[0], fb[1]], [self.smwb])
            self.norm_rows(lambda tt: I["x_own"][t0 + tt * 128: t0 + (tt + 1) * 128, :], 4, a1[:, :, 0], b1[:, :, 0])
            for hc in (0, 1):
                s = self.wload(Sc["w_in_b"][2 + hc], "w_in_b")
                XC = self.FS3(2, 4, T)
                self.load(XC[:, :, :], Sc["xc_st"][:, 4 * hc:4 * hc + 4, t0:t0 + T], fb[2], reads=[self.scrb["xc_st"]])
                XC, H = self._xb_gates(hc, T, 1, (0, T))
                HF = self.FS3(2, 4, T)
                self.load(HF[:, :, :], Sc["hf_st"][:, 4 * hc:4 * hc + 4, t0:t0 + T], fb[2], reads=[self.scrb["hf_st"]])
                self.tt("pool", H[:, :, :], H[:, :, :], HF[:, :, :], ALU.add, [fb[6], fb[2]], [fb[6]])
                wt = self.wb3(s)
                for c in range(4):
                    oc = 4 * hc + c
                    pf, pfb = self.next_pf()
                    self.mm(pf[:, :], [(wt[:, kc, c * 128:(c + 1) * 128], self.HNT[:, kc, :]) for kc in range(KC)],
                            self.hntb + [self.wbb[s]], [pfb])
                    g = c % 2
                    self.act(self.GT[:, g, :], pf[:, :], AF.Gelu_apprx_tanh, [pfb], [self.gtb[g]])
                    self.tt("dve", self.YQ[:, oc, :], self.GT[:, g, :], H[:, c, :], ALU.mult, [self.gtb[g], fb[6]], [self.yqb])
            for hc in (0, 1):
                s = self.wload(Sc["w_in_b"][4 + hc], "w_in_b")
                wt = self.wb3(s)
                for c in range(4):
                    pf, pfb = self.next_pf()
                    self.mm(pf[:, :], [(wt[:, kc, c * 128:(c + 1) * 128], self.HNT[:, kc, :]) for kc in range(KC)],
                            self.hntb + [self.wbb[s]], [pfb])
                    self.act(self.UG[:, 4 * hc + c, :], pf[:, :], AF.Gelu_apprx_tanh, [pfb], [self.ugb])
            sv = [self.wload(Sc["w_in_b"][6], "w_in_b"), self.wload(Sc["w_in_b"][7], "w_in_b")]
            VG = self.FS3(8, 2, 1024)
            SQ = self.FS(5, 1024)
            sm = self.smallb
            for tt in range(4):
                vg = VG[:, tt % 2, :]
                for cs in range(2):
                    wt = self.wb3(sv[cs])
                    pf, pfb = self.next_pf()
                    self.mm(pf[:, :], [(self.HNT[:, kc, tt * 128:(tt + 1) * 128], wt[:, kc, :]) for kc in range(KC)],
                            self.hntb + [self.wbb[sv[cs]]], [pfb])
                    self.act(vg[:, cs * 512:(cs + 1) * 512], pf[:, :], AF.Gelu_apprx_tanh, [pfb], [fb[8]])
                vg3 = vg.rearrange("p (g c) -> p g c", g=8)
                sq3 = SQ.rearrange("p (g c) -> p g c", g=8)
                su = self.SMALL[:, 64:72]
                ss = self.SMALL[:, 72:80]
                mean = self.SMALL[:, 80:88]
                var = self.SMALL[:, 88:96]
                nmr = self.SMALL[:, 96:104]
                self.S.op("dve", (lambda vg3=vg3: lambda e: e.tensor_reduce(out=su, in_=vg3, axis=AX.X, op=ALU.add))(),
                          [fb[8]], [sm])
                self.tt("pool", SQ, vg, vg, ALU.mult, [fb[8]], [fb[5]])
                self.S.op("dve", lambda e: e.tensor_reduce(out=ss, in_=sq3, axis=AX.X, op=ALU.add), [fb[5]], [sm])
                self.ts("dve", mean, su, 1.0 / 128, None, ALU.mult, None, [sm], [sm])
                self.tt("dve", var, mean, mean, ALU.mult, [sm], [sm])
                self.stt(var, ss, 1.0 / 128, var, ALU.mult, ALU.subtract, [sm], [sm])
                self.act(var, var, AF.Ln, [sm], [sm], bias=self.EPSC[:, 0:1])
                self.act(var, var, AF.Exp, [sm], [sm], scale=-0.5)
                self.tt("dve", nmr, mean, var, ALU.mult, [sm], [sm])
                self.tt("dve", sq3, vg3, var.unsqueeze(2).to_broadcast([P, 8, 128]), ALU.mult, [fb[8], sm], [fb[5]])
                self.tt("dve", self.VGN[:, tt, :].rearrange("p (g c) -> p g c", g=8), sq3,
                        nmr.unsqueeze(2).to_broadcast([P, 8, 128]), ALU.subtract, [fb[5], sm], [self.vgnb])
            self.load(self.FS(9, 1024), I["bsR"][:, :], fb[9])
            for g in range(8):
                pf, pfb = self.next_pf()
                groups = [(pf[:, tt * 128:(tt + 1) * 128],
                           [(self.VGN[:, tt, g * 128:(g + 1) * 128], self.WST[:, g * 128:(g + 1) * 128])])
                          for tt in range(4)]
                self.mm_multi(groups, [self.vgnb, self.smwb], [pfb])
                tmp = self.FS(5, 512, off=1024)
                self.tt("dve", tmp.rearrange("p (a b) -> p a b", a=4), pf[:, :].rearrange("p (a b) -> p a b", a=4),
                        self.FS(9, 128, off=g * 128).unsqueeze(1).to_broadcast([P, 4, 128]), ALU.add,
                        [pfb, fb[9]], [fb[5]])
                self.tt("dve", self.YQ[:, 8 + g, :], tmp, self.UG[:, g, :], ALU.mult, [fb[5], self.ugb], [self.yqb])
            self.load(self.FS(7), Sc["g1row"][:, :], fb[7], reads=[self.scrb["g1row"]])
            for tt in range(4):
                s = tt % 2
                xs = self.FS(s)
                self.load(xs, I["x_own"][t0 + tt * 128: t0 + (tt + 1) * 128, :], fb[s])
                for ds in range(4):
                    ws = self.wload(Sc["w_out_b"][ds], "w_out_b")
                    wt = self.wb3(ws)
                    pf, pfb = self.next_pf()
                    self.mm(pf[:, :], [(self.YQ[:, kc, tt * 128:(tt + 1) * 128], wt[:, kc, :]) for kc in range(KC)],
                            [self.yqb, self.wbb[ws]], [pfb])
                    tmp = self.FS(5, 512, off=1536)
                    self.tt("dve", tmp, pf[:, :], self.FS(7, 512, off=ds * 512), ALU.mult, [pfb, fb[7]], [fb[5]])
                    self.tt("pool", xs[:, ds * 512:(ds + 1) * 512], xs[:, ds * 512:(ds + 1) * 512], tmp, ALU.add,
                            [fb[5], fb[s]], [fb[s]])
                self.store(Sc["x1_st"][t0 + tt * 128: t0 + (tt + 1) * 128, :], xs, fb[s], [fb[s]], [self.scrb["x1_st"]])
                self.norm_tt(xs, fb[s], tt, a2, b2, [cb])
            if self.dbg == "x1":
                continue
            self.peer(j)
            self.final(j)
        if self.dbg == "x1":
            self.S.dma(lambda e: e.dma_start(out=self.OUT[:, :], in_=Sc["x1_st"][:, :]), self.outb,
                       [self.scrb["x1_st"]], [self.outb])

    def peer(self, j):
        I, Sc = self.I, self.Sc
        fb = self.fsb
        cb = self.cstb
        sm = self.smallb
        self.load(self.FS(8), I["kT_l"][:, :], fb[8])
        self.cp("dve", self.SMW[:, 0:2048], self.FS(8), [fb[8]], [self.smwb])
        kT = self.SMW[:, 0:2048].rearrange("p (a h e) -> p a h e", a=2, h=8)
        for qt in range(4):
            ws = self.wload(Sc["wq_b"][qt], "wq_b")
            wt = self.wb3(ws)
            for c in range(4):
                pf, pfb = self.next_pf()
                self.mm(pf[:, :], [(wt[:, kc, c * 128:(c + 1) * 128], self.HNT[:, kc, :]) for kc in range(KC)],
                        self.hntb + [self.wbb[ws]], [pfb])
                self.cp("act", self.YQ[:, 4 * qt + c, :], pf[:, :], [pfb], [self.yqb])
        for tt in range(4):
            ssb = self.FSA[:, (4 + tt) * 2048:(5 + tt) * 2048].rearrange("p (h a e) -> p h a e", h=8, a=2)
            for h2 in range(4):
                pf, pfb = self.next_pf()
                groups = []
                for hh in range(2):
                    h = 2 * h2 + hh
                    for a in range(2):
                        groups.append((pf[:, (2 * hh + a) * 128:(2 * hh + a + 1) * 128],
                                       [(self.YQ[:, 2 * h + a, tt * 128:(tt + 1) * 128], kT[:, a, h, :])]))
                self.mm_multi(groups, [self.yqb, self.smwb], [pfb])
                self.cp("act", ssb[:, 2 * h2:2 * h2 + 2, :, :],
                        pf[:, :].rearrange("p (h a e) -> p h a e", h=2, a=2), [pfb], [fb[4 + tt]])
        V1 = self.SMALL[:, 256:272]
        V2 = self.SMALL[:, 272:288]
        C16 = self.SMALL[:, 288:304]
        CALL = self.SMALL[:, 328:456].rearrange("p (h k) -> p h k", h=8)
        CEX = self.SMALL[:, 456:584].rearrange("p (h k) -> p h k", h=8)
        TMPK = self.FS(9, 256, off=3072 // 2)
        CAND = self.FS(9, 256, off=1792)
        for tt in range(4):
            ssb = self.FSA[:, (4 + tt) * 2048:(5 + tt) * 2048].rearrange("p (h a e) -> p h a e", h=8, a=2)
            for h in range(8):
                for a, V in ((0, V1), (1, V2)):
                    src = ssb[:, h, a, :]
                    self.S.op("dve", (lambda V=V, src=src: lambda e: e.max(out=V[:, 0:8], in_=src))(), [fb[4 + tt]], [sm])
                    self.S.op("dve", (lambda V=V, src=src: lambda e: e.match_replace(
                        out=TMPK[:, 0:128], in_to_replace=V[:, 0:8], in_values=src, imm_value=NEG))(),
                        [fb[4 + tt], sm], [fb[9]], attach=False)
                    self.S.op("dve", (lambda V=V: lambda e: e.max(out=V[:, 8:16], in_=TMPK[:, 0:128]))(), [fb[9]], [sm])
                self.tt("pool", CAND.rearrange("p (a b) -> p a b", a=16), V1.unsqueeze(2).to_broadcast([P, 16, 16]),
                        V2.unsqueeze(1).to_broadcast([P, 16, 16]), ALU.add, [sm], [fb[9]])
                self.S.op("dve", lambda e: e.max(out=C16[:, 0:8], in_=CAND), [fb[9]], [sm])
                self.S.op("dve", lambda e: e.match_replace(out=TMPK, in_to_replace=C16[:, 0:8], in_values=CAND,
                                                           imm_value=NEG), [fb[9], sm], [fb[9]], attach=False)
                self.S.op("dve", lambda e: e.max(out=C16[:, 8:16], in_=TMPK), [fb[9]], [sm])
                self.cp("dve", CALL[:, h, :], C16, [sm], [sm])
            tau8 = self.SMALL[:, 128 + tt * 8: 128 + tt * 8 + 8]
            nb8 = self.SMALL[:, 160 + tt * 8: 160 + tt * 8 + 8]
            zs8 = self.SMALL[:, 304:312]
            self.cp("dve", tau8, CALL[:, :, 15], [sm], [sm])
            self.tt("dve", CEX, CALL, CALL[:, :, 0:1].to_broadcast([P, 8, 16]), ALU.subtract, [sm], [sm])
            self.act(CEX, CEX, AF.Exp, [sm], [sm])
            self.S.op("dve", lambda e: e.tensor_reduce(out=zs8, in_=CEX, axis=AX.X, op=ALU.add), [sm], [sm])
            self.act(zs8, zs8, AF.Ln, [sm], [sm])
            self.tt("dve", nb8, zs8, CALL[:, :, 0], ALU.add, [sm], [sm])
            self.ts("dve", nb8, nb8, -1.0, None, ALU.mult, None, [sm], [sm])
        tau_all = self.SMALL[:, 128:160]
        nb_all = self.SMALL[:, 160:192]
        bias2 = self.SMALL[:, 192:224]
        self.ts("dve", tau_all, tau_all, -2.0e-5, None, ALU.add, None, [sm], [sm])
        self.tt("dve", bias2, tau_all, nb_all, ALU.add, [sm], [sm])
        for tt in range(4):
            ssb = self.FSA[:, (4 + tt) * 2048:(5 + tt) * 2048].rearrange("p (h a e) -> p h a e", h=8, a=2)
            self.tt("dve", ssb[:, :, 0, :], ssb[:, :, 0, :],
                    tau_all[:, tt * 8:(tt + 1) * 8].unsqueeze(2).to_broadcast([P, 8, 128]), ALU.subtract,
                    [fb[4 + tt], sm], [fb[4 + tt]])
        ZB = [self.FS(8), self.FS(9)]
        zbuf = [fb[8], fb[9]]
        yq2 = self.YQ[:, :, :].rearrange("p a n -> p (a n)")
        EB = [yq2[:, 0:2048], yq2[:, 2048:4096]]
        W2 = [yq2[:, 4096:6144], yq2[:, 6144:8192]]
        ebuf = [self.S.buf("eb0"), self.S.buf("eb1")]
        w2buf = [self.S.buf("w2b0"), self.S.buf("w2b1")]
        G2 = self.XCB[:, :, :].rearrange("p a n -> p (a n)")
        g2b = self.xcbb
        self.S.op("dve", lambda e: e.memset(yq2[:, 4096:4100], 0.0), [], [self.yqb, self.wbb[2]] + ebuf + w2buf + self.vhb)
        GA = self.XN[:, :].rearrange("p (a n) -> p a n", a=4)
        gab = self.xnb
        WACT = [self.UG, self.VGN[:, :, :].rearrange("p a n -> p (a n)").rearrange("p (c t) -> p c t", c=8)]
        wactb = [self.ugb, self.vgnb]
        OUTS = self.FSA[:, 0:4 * 2048].rearrange("p (t d) -> p t d", t=4)
        zi = 0
        vhalf = [0]

        def vstep(gv, ds, tt, vt, vtb):
            WTv = WACT[gv % 2]
            wtbv = wactb[gv % 2]
            pf, pfb = self.next_pf()
            self.mm(pf[:, :], [(WTv[:, c, tt * 128:(tt + 1) * 128], vt[:, c, :]) for c in range(8)],
                    [wtbv, vtb], [pfb])
            dst = OUTS[:, tt, ds * 512:(ds + 1) * 512]
            if gv == 0:
                self.cp("dve", dst, pf[:, :], [pfb], [fb[tt]])
            else:
                self.tt("dve", dst, dst, pf[:, :], ALU.add, [pfb, fb[tt]], [fb[tt]])

        def vload(gv, ds):
            hh = vhalf[0] % 2
            vhalf[0] += 1
            dstv = self.WB[:, 2, hh * 4096:(hh + 1) * 4096]
            self.load(dstv, Sc["v_b"][4 * gv + ds], self.vhb[hh], reads=[self.scrb["v_b"]])
            return dstv.rearrange("p (c n) -> p c n", c=8), self.vhb[hh]

        for g in range(17):
            pend = []
            if g > 0:
                for ds in range(4):
                    for tt in range(4):
                        pend.append((g - 1, ds, tt))
            vcur = {}
            if g == 16:
                for (gv, ds, tt) in pend:
                    if ds not in vcur:
                        vcur[ds] = vload(gv, ds)
                    vstep(gv, ds, tt, *vcur[ds])
                break
            WT_ = WACT[g % 2]
            wtb = wactb[g % 2]
            us = []
            for sl in range(2):
                self.load(self.WB[:, sl, :], Sc["uT_b"][2 * g + sl], self.wbb[sl], reads=[self.scrb["uT_b"]])
                us.append(sl)
            for tt in range(4):
                ssb = self.FSA[:, (4 + tt) * 2048:(5 + tt) * 2048].rearrange("p (h a e) -> p h a e", h=8, a=2)
                w2 = W2[tt % 2]
                w2b = w2buf[tt % 2]
                for hb in range(4):
                    z = ZB[zi % 2]
                    zb = zbuf[zi % 2]
                    eb = EB[zi % 2]
                    ebb = ebuf[zi % 2]
                    zi += 1
                    self.tt("pool", z.rearrange("p (j a b) -> p j a b", j=2, a=8),
                            ssb[:, 2 * hb:2 * hb + 2, 0, 8 * g:8 * g + 8].unsqueeze(3).to_broadcast([P, 2, 8, 128]),
                            ssb[:, 2 * hb:2 * hb + 2, 1, :].unsqueeze(2).to_broadcast([P, 2, 8, 128]), ALU.add,
                            [fb[4 + tt]], [zb])
                    for jh in range(2):
                        h = 2 * hb + jh
                        self.act(eb[:, jh * 1024:(jh + 1) * 1024], z[:, jh * 1024:(jh + 1) * 1024], AF.Exp,
                                 [zb, sm], [ebb], bias=bias2[:, tt * 8 + h: tt * 8 + h + 1])
                    if hb == 0:
                        self.stt(w2, z, 0.0, eb, ALU.is_ge, ALU.mult, [zb, ebb], [w2b])
                    else:
                        self.stt(G2, z, 0.0, eb, ALU.is_ge, ALU.mult, [zb, ebb], [g2b])
                        self.tt("dve", w2, w2, G2, ALU.add, [g2b, w2b], [w2b])
                    if pend:
                        gv, ds, ttv = pend.pop(0)
                        if ds not in vcur:
                            vcur[ds] = vload(gv, ds)
                        vstep(gv, ds, ttv, *vcur[ds])
                wcur = w2[:, 0:1024]
                self.tt("dve", wcur, wcur, w2[:, 1024:2048], ALU.add, [w2b], [w2b])
                for sl in range(2):
                    ut = self.wb3(us[sl])
                    pf, pfb = self.next_pf()
                    self.mm(pf[:, :], [(self.HNT[:, kc, tt * 128:(tt + 1) * 128], ut[:, kc, :]) for kc in range(KC)],
                            self.hntb + [self.wbb[us[sl]]], [pfb])
                    ga = GA[:, sl, :]
                    wa = GA[:, 2 + sl, :]
                    self.act(ga, pf[:, :], AF.Gelu_apprx_tanh, [pfb], [gab])
                    self.tt("dve", wa, ga, wcur[:, sl * 512:(sl + 1) * 512], ALU.mult, [gab, w2b], [gab])
                    pb, pbb = self.next_pb()
                    self.trs([(pb[:, c * 128:(c + 1) * 128], wa[:, c * 128:(c + 1) * 128]) for c in range(4)],
                             [gab], [pbb])
                    self.cp("act", WT_[:, 4 * sl:4 * sl + 4, tt * 128:(tt + 1) * 128],
                            pb[:, 0:512].rearrange("p (c t) -> p c t", c=4), [pbb], [wtb])
        self._wslot = 0
        self.S.op("dve", lambda e: e.memset(yq2[:, 4096:4100], 0.0), [], [self.yqb, self.wbb[2]] + ebuf + w2buf + self.vhb)

    def final(self, j):
        I, Sc = self.I, self.Sc
        fb = self.fsb
        sm = self.smallb
        t0 = j * T
        self.load(self.FS(5), Sc["g2row"][:, :], fb[5], reads=[self.scrb["g2row"]])
        self.load(self.FS(6), I["fgR"][:, :], fb[6])
        for tt in range(4):
            x1 = self.FS(4)
            self.load(x1, Sc["x1_st"][t0 + tt * 128: t0 + (tt + 1) * 128, :], fb[4], reads=[self.scrb["x1_st"]])
            o = self.FS(tt)
            self.tt("dve", o, o, self.FS(5), ALU.mult, [fb[tt], fb[5]], [fb[tt]])
            self.tt("pool", o, o, x1, ALU.add, [fb[tt], fb[4]], [fb[tt]])
            ssq = self.SMALL[:, 8:9]
            rstd = self.SMALL[:, 9:10]
            self.act(self.XN[:, :], o, AF.Square, [fb[tt]], [self.xnb, sm], accum=ssq)
            self.act(rstd, ssq, AF.Ln, [sm], [sm], scale=1.0 / D, bias=self.EPSC[:, 0:1])
            self.act(rstd, rstd, AF.Exp, [sm], [sm], scale=-0.5)
            self.stt(o, o, rstd, self.FS(6), ALU.mult, ALU.mult, [fb[tt], fb[6], sm], [fb[tt]])
            self.store(self.OUT[t0 + tt * 128: t0 + (tt + 1) * 128, :], o, fb[tt], [fb[tt]], [self.outb])


def _wtiles(w, ncols):
    K_, C = w.shape
    t = w.reshape(KC, P, C // ncols, ncols).transpose(2, 1, 0, 3)
    return np.ascontiguousarray(t).reshape(C // ncols, P, KC * ncols)


def _fm(v, n):
    return np.ascontiguousarray(v.reshape(n, P).T)


def prepare_inputs(inp, NT):
    f = lambda a: np.asarray(a, dtype=np.float32)
    x, c, ctx, c_ctx = f(inp["x"]), f(inp["c"]), f(inp["ctx"]), f(inp["c_ctx"])
    w_mod, b_mod = f(inp["w_mod"])[0], f(inp["b_mod"])[0]
    shared = {}
    shared["w_mod_l"] = _wtiles(w_mod, 512)
    shared["bmodF"] = _fm(b_mod, 96)
    shared["bmodR"] = np.ascontiguousarray(np.broadcast_to(
        np.concatenate([b_mod[2 * D:3 * D], b_mod[5 * D:6 * D]])[None, :], (P, 2 * D)))
    shared["gF"] = np.ascontiguousarray(np.concatenate([_fm(f(inp["norm1_g"])[0], 16), _fm(f(inp["norm2_g"])[0], 16)], axis=1))
    shared["fgR"] = np.ascontiguousarray(np.broadcast_to(f(inp["final_g"])[None, :], (P, D)))
    shared["w_in_l"] = _wtiles(f(inp["w_in"])[0], 512)
    shared["w_out_l"] = _wtiles(f(inp["w_out"])[0], 512)
    shared["wq_l"] = _wtiles(f(inp["peer_wq"])[0], 512)
    shared["uT_l"] = _wtiles(np.ascontiguousarray(f(inp["peer_u"])[0].T), 512)
    v = f(inp["peer_v"])[0]
    shared["v_l"] = np.ascontiguousarray(v.reshape(16, 8, P, 4, 512).transpose(0, 3, 2, 1, 4)).reshape(64, P, 4096)
    cw, cbias = f(inp["conv_w"])[0], f(inp["conv_b"])[0]
    conv = np.concatenate([cw.reshape(4, 8, P), cbias.reshape(1, 8, P)], axis=0)
    shared["conv_l"] = np.ascontiguousarray(conv.transpose(2, 1, 0)).reshape(P, 40)
    wa, wx = f(inp["lru_wa"])[0], f(inp["lru_wx"])[0]
    lw = np.stack([wa, wx], axis=0)
    shared["lru_w"] = np.ascontiguousarray(lw.transpose(3, 0, 1, 2, 4)).reshape(P, 4096)
    lv = np.stack([f(inp["lru_ba"])[0], f(inp["lru_bx"])[0], f(inp["lru_lambda"])[0]], axis=0)
    shared["lru_v"] = np.ascontiguousarray(lv.reshape(3, 2, 8, P).transpose(3, 0, 1, 2)).reshape(P, 48)
    sw = f(inp["sgu_w"])[0]
    shared["wsT_l"] = np.ascontiguousarray(sw.transpose(2, 0, 1)).reshape(P, 1024)
    shared["bsR"] = np.ascontiguousarray(np.broadcast_to(f(inp["sgu_b"])[0].reshape(1, 1024), (P, 1024)))
    k = np.stack([f(inp["peer_k1"])[0], f(inp["peer_k2"])[0]], axis=0)
    shared["kT_l"] = np.ascontiguousarray(k.transpose(3, 0, 1, 2)).reshape(P, 2048)
    maps = []
    L = 4 * NT
    for core in range(8):
        b, q = core // 4, core % 4
        m = dict(shared)
        m["x_own"] = np.ascontiguousarray(x[b, q * NT:(q + 1) * NT])
        m["x_b"] = np.ascontiguousarray(x[b])
        halo = np.zeros((P, D), np.float32)
        if q > 0:
            halo[0:2] = x[b, q * NT - 2:q * NT]
        if q < 3:
            halo[2] = x[b, (q + 1) * NT]
        m["x_halo"] = halo
        m["ctx_b"] = np.ascontiguousarray(ctx[b])
        m["s_in"] = np.ascontiguousarray(np.stack([_fm(c[b], 16), _fm(c_ctx, 16)], axis=2)).reshape(P, 32)
        sel = np.zeros((P, 12), np.float32)
        sel[:, q] = 1.0
        sel[:, 4 + q] = 1.0
        sel[:, 8] = sel[:, 9] = 1.0 if q > 0 else 0.0
        sel[:, 10] = 1.0 if q < 3 else 0.0
        m["sel"] = sel
        maps.append(m)
    return maps


_NC_CACHE = {}


def run(inp, NT, dbg=None):
    key = (NT, dbg)
    if key not in _NC_CACHE:
        _NC_CACHE[key] = K(NT, dbg).build()
    nc = _NC_CACHE[key]
    maps = prepare_inputs(inp, NT)
    res = run_bass_kernel_spmd(nc, maps, core_ids=list(range(8)))
    B = 2
    out = np.zeros((B, 4 * NT, D), np.float32)
    for core in range(8):
        b, q = core // 4, core % 4
        out[b, q * NT:(q + 1) * NT] = res.results[core]["out"]
    return out, res


def kernel(**inputs):
    import os
    NT = np.asarray(inputs["x"]).shape[1] // 4
    if os.environ.get("KPROBE_NT"):
        NT2 = int(os.environ["KPROBE_NT"])
        inp = dict(inputs)
        inp["x"] = np.ascontiguousarray(np.asarray(inputs["x"])[:, :4 * NT2])
        o, _ = run(inp, NT2)
        out = np.zeros((2, 4 * NT, D), np.float32)
        out[:, :4 * NT2] = o
        return out
    out, _ = run(inputs, NT)
    return out
```

```python
import numpy as np
from contextlib import ExitStack
import concourse.bass as bass
import concourse.mybir as mybir
from concourse.bass_utils import run_bass_kernel_spmd

F32 = mybir.dt.float32
BF16 = mybir.dt.bfloat16
AF = mybir.ActivationFunctionType
ALU = mybir.AluOpType
AX = mybir.AxisListType

P = 128
D = 2048
KC = 16
T = 512
EPS = 1e-6
NEG = -1.0e30


class Buf:
    __slots__ = ("name", "last_w", "readers", "dsem", "dcount")

    def __init__(self, name):
        self.name = name
        self.last_w = None
        self.readers = {}
        self.dsem = None
        self.dcount = 0


class Sched:
    ENG = ("pe", "act", "dve", "pool", "sp")
    COMPUTE = ("pe", "act", "dve", "pool")

    def __init__(self, nc, stack):
        self.nc = nc
        self.stack = stack
        self.e = {}
        for n in self.ENG:
            sem = stack.enter_context(nc.semaphore("s_" + n))
            self.e[n] = dict(sem=sem, count=0, ops=[], waited={})
        self.nb = 0

    def buf(self, name, dma=False):
        self.nb += 1
        b = Buf("%s_%d" % (name, self.nb))
        if dma:
            b.dsem = self.stack.enter_context(self.nc.semaphore("d%d" % self.nb))
        return b

    def _collect(self, eng, reads, writes):
        need = {}

        def add(ev, raw):
            if ev is None:
                return
            key, sem, val = ev
            if key == eng and (eng == "pe" or not raw):
                return
            if key not in need or need[key][1] < val:
                need[key] = (sem, val)

        for b in reads:
            if b.last_w:
                for ev in b.last_w.values():
                    add(ev, True)
        for b in writes:
            if b.last_w:
                for ev in b.last_w.values():
                    add(ev, False)
            for ev in b.readers.values():
                add(ev, False)
        E = self.e[eng]
        waits = []
        for key, (sem, val) in need.items():
            if E["waited"].get(key, 0) >= val:
                continue
            E["waited"][key] = val
            waits.append((key, sem, val))
        return waits

    def _update(self, ev, reads, writes):
        key = ev[0]
        for b in reads:
            old = b.readers.get(key)
            if old is None or old[2] < ev[2]:
                b.readers[key] = ev
        for b in writes:
            if b.last_w is None:
                b.last_w = {}
            b.last_w[key] = ev
            b.readers = {}

    def op(self, eng, fn, reads=(), writes=(), attach=True):
        E = self.e[eng]
        waits = self._collect(eng, reads, writes)
        E["count"] += 1
        ev = (eng, E["sem"], E["count"])
        E["ops"].append(dict(waits=waits, fn=fn, kind="op", idx=E["count"], attach=attach and eng != "pe"))
        self._update(ev, reads, writes)
        return ev

    def dma(self, fn, owner, reads=(), writes=(), queue="sp"):
        Q = self.e[queue]
        waits = self._collect(queue, reads, writes)
        owner.dcount += 16
        ev = ("d_" + owner.name, owner.dsem, owner.dcount)
        Q["ops"].append(dict(waits=waits, fn=fn, kind="dma", inc=(owner.dsem, 16), attach=True))
        self._update(ev, reads, writes)
        return ev

    def final_wait(self, eng, bufs):
        waits = self._collect(eng, bufs, bufs)
        self.e[eng]["ops"].append(dict(waits=waits, fn=None, kind="wait", attach=False))

    def emit(self):
        nc = self.nc
        waited = {n: set() for n in self.COMPUTE}
        for n in self.ENG:
            for o in self.e[n]["ops"]:
                for key, sem, val in o["waits"]:
                    if key in waited:
                        waited[key].add(val)
        rank = {}
        for n in self.COMPUTE:
            rank[n] = {v: i + 1 for i, v in enumerate(sorted(waited[n]))}
        with nc.Block() as block:
            def run(name):
                def body(e):
                    for o in self.e[name]["ops"]:
                        ws = []
                        for key, sem, val in o["waits"]:
                            ws.append((sem, rank[key][val] if key in rank else val))
                        fn = o["fn"]
                        att = None
                        if fn is not None and o["attach"] and ws:
                            att = ws.pop()
                        for sem, val in ws:
                            e.wait_ge(sem, val)
                        if fn is None:
                            continue
                        r = fn(e)
                        first, last = r if isinstance(r, tuple) else (r, r)
                        if att is not None:
                            first._wait_ge(att[0], att[1])
                        if o["kind"] == "dma":
                            last.then_inc(o["inc"][0], o["inc"][1])
                        elif o["idx"] in waited[name]:
                            last.then_inc(self.e[name]["sem"], 1)
                return body
            block.tensor(run("pe"))
            block.scalar(run("act"))
            block.vector(run("dve"))
            block.gpsimd(run("pool"))
            block.sync(run("sp"))


class K:
    def __init__(self, NT, dbg=None):
        self.NT = NT
        self.NTL = NT // T
        self.dbg = dbg

    def act(self, out, in_, func, reads, writes, bias=None, scale=None, accum=None):
        kw = {}
        if bias is not None:
            kw["bias"] = bias
        if scale is not None:
            kw["scale"] = scale
        if accum is not None:
            kw["accum_out"] = accum
        return self.S.op("act", lambda e: e.activation(out=out, in_=in_, func=func, **kw), reads, writes,
                         attach=(accum is None))

    def tt(self, eng, out, in0, in1, op, reads, writes):
        return self.S.op(eng, lambda e: e.tensor_tensor(out=out, in0=in0, in1=in1, op=op), reads, writes)

    def ts(self, eng, out, in0, s1, s2, op0, op1, reads, writes):
        if op1 is None:
            return self.S.op(eng, lambda e: e.tensor_scalar(out=out, in0=in0, scalar1=s1, scalar2=None, op0=op0), reads, writes)
        return self.S.op(eng, lambda e: e.tensor_scalar(out=out, in0=in0, scalar1=s1, scalar2=s2, op0=op0, op1=op1), reads, writes)

    def stt(self, out, in0, scalar, in1, op0, op1, reads, writes):
        return self.S.op("dve", lambda e: e.scalar_tensor_tensor(out=out, in0=in0, scalar=scalar, in1=in1, op0=op0, op1=op1), reads, writes)

    def cp(self, eng, out, in_, reads, writes):
        if eng == "act":
            return self.S.op("act", lambda e: e.activation(out=out, in_=in_, func=AF.Copy), reads, writes)
        return self.S.op(eng, lambda e: e.tensor_copy(out=out, in_=in_), reads, writes)

    def mm(self, out, pairs, reads, writes):
        def fn(e):
            n = len(pairs)
            ins = None
            for i, (l, r) in enumerate(pairs):
                ins = e.matmul(out, l, r, start=(i == 0), stop=(i == n - 1))
            return ins
        return self.S.op("pe", fn, reads, writes)

    def mm_multi(self, groups, reads, writes):
        def fn(e):
            ins = None
            for out, pairs in groups:
                n = len(pairs)
                for i, (l, r) in enumerate(pairs):
                    ins = e.matmul(out, l, r, start=(i == 0), stop=(i == n - 1))
            return ins
        return self.S.op("pe", fn, reads, writes)

    def trs(self, items, reads, writes):
        ident = self.IDENT

        def fn(e):
            ins = None
            for out, in_ in items:
                ins = e.transpose(out=out, in_=in_, identity=ident[:])
            return ins
        return self.S.op("pe", fn, reads, writes)

    def load(self, out, in_, owner, reads=(), writes=None):
        if writes is None:
            writes = [owner]
        return self.S.dma(lambda e: e.dma_start(out=out, in_=in_, allow_slow_non_contiguous=True), owner, reads, writes)

    def store(self, out, in_, owner, reads, writes=()):
        return self.S.dma(lambda e: e.dma_start(out=out, in_=in_, allow_slow_non_contiguous=True), owner, reads, writes)

    def build(self):
        NT, NTL = self.NT, self.NTL
        nc = bass.Bass("TRN2", target_bir_lowering=False)
        self.nc = nc

        def din(name, shape, dt=F32):
            return nc.dram_tensor(name, list(shape), dt, kind="ExternalInput").ap()

        def dscr(name, shape, dt):
            return nc.dram_tensor(name, list(shape), dt, kind="Internal").ap()

        I = {}
        I["x_own"] = din("x_own", [NT, D])
        I["x_b"] = din("x_b", [4 * NT, D])
        I["x_halo"] = din("x_halo", [P, D])
        I["ctx_b"] = din("ctx_b", [256, D])
        I["s_in"] = din("s_in", [P, 32])
        I["sel"] = din("sel", [P, 12])
        I["w_mod_l"] = din("w_mod_l", [24, P, 8192])
        I["bmodF"] = din("bmodF", [P, 96])
        I["bmodR"] = din("bmodR", [P, 4096])
        I["gF"] = din("gF", [P, 32])
        I["fgR"] = din("fgR", [P, D])
        I["w_in_l"] = din("w_in_l", [8, P, 8192])
        I["w_out_l"] = din("w_out_l", [4, P, 8192])
        I["wq_l"] = din("wq_l", [4, P, 8192])
        I["uT_l"] = din("uT_l", [32, P, 8192])
        I["v_l"] = din("v_l", [64, P, 4096])
        I["conv_l"] = din("conv_l", [P, 40])
        I["lru_w"] = din("lru_w", [P, 4096])
        I["lru_v"] = din("lru_v", [P, 48])
        I["wsT_l"] = din("wsT_l", [P, 1024])
        I["bsR"] = din("bsR", [P, 1024])
        I["kT_l"] = din("kT_l", [P, 2048])
        self.I = I
        OUT = nc.dram_tensor("out", [NT, D], F32, kind="ExternalOutput").ap()
        self.OUT = OUT
        if self.dbg:
            self.DBG = nc.dram_tensor("dbg", [P, 4096], F32, kind="ExternalOutput").ap()
        Sc = {}
        Sc["w_in_b"] = dscr("w_in_b", [8, P, 8192], BF16)
        Sc["w_out_b"] = dscr("w_out_b", [4, P, 8192], BF16)
        Sc["wq_b"] = dscr("wq_b", [4, P, 8192], BF16)
        Sc["uT_b"] = dscr("uT_b", [32, P, 8192], BF16)
        Sc["v_b"] = dscr("v_b", [64, P, 4096], BF16)
        Sc["xc_st"] = dscr("xc_st", [P, 8, NT], F32)
        Sc["hf_st"] = dscr("hf_st", [P, 8, NT], F32)
        Sc["x1_st"] = dscr("x1_st", [NT, D], F32)
        Sc["g1row"] = dscr("g1row", [P, D], F32)
        Sc["g2row"] = dscr("g2row", [P, D], F32)
        self.Sc = Sc

        with ExitStack() as st:
            S = Sched(nc, st)
            self.S = S

            def sb(name, shape, dt):
                return st.enter_context(nc.sbuf_tensor(name, list(shape), dt))

            self.FSA = sb("FSA", [P, 10 * 2048], F32)
            self.fsb = [S.buf("fs%d" % i, dma=True) for i in range(10)]
            self.HNT = sb("HNT", [P, KC, T], BF16)
            self.hntb = [S.buf("hnt%d" % i) for i in range(4)]
            self.YQ = sb("YQ", [P, KC, T], BF16)
            self.yqb = S.buf("yq")
            self.UG = sb("UG", [P, 8, T], BF16)
            self.ugb = S.buf("ug")
            self.VGN = sb("VGN", [P, 4, 1024], BF16)
            self.vgnb = S.buf("vgn")
            self.XCB = sb("XCB", [P, 4, T], BF16)
            self.xcbb = S.buf("xcb")
            self.XN = sb("XN", [P, 2048], BF16)
            self.xnb = S.buf("xn")
            self.WB = sb("WB", [P, 3, 8192], BF16)
            self.wbb = [S.buf("wb%d" % i, dma=True) for i in range(3)]
            self.vhb = [S.buf("vh0", dma=True), S.buf("vh1", dma=True)]
            self.cva = [S.buf("cva0", dma=True), S.buf("cva1", dma=True)]
            self.cvb = [S.buf("cvb%d" % i, dma=True) for i in range(4)]
            self.GT = sb("GT", [P, 2, T], BF16)
            self.gtb = [S.buf("gt0"), S.buf("gt1")]
            self.IDENT = sb("IDENT", [P, P], BF16)
            self.SMW = sb("SMW", [P, 4096], BF16)
            self.smwb = S.buf("smw")
            self.WST = sb("WST", [P, 1024], BF16)
            self.CST = sb("CST", [P, 512], F32)
            self.cstb = S.buf("cst", dma=True)
            self.SMALL = sb("SMALL", [P, 640], F32)
            self.smallb = S.buf("small")
            self.EPSC = sb("EPSC", [P, 2], F32)
            self.STT = sb("STT", [P, 128], F32)
            self.sttb = S.buf("stt")
            self.PF = [st.enter_context(nc.psum_tensor("PF%d" % i, [P, 512], F32)) for i in range(6)]
            self.pfb = [S.buf("pf%d" % i) for i in range(6)]
            self.PB = [st.enter_context(nc.psum_tensor("PB%d" % i, [P, 1024], BF16)) for i in range(2)]
            self.pbb = [S.buf("pb%d" % i) for i in range(2)]
            self.pfi = 0
            self.pbi = 0
            self.outb = S.buf("outd", dma=True)
            self.scrb = {k: S.buf("scr_" + k, dma=True) for k in Sc}

            self.phase_consts()
            self.phase_mods()
            self.phase_convert()
            self.phase_chains()
            self.phase_own_f()
            self.phase_own_r()

            allb = self.fsb + self.wbb + [self.outb, self.cstb] + list(self.scrb.values())
            S.final_wait("sp", allb)
            S.emit()
        return nc

    def FS(self, i, n=2048, off=0):
        return self.FSA[:, i * 2048 + off: i * 2048 + off + n]

    def FS3(self, i, a, b, off=0):
        return self.FSA[:, i * 2048 + off: i * 2048 + off + a * b].rearrange("p (a b) -> p a b", a=a)

    def next_pf(self):
        i = self.pfi
        self.pfi = (self.pfi + 1) % 6
        return self.PF[i], self.pfb[i]

    def next_pb(self):
        i = self.pbi
        self.pbi = (self.pbi + 1) % 2
        return self.PB[i], self.pbb[i]

    def wb3(self, s):
        return self.WB[:, s, :].rearrange("p (k n) -> p k n", k=KC)

    C_CONV = 0
    C_LV = 40
    C_CL = 88
    C_CL2 = 104
    C_SEL = 120
    C_GF = 132
    C_SIN = 164
    C_MODF = 196
    C_A1 = 324
    C_B1 = 356
    C_A2 = 388
    C_B2 = 404
    C_BMF = 420
    C_END = 484

    def C(self, off, n):
        return self.CST[:, off:off + n]

    def phase_consts(self):
        S, I = self.S, self.I
        cb = self.cstb
        S.op("dve", lambda e: e.memset(self.EPSC[:, :], EPS), [], [self.smallb])
        idf = self.FS(0, 128)
        S.op("pool", lambda e: e.iota(idf, pattern=[[1, 128]], base=0, channel_multiplier=-1,
                                      allow_small_or_imprecise_dtypes=True), [], [self.fsb[0]])
        S.op("dve", lambda e: e.tensor_single_scalar(out=self.IDENT[:], in_=idf, scalar=0.0, op=ALU.is_equal),
             [self.fsb[0]], [self.smwb])
        self.load(self.C(self.C_CONV, 40), I["conv_l"][:, :], cb)
        self.load(self.C(self.C_LV, 48), I["lru_v"][:, :], cb)
        self.load(self.C(self.C_SEL, 12), I["sel"][:, :], cb)
        self.load(self.C(self.C_GF, 32), I["gF"][:, :], cb)
        self.load(self.C(self.C_SIN, 32), I["s_in"][:, :], cb)
        for jj, j in enumerate((0, 1, 3, 4)):
            self.load(self.C(self.C_BMF + 16 * jj, 16), I["bmodF"][:, 16 * j:16 * j + 16], cb)
        lam = self.C(self.C_LV + 32, 16)
        cl = self.C(self.C_CL, 16)
        cl2 = self.C(self.C_CL2, 16)
        self.act(cl, lam, AF.Exp, [cb], [cb], scale=-1.0)
        self.act(cl, cl, AF.Ln, [cb], [cb], bias=1.0)
        self.ts("dve", cl2, cl, -16.0, None, ALU.mult, None, [cb], [cb])
        self.ts("dve", cl, cl, -8.0, None, ALU.mult, None, [cb], [cb])
        sin = self.C(self.C_SIN, 32)
        self.act(sin, sin, AF.Silu, [cb], [cb])
        self.load(self.FS(0, 4096), I["lru_w"][:, :], self.fsb[0], writes=[self.fsb[0], self.fsb[1]])
        self.cp("dve", self.SMW[:, :], self.FS(0, 4096), [self.fsb[0], self.fsb[1]], [self.smwb])
        self.load(self.FS(2, 1024), I["wsT_l"][:, :], self.fsb[2])
        self.cp("dve", self.WST[:, :], self.FS(2, 1024), [self.fsb[2]], [self.smwb])

    def phase_mods(self):
        S, I, Sc = self.S, self.I, self.Sc
        cb = self.cstb
        sin3 = self.C(self.C_SIN, 32).rearrange("p (k j) -> p k j", k=KC)
        srep = self.FS3(8, KC, 128)
        self.cp("dve", srep, sin3[:, :, 0:1].to_broadcast([P, KC, 128]), [cb], [self.fsb[8]])
        modf = self.C(self.C_MODF, 128).rearrange("p (j k c) -> p j k c", j=4, k=KC)
        bmf = self.C(self.C_BMF, 64).rearrange("p (j k) -> p j k", j=4)
        fm = {0: 0, 1: 1, 3: 2, 4: 3}
        rowdst = {2: ("g1row", Sc["g1row"]), 5: ("g2row", Sc["g2row"])}
        rb = 0
        for pi in range(24):
            j = pi // 4
            s0 = 4 * (pi % 2)
            bufs = self.fsb[s0:s0 + 4]
            pan = self.FSA[:, s0 * 2048:(s0 + 4) * 2048].rearrange("p (k n) -> p k n", k=KC)
            self.load(self.FSA[:, s0 * 2048:(s0 + 4) * 2048], I["w_mod_l"][pi], bufs[0], writes=bufs)
            if j in fm:
                pf, pfb = self.next_pf()
                groups = []
                for cc in range(4):
                    groups.append((pf[:, 2 * cc:2 * cc + 2],
                                   [(pan[:, kc, cc * 128:(cc + 1) * 128], sin3[:, kc, :]) for kc in range(KC)]))
                self.mm_multi(groups, bufs + [cb], [pfb])
                jj = fm[j]
                kc0 = 4 * (pi % 4)
                self.tt("dve", modf[:, jj, kc0:kc0 + 4, :], pf[:, 0:8].rearrange("p (c j) -> p c j", c=4),
                        bmf[:, jj, kc0:kc0 + 4].unsqueeze(2).to_broadcast([P, 4, 2]), ALU.add, [pfb, cb], [cb])
            else:
                pf, pfb = self.next_pf()
                self.mm(pf[:, :], [(srep[:, kc, :], pan[:, kc, :]) for kc in range(KC)], bufs + [self.fsb[8]], [pfb])
                ds = pi % 4
                which = 0 if j == 2 else 1
                bm = self.FS(9, 512, off=512 * (rb % 2))
                rt = self.FS(9, 512, off=1024 + 512 * (rb % 2))
                rb += 1
                self.load(bm, I["bmodR"][:, which * 2048 + ds * 512: which * 2048 + ds * 512 + 512], self.fsb[9])
                self.tt("dve", rt, pf[:, :], bm, ALU.add, [pfb, self.fsb[9]], [self.fsb[9]])
                nm, dst = rowdst[j]
                self.store(dst[:, ds * 512:(ds + 1) * 512], rt, self.scrb[nm], [self.fsb[9]], [self.scrb[nm]])
        gf = self.C(self.C_GF, 32)
        a1 = self.C(self.C_A1, 32).rearrange("p (k c) -> p k c", k=KC)
        b1 = self.C(self.C_B1, 32).rearrange("p (k c) -> p k c", k=KC)
        self.ts("dve", a1, modf[:, 1, :, :], 1.0, None, ALU.add, None, [cb], [cb])
        self.tt("dve", a1, a1, gf[:, 0:16].unsqueeze(2).to_broadcast([P, KC, 2]), ALU.mult, [cb], [cb])
        self.cp("dve", b1, modf[:, 0, :, :], [cb], [cb])
        a2 = self.C(self.C_A2, 16)
        b2 = self.C(self.C_B2, 16)
        self.ts("dve", a2, modf[:, 3, :, 0], 1.0, None, ALU.add, None, [cb], [cb])
        self.tt("dve", a2, a2, gf[:, 16:32], ALU.mult, [cb], [cb])
        self.cp("dve", b2, modf[:, 2, :, 0], [cb], [cb])

    def phase_convert(self):
        I, Sc = self.I, self.Sc
        jobs = []
        self.cvjobs = []
        for nm_s, nm_d, n, F in (("uT_l", "uT_b", 32, 8192), ("v_l", "v_b", 64, 4096)):
            for i in range(n):
                for h in range(F // 1024):
                    self.cvjobs.append((I[nm_s][i][:, h * 1024:(h + 1) * 1024], Sc[nm_d][i][:, h * 1024:(h + 1) * 1024], nm_d))
        self.cvk = 0
        self.cvjobs = []
        for nm_s, nm_d, n, F in (("w_in_l", "w_in_b", 8, 8192), ("w_out_l", "w_out_b", 4, 8192),
                                 ("wq_l", "wq_b", 4, 8192), ("uT_l", "uT_b", 32, 8192), ("v_l", "v_b", 64, 4096)):
            for i in range(n):
                for h in range(F // 4096):
                    jobs.append((I[nm_s][i][:, h * 4096:(h + 1) * 4096], Sc[nm_d][i][:, h * 4096:(h + 1) * 4096], nm_d))
        for ji, (src, dst, nm) in enumerate(jobs):
            s0 = 2 * (ji % 3)
            bufs = self.fsb[s0:s0 + 2]
            w = ji % 3
            half = self.WB[:, w, 0:4096]
            self.load(self.FSA[:, s0 * 2048:(s0 + 2) * 2048], src, bufs[0], writes=bufs)
            eng = "dve" if ji % 2 == 0 else "act"
            self.cp(eng, half, self.FSA[:, s0 * 2048:(s0 + 2) * 2048], bufs, [self.wbb[w]])
            self.store(dst, half, self.wbb[w], [self.wbb[w]], [self.scrb[nm]])

    def cv_emit(self, n):
        ugf = self.UG[:, :, :].rearrange("p a n -> p (a n)")
        for _ in range(n):
            if self.cvk >= len(self.cvjobs):
                return
            k = self.cvk
            self.cvk += 1
            src_, dst_, nm = self.cvjobs[k]
            a = k % 2
            b = k % 4
            st = self.FS(9, 1024, off=1024 * a)
            ob = ugf[:, b * 1024:(b + 1) * 1024]
            self.load(st, src_, self.cva[a])
            self.cp("pool", ob, st, [self.cva[a]], [self.cvb[b]])
            self.store(dst_, ob, self.cvb[b], [self.cvb[b]], [self.scrb[nm]])

    def norm_tt(self, xs, xsb, tt, A, B, areads, hn=None):
        HN, hnb = hn if hn is not None else (self.HNT, self.hntb)
        sm = self.smallb
        ssq = self.SMALL[:, 0:1]
        rstd = self.SMALL[:, 1:2]
        self.act(self.XN[:, :], xs, AF.Square, [xsb], [self.xnb, sm], accum=ssq)
        self.act(rstd, ssq, AF.Ln, [sm], [sm], scale=1.0 / D, bias=self.EPSC[:, 0:1])
        self.act(rstd, rstd, AF.Exp, [sm], [sm], scale=-0.5)
        self.act(self.XN[:, :], xs, AF.Copy, [xsb, sm], [self.xnb], scale=rstd)
        for half in range(2):
            pb, pbb = self.next_pb()
            self.trs([(pb[:, i * 128:(i + 1) * 128], self.XN[:, (half * 8 + i) * 128:(half * 8 + i + 1) * 128])
                      for i in range(8)], [self.xnb], [pbb])
            dst = HN[:, half * 8:half * 8 + 8, tt * 128:(tt + 1) * 128]
            pv = pb[:, :].rearrange("p (k n) -> p k n", k=8)
            self.tt("dve", dst, pv, A[:, half * 8:half * 8 + 8].unsqueeze(2).to_broadcast([P, 8, 128]), ALU.mult,
                    [pbb] + areads, [hnb[tt]])
            self.tt("pool", dst, dst, B[:, half * 8:half * 8 + 8].unsqueeze(2).to_broadcast([P, 8, 128]), ALU.add,
                    [hnb[tt]] + areads, [hnb[tt]])

    def norm_rows(self, rows_fn, ntt, A, B, hn=None, tts=None):
        for tt in (range(ntt) if tts is None else tts):
            s = tt % 2
            self.load(self.FS(s), rows_fn(tt), self.fsb[s])
            self.norm_tt(self.FS(s), self.fsb[s], tt, A, B, [self.cstb], hn=hn)

    def xb_half(self, hc, ncur, wxs, mode, d, scan_rng, snap_col=None, snap_idx=None, stash=None, from_stash=None,
                hn=None, filler=None):
        cb, sb_ = self.cstb, self.sttb
        nwin = ncur
        XC = self.FS3(2, 4, T)
        R = self.FS3(3, 4, T)
        Ii = self.FS3(4, 4, T)
        TM = self.FS3(5, 4, T)
        H = self.FS3(6, 4, T)
        fb = self.fsb
        conv = self.C(self.C_CONV, 40)
        if mode is not None:
            EXT = self.FSA[:, 7 * 2048: 7 * 2048 + 4 * 516].rearrange("p (c n) -> p c n", c=4)
            eb = [fb[7], fb[8]]
            sav = self.STT[:, 80:104].rearrange("p (c n) -> p c n", c=8)
            co = 3 if mode == "f" else 0
            for c in range(4):
                oc = 4 * hc + c
                pf, pfb = self.next_pf()
                wt = self.wb3(wxs[oc // 4])
                HN, hnb = hn if hn is not None else (self.HNT, self.hntb)
                self.mm(pf[:, 0:ncur], [(wt[:, kc, (oc % 4) * 128:(oc % 4 + 1) * 128], HN[:, kc, 0:ncur])
                                        for kc in range(KC)], hnb + [self.wbb[wxs[oc // 4]]], [pfb])
                self.cp("act", EXT[:, c, co:co + ncur], pf[:, 0:ncur], [pfb], eb)
            if mode == "f":
                self.cp("dve", EXT[:, :, 0:3], sav[:, 4 * hc:4 * hc + 4, :], [sb_], eb)
                self.cp("dve", sav[:, 4 * hc:4 * hc + 4, :], EXT[:, :, ncur:ncur + 3], eb, [sb_])
            else:
                self.cp("dve", EXT[:, :, ncur:ncur + 3], sav[:, 4 * hc:4 * hc + 4, :], [sb_], eb)
                self.cp("dve", sav[:, 4 * hc:4 * hc + 4, :], EXT[:, :, 0:3], eb, [sb_])
            for c in range(4):
                oc = 4 * hc + c
                w = lambda k: conv[:, oc * 5 + k: oc * 5 + k + 1]
                self.ts("pool", XC[:, c, 0:nwin], EXT[:, c, 0:nwin], w(0), w(4), ALU.mult, ALU.add, eb + [cb], [fb[2]])
                for k in range(1, 4):
                    self.stt(XC[:, c, 0:nwin], EXT[:, c, k:k + nwin], w(k), XC[:, c, 0:nwin], ALU.mult, ALU.add,
                             eb + [cb, fb[2]], [fb[2]])
        else:
            self.load(XC[:, :, :], from_stash, fb[2], reads=[self.scrb["xc_st"]])
        if filler is not None:
            filler()
        return self._xb_gates(hc, nwin, d, scan_rng, snap_col, snap_idx)

    def load_wx(self, tiles, slots, src):
        for tno, s in zip(tiles, slots):
            self.load(self.WB[:, s, :], src[tno], self.wbb[s], reads=[self.scrb["w_in_b"]])

    def zero_sav(self):
        self.S.op("dve", lambda e: e.memset(self.STT[:, 80:104], 0.0), [], [self.sttb])

    def phase_chains(self):
        NT, NTL = self.NT, self.NTL
        I, Sc = self.I, self.Sc
        a1 = self.C(self.C_A1, 32).rearrange("p (k c) -> p k c", k=KC)
        b1 = self.C(self.C_B1, 32).rearrange("p (k c) -> p k c", k=KC)
        self.load_wx([0, 1], [0, 1], Sc["w_in_b"])
        self.S.op("dve", lambda e: e.memset(self.STT[:, :], 0.0), [], [self.sttb])
        self.S.op("pool", lambda e: e.memset(self.HNT[:, :, 256:384], 0.0), [], self.hntb)
        for d in (0, 1):
            self.zero_sav()
            self.norm_rows(lambda tt: I["ctx_b"][tt * 128:(tt + 1) * 128, :], 2, a1[:, :, 1], b1[:, :, 1])
            self.S.op("pool", lambda e: e.memset(self.HNT[:, :, 256:257], 0.0), [], self.hntb)
            for hc in (0, 1):
                self.xb_half(hc, 257, [0, 1], "f", d, (1, 257))
            base = 16 if d == 0 else 48
            idx = 0 if d == 0 else 3
            self.cp("dve", self.STT[:, base + 8 * idx: base + 8 * idx + 8], self.STT[:, 8 * d:8 * d + 8],
                    [self.sttb], [self.sttb])
        sel = self.C(self.C_SEL, 12)
        self.norm_rows(lambda tt: I["x_halo"][:, :], 1, a1[:, :, 0], b1[:, :, 0])
        sav = self.STT[:, 80:104].rearrange("p (c n) -> p c n", c=8)
        hal = self.SMALL[:, 16:48].rearrange("p (c n) -> p c n", c=8)
        for oc in range(8):
            pf, pfb = self.next_pf()
            wt = self.wb3(oc // 4)
            self.mm(pf[:, 0:128], [(wt[:, kc, (oc % 4) * 128:(oc % 4 + 1) * 128], self.HNT[:, kc, 0:128])
                                   for kc in range(KC)], self.hntb + [self.wbb[oc // 4]], [pfb])
            self.tt("dve", hal[:, oc, :], pf[:, 0:4], sel[:, 8:12], ALU.mult, [pfb, self.cstb], [self.smallb])

        items = []
        for j in range(3 * NTL + 1):
            if j == 0:
                rng = (1, T)
            elif j == 3 * NTL:
                rng = (0, 1)
            else:
                rng = (0, T)
            snap = j // NTL if (j > 0 and j % NTL == 0) else None
            items.append(dict(rows=(lambda tt, j=j: I["x_b"][j * T + tt * 128: j * T + (tt + 1) * 128, :]),
                              mode="f", d=0, rng=rng, snap_col=0, snap_idx=snap,
                              pre=(self.zero_sav if j == 0 else None), post=None))
        for j in range(4 * NTL - 1, NTL - 2, -1):
            if j == 4 * NTL - 1:
                rng = (0, T - 2)
            elif j == NTL - 1:
                rng = (T - 2, T)
            else:
                rng = (0, T)
            snap = None
            if (j + 1) % NTL == 0 and j != 4 * NTL - 1:
                snap = (j + 1) // NTL - 1
            items.append(dict(rows=(lambda tt, j=j: I["x_b"][j * T + tt * 128: j * T + (tt + 1) * 128, :]),
                              mode="b", d=1, rng=rng, snap_col=T - 2, snap_idx=snap,
                              pre=(self.zero_sav if j == 4 * NTL - 1 else None), post=None))

        def own_pre():
            for d in (0, 1):
                base = 16 if d == 0 else 48
                stt = self.STT[:, 8 * d:8 * d + 8]
                self.ts("dve", stt, self.STT[:, base:base + 8], sel[:, 4 * d:4 * d + 1], None, ALU.mult, None,
                        [self.sttb, self.cstb], [self.sttb])
                for i in range(1, 4):
                    self.stt(stt, self.STT[:, base + 8 * i: base + 8 * i + 8], sel[:, 4 * d + i:4 * d + i + 1], stt,
                             ALU.mult, ALU.add, [self.sttb, self.cstb], [self.sttb])
            self.S.op("dve", lambda e: e.memset(self.STT[:, 80:104], 0.0), [], [self.sttb])
            self.cp("dve", sav[:, :, 1:3], hal[:, :, 0:2], [self.smallb], [self.sttb])

        def own_post(j):
            def f(hc, XC, H, rng):
                lo, hi = rng
                p0 = j * T - 1
                self.store(Sc["xc_st"][:, 4 * hc:4 * hc + 4, p0 + lo:p0 + hi], XC[:, :, lo:hi], self.fsb[2],
                           [self.fsb[2]], [self.scrb["xc_st"]])
                self.store(Sc["hf_st"][:, 4 * hc:4 * hc + 4, p0 + lo:p0 + hi], H[:, :, lo:hi], self.fsb[6],
                           [self.fsb[6]], [self.scrb["hf_st"]])
            return f

        for j in range(NTL):
            rng = (1, T) if j == 0 else (0, T)
            items.append(dict(rows=(lambda tt, j=j: I["x_own"][j * T + tt * 128: j * T + (tt + 1) * 128, :]),
                              mode="f", d=0, rng=rng, snap_col=None, snap_idx=None,
                              pre=(own_pre if j == 0 else None), post=own_post(j)))

        hntb2 = [self.S.buf("hn2_%d" % i) for i in range(4)]
        self.S.op("dve", lambda e: e.memset(self.EPSC[:, 1:2], 0.0), [], [self.yqb] + hntb2)
        self.S.op("pool", lambda e: e.memset(self.SMALL[:, 600:601], 0.0), [], [self.fsb[9], self.ugb] + self.cva + self.cvb)
        nhalf = 2 * (len(items))
        per = (len(self.cvjobs) + nhalf - 3) // (nhalf - 2) + 1
        HNS = [(self.HNT, self.hntb), (self.YQ, hntb2)]
        al, bl = a1[:, :, 0], b1[:, :, 0]
        self.norm_rows(items[0]["rows"], 4, al, bl, hn=HNS[0])
        for k, it in enumerate(items):
            hn = HNS[k % 2]
            nxt = items[k + 1] if k + 1 < len(items) else None
            if it["pre"] is not None:
                it["pre"]()
            for hc in (0, 1):
                filler = None
                if nxt is not None:
                    filler = (lambda nxt=nxt, hc=hc, k=k: (self.norm_rows(
                        nxt["rows"], 4, al, bl, hn=HNS[(k + 1) % 2], tts=(2 * hc, 2 * hc + 1)), self.cv_emit(per)))
                XC, H = self.xb_half(hc, T, [0, 1], it["mode"], it["d"], it["rng"], snap_col=it["snap_col"],
                                     snap_idx=it["snap_idx"], hn=hn, filler=filler)
                if it["post"] is not None:
                    it["post"](hc, XC, H, it["rng"])
        for hc in (0, 1):
            XC, H = self.xb_half_virtual(hc, hal, (0, 1))
            own_post(NTL)(hc, XC, H, (0, 1))
        self.cv_emit(len(self.cvjobs))
        self.S.op("dve", lambda e: e.memset(self.EPSC[:, 1:2], 0.0), [], [self.yqb] + hntb2)
        self.S.op("pool", lambda e: e.memset(self.SMALL[:, 600:601], 0.0), [], [self.fsb[9], self.ugb] + self.cva + self.cvb)

    def phase_own_f(self):
        return

    def xb_half_virtual(self, hc, hal, rng):
        cb, sb_ = self.cstb, self.sttb
        fb = self.fsb
        EXT = self.FSA[:, 7 * 2048: 7 * 2048 + 4 * 516].rearrange("p (c n) -> p c n", c=4)
        eb = [fb[7], fb[8]]
        sav = self.STT[:, 80:104].rearrange("p (c n) -> p c n", c=8)
        XC = self.FS3(2, 4, T)
        conv = self.C(self.C_CONV, 40)
        n = 8
        self.S.op("dve", lambda e: e.memset(EXT[:, :, 0:16], 0.0), [], eb)
        self.cp("dve", EXT[:, :, 0:3], sav[:, 4 * hc:4 * hc + 4, :], [sb_], eb)
        self.cp("dve", EXT[:, :, 3:4], hal[:, 4 * hc:4 * hc + 4, 2:3], [self.smallb], eb)
        for c in range(4):
            oc = 4 * hc + c
            w = lambda k: conv[:, oc * 5 + k: oc * 5 + k + 1]
            self.ts("pool", XC[:, c, 0:n], EXT[:, c, 0:n], w(0), w(4), ALU.mult, ALU.add, eb + [cb], [fb[2]])
            for k in range(1, 4):
                self.stt(XC[:, c, 0:n], EXT[:, c, k:k + n], w(k), XC[:, c, 0:n], ALU.mult, ALU.add,
                         eb + [cb, fb[2]], [fb[2]])
        return self.xb_tail(hc, n, 0, rng)

    def xb_tail(self, hc, nwin, d, scan_rng):
        return self._xb_gates(hc, nwin, d, scan_rng)

    def _xb_gates(self, hc, nwin, d, scan_rng, snap_col=None, snap_idx=None):
        cb, sb_ = self.cstb, self.sttb
        fb = self.fsb
        XC = self.FS3(2, 4, T)
        R = self.FS3(3, 4, T)
        Ii = self.FS3(4, 4, T)
        TM = self.FS3(5, 4, T)
        H = self.FS3(6, 4, T)
        self.cp("act", self.XCB[:, :, 0:nwin], XC[:, :, 0:nwin], [fb[2]], [self.xcbb])
        lw = self.SMW[:, :].rearrange("p (m d h j) -> p m d h j", m=2, d=2, h=8)
        lv = self.C(self.C_LV, 48).rearrange("p (m d h) -> p m d h", m=3, d=2)
        for m, dstt, dbuf in ((0, R, fb[3]), (1, Ii, fb[4])):
            for c in range(4):
                oc = 4 * hc + c
                pf, pfb = self.next_pf()
                self.mm(pf[:, 0:nwin], [(lw[:, m, d, oc, :], self.XCB[:, c, 0:nwin])], [self.xcbb, self.smwb], [pfb])
                self.act(dstt[:, c, 0:nwin], pf[:, 0:nwin], AF.Sigmoid, [pfb, cb], [dbuf], bias=lv[:, m, d, oc:oc + 1])
        cl = self.C(self.C_CL, 16).rearrange("p (d h) -> p d h", d=2)
        cl2 = self.C(self.C_CL2, 16).rearrange("p (d h) -> p d h", d=2)
        for c in range(4):
            oc = 4 * hc + c
            self.act(TM[:, c, 0:nwin], R[:, c, 0:nwin], AF.Exp, [fb[3], cb], [fb[5]], scale=cl2[:, d, oc:oc + 1])
            self.act(R[:, c, 0:nwin], R[:, c, 0:nwin], AF.Exp, [fb[3], cb], [fb[3]], scale=cl[:, d, oc:oc + 1])
        self.act(TM[:, :, 0:nwin], TM[:, :, 0:nwin], AF.Ln, [fb[5]], [fb[5]], scale=-1.0, bias=1.0)
        self.act(TM[:, :, 0:nwin], TM[:, :, 0:nwin], AF.Exp, [fb[5]], [fb[5]], scale=0.5)
        self.tt("dve", Ii[:, :, 0:nwin], Ii[:, :, 0:nwin], TM[:, :, 0:nwin], ALU.mult, [fb[4], fb[5]], [fb[4]])
        self.tt("dve", Ii[:, :, 0:nwin], Ii[:, :, 0:nwin], XC[:, :, 0:nwin], ALU.mult, [fb[4], fb[2]], [fb[4]])
        lo, hi = scan_rng
        stt = self.STT[:, 8 * d:8 * d + 8]
        for c in range(4):
            oc = 4 * hc + c
            if d == 0:
                o, a, b = H[:, c, lo:hi], R[:, c, lo:hi], Ii[:, c, lo:hi]
            else:
                o = H[:, c, lo:hi][:, ::-1]
                a = R[:, c, lo:hi][:, ::-1]
                b = Ii[:, c, lo:hi][:, ::-1]
            ini = stt[:, oc:oc + 1]
            self.S.op("dve", (lambda o=o, a=a, b=b, ini=ini: (lambda e: e.tensor_tensor_scan(
                out=o, data0=a, data1=b, initial=ini, op0=ALU.mult, op1=ALU.add)))(),
                [fb[3], fb[4], sb_], [fb[6]])
        last = hi - 1 if d == 0 else lo
        self.cp("dve", stt[:, 4 * hc:4 * hc + 4], H[:, :, last], [fb[6]], [sb_])
        if snap_idx is not None:
            base = 16 if d == 0 else 48
            sn = self.STT[:, base + 8 * snap_idx + 4 * hc: base + 8 * snap_idx + 4 * hc + 4]
            self.cp("dve", sn, H[:, :, snap_col], [fb[6]], [sb_])
        return XC, H

    def wstream(self, srcs):
        st = {"i": 0, "n": len(srcs), "srcs": srcs, "slot": getattr(self, "_wslot", 0)}
        return st

    def wload(self, src, scr_name, half=False):
        s = getattr(self, "_wslot", 0)
        self._wslot = (s + 1) % 3
        if half:
            self.load(self.WB[:, s, 0:4096], src, self.wbb[s], reads=[self.scrb[scr_name]])
        else:
            self.load(self.WB[:, s, :], src, self.wbb[s], reads=[self.scrb[scr_name]])
        return s

    def phase_own_r(self):
        NT, NTL = self.NT, self.NTL
        I, Sc = self.I, self.Sc
        cb = self.cstb
        fb = self.fsb
        a1 = self.C(self.C_A1, 32).rearrange("p (k c) -> p k c", k=KC)
        b1 = self.C(self.C_B1, 32).rearrange("p (k c) -> p k c", k=KC)
        a2 = self.C(self.C_A2, 16)
        b2 = self.C(self.C_B2, 16)
        self._wslot = 0
        for j in range(NTL - 1, -1, -1):
            t0 = j * T
            self.load(self.FS(0, 4096), I["lru_w"][:, :], fb[0], writes=[fb[0], fb[1]])
            self.cp("dve", self.SMW[:, :], self.FS(0, 4096), [fb[0], fb[1]], [self.smwb])
            self.norm_rows(lambda tt: I["x_own"][t0 + tt * 128: t0 + (tt + 1) * 128, :], 4, a1[:, :, 0], b1[:, :, 0])
            for hc in (0, 1):
                s = self.wload(Sc["w_in_b"][2 + hc], "w_in_b")
                XC = self.FS3(2, 4, T)
                self.load(XC[:, :, :], Sc["xc_st"][:, 4 * hc:4 * hc + 4, t0:t0 + T], fb[2], reads=[self.scrb["xc_st"]])
                XC, H = self._xb_gates(hc, T, 1, (0, T))
                HF = self.FS3(2, 4, T)
                self.load(HF[:, :, :], Sc["hf_st"][:, 4 * hc:4 * hc + 4, t0:t0 + T], fb[2], reads=[self.scrb["hf_st"]])
                self.tt("pool", H[:, :, :], H[:, :, :], HF[:, :, :], ALU.add, [fb[6], fb[2]], [fb[6]])
                wt = self.wb3(s)
                for c in range(4):
                    oc = 4 * hc + c
                    pf, pfb = self.next_pf()
                    self.mm(pf[:, :], [(wt[:, kc, c * 128:(c + 1) * 128], self.HNT[:, kc, :]) for kc in range(KC)],
                            self.hntb + [self.wbb[s]], [pfb])
                    g = c % 2
                    self.act(self.GT[:, g, :], pf[:, :], AF.Gelu_apprx_tanh, [pfb], [self.gtb[g]])
                    self.tt("dve", self.YQ[:, oc, :], self.GT[:, g, :], H[:, c, :], ALU.mult, [self.gtb[g], fb[6]], [self.yqb])
            for hc in (0, 1):
                s = self.wload(Sc["w_in_b"][4 + hc], "w_in_b")
                wt = self.wb3(s)
                for c in range(4):
                    pf, pfb = self.next_pf()
                    self.mm(pf[:, :], [(wt[:, kc, c * 128:(c + 1) * 128], self.HNT[:, kc, :]) for kc in range(KC)],
                            self.hntb + [self.wbb[s]], [pfb])
                    self.act(self.UG[:, 4 * hc + c, :], pf[:, :], AF.Gelu_apprx_tanh, [pfb], [self.ugb])
            sv = [self.wload(Sc["w_in_b"][6], "w_in_b"), self.wload(Sc["w_in_b"][7], "w_in_b")]
            VG = self.FS3(8, 2, 1024)
            SQ = self.FS(5, 1024)
            sm = self.smallb
            for tt in range(4):
                vg = VG[:, tt % 2, :]
                for cs in range(2):
                    wt = self.wb3(sv[cs])
                    pf, pfb = self.next_pf()
                    self.mm(pf[:, :], [(self.HNT[:, kc, tt * 128:(tt + 1) * 128], wt[:, kc, :]) for kc in range(KC)],
                            self.hntb + [self.wbb[sv[cs]]], [pfb])
                    self.act(vg[:, cs * 512:(cs + 1) * 512], pf[:, :], AF.Gelu_apprx_tanh, [pfb], [fb[8]])
                vg3 = vg.rearrange("p (g c) -> p g c", g=8)
                sq3 = SQ.rearrange("p (g c) -> p g c", g=8)
                su = self.SMALL[:, 64:72]
                ss = self.SMALL[:, 72:80]
                mean = self.SMALL[:, 80:88]
                var = self.SMALL[:, 88:96]
                nmr = self.SMALL[:, 96:104]
                self.S.op("dve", (lambda vg3=vg3: lambda e: e.tensor_reduce(out=su, in_=vg3, axis=AX.X, op=ALU.add))(),
                          [fb[8]], [sm])
                self.tt("pool", SQ, vg, vg, ALU.mult, [fb[8]], [fb[5]])
                self.S.op("dve", lambda e: e.tensor_reduce(out=ss, in_=sq3, axis=AX.X, op=ALU.add), [fb[5]], [sm])
                self.ts("dve", mean, su, 1.0 / 128, None, ALU.mult, None, [sm], [sm])
                self.tt("dve", var, mean, mean, ALU.mult, [sm], [sm])
                self.stt(var, ss, 1.0 / 128, var, ALU.mult, ALU.subtract, [sm], [sm])
                self.act(var, var, AF.Ln, [sm], [sm], bias=self.EPSC[:, 0:1])
                self.act(var, var, AF.Exp, [sm], [sm], scale=-0.5)
                self.tt("dve", nmr, mean, var, ALU.mult, [sm], [sm])
                self.tt("dve", sq3, vg3, var.unsqueeze(2).to_broadcast([P, 8, 128]), ALU.mult, [fb[8], sm], [fb[5]])
                self.tt("dve", self.VGN[:, tt, :].rearrange("p (g c) -> p g c", g=8), sq3,
                        nmr.unsqueeze(2).to_broadcast([P, 8, 128]), ALU.subtract, [fb[5], sm], [self.vgnb])
            self.load(self.FS(9, 1024), I["bsR"][:, :], fb[9])
            for g in range(8):
                pf, pfb = self.next_pf()
                groups = [(pf[:, tt * 128:(tt + 1) * 128],
                           [(self.VGN[:, tt, g * 128:(g + 1) * 128], self.WST[:, g * 128:(g + 1) * 128])])
                          for tt in range(4)]
                self.mm_multi(groups, [self.vgnb, self.smwb], [pfb])
                tmp = self.FS(5, 512, off=1024)
                self.tt("dve", tmp.rearrange("p (a b) -> p a b", a=4), pf[:, :].rearrange("p (a b) -> p a b", a=4),
                        self.FS(9, 128, off=g * 128).unsqueeze(1).to_broadcast([P, 4, 128]), ALU.add,
                        [pfb, fb[9]], [fb[5]])
                self.tt("dve", self.YQ[:, 8 + g, :], tmp, self.UG[:, g, :], ALU.mult, [fb[5], self.ugb], [self.yqb])
            self.load(self.FS(7), Sc["g1row"][:, :], fb[7], reads=[self.scrb["g1row"]])
            for tt in range(4):
                s = tt % 2
                xs = self.FS(s)
                self.load(xs, I["x_own"][t0 + tt * 128: t0 + (tt + 1) * 128, :], fb[s])
                for ds in range(4):
                    ws = self.wload(Sc["w_out_b"][ds], "w_out_b")
                    wt = self.wb3(ws)
                    pf, pfb = self.next_pf()
                    self.mm(pf[:, :], [(self.YQ[:, kc, tt * 128:(tt + 1) * 128], wt[:, kc, :]) for kc in range(KC)],
                            [self.yqb, self.wbb[ws]], [pfb])
                    tmp = self.FS(5, 512, off=1536)
                    self.tt("dve", tmp, pf[:, :], self.FS(7, 512, off=ds * 512), ALU.mult, [pfb, fb[7]], [fb[5]])
                    self.tt("pool", xs[:, ds * 512:(ds + 1) * 512], xs[:, ds * 512:(ds + 1) * 512], tmp, ALU.add,
                            [fb[5], fb[s]], [fb[s]])
                self.store(Sc["x1_st"][t0 + tt * 128: t0 + (tt + 1) * 128, :], xs, fb[s], [fb[s]], [self.scrb["x1_st"]])
                self.norm_tt(xs, fb[s], tt, a2, b2, [cb])
            if self.dbg == "x1":
                continue
            self.peer(j)
            self.final(j)
        if self.dbg == "x1":
            self.S.dma(lambda e: e.dma_start(out=self.OUT[:, :], in_=Sc["x1_st"][:, :]), self.outb,
                       [self.scrb["x1_st"]], [self.outb])

    def peer(self, j):
        I, Sc = self.I, self.Sc
        fb = self.fsb
        cb = self.cstb
        sm = self.smallb
        self.load(self.FS(8), I["kT_l"][:, :], fb[8])
        self.cp("dve", self.SMW[:, 0:2048], self.FS(8), [fb[8]], [self.smwb])
        kT = self.SMW[:, 0:2048].rearrange("p (a h e) -> p a h e", a=2, h=8)
        for qt in range(4):
            ws = self.wload(Sc["wq_b"][qt], "wq_b")
            wt = self.wb3(ws)
            for c in range(4):
                pf, pfb = self.next_pf()
                self.mm(pf[:, :], [(wt[:, kc, c * 128:(c + 1) * 128], self.HNT[:, kc, :]) for kc in range(KC)],
                        self.hntb + [self.wbb[ws]], [pfb])
                self.cp("act", self.YQ[:, 4 * qt + c, :], pf[:, :], [pfb], [self.yqb])
        for tt in range(4):
            ssb = self.FSA[:, (4 + tt) * 2048:(5 + tt) * 2048].rearrange("p (h a e) -> p h a e", h=8, a=2)
            for h2 in range(4):
                pf, pfb = self.next_pf()
                groups = []
                for hh in range(2):
                    h = 2 * h2 + hh
                    for a in range(2):
                        groups.append((pf[:, (2 * hh + a) * 128:(2 * hh + a + 1) * 128],
                                       [(self.YQ[:, 2 * h + a, tt * 128:(tt + 1) * 128], kT[:, a, h, :])]))
                self.mm_multi(groups, [self.yqb, self.smwb], [pfb])
                self.cp("act", ssb[:, 2 * h2:2 * h2 + 2, :, :],
                        pf[:, :].rearrange("p (h a e) -> p h a e", h=2, a=2), [pfb], [fb[4 + tt]])
        V1 = self.SMALL[:, 256:272]
        V2 = self.SMALL[:, 272:288]
        C16 = self.SMALL[:, 288:304]
        CALL = self.SMALL[:, 328:456].rearrange("p (h k) -> p h k", h=8)
        CEX = self.SMALL[:, 456:584].rearrange("p (h k) -> p h k", h=8)
        TMPK = self.FS(9, 256, off=3072 // 2)
        CAND = self.FS(9, 256, off=1792)
        TMH = [self.FS(9, 128, off=1024), self.FS(9, 128, off=1152)]
        tkv = [self.S.buf("tkv0"), self.S.buf("tkv1")]
        tkt = [self.S.buf("tkt0"), self.S.buf("tkt1")]
        self.S.op("dve", lambda e: e.memset(self.SMALL[:, 600:601], 0.0), [], [sm, fb[9]] + tkv + tkt)
        for tt in range(4):
            ssb = self.FSA[:, (4 + tt) * 2048:(5 + tt) * 2048].rearrange("p (h a e) -> p h a e", h=8, a=2)
            for h in range(8):
                srcs = [ssb[:, h, 0, :], ssb[:, h, 1, :]]
                Vs = [V1, V2]
                for a in (0, 1):
                    self.S.op("dve", (lambda V=Vs[a], s_=srcs[a]: lambda e: e.max(out=V[:, 0:8], in_=s_))(),
                              [fb[4 + tt]], [tkv[a]])
                for a in (0, 1):
                    self.S.op("dve", (lambda V=Vs[a], s_=srcs[a], tm=TMH[a]: lambda e: e.match_replace(
                        out=tm, in_to_replace=V[:, 0:8], in_values=s_, imm_value=NEG))(),
                        [fb[4 + tt], tkv[a]], [tkt[a]], attach=False)
                for a in (0, 1):
                    self.S.op("dve", (lambda V=Vs[a], tm=TMH[a]: lambda e: e.max(out=V[:, 8:16], in_=tm))(),
                              [tkt[a]], [tkv[a]])
                self.tt("pool", CAND.rearrange("p (a b) -> p a b", a=16), V1.unsqueeze(2).to_broadcast([P, 16, 16]),
                        V2.unsqueeze(1).to_broadcast([P, 16, 16]), ALU.add, tkv, [fb[9]])
                self.S.op("dve", lambda e: e.max(out=C16[:, 0:8], in_=CAND), [fb[9]], [sm])
                self.S.op("dve", lambda e: e.match_replace(out=TMPK, in_to_replace=C16[:, 0:8], in_values=CAND,
                                                           imm_value=NEG), [fb[9], sm], [fb[9]], attach=False)
                self.S.op("dve", lambda e: e.max(out=C16[:, 8:16], in_=TMPK), [fb[9]], [sm])
                self.cp("dve", CALL[:, h, :], C16, [sm], [sm])
            tau8 = self.SMALL[:, 128 + tt * 8: 128 + tt * 8 + 8]
            nb8 = self.SMALL[:, 160 + tt * 8: 160 + tt * 8 + 8]
            zs8 = self.SMALL[:, 304:312]
            self.cp("dve", tau8, CALL[:, :, 15], [sm], [sm])
            self.tt("dve", CEX, CALL, CALL[:, :, 0:1].to_broadcast([P, 8, 16]), ALU.subtract, [sm], [sm])
            self.act(CEX, CEX, AF.Exp, [sm], [sm])
            self.S.op("dve", lambda e: e.tensor_reduce(out=zs8, in_=CEX, axis=AX.X, op=ALU.add), [sm], [sm])
            self.act(zs8, zs8, AF.Ln, [sm], [sm])
            self.tt("dve", nb8, zs8, CALL[:, :, 0], ALU.add, [sm], [sm])
            self.ts("dve", nb8, nb8, -1.0, None, ALU.mult, None, [sm], [sm])
        self.S.op("dve", lambda e: e.memset(self.SMALL[:, 600:601], 0.0), [], [sm, fb[9]] + tkv + tkt)
        tau_all = self.SMALL[:, 128:160]
        nb_all = self.SMALL[:, 160:192]
        bias2 = self.SMALL[:, 192:224]
        self.ts("dve", tau_all, tau_all, -2.0e-5, None, ALU.add, None, [sm], [sm])
        self.tt("dve", bias2, tau_all, nb_all, ALU.add, [sm], [sm])
        for tt in range(4):
            ssb = self.FSA[:, (4 + tt) * 2048:(5 + tt) * 2048].rearrange("p (h a e) -> p h a e", h=8, a=2)
            self.tt("dve", ssb[:, :, 0, :], ssb[:, :, 0, :],
                    tau_all[:, tt * 8:(tt + 1) * 8].unsqueeze(2).to_broadcast([P, 8, 128]), ALU.subtract,
                    [fb[4 + tt], sm], [fb[4 + tt]])
        ZB = [self.FS(8), self.FS(9)]
        zbuf = [fb[8], fb[9]]
        yq2 = self.YQ[:, :, :].rearrange("p a n -> p (a n)")
        EB = [yq2[:, 0:2048], yq2[:, 2048:4096]]
        W2 = [yq2[:, 4096:6144], yq2[:, 6144:8192]]
        ebuf = [self.S.buf("eb0"), self.S.buf("eb1")]
        w2buf = [self.S.buf("w2b0"), self.S.buf("w2b1")]
        G2 = self.XCB[:, :, :].rearrange("p a n -> p (a n)")
        g2b = self.xcbb
        self.S.op("dve", lambda e: e.memset(yq2[:, 4096:4100], 0.0), [], [self.yqb, self.wbb[2]] + ebuf + w2buf + self.vhb)
        GA = self.XN[:, :].rearrange("p (a n) -> p a n", a=4)
        gab = self.xnb
        WACT = [self.UG, self.VGN[:, :, :].rearrange("p a n -> p (a n)").rearrange("p (c t) -> p c t", c=8)]
        wactb = [self.ugb, self.vgnb]
        OUTS = self.FSA[:, 0:4 * 2048].rearrange("p (t d) -> p t d", t=4)
        zi = 0
        vhalf = [0]

        vfly = []

        def vadd():
            pf, pfb, gv, ds, tt = vfly.pop(0)
            dst = OUTS[:, tt, ds * 512:(ds + 1) * 512]
            if gv == 0:
                self.cp("dve", dst, pf[:, :], [pfb], [fb[tt]])
            else:
                self.tt("dve", dst, dst, pf[:, :], ALU.add, [pfb, fb[tt]], [fb[tt]])

        def vstep(gv, ds, tt, vt, vtb):
            WTv = WACT[gv % 2]
            wtbv = wactb[gv % 2]
            pf, pfb = self.next_pf()
            self.mm(pf[:, :], [(WTv[:, c, tt * 128:(tt + 1) * 128], vt[:, c, :]) for c in range(8)],
                    [wtbv, vtb], [pfb])
            vfly.append((pf, pfb, gv, ds, tt))
            if len(vfly) > 1:
                vadd()

        def vload(gv, ds):
            hh = vhalf[0] % 2
            vhalf[0] += 1
            dstv = self.WB[:, 2, hh * 4096:(hh + 1) * 4096]
            self.load(dstv, Sc["v_b"][4 * gv + ds], self.vhb[hh], reads=[self.scrb["v_b"]])
            return dstv.rearrange("p (c n) -> p c n", c=8), self.vhb[hh]

        for g in range(17):
            pend = []
            if g > 0:
                for ds in range(4):
                    for tt in range(4):
                        pend.append((g - 1, ds, tt))
            vcur = {}
            if g == 16:
                for (gv, ds, tt) in pend:
                    if ds not in vcur:
                        vcur[ds] = vload(gv, ds)
                    vstep(gv, ds, tt, *vcur[ds])
                while vfly:
                    vadd()
                break
            WT_ = WACT[g % 2]
            wtb = wactb[g % 2]
            us = []
            for sl in range(2):
                self.load(self.WB[:, sl, :], Sc["uT_b"][2 * g + sl], self.wbb[sl], reads=[self.scrb["uT_b"]])
                us.append(sl)
            for tt in range(4):
                ssb = self.FSA[:, (4 + tt) * 2048:(5 + tt) * 2048].rearrange("p (h a e) -> p h a e", h=8, a=2)
                w2 = W2[tt % 2]
                w2b = w2buf[tt % 2]
                for hb in range(4):
                    z = ZB[zi % 2]
                    zb = zbuf[zi % 2]
                    eb = EB[zi % 2]
                    ebb = ebuf[zi % 2]
                    zi += 1
                    self.tt("pool", z.rearrange("p (j a b) -> p j a b", j=2, a=8),
                            ssb[:, 2 * hb:2 * hb + 2, 0, 8 * g:8 * g + 8].unsqueeze(3).to_broadcast([P, 2, 8, 128]),
                            ssb[:, 2 * hb:2 * hb + 2, 1, :].unsqueeze(2).to_broadcast([P, 2, 8, 128]), ALU.add,
                            [fb[4 + tt]], [zb])
                    for jh in range(2):
                        h = 2 * hb + jh
                        self.act(eb[:, jh * 1024:(jh + 1) * 1024], z[:, jh * 1024:(jh + 1) * 1024], AF.Exp,
                                 [zb, sm], [ebb], bias=bias2[:, tt * 8 + h: tt * 8 + h + 1])
                    if hb == 0:
                        self.stt(w2, z, 0.0, eb, ALU.is_ge, ALU.mult, [zb, ebb], [w2b])
                    else:
                        self.stt(G2, z, 0.0, eb, ALU.is_ge, ALU.mult, [zb, ebb], [g2b])
                        self.tt("dve", w2, w2, G2, ALU.add, [g2b, w2b], [w2b])
                    if pend:
                        gv, ds, ttv = pend.pop(0)
                        if ds not in vcur:
                            vcur[ds] = vload(gv, ds)
                        vstep(gv, ds, ttv, *vcur[ds])
                wcur = w2[:, 0:1024]
                self.tt("dve", wcur, wcur, w2[:, 1024:2048], ALU.add, [w2b], [w2b])
                for sl in range(2):
                    ut = self.wb3(us[sl])
                    pf, pfb = self.next_pf()
                    self.mm(pf[:, :], [(self.HNT[:, kc, tt * 128:(tt + 1) * 128], ut[:, kc, :]) for kc in range(KC)],
                            self.hntb + [self.wbb[us[sl]]], [pfb])
                    ga = GA[:, sl, :]
                    wa = GA[:, 2 + sl, :]
                    self.act(ga, pf[:, :], AF.Gelu_apprx_tanh, [pfb], [gab])
                    self.tt("dve", wa, ga, wcur[:, sl * 512:(sl + 1) * 512], ALU.mult, [gab, w2b], [gab])
                    pb, pbb = self.next_pb()
                    self.trs([(pb[:, c * 128:(c + 1) * 128], wa[:, c * 128:(c + 1) * 128]) for c in range(4)],
                             [gab], [pbb])
                    self.cp("act", WT_[:, 4 * sl:4 * sl + 4, tt * 128:(tt + 1) * 128],
                            pb[:, 0:512].rearrange("p (c t) -> p c t", c=4), [pbb], [wtb])
            while vfly:
                vadd()
        self._wslot = 0
        self.S.op("dve", lambda e: e.memset(yq2[:, 4096:4100], 0.0), [], [self.yqb, self.wbb[2]] + ebuf + w2buf + self.vhb)

    def final(self, j):
        I, Sc = self.I, self.Sc
        fb = self.fsb
        sm = self.smallb
        t0 = j * T
        self.load(self.FS(5), Sc["g2row"][:, :], fb[5], reads=[self.scrb["g2row"]])
        self.load(self.FS(6), I["fgR"][:, :], fb[6])
        for tt in range(4):
            x1 = self.FS(4)
            self.load(x1, Sc["x1_st"][t0 + tt * 128: t0 + (tt + 1) * 128, :], fb[4], reads=[self.scrb["x1_st"]])
            o = self.FS(tt)
            self.tt("dve", o, o, self.FS(5), ALU.mult, [fb[tt], fb[5]], [fb[tt]])
            self.tt("pool", o, o, x1, ALU.add, [fb[tt], fb[4]], [fb[tt]])
            ssq = self.SMALL[:, 8:9]
            rstd = self.SMALL[:, 9:10]
            self.act(self.XN[:, :], o, AF.Square, [fb[tt]], [self.xnb, sm], accum=ssq)
            self.act(rstd, ssq, AF.Ln, [sm], [sm], scale=1.0 / D, bias=self.EPSC[:, 0:1])
            self.act(rstd, rstd, AF.Exp, [sm], [sm], scale=-0.5)
            self.stt(o, o, rstd, self.FS(6), ALU.mult, ALU.mult, [fb[tt], fb[6], sm], [fb[tt]])
            self.store(self.OUT[t0 + tt * 128: t0 + (tt + 1) * 128, :], o, fb[tt], [fb[tt]], [self.outb])


def _wtiles(w, ncols):
    K_, C = w.shape
    t = w.reshape(KC, P, C // ncols, ncols).transpose(2, 1, 0, 3)
    return np.ascontiguousarray(t).reshape(C // ncols, P, KC * ncols)


def _fm(v, n):
    return np.ascontiguousarray(v.reshape(n, P).T)


def prepare_inputs(inp, NT):
    f = lambda a: np.asarray(a, dtype=np.float32)
    x, c, ctx, c_ctx = f(inp["x"]), f(inp["c"]), f(inp["ctx"]), f(inp["c_ctx"])
    w_mod, b_mod = f(inp["w_mod"])[0], f(inp["b_mod"])[0]
    shared = {}
    shared["w_mod_l"] = _wtiles(w_mod, 512)
    shared["bmodF"] = _fm(b_mod, 96)
    shared["bmodR"] = np.ascontiguousarray(np.broadcast_to(
        np.concatenate([b_mod[2 * D:3 * D], b_mod[5 * D:6 * D]])[None, :], (P, 2 * D)))
    shared["gF"] = np.ascontiguousarray(np.concatenate([_fm(f(inp["norm1_g"])[0], 16), _fm(f(inp["norm2_g"])[0], 16)], axis=1))
    shared["fgR"] = np.ascontiguousarray(np.broadcast_to(f(inp["final_g"])[None, :], (P, D)))
    shared["w_in_l"] = _wtiles(f(inp["w_in"])[0], 512)
    shared["w_out_l"] = _wtiles(f(inp["w_out"])[0], 512)
    shared["wq_l"] = _wtiles(f(inp["peer_wq"])[0], 512)
    shared["uT_l"] = _wtiles(np.ascontiguousarray(f(inp["peer_u"])[0].T), 512)
    v = f(inp["peer_v"])[0]
    shared["v_l"] = np.ascontiguousarray(v.reshape(16, 8, P, 4, 512).transpose(0, 3, 2, 1, 4)).reshape(64, P, 4096)
    cw, cbias = f(inp["conv_w"])[0], f(inp["conv_b"])[0]
    conv = np.concatenate([cw.reshape(4, 8, P), cbias.reshape(1, 8, P)], axis=0)
    shared["conv_l"] = np.ascontiguousarray(conv.transpose(2, 1, 0)).reshape(P, 40)
    wa, wx = f(inp["lru_wa"])[0], f(inp["lru_wx"])[0]
    lw = np.stack([wa, wx], axis=0)
    shared["lru_w"] = np.ascontiguousarray(lw.transpose(3, 0, 1, 2, 4)).reshape(P, 4096)
    lv = np.stack([f(inp["lru_ba"])[0], f(inp["lru_bx"])[0], f(inp["lru_lambda"])[0]], axis=0)
    shared["lru_v"] = np.ascontiguousarray(lv.reshape(3, 2, 8, P).transpose(3, 0, 1, 2)).reshape(P, 48)
    sw = f(inp["sgu_w"])[0]
    shared["wsT_l"] = np.ascontiguousarray(sw.transpose(2, 0, 1)).reshape(P, 1024)
    shared["bsR"] = np.ascontiguousarray(np.broadcast_to(f(inp["sgu_b"])[0].reshape(1, 1024), (P, 1024)))
    k = np.stack([f(inp["peer_k1"])[0], f(inp["peer_k2"])[0]], axis=0)
    shared["kT_l"] = np.ascontiguousarray(k.transpose(3, 0, 1, 2)).reshape(P, 2048)
    maps = []
    L = 4 * NT
    for core in range(8):
        b, q = core // 4, core % 4
        m = dict(shared)
        m["x_own"] = np.ascontiguousarray(x[b, q * NT:(q + 1) * NT])
        m["x_b"] = np.ascontiguousarray(x[b])
        halo = np.zeros((P, D), np.float32)
        if q > 0:
            halo[0:2] = x[b, q * NT - 2:q * NT]
        if q < 3:
            halo[2] = x[b, (q + 1) * NT]
        m["x_halo"] = halo
        m["ctx_b"] = np.ascontiguousarray(ctx[b])
        m["s_in"] = np.ascontiguousarray(np.stack([_fm(c[b], 16), _fm(c_ctx, 16)], axis=2)).reshape(P, 32)
        sel = np.zeros((P, 12), np.float32)
        sel[:, q] = 1.0
        sel[:, 4 + q] = 1.0
        sel[:, 8] = sel[:, 9] = 1.0 if q > 0 else 0.0
        sel[:, 10] = 1.0 if q < 3 else 0.0
        m["sel"] = sel
        maps.append(m)
    return maps


_NC_CACHE = {}


def run(inp, NT, dbg=None):
    key = (NT, dbg)
    if key not in _NC_CACHE:
        _NC_CACHE[key] = K(NT, dbg).build()
    nc = _NC_CACHE[key]
    maps = prepare_inputs(inp, NT)
    res = run_bass_kernel_spmd(nc, maps, core_ids=list(range(8)))
    B = 2
    out = np.zeros((B, 4 * NT, D), np.float32)
    for core in range(8):
        b, q = core // 4, core % 4
        out[b, q * NT:(q + 1) * NT] = res.results[core]["out"]
    return out, res


def kernel(**inputs):
    import os
    NT = np.asarray(inputs["x"]).shape[1] // 4
    if os.environ.get("KPROBE_NT"):
        NT2 = int(os.environ["KPROBE_NT"])
        inp = dict(inputs)
        inp["x"] = np.ascontiguousarray(np.asarray(inputs["x"])[:, :4 * NT2])
        o, _ = run(inp, NT2)
        out = np.zeros((2, 4 * NT, D), np.float32)
        out[:, :4 * NT2] = o
        return out
    out, _ = run(inputs, NT)
    return out
```

```python
import numpy as np
from contextlib import ExitStack
import concourse.bass as bass
import concourse.mybir as mybir
from concourse.bass_utils import run_bass_kernel_spmd

F32 = mybir.dt.float32
BF16 = mybir.dt.bfloat16
AF = mybir.ActivationFunctionType
ALU = mybir.AluOpType
AX = mybir.AxisListType

P = 128
D = 2048
KC = 16
T = 512
EPS = 1e-6
NEG = -1.0e30


class Buf:
    __slots__ = ("name", "last_w", "readers", "dsem", "dcount")

    def __init__(self, name):
        self.name = name
        self.last_w = None
        self.readers = {}
        self.dsem = None
        self.dcount = 0


class Sched:
    ENG = ("pe", "act", "dve", "pool", "sp")
    COMPUTE = ("pe", "act", "dve", "pool")

    def __init__(self, nc, stack):
        self.nc = nc
        self.stack = stack
        self.e = {}
        for n in self.ENG:
            sem = stack.enter_context(nc.semaphore("s_" + n))
            self.e[n] = dict(sem=sem, count=0, ops=[], waited={})
        self.nb = 0

    def buf(self, name, dma=False):
        self.nb += 1
        b = Buf("%s_%d" % (name, self.nb))
        if dma:
            b.dsem = self.stack.enter_context(self.nc.semaphore("d%d" % self.nb))
        return b

    def _collect(self, eng, reads, writes):
        need = {}

        def add(ev, raw):
            if ev is None:
                return
            key, sem, val = ev
            if key == eng and (eng == "pe" or not raw):
                return
            if key not in need or need[key][1] < val:
                need[key] = (sem, val)

        for b in reads:
            if b.last_w:
                for ev in b.last_w.values():
                    add(ev, True)
        for b in writes:
            if b.last_w:
                for ev in b.last_w.values():
                    add(ev, False)
            for ev in b.readers.values():
                add(ev, False)
        E = self.e[eng]
        waits = []
        for key, (sem, val) in need.items():
            if E["waited"].get(key, 0) >= val:
                continue
            E["waited"][key] = val
            waits.append((key, sem, val))
        return waits

    def _update(self, ev, reads, writes):
        key = ev[0]
        for b in reads:
            old = b.readers.get(key)
            if old is None or old[2] < ev[2]:
                b.readers[key] = ev
        for b in writes:
            if b.last_w is None:
                b.last_w = {}
            b.last_w[key] = ev
            b.readers = {}

    def op(self, eng, fn, reads=(), writes=(), attach=True):
        E = self.e[eng]
        waits = self._collect(eng, reads, writes)
        E["count"] += 1
        ev = (eng, E["sem"], E["count"])
        E["ops"].append(dict(waits=waits, fn=fn, kind="op", idx=E["count"], attach=attach and eng != "pe"))
        self._update(ev, reads, writes)
        return ev

    def dma(self, fn, owner, reads=(), writes=(), queue="sp"):
        Q = self.e[queue]
        waits = self._collect(queue, reads, writes)
        owner.dcount += 16
        ev = ("d_" + owner.name, owner.dsem, owner.dcount)
        Q["ops"].append(dict(waits=waits, fn=fn, kind="dma", inc=(owner.dsem, 16), attach=True))
        self._update(ev, reads, writes)
        return ev

    def final_wait(self, eng, bufs):
        waits = self._collect(eng, bufs, bufs)
        self.e[eng]["ops"].append(dict(waits=waits, fn=None, kind="wait", attach=False))

    def emit(self):
        nc = self.nc
        waited = {n: set() for n in self.COMPUTE}
        for n in self.ENG:
            for o in self.e[n]["ops"]:
                for key, sem, val in o["waits"]:
                    if key in waited:
                        waited[key].add(val)
        rank = {}
        for n in self.COMPUTE:
            rank[n] = {v: i + 1 for i, v in enumerate(sorted(waited[n]))}
        with nc.Block() as block:
            def run(name):
                def body(e):
                    for o in self.e[name]["ops"]:
                        ws = []
                        for key, sem, val in o["waits"]:
                            ws.append((sem, rank[key][val] if key in rank else val))
                        fn = o["fn"]
                        att = None
                        if fn is not None and o["attach"] and ws:
                            att = ws.pop()
                        for sem, val in ws:
                            e.wait_ge(sem, val)
                        if fn is None:
                            continue
                        r = fn(e)
                        first, last = r if isinstance(r, tuple) else (r, r)
                        if att is not None:
                            first._wait_ge(att[0], att[1])
                        if o["kind"] == "dma":
                            last.then_inc(o["inc"][0], o["inc"][1])
                        elif o["idx"] in waited[name]:
                            last.then_inc(self.e[name]["sem"], 1)
                return body
            block.tensor(run("pe"))
            block.scalar(run("act"))
            block.vector(run("dve"))
            block.gpsimd(run("pool"))
            block.sync(run("sp"))


class K:
    def __init__(self, NT, dbg=None):
        self.NT = NT
        self.NTL = NT // T
        self.dbg = dbg

    def act(self, out, in_, func, reads, writes, bias=None, scale=None, accum=None):
        kw = {}
        if bias is not None:
            kw["bias"] = bias
        if scale is not None:
            kw["scale"] = scale
        if accum is not None:
            kw["accum_out"] = accum
        return self.S.op("act", lambda e: e.activation(out=out, in_=in_, func=func, **kw), reads, writes,
                         attach=(accum is None))

    def tt(self, eng, out, in0, in1, op, reads, writes):
        return self.S.op(eng, lambda e: e.tensor_tensor(out=out, in0=in0, in1=in1, op=op), reads, writes)

    def ts(self, eng, out, in0, s1, s2, op0, op1, reads, writes):
        if op1 is None:
            return self.S.op(eng, lambda e: e.tensor_scalar(out=out, in0=in0, scalar1=s1, scalar2=None, op0=op0), reads, writes)
        return self.S.op(eng, lambda e: e.tensor_scalar(out=out, in0=in0, scalar1=s1, scalar2=s2, op0=op0, op1=op1), reads, writes)

    def stt(self, out, in0, scalar, in1, op0, op1, reads, writes):
        return self.S.op("dve", lambda e: e.scalar_tensor_tensor(out=out, in0=in0, scalar=scalar, in1=in1, op0=op0, op1=op1), reads, writes)

    def cp(self, eng, out, in_, reads, writes):
        if eng == "act":
            return self.S.op("act", lambda e: e.activation(out=out, in_=in_, func=AF.Copy), reads, writes)
        return self.S.op(eng, lambda e: e.tensor_copy(out=out, in_=in_), reads, writes)

    def mm(self, out, pairs, reads, writes):
        def fn(e):
            n = len(pairs)
            ins = None
            for i, (l, r) in enumerate(pairs):
                ins = e.matmul(out, l, r, start=(i == 0), stop=(i == n - 1))
            return ins
        return self.S.op("pe", fn, reads, writes)

    def mm_multi(self, groups, reads, writes):
        def fn(e):
            ins = None
            for out, pairs in groups:
                n = len(pairs)
                for i, (l, r) in enumerate(pairs):
                    ins = e.matmul(out, l, r, start=(i == 0), stop=(i == n - 1))
            return ins
        return self.S.op("pe", fn, reads, writes)

    def trs(self, items, reads, writes):
        ident = self.IDENT

        def fn(e):
            ins = None
            for out, in_ in items:
                ins = e.transpose(out=out, in_=in_, identity=ident[:])
            return ins
        return self.S.op("pe", fn, reads, writes)

    def load(self, out, in_, owner, reads=(), writes=None):
        if writes is None:
            writes = [owner]
        return self.S.dma(lambda e: e.dma_start(out=out, in_=in_, allow_slow_non_contiguous=True), owner, reads, writes)

    def store(self, out, in_, owner, reads, writes=()):
        return self.S.dma(lambda e: e.dma_start(out=out, in_=in_, allow_slow_non_contiguous=True), owner, reads, writes)

    def build(self):
        NT, NTL = self.NT, self.NTL
        nc = bass.Bass("TRN2", target_bir_lowering=False)
        self.nc = nc

        def din(name, shape, dt=F32):
            return nc.dram_tensor(name, list(shape), dt, kind="ExternalInput").ap()

        def dscr(name, shape, dt):
            return nc.dram_tensor(name, list(shape), dt, kind="Internal").ap()

        I = {}
        I["x_own"] = din("x_own", [NT, D])
        I["x_b"] = din("x_b", [4 * NT, D])
        I["x_halo"] = din("x_halo", [P, D])
        I["ctx_b"] = din("ctx_b", [256, D])
        I["s_in"] = din("s_in", [P, 32])
        I["sel"] = din("sel", [P, 12])
        I["w_mod_l"] = din("w_mod_l", [24, P, 8192])
        I["bmodF"] = din("bmodF", [P, 96])
        I["bmodR"] = din("bmodR", [P, 4096])
        I["gF"] = din("gF", [P, 32])
        I["fgR"] = din("fgR", [P, D])
        I["w_in_l"] = din("w_in_l", [8, P, 8192])
        I["w_out_l"] = din("w_out_l", [4, P, 8192])
        I["wq_l"] = din("wq_l", [4, P, 8192])
        I["uT_l"] = din("uT_l", [32, P, 8192])
        I["v_l"] = din("v_l", [64, P, 4096])
        I["conv_l"] = din("conv_l", [P, 40])
        I["lru_w"] = din("lru_w", [P, 4096])
        I["lru_v"] = din("lru_v", [P, 48])
        I["wsT_l"] = din("wsT_l", [P, 1024])
        I["bsR"] = din("bsR", [P, 1024])
        I["kT_l"] = din("kT_l", [P, 2048])
        self.I = I
        OUT = nc.dram_tensor("out", [NT, D], F32, kind="ExternalOutput").ap()
        self.OUT = OUT
        if self.dbg:
            self.DBG = nc.dram_tensor("dbg", [P, 4096], F32, kind="ExternalOutput").ap()
        Sc = {}
        Sc["w_in_b"] = dscr("w_in_b", [8, P, 8192], BF16)
        Sc["w_out_b"] = dscr("w_out_b", [4, P, 8192], BF16)
        Sc["wq_b"] = dscr("wq_b", [4, P, 8192], BF16)
        Sc["uT_b"] = dscr("uT_b", [32, P, 8192], BF16)
        Sc["v_b"] = dscr("v_b", [64, P, 4096], BF16)
        Sc["xc_st"] = dscr("xc_st", [P, 8, NT], F32)
        Sc["hf_st"] = dscr("hf_st", [P, 8, NT], F32)
        Sc["x1_st"] = dscr("x1_st", [NT, D], F32)
        Sc["g1row"] = dscr("g1row", [P, D], F32)
        Sc["g2row"] = dscr("g2row", [P, D], F32)
        self.Sc = Sc

        with ExitStack() as st:
            S = Sched(nc, st)
            self.S = S

            def sb(name, shape, dt):
                return st.enter_context(nc.sbuf_tensor(name, list(shape), dt))

            self.FSA = sb("FSA", [P, 10 * 2048], F32)
            self.fsb = [S.buf("fs%d" % i, dma=True) for i in range(10)]
            self.HNT = sb("HNT", [P, KC, T], BF16)
            self.hntb = [S.buf("hnt%d" % i) for i in range(4)]
            self.YQ = sb("YQ", [P, KC, T], BF16)
            self.yqb = S.buf("yq")
            self.UG = sb("UG", [P, 8, T], BF16)
            self.ugb = S.buf("ug")
            self.VGN = sb("VGN", [P, 4, 1024], BF16)
            self.vgnb = S.buf("vgn")
            self.XCB = sb("XCB", [P, 4, T], BF16)
            self.xcbb = S.buf("xcb")
            self.XN = sb("XN", [P, 2048], BF16)
            self.xnb = S.buf("xn")
            self.WB = sb("WB", [P, 3, 8192], BF16)
            self.wbb = [S.buf("wb%d" % i, dma=True) for i in range(3)]
            self.vhb = [S.buf("vh0", dma=True), S.buf("vh1", dma=True)]
            self.cva = [S.buf("cva0", dma=True), S.buf("cva1", dma=True)]
            self.cvb = [S.buf("cvb%d" % i, dma=True) for i in range(4)]
            self.GT = sb("GT", [P, 2, T], BF16)
            self.gtb = [S.buf("gt0"), S.buf("gt1")]
            self.IDENT = sb("IDENT", [P, P], BF16)
            self.SMW = sb("SMW", [P, 4096], BF16)
            self.smwb = S.buf("smw")
            self.WST = sb("WST", [P, 1024], BF16)
            self.CST = sb("CST", [P, 512], F32)
            self.cstb = S.buf("cst", dma=True)
            self.SMALL = sb("SMALL", [P, 640], F32)
            self.smallb = S.buf("small")
            self.EPSC = sb("EPSC", [P, 2], F32)
            self.STT = sb("STT", [P, 128], F32)
            self.sttb = S.buf("stt")
            self.PF = [st.enter_context(nc.psum_tensor("PF%d" % i, [P, 512], F32)) for i in range(6)]
            self.pfb = [S.buf("pf%d" % i) for i in range(6)]
            self.PB = [st.enter_context(nc.psum_tensor("PB%d" % i, [P, 1024], BF16)) for i in range(2)]
            self.pbb = [S.buf("pb%d" % i) for i in range(2)]
            self.pfi = 0
            self.pbi = 0
            self.outb = S.buf("outd", dma=True)
            self.scrb = {k: S.buf("scr_" + k, dma=True) for k in Sc}

            self.phase_consts()
            self.phase_mods()
            self.phase_convert()
            self.phase_chains()
            self.phase_own_f()
            self.phase_own_r()

            allb = self.fsb + self.wbb + [self.outb, self.cstb] + list(self.scrb.values())
            S.final_wait("sp", allb)
            S.emit()
        return nc

    def FS(self, i, n=2048, off=0):
        return self.FSA[:, i * 2048 + off: i * 2048 + off + n]

    def FS3(self, i, a, b, off=0):
        return self.FSA[:, i * 2048 + off: i * 2048 + off + a * b].rearrange("p (a b) -> p a b", a=a)

    def next_pf(self):
        i = self.pfi
        self.pfi = (self.pfi + 1) % 6
        return self.PF[i], self.pfb[i]

    def next_pb(self):
        i = self.pbi
        self.pbi = (self.pbi + 1) % 2
        return self.PB[i], self.pbb[i]

    def wb3(self, s):
        return self.WB[:, s, :].rearrange("p (k n) -> p k n", k=KC)

    C_CONV = 0
    C_LV = 40
    C_CL = 88
    C_CL2 = 104
    C_SEL = 120
    C_GF = 132
    C_SIN = 164
    C_MODF = 196
    C_A1 = 324
    C_B1 = 356
    C_A2 = 388
    C_B2 = 404
    C_BMF = 420
    C_END = 484

    def C(self, off, n):
        return self.CST[:, off:off + n]

    def phase_consts(self):
        S, I = self.S, self.I
        cb = self.cstb
        S.op("dve", lambda e: e.memset(self.EPSC[:, :], EPS), [], [self.smallb])
        idf = self.FS(0, 128)
        S.op("pool", lambda e: e.iota(idf, pattern=[[1, 128]], base=0, channel_multiplier=-1,
                                      allow_small_or_imprecise_dtypes=True), [], [self.fsb[0]])
        S.op("dve", lambda e: e.tensor_single_scalar(out=self.IDENT[:], in_=idf, scalar=0.0, op=ALU.is_equal),
             [self.fsb[0]], [self.smwb])
        self.load(self.C(self.C_CONV, 40), I["conv_l"][:, :], cb)
        self.load(self.C(self.C_LV, 48), I["lru_v"][:, :], cb)
        self.load(self.C(self.C_SEL, 12), I["sel"][:, :], cb)
        self.load(self.C(self.C_GF, 32), I["gF"][:, :], cb)
        self.load(self.C(self.C_SIN, 32), I["s_in"][:, :], cb)
        for jj, j in enumerate((0, 1, 3, 4)):
            self.load(self.C(self.C_BMF + 16 * jj, 16), I["bmodF"][:, 16 * j:16 * j + 16], cb)
        lam = self.C(self.C_LV + 32, 16)
        cl = self.C(self.C_CL, 16)
        cl2 = self.C(self.C_CL2, 16)
        self.act(cl, lam, AF.Exp, [cb], [cb], scale=-1.0)
        self.act(cl, cl, AF.Ln, [cb], [cb], bias=1.0)
        self.ts("dve", cl2, cl, -16.0, None, ALU.mult, None, [cb], [cb])
        self.ts("dve", cl, cl, -8.0, None, ALU.mult, None, [cb], [cb])
        sin = self.C(self.C_SIN, 32)
        self.act(sin, sin, AF.Silu, [cb], [cb])
        self.load(self.FS(0, 4096), I["lru_w"][:, :], self.fsb[0], writes=[self.fsb[0], self.fsb[1]])
        self.cp("dve", self.SMW[:, :], self.FS(0, 4096), [self.fsb[0], self.fsb[1]], [self.smwb])
        self.load(self.FS(2, 1024), I["wsT_l"][:, :], self.fsb[2])
        self.cp("dve", self.WST[:, :], self.FS(2, 1024), [self.fsb[2]], [self.smwb])

    def phase_mods(self):
        S, I, Sc = self.S, self.I, self.Sc
        cb = self.cstb
        sin3 = self.C(self.C_SIN, 32).rearrange("p (k j) -> p k j", k=KC)
        srep = self.FS3(8, KC, 128)
        self.cp("dve", srep, sin3[:, :, 0:1].to_broadcast([P, KC, 128]), [cb], [self.fsb[8]])
        modf = self.C(self.C_MODF, 128).rearrange("p (j k c) -> p j k c", j=4, k=KC)
        bmf = self.C(self.C_BMF, 64).rearrange("p (j k) -> p j k", j=4)
        fm = {0: 0, 1: 1, 3: 2, 4: 3}
        rowdst = {2: ("g1row", Sc["g1row"]), 5: ("g2row", Sc["g2row"])}
        rb = 0
        for pi in range(24):
            j = pi // 4
            s0 = 4 * (pi % 2)
            bufs = self.fsb[s0:s0 + 4]
            pan = self.FSA[:, s0 * 2048:(s0 + 4) * 2048].rearrange("p (k n) -> p k n", k=KC)
            self.load(self.FSA[:, s0 * 2048:(s0 + 4) * 2048], I["w_mod_l"][pi], bufs[0], writes=bufs)
            if j in fm:
                pf, pfb = self.next_pf()
                groups = []
                for cc in range(4):
                    groups.append((pf[:, 2 * cc:2 * cc + 2],
                                   [(pan[:, kc, cc * 128:(cc + 1) * 128], sin3[:, kc, :]) for kc in range(KC)]))
                self.mm_multi(groups, bufs + [cb], [pfb])
                jj = fm[j]
                kc0 = 4 * (pi % 4)
                self.tt("dve", modf[:, jj, kc0:kc0 + 4, :], pf[:, 0:8].rearrange("p (c j) -> p c j", c=4),
                        bmf[:, jj, kc0:kc0 + 4].unsqueeze(2).to_broadcast([P, 4, 2]), ALU.add, [pfb, cb], [cb])
            else:
                pf, pfb = self.next_pf()
                self.mm(pf[:, :], [(srep[:, kc, :], pan[:, kc, :]) for kc in range(KC)], bufs + [self.fsb[8]], [pfb])
                ds = pi % 4
                which = 0 if j == 2 else 1
                bm = self.FS(9, 512, off=512 * (rb % 2))
                rt = self.FS(9, 512, off=1024 + 512 * (rb % 2))
                rb += 1
                self.load(bm, I["bmodR"][:, which * 2048 + ds * 512: which * 2048 + ds * 512 + 512], self.fsb[9])
                self.tt("dve", rt, pf[:, :], bm, ALU.add, [pfb, self.fsb[9]], [self.fsb[9]])
                nm, dst = rowdst[j]
                self.store(dst[:, ds * 512:(ds + 1) * 512], rt, self.scrb[nm], [self.fsb[9]], [self.scrb[nm]])
        gf = self.C(self.C_GF, 32)
        a1 = self.C(self.C_A1, 32).rearrange("p (k c) -> p k c", k=KC)
        b1 = self.C(self.C_B1, 32).rearrange("p (k c) -> p k c", k=KC)
        self.ts("dve", a1, modf[:, 1, :, :], 1.0, None, ALU.add, None, [cb], [cb])
        self.tt("dve", a1, a1, gf[:, 0:16].unsqueeze(2).to_broadcast([P, KC, 2]), ALU.mult, [cb], [cb])
        self.cp("dve", b1, modf[:, 0, :, :], [cb], [cb])
        a2 = self.C(self.C_A2, 16)
        b2 = self.C(self.C_B2, 16)
        self.ts("dve", a2, modf[:, 3, :, 0], 1.0, None, ALU.add, None, [cb], [cb])
        self.tt("dve", a2, a2, gf[:, 16:32], ALU.mult, [cb], [cb])
        self.cp("dve", b2, modf[:, 2, :, 0], [cb], [cb])

    def phase_convert(self):
        I, Sc = self.I, self.Sc
        jobs = []
        self.cvjobs = []
        for nm_s, nm_d, n, F in (("uT_l", "uT_b", 32, 8192), ("v_l", "v_b", 64, 4096)):
            for i in range(n):
                for h in range(F // 1024):
                    self.cvjobs.append((I[nm_s][i][:, h * 1024:(h + 1) * 1024], Sc[nm_d][i][:, h * 1024:(h + 1) * 1024], nm_d))
        self.cvk = 0
        self.cvjobs = []
        for nm_s, nm_d, n, F in (("w_in_l", "w_in_b", 8, 8192), ("w_out_l", "w_out_b", 4, 8192),
                                 ("wq_l", "wq_b", 4, 8192), ("uT_l", "uT_b", 32, 8192), ("v_l", "v_b", 64, 4096)):
            for i in range(n):
                for h in range(F // 4096):
                    jobs.append((I[nm_s][i][:, h * 4096:(h + 1) * 4096], Sc[nm_d][i][:, h * 4096:(h + 1) * 4096], nm_d))
        for ji, (src, dst, nm) in enumerate(jobs):
            s0 = 2 * (ji % 3)
            bufs = self.fsb[s0:s0 + 2]
            w = ji % 3
            half = self.WB[:, w, 0:4096]
            self.load(self.FSA[:, s0 * 2048:(s0 + 2) * 2048], src, bufs[0], writes=bufs)
            eng = "dve" if ji % 2 == 0 else "act"
            self.cp(eng, half, self.FSA[:, s0 * 2048:(s0 + 2) * 2048], bufs, [self.wbb[w]])
            self.store(dst, half, self.wbb[w], [self.wbb[w]], [self.scrb[nm]])

    def cv_emit(self, n):
        ugf = self.UG[:, :, :].rearrange("p a n -> p (a n)")
        for _ in range(n):
            if self.cvk >= len(self.cvjobs):
                return
            k = self.cvk
            self.cvk += 1
            src_, dst_, nm = self.cvjobs[k]
            a = k % 2
            b = k % 4
            st = self.FS(9, 1024, off=1024 * a)
            ob = ugf[:, b * 1024:(b + 1) * 1024]
            self.load(st, src_, self.cva[a])
            self.cp("pool", ob, st, [self.cva[a]], [self.cvb[b]])
            self.store(dst_, ob, self.cvb[b], [self.cvb[b]], [self.scrb[nm]])

    def norm_tt(self, xs, xsb, tt, A, B, areads, hn=None):
        HN, hnb = hn if hn is not None else (self.HNT, self.hntb)
        sm = self.smallb
        ssq = self.SMALL[:, 0:1]
        rstd = self.SMALL[:, 1:2]
        self.act(self.XN[:, :], xs, AF.Square, [xsb], [self.xnb, sm], accum=ssq)
        self.act(rstd, ssq, AF.Ln, [sm], [sm], scale=1.0 / D, bias=self.EPSC[:, 0:1])
        self.act(rstd, rstd, AF.Exp, [sm], [sm], scale=-0.5)
        self.act(self.XN[:, :], xs, AF.Copy, [xsb, sm], [self.xnb], scale=rstd)
        for half in range(2):
            pb, pbb = self.next_pb()
            self.trs([(pb[:, i * 128:(i + 1) * 128], self.XN[:, (half * 8 + i) * 128:(half * 8 + i + 1) * 128])
                      for i in range(8)], [self.xnb], [pbb])
            dst = HN[:, half * 8:half * 8 + 8, tt * 128:(tt + 1) * 128]
            pv = pb[:, :].rearrange("p (k n) -> p k n", k=8)
            self.tt("dve", dst, pv, A[:, half * 8:half * 8 + 8].unsqueeze(2).to_broadcast([P, 8, 128]), ALU.mult,
                    [pbb] + areads, [hnb[tt]])
            self.tt("pool", dst, dst, B[:, half * 8:half * 8 + 8].unsqueeze(2).to_broadcast([P, 8, 128]), ALU.add,
                    [hnb[tt]] + areads, [hnb[tt]])

    def norm_rows(self, rows_fn, ntt, A, B, hn=None, tts=None):
        for tt in (range(ntt) if tts is None else tts):
            s = tt % 2
            self.load(self.FS(s), rows_fn(tt), self.fsb[s])
            self.norm_tt(self.FS(s), self.fsb[s], tt, A, B, [self.cstb], hn=hn)

    def xb_half(self, hc, ncur, wxs, mode, d, scan_rng, snap_col=None, snap_idx=None, stash=None, from_stash=None,
                hn=None, filler=None):
        cb, sb_ = self.cstb, self.sttb
        nwin = ncur
        XC = self.FS3(2, 4, T)
        R = self.FS3(3, 4, T)
        Ii = self.FS3(4, 4, T)
        TM = self.FS3(5, 4, T)
        H = self.FS3(6, 4, T)
        fb = self.fsb
        conv = self.C(self.C_CONV, 40)
        if mode is not None:
            EXT = self.FSA[:, 7 * 2048: 7 * 2048 + 4 * 516].rearrange("p (c n) -> p c n", c=4)
            eb = [fb[7], fb[8]]
            sav = self.STT[:, 80:104].rearrange("p (c n) -> p c n", c=8)
            co = 3 if mode == "f" else 0
            for c in range(4):
                oc = 4 * hc + c
                pf, pfb = self.next_pf()
                wt = self.wb3(wxs[oc // 4])
                HN, hnb = hn if hn is not None else (self.HNT, self.hntb)
                self.mm(pf[:, 0:ncur], [(wt[:, kc, (oc % 4) * 128:(oc % 4 + 1) * 128], HN[:, kc, 0:ncur])
                                        for kc in range(KC)], hnb + [self.wbb[wxs[oc // 4]]], [pfb])
                self.cp("act", EXT[:, c, co:co + ncur], pf[:, 0:ncur], [pfb], eb)
            if mode == "f":
                self.cp("dve", EXT[:, :, 0:3], sav[:, 4 * hc:4 * hc + 4, :], [sb_], eb)
                self.cp("dve", sav[:, 4 * hc:4 * hc + 4, :], EXT[:, :, ncur:ncur + 3], eb, [sb_])
            else:
                self.cp("dve", EXT[:, :, ncur:ncur + 3], sav[:, 4 * hc:4 * hc + 4, :], [sb_], eb)
                self.cp("dve", sav[:, 4 * hc:4 * hc + 4, :], EXT[:, :, 0:3], eb, [sb_])
            for c in range(4):
                oc = 4 * hc + c
                w = lambda k: conv[:, oc * 5 + k: oc * 5 + k + 1]
                self.ts("pool", XC[:, c, 0:nwin], EXT[:, c, 0:nwin], w(0), w(4), ALU.mult, ALU.add, eb + [cb], [fb[2]])
                for k in range(1, 4):
                    self.stt(XC[:, c, 0:nwin], EXT[:, c, k:k + nwin], w(k), XC[:, c, 0:nwin], ALU.mult, ALU.add,
                             eb + [cb, fb[2]], [fb[2]])
        else:
            self.load(XC[:, :, :], from_stash, fb[2], reads=[self.scrb["xc_st"]])
        if filler is not None:
            filler()
        return self._xb_gates(hc, nwin, d, scan_rng, snap_col, snap_idx)

    def load_wx(self, tiles, slots, src):
        for tno, s in zip(tiles, slots):
            self.load(self.WB[:, s, :], src[tno], self.wbb[s], reads=[self.scrb["w_in_b"]])

    def zero_sav(self):
        self.S.op("dve", lambda e: e.memset(self.STT[:, 80:104], 0.0), [], [self.sttb])

    def phase_chains(self):
        NT, NTL = self.NT, self.NTL
        I, Sc = self.I, self.Sc
        a1 = self.C(self.C_A1, 32).rearrange("p (k c) -> p k c", k=KC)
        b1 = self.C(self.C_B1, 32).rearrange("p (k c) -> p k c", k=KC)
        self.load_wx([0, 1], [0, 1], Sc["w_in_b"])
        self.S.op("dve", lambda e: e.memset(self.STT[:, :], 0.0), [], [self.sttb])
        self.S.op("pool", lambda e: e.memset(self.HNT[:, :, 256:384], 0.0), [], self.hntb)
        for d in (0, 1):
            self.zero_sav()
            self.norm_rows(lambda tt: I["ctx_b"][tt * 128:(tt + 1) * 128, :], 2, a1[:, :, 1], b1[:, :, 1])
            self.S.op("pool", lambda e: e.memset(self.HNT[:, :, 256:257], 0.0), [], self.hntb)
            for hc in (0, 1):
                self.xb_half(hc, 257, [0, 1], "f", d, (1, 257))
            base = 16 if d == 0 else 48
            idx = 0 if d == 0 else 3
            self.cp("dve", self.STT[:, base + 8 * idx: base + 8 * idx + 8], self.STT[:, 8 * d:8 * d + 8],
                    [self.sttb], [self.sttb])
        sel = self.C(self.C_SEL, 12)
        self.norm_rows(lambda tt: I["x_halo"][:, :], 1, a1[:, :, 0], b1[:, :, 0])
        sav = self.STT[:, 80:104].rearrange("p (c n) -> p c n", c=8)
        hal = self.SMALL[:, 16:48].rearrange("p (c n) -> p c n", c=8)
        for oc in range(8):
            pf, pfb = self.next_pf()
            wt = self.wb3(oc // 4)
            self.mm(pf[:, 0:128], [(wt[:, kc, (oc % 4) * 128:(oc % 4 + 1) * 128], self.HNT[:, kc, 0:128])
                                   for kc in range(KC)], self.hntb + [self.wbb[oc // 4]], [pfb])
            self.tt("dve", hal[:, oc, :], pf[:, 0:4], sel[:, 8:12], ALU.mult, [pfb, self.cstb], [self.smallb])

        items = []
        for j in range(3 * NTL + 1):
            if j == 0:
                rng = (1, T)
            elif j == 3 * NTL:
                rng = (0, 1)
            else:
                rng = (0, T)
            snap = j // NTL if (j > 0 and j % NTL == 0) else None
            items.append(dict(rows=(lambda tt, j=j: I["x_b"][j * T + tt * 128: j * T + (tt + 1) * 128, :]),
                              mode="f", d=0, rng=rng, snap_col=0, snap_idx=snap,
                              pre=(self.zero_sav if j == 0 else None), post=None))
        for j in range(4 * NTL - 1, NTL - 2, -1):
            if j == 4 * NTL - 1:
                rng = (0, T - 2)
            elif j == NTL - 1:
                rng = (T - 2, T)
            else:
                rng = (0, T)
            snap = None
            if (j + 1) % NTL == 0 and j != 4 * NTL - 1:
                snap = (j + 1) // NTL - 1
            items.append(dict(rows=(lambda tt, j=j: I["x_b"][j * T + tt * 128: j * T + (tt + 1) * 128, :]),
                              mode="b", d=1, rng=rng, snap_col=T - 2, snap_idx=snap,
                              pre=(self.zero_sav if j == 4 * NTL - 1 else None), post=None))

        def own_pre():
            for d in (0, 1):
                base = 16 if d == 0 else 48
                stt = self.STT[:, 8 * d:8 * d + 8]
                self.ts("dve", stt, self.STT[:, base:base + 8], sel[:, 4 * d:4 * d + 1], None, ALU.mult, None,
                        [self.sttb, self.cstb], [self.sttb])
                for i in range(1, 4):
                    self.stt(stt, self.STT[:, base + 8 * i: base + 8 * i + 8], sel[:, 4 * d + i:4 * d + i + 1], stt,
                             ALU.mult, ALU.add, [self.sttb, self.cstb], [self.sttb])
            self.S.op("dve", lambda e: e.memset(self.STT[:, 80:104], 0.0), [], [self.sttb])
            self.cp("dve", sav[:, :, 1:3], hal[:, :, 0:2], [self.smallb], [self.sttb])

        def own_post(j):
            def f(hc, XC, H, rng):
                lo, hi = rng
                p0 = j * T - 1
                self.store(Sc["xc_st"][:, 4 * hc:4 * hc + 4, p0 + lo:p0 + hi], XC[:, :, lo:hi], self.fsb[2],
                           [self.fsb[2]], [self.scrb["xc_st"]])
                self.store(Sc["hf_st"][:, 4 * hc:4 * hc + 4, p0 + lo:p0 + hi], H[:, :, lo:hi], self.fsb[6],
                           [self.fsb[6]], [self.scrb["hf_st"]])
            return f

        for j in range(NTL):
            rng = (1, T) if j == 0 else (0, T)
            items.append(dict(rows=(lambda tt, j=j: I["x_own"][j * T + tt * 128: j * T + (tt + 1) * 128, :]),
                              mode="f", d=0, rng=rng, snap_col=None, snap_idx=None,
                              pre=(own_pre if j == 0 else None), post=own_post(j)))

        hntb2 = [self.S.buf("hn2_%d" % i) for i in range(4)]
        self.S.op("dve", lambda e: e.memset(self.EPSC[:, 1:2], 0.0), [], [self.yqb] + hntb2)
        self.S.op("pool", lambda e: e.memset(self.SMALL[:, 600:601], 0.0), [], [self.fsb[9], self.ugb] + self.cva + self.cvb)
        nhalf = 2 * (len(items))
        per = (len(self.cvjobs) + nhalf - 3) // (nhalf - 2) + 1
        HNS = [(self.HNT, self.hntb), (self.YQ, hntb2)]
        al, bl = a1[:, :, 0], b1[:, :, 0]
        self.norm_rows(items[0]["rows"], 4, al, bl, hn=HNS[0])
        for k, it in enumerate(items):
            hn = HNS[k % 2]
            nxt = items[k + 1] if k + 1 < len(items) else None
            if it["pre"] is not None:
                it["pre"]()
            for hc in (0, 1):
                filler = None
                if nxt is not None:
                    filler = (lambda nxt=nxt, hc=hc, k=k: (self.norm_rows(
                        nxt["rows"], 4, al, bl, hn=HNS[(k + 1) % 2], tts=(2 * hc, 2 * hc + 1)), self.cv_emit(per)))
                XC, H = self.xb_half(hc, T, [0, 1], it["mode"], it["d"], it["rng"], snap_col=it["snap_col"],
                                     snap_idx=it["snap_idx"], hn=hn, filler=filler)
                if it["post"] is not None:
                    it["post"](hc, XC, H, it["rng"])
        for hc in (0, 1):
            XC, H = self.xb_half_virtual(hc, hal, (0, 1))
            own_post(NTL)(hc, XC, H, (0, 1))
        self.cv_emit(len(self.cvjobs))
        self.S.op("dve", lambda e: e.memset(self.EPSC[:, 1:2], 0.0), [], [self.yqb] + hntb2)
        self.S.op("pool", lambda e: e.memset(self.SMALL[:, 600:601], 0.0), [], [self.fsb[9], self.ugb] + self.cva + self.cvb)

    def phase_own_f(self):
        return

    def xb_half_virtual(self, hc, hal, rng):
        cb, sb_ = self.cstb, self.sttb
        fb = self.fsb
        EXT = self.FSA[:, 7 * 2048: 7 * 2048 + 4 * 516].rearrange("p (c n) -> p c n", c=4)
        eb = [fb[7], fb[8]]
        sav = self.STT[:, 80:104].rearrange("p (c n) -> p c n", c=8)
        XC = self.FS3(2, 4, T)
        conv = self.C(self.C_CONV, 40)
        n = 8
        self.S.op("dve", lambda e: e.memset(EXT[:, :, 0:16], 0.0), [], eb)
        self.cp("dve", EXT[:, :, 0:3], sav[:, 4 * hc:4 * hc + 4, :], [sb_], eb)
        self.cp("dve", EXT[:, :, 3:4], hal[:, 4 * hc:4 * hc + 4, 2:3], [self.smallb], eb)
        for c in range(4):
            oc = 4 * hc + c
            w = lambda k: conv[:, oc * 5 + k: oc * 5 + k + 1]
            self.ts("pool", XC[:, c, 0:n], EXT[:, c, 0:n], w(0), w(4), ALU.mult, ALU.add, eb + [cb], [fb[2]])
            for k in range(1, 4):
                self.stt(XC[:, c, 0:n], EXT[:, c, k:k + n], w(k), XC[:, c, 0:n], ALU.mult, ALU.add,
                         eb + [cb, fb[2]], [fb[2]])
        return self.xb_tail(hc, n, 0, rng)

    def xb_tail(self, hc, nwin, d, scan_rng):
        return self._xb_gates(hc, nwin, d, scan_rng)

    def _xb_gates(self, hc, nwin, d, scan_rng, snap_col=None, snap_idx=None):
        cb, sb_ = self.cstb, self.sttb
        fb = self.fsb
        XC = self.FS3(2, 4, T)
        R = self.FS3(3, 4, T)
        Ii = self.FS3(4, 4, T)
        TM = self.FS3(5, 4, T)
        H = self.FS3(6, 4, T)
        self.cp("act", self.XCB[:, :, 0:nwin], XC[:, :, 0:nwin], [fb[2]], [self.xcbb])
        lw = self.SMW[:, :].rearrange("p (m d h j) -> p m d h j", m=2, d=2, h=8)
        lv = self.C(self.C_LV, 48).rearrange("p (m d h) -> p m d h", m=3, d=2)
        for m, dstt, dbuf in ((0, R, fb[3]), (1, Ii, fb[4])):
            for c in range(4):
                oc = 4 * hc + c
                pf, pfb = self.next_pf()
                self.mm(pf[:, 0:nwin], [(lw[:, m, d, oc, :], self.XCB[:, c, 0:nwin])], [self.xcbb, self.smwb], [pfb])
                self.act(dstt[:, c, 0:nwin], pf[:, 0:nwin], AF.Sigmoid, [pfb, cb], [dbuf], bias=lv[:, m, d, oc:oc + 1])
        cl = self.C(self.C_CL, 16).rearrange("p (d h) -> p d h", d=2)
        cl2 = self.C(self.C_CL2, 16).rearrange("p (d h) -> p d h", d=2)
        for c in range(4):
            oc = 4 * hc + c
            self.act(TM[:, c, 0:nwin], R[:, c, 0:nwin], AF.Exp, [fb[3], cb], [fb[5]], scale=cl2[:, d, oc:oc + 1])
            self.act(R[:, c, 0:nwin], R[:, c, 0:nwin], AF.Exp, [fb[3], cb], [fb[3]], scale=cl[:, d, oc:oc + 1])
        self.act(TM[:, :, 0:nwin], TM[:, :, 0:nwin], AF.Ln, [fb[5]], [fb[5]], scale=-1.0, bias=1.0)
        self.act(TM[:, :, 0:nwin], TM[:, :, 0:nwin], AF.Exp, [fb[5]], [fb[5]], scale=0.5)
        self.tt("dve", Ii[:, :, 0:nwin], Ii[:, :, 0:nwin], TM[:, :, 0:nwin], ALU.mult, [fb[4], fb[5]], [fb[4]])
        self.tt("dve", Ii[:, :, 0:nwin], Ii[:, :, 0:nwin], XC[:, :, 0:nwin], ALU.mult, [fb[4], fb[2]], [fb[4]])
        lo, hi = scan_rng
        stt = self.STT[:, 8 * d:8 * d + 8]
        for c in range(4):
            oc = 4 * hc + c
            if d == 0:
                o, a, b = H[:, c, lo:hi], R[:, c, lo:hi], Ii[:, c, lo:hi]
            else:
                o = H[:, c, lo:hi][:, ::-1]
                a = R[:, c, lo:hi][:, ::-1]
                b = Ii[:, c, lo:hi][:, ::-1]
            ini = stt[:, oc:oc + 1]
            self.S.op("dve", (lambda o=o, a=a, b=b, ini=ini: (lambda e: e.tensor_tensor_scan(
                out=o, data0=a, data1=b, initial=ini, op0=ALU.mult, op1=ALU.add)))(),
                [fb[3], fb[4], sb_], [fb[6]])
        last = hi - 1 if d == 0 else lo
        self.cp("dve", stt[:, 4 * hc:4 * hc + 4], H[:, :, last], [fb[6]], [sb_])
        if snap_idx is not None:
            base = 16 if d == 0 else 48
            sn = self.STT[:, base + 8 * snap_idx + 4 * hc: base + 8 * snap_idx + 4 * hc + 4]
            self.cp("dve", sn, H[:, :, snap_col], [fb[6]], [sb_])
        return XC, H

    def wstream(self, srcs):
        st = {"i": 0, "n": len(srcs), "srcs": srcs, "slot": getattr(self, "_wslot", 0)}
        return st

    def wload(self, src, scr_name, half=False):
        s = getattr(self, "_wslot", 0)
        self._wslot = (s + 1) % 3
        if half:
            self.load(self.WB[:, s, 0:4096], src, self.wbb[s], reads=[self.scrb[scr_name]])
        else:
            self.load(self.WB[:, s, :], src, self.wbb[s], reads=[self.scrb[scr_name]])
        return s

    def phase_own_r(self):
        NT, NTL = self.NT, self.NTL
        I, Sc = self.I, self.Sc
        cb = self.cstb
        fb = self.fsb
        a1 = self.C(self.C_A1, 32).rearrange("p (k c) -> p k c", k=KC)
        b1 = self.C(self.C_B1, 32).rearrange("p (k c) -> p k c", k=KC)
        a2 = self.C(self.C_A2, 16)
        b2 = self.C(self.C_B2, 16)
        self._wslot = 0
        for j in range(NTL - 1, -1, -1):
            t0 = j * T
            self.load(self.FS(0, 4096), I["lru_w"][:, :], fb[0], writes=[fb[0], fb[1]])
            self.cp("dve", self.SMW[:, :], self.FS(0, 4096), [fb[0], fb[1]], [self.smwb])
            self.norm_rows(lambda tt: I["x_own"][t0 + tt * 128: t0 + (tt + 1) * 128, :], 4, a1[:, :, 0], b1[:, :, 0])
            for hc in (0, 1):
                s = self.wload(Sc["w_in_b"][2 + hc], "w_in_b")
                XC = self.FS3(2, 4, T)
                self.load(XC[:, :, :], Sc["xc_st"][:, 4 * hc:4 * hc + 4, t0:t0 + T], fb[2], reads=[self.scrb["xc_st"]])
                XC, H = self._xb_gates(hc, T, 1, (0, T))
                HF = self.FS3(2, 4, T)
                self.load(HF[:, :, :], Sc["hf_st"][:, 4 * hc:4 * hc + 4, t0:t0 + T], fb[2], reads=[self.scrb["hf_st"]])
                self.tt("pool", H[:, :, :], H[:, :, :], HF[:, :, :], ALU.add, [fb[6], fb[2]], [fb[6]])
                wt = self.wb3(s)
                for c in range(4):
                    oc = 4 * hc + c
                    pf, pfb = self.next_pf()
                    self.mm(pf[:, :], [(wt[:, kc, c * 128:(c + 1) * 128], self.HNT[:, kc, :]) for kc in range(KC)],
                            self.hntb + [self.wbb[s]], [pfb])
                    g = c % 2
                    self.act(self.GT[:, g, :], pf[:, :], AF.Gelu_apprx_tanh, [pfb], [self.gtb[g]])
                    self.tt("dve", self.YQ[:, oc, :], self.GT[:, g, :], H[:, c, :], ALU.mult, [self.gtb[g], fb[6]], [self.yqb])
            for hc in (0, 1):
                s = self.wload(Sc["w_in_b"][4 + hc], "w_in_b")
                wt = self.wb3(s)
                for c in range(4):
                    pf, pfb = self.next_pf()
                    self.mm(pf[:, :], [(wt[:, kc, c * 128:(c + 1) * 128], self.HNT[:, kc, :]) for kc in range(KC)],
                            self.hntb + [self.wbb[s]], [pfb])
                    self.act(self.UG[:, 4 * hc + c, :], pf[:, :], AF.Gelu_apprx_tanh, [pfb], [self.ugb])
            sv = [self.wload(Sc["w_in_b"][6], "w_in_b"), self.wload(Sc["w_in_b"][7], "w_in_b")]
            VG = self.FS3(8, 2, 1024)
            SQ = self.FS(5, 1024)
            sm = self.smallb
            for tt in range(4):
                vg = VG[:, tt % 2, :]
                for cs in range(2):
                    wt = self.wb3(sv[cs])
                    pf, pfb = self.next_pf()
                    self.mm(pf[:, :], [(self.HNT[:, kc, tt * 128:(tt + 1) * 128], wt[:, kc, :]) for kc in range(KC)],
                            self.hntb + [self.wbb[sv[cs]]], [pfb])
                    self.act(vg[:, cs * 512:(cs + 1) * 512], pf[:, :], AF.Gelu_apprx_tanh, [pfb], [fb[8]])
                vg3 = vg.rearrange("p (g c) -> p g c", g=8)
                sq3 = SQ.rearrange("p (g c) -> p g c", g=8)
                su = self.SMALL[:, 64:72]
                ss = self.SMALL[:, 72:80]
                mean = self.SMALL[:, 80:88]
                var = self.SMALL[:, 88:96]
                nmr = self.SMALL[:, 96:104]
                self.S.op("dve", (lambda vg3=vg3: lambda e: e.tensor_reduce(out=su, in_=vg3, axis=AX.X, op=ALU.add))(),
                          [fb[8]], [sm])
                self.tt("pool", SQ, vg, vg, ALU.mult, [fb[8]], [fb[5]])
                self.S.op("dve", lambda e: e.tensor_reduce(out=ss, in_=sq3, axis=AX.X, op=ALU.add), [fb[5]], [sm])
                self.ts("dve", mean, su, 1.0 / 128, None, ALU.mult, None, [sm], [sm])
                self.tt("dve", var, mean, mean, ALU.mult, [sm], [sm])
                self.stt(var, ss, 1.0 / 128, var, ALU.mult, ALU.subtract, [sm], [sm])
                self.act(var, var, AF.Ln, [sm], [sm], bias=self.EPSC[:, 0:1])
                self.act(var, var, AF.Exp, [sm], [sm], scale=-0.5)
                self.tt("dve", nmr, mean, var, ALU.mult, [sm], [sm])
                self.tt("dve", sq3, vg3, var.unsqueeze(2).to_broadcast([P, 8, 128]), ALU.mult, [fb[8], sm], [fb[5]])
                self.tt("dve", self.VGN[:, tt, :].rearrange("p (g c) -> p g c", g=8), sq3,
                        nmr.unsqueeze(2).to_broadcast([P, 8, 128]), ALU.subtract, [fb[5], sm], [self.vgnb])
            self.load(self.FS(9, 1024), I["bsR"][:, :], fb[9])
            for g in range(8):
                pf, pfb = self.next_pf()
                groups = [(pf[:, tt * 128:(tt + 1) * 128],
                           [(self.VGN[:, tt, g * 128:(g + 1) * 128], self.WST[:, g * 128:(g + 1) * 128])])
                          for tt in range(4)]
                self.mm_multi(groups, [self.vgnb, self.smwb], [pfb])
                tmp = self.FS(5, 512, off=1024)
                self.tt("dve", tmp.rearrange("p (a b) -> p a b", a=4), pf[:, :].rearrange("p (a b) -> p a b", a=4),
                        self.FS(9, 128, off=g * 128).unsqueeze(1).to_broadcast([P, 4, 128]), ALU.add,
                        [pfb, fb[9]], [fb[5]])
                self.tt("dve", self.YQ[:, 8 + g, :], tmp, self.UG[:, g, :], ALU.mult, [fb[5], self.ugb], [self.yqb])
            self.load(self.FS(7), Sc["g1row"][:, :], fb[7], reads=[self.scrb["g1row"]])
            for tt in range(4):
                s = tt % 2
                xs = self.FS(s)
                self.load(xs, I["x_own"][t0 + tt * 128: t0 + (tt + 1) * 128, :], fb[s])
                for ds in range(4):
                    ws = self.wload(Sc["w_out_b"][ds], "w_out_b")
                    wt = self.wb3(ws)
                    pf, pfb = self.next_pf()
                    self.mm(pf[:, :], [(self.YQ[:, kc, tt * 128:(tt + 1) * 128], wt[:, kc, :]) for kc in range(KC)],
                            [self.yqb, self.wbb[ws]], [pfb])
                    tmp = self.FS(5, 512, off=1536)
                    self.tt("dve", tmp, pf[:, :], self.FS(7, 512, off=ds * 512), ALU.mult, [pfb, fb[7]], [fb[5]])
                    self.tt("pool", xs[:, ds * 512:(ds + 1) * 512], xs[:, ds * 512:(ds + 1) * 512], tmp, ALU.add,
                            [fb[5], fb[s]], [fb[s]])
                self.store(Sc["x1_st"][t0 + tt * 128: t0 + (tt + 1) * 128, :], xs, fb[s], [fb[s]], [self.scrb["x1_st"]])
                self.norm_tt(xs, fb[s], tt, a2, b2, [cb])
            if self.dbg == "x1":
                continue
            self.peer(j)
            self.final(j)
        if self.dbg == "x1":
            self.S.dma(lambda e: e.dma_start(out=self.OUT[:, :], in_=Sc["x1_st"][:, :]), self.outb,
                       [self.scrb["x1_st"]], [self.outb])

    def peer(self, j):
        I, Sc = self.I, self.Sc
        fb = self.fsb
        cb = self.cstb
        sm = self.smallb
        self.load(self.FS(8), I["kT_l"][:, :], fb[8])
        self.cp("dve", self.SMW[:, 0:2048], self.FS(8), [fb[8]], [self.smwb])
        kT = self.SMW[:, 0:2048].rearrange("p (a h e) -> p a h e", a=2, h=8)
        for qt in range(4):
            ws = self.wload(Sc["wq_b"][qt], "wq_b")
            wt = self.wb3(ws)
            for c in range(4):
                pf, pfb = self.next_pf()
                self.mm(pf[:, :], [(wt[:, kc, c * 128:(c + 1) * 128], self.HNT[:, kc, :]) for kc in range(KC)],
                        self.hntb + [self.wbb[ws]], [pfb])
                self.cp("act", self.YQ[:, 4 * qt + c, :], pf[:, :], [pfb], [self.yqb])
        for tt in range(4):
            ssb = self.FSA[:, (4 + tt) * 2048:(5 + tt) * 2048].rearrange("p (h a e) -> p h a e", h=8, a=2)
            for h2 in range(4):
                pf, pfb = self.next_pf()
                groups = []
                for hh in range(2):
                    h = 2 * h2 + hh
                    for a in range(2):
                        groups.append((pf[:, (2 * hh + a) * 128:(2 * hh + a + 1) * 128],
                                       [(self.YQ[:, 2 * h + a, tt * 128:(tt + 1) * 128], kT[:, a, h, :])]))
                self.mm_multi(groups, [self.yqb, self.smwb], [pfb])
                self.cp("act", ssb[:, 2 * h2:2 * h2 + 2, :, :],
                        pf[:, :].rearrange("p (h a e) -> p h a e", h=2, a=2), [pfb], [fb[4 + tt]])
        V1 = self.SMALL[:, 256:272]
        V2 = self.SMALL[:, 272:288]
        C16 = self.SMALL[:, 288:304]
        CALL = self.SMALL[:, 328:456].rearrange("p (h k) -> p h k", h=8)
        CEX = self.SMALL[:, 456:584].rearrange("p (h k) -> p h k", h=8)
        TMPK = self.FS(9, 256, off=3072 // 2)
        CAND = self.FS(9, 256, off=1792)
        TMH = [self.FS(9, 128, off=1024), self.FS(9, 128, off=1152)]
        tkv = [self.S.buf("tkv0"), self.S.buf("tkv1")]
        tkt = [self.S.buf("tkt0"), self.S.buf("tkt1")]
        self.S.op("dve", lambda e: e.memset(self.SMALL[:, 600:601], 0.0), [], [sm, fb[9]] + tkv + tkt)
        for tt in range(4):
            ssb = self.FSA[:, (4 + tt) * 2048:(5 + tt) * 2048].rearrange("p (h a e) -> p h a e", h=8, a=2)
            for h in range(8):
                srcs = [ssb[:, h, 0, :], ssb[:, h, 1, :]]
                Vs = [V1, V2]
                for a in (0, 1):
                    self.S.op("dve", (lambda V=Vs[a], s_=srcs[a]: lambda e: e.max(out=V[:, 0:8], in_=s_))(),
                              [fb[4 + tt]], [tkv[a]])
                for a in (0, 1):
                    self.S.op("dve", (lambda V=Vs[a], s_=srcs[a], tm=TMH[a]: lambda e: e.match_replace(
                        out=tm, in_to_replace=V[:, 0:8], in_values=s_, imm_value=NEG))(),
                        [fb[4 + tt], tkv[a]], [tkt[a]], attach=False)
                for a in (0, 1):
                    self.S.op("dve", (lambda V=Vs[a], tm=TMH[a]: lambda e: e.max(out=V[:, 8:16], in_=tm))(),
                              [tkt[a]], [tkv[a]])
                self.tt("pool", CAND.rearrange("p (a b) -> p a b", a=16), V1.unsqueeze(2).to_broadcast([P, 16, 16]),
                        V2.unsqueeze(1).to_broadcast([P, 16, 16]), ALU.add, tkv, [fb[9]])
                self.S.op("dve", lambda e: e.max(out=C16[:, 0:8], in_=CAND), [fb[9]], [sm])
                self.S.op("dve", lambda e: e.match_replace(out=TMPK, in_to_replace=C16[:, 0:8], in_values=CAND,
                                                           imm_value=NEG), [fb[9], sm], [fb[9]], attach=False)
                self.S.op("dve", lambda e: e.max(out=C16[:, 8:16], in_=TMPK), [fb[9]], [sm])
                self.cp("dve", CALL[:, h, :], C16, [sm], [sm])
            tau8 = self.SMALL[:, 128 + tt * 8: 128 + tt * 8 + 8]
            nb8 = self.SMALL[:, 160 + tt * 8: 160 + tt * 8 + 8]
            zs8 = self.SMALL[:, 304:312]
            self.cp("dve", tau8, CALL[:, :, 15], [sm], [sm])
            self.tt("dve", CEX, CALL, CALL[:, :, 0:1].to_broadcast([P, 8, 16]), ALU.subtract, [sm], [sm])
            self.act(CEX, CEX, AF.Exp, [sm], [sm])
            self.S.op("dve", lambda e: e.tensor_reduce(out=zs8, in_=CEX, axis=AX.X, op=ALU.add), [sm], [sm])
            self.act(zs8, zs8, AF.Ln, [sm], [sm])
            self.tt("dve", nb8, zs8, CALL[:, :, 0], ALU.add, [sm], [sm])
            self.ts("dve", nb8, nb8, -1.0, None, ALU.mult, None, [sm], [sm])
        self.S.op("dve", lambda e: e.memset(self.SMALL[:, 600:601], 0.0), [], [sm, fb[9]] + tkv + tkt)
        tau_all = self.SMALL[:, 128:160]
        nb_all = self.SMALL[:, 160:192]
        bias2 = self.SMALL[:, 192:224]
        self.ts("dve", tau_all, tau_all, -2.0e-5, None, ALU.add, None, [sm], [sm])
        self.tt("dve", bias2, tau_all, nb_all, ALU.add, [sm], [sm])
        for tt in range(4):
            ssb = self.FSA[:, (4 + tt) * 2048:(5 + tt) * 2048].rearrange("p (h a e) -> p h a e", h=8, a=2)
            self.tt("dve", ssb[:, :, 0, :], ssb[:, :, 0, :],
                    tau_all[:, tt * 8:(tt + 1) * 8].unsqueeze(2).to_broadcast([P, 8, 128]), ALU.subtract,
                    [fb[4 + tt], sm], [fb[4 + tt]])
        ZB = [self.FS(8), self.FS(9)]
        zbuf = [fb[8], fb[9]]
        yq2 = self.YQ[:, :, :].rearrange("p a n -> p (a n)")
        EB = [yq2[:, 0:2048], yq2[:, 2048:4096]]
        W2 = [yq2[:, 4096:6144], yq2[:, 6144:8192]]
        ebuf = [self.S.buf("eb0"), self.S.buf("eb1")]
        w2buf = [self.S.buf("w2b0"), self.S.buf("w2b1")]
        G2 = self.XCB[:, :, :].rearrange("p a n -> p (a n)")
        g2b = self.xcbb
        self.S.op("dve", lambda e: e.memset(yq2[:, 4096:4100], 0.0), [], [self.yqb, self.wbb[2]] + ebuf + w2buf + self.vhb)
        GA = self.XN[:, :].rearrange("p (a n) -> p a n", a=4)
        gab = self.xnb
        WACT = [self.UG, self.VGN[:, :, :].rearrange("p a n -> p (a n)").rearrange("p (c t) -> p c t", c=8)]
        wactb = [self.ugb, self.vgnb]
        OUTS = self.FSA[:, 0:4 * 2048].rearrange("p (t d) -> p t d", t=4)
        zi = 0
        vhalf = [0]

        vfly = []

        def vadd():
            pf, pfb, gv, ds, tt = vfly.pop(0)
            dst = OUTS[:, tt, ds * 512:(ds + 1) * 512]
            if gv == 0:
                self.cp("dve", dst, pf[:, :], [pfb], [fb[tt]])
            else:
                self.tt("dve", dst, dst, pf[:, :], ALU.add, [pfb, fb[tt]], [fb[tt]])

        def vstep(gv, ds, tt, vt, vtb):
            WTv = WACT[gv % 2]
            wtbv = wactb[gv % 2]
            pf, pfb = self.next_pf()
            self.mm(pf[:, :], [(WTv[:, c, tt * 128:(tt + 1) * 128], vt[:, c, :]) for c in range(8)],
                    [wtbv, vtb], [pfb])
            vfly.append((pf, pfb, gv, ds, tt))
            if len(vfly) > 1:
                vadd()

        def vload(gv, ds):
            hh = vhalf[0] % 2
            vhalf[0] += 1
            dstv = self.WB[:, 2, hh * 4096:(hh + 1) * 4096]
            self.load(dstv, Sc["v_b"][4 * gv + ds], self.vhb[hh], reads=[self.scrb["v_b"]])
            return dstv.rearrange("p (c n) -> p c n", c=8), self.vhb[hh]

        for g in range(17):
            pend = []
            if g > 0:
                for ds in range(4):
                    for tt in range(4):
                        pend.append((g - 1, ds, tt))
            vcur = {}
            if g == 16:
                for (gv, ds, tt) in pend:
                    if ds not in vcur:
                        vcur[ds] = vload(gv, ds)
                    vstep(gv, ds, tt, *vcur[ds])
                while vfly:
                    vadd()
                break
            WT_ = WACT[g % 2]
            wtb = wactb[g % 2]
            us = []
            for sl in range(2):
                self.load(self.WB[:, sl, :], Sc["uT_b"][2 * g + sl], self.wbb[sl], reads=[self.scrb["uT_b"]])
                us.append(sl)
            for tt in range(4):
                ssb = self.FSA[:, (4 + tt) * 2048:(5 + tt) * 2048].rearrange("p (h a e) -> p h a e", h=8, a=2)
                w2 = W2[tt % 2]
                w2b = w2buf[tt % 2]
                upf = []
                for sl in range(2):
                    ut = self.wb3(us[sl])
                    pf, pfb = self.next_pf()
                    self.mm(pf[:, :], [(self.HNT[:, kc, tt * 128:(tt + 1) * 128], ut[:, kc, :]) for kc in range(KC)],
                            self.hntb + [self.wbb[us[sl]]], [pfb])
                    upf.append((pf, pfb))
                for hb in range(4):
                    z = ZB[zi % 2]
                    zb = zbuf[zi % 2]
                    eb = EB[zi % 2]
                    ebb = ebuf[zi % 2]
                    zi += 1
                    self.tt("pool", z.rearrange("p (j a b) -> p j a b", j=2, a=8),
                            ssb[:, 2 * hb:2 * hb + 2, 0, 8 * g:8 * g + 8].unsqueeze(3).to_broadcast([P, 2, 8, 128]),
                            ssb[:, 2 * hb:2 * hb + 2, 1, :].unsqueeze(2).to_broadcast([P, 2, 8, 128]), ALU.add,
                            [fb[4 + tt]], [zb])
                    for jh in range(2):
                        h = 2 * hb + jh
                        self.act(eb[:, jh * 1024:(jh + 1) * 1024], z[:, jh * 1024:(jh + 1) * 1024], AF.Exp,
                                 [zb, sm], [ebb], bias=bias2[:, tt * 8 + h: tt * 8 + h + 1])
                    if hb == 0:
                        self.stt(w2, z, 0.0, eb, ALU.is_ge, ALU.mult, [zb, ebb], [w2b])
                    else:
                        self.stt(G2, z, 0.0, eb, ALU.is_ge, ALU.mult, [zb, ebb], [g2b])
                        self.tt("dve", w2, w2, G2, ALU.add, [g2b, w2b], [w2b])
                    if pend:
                        gv, ds, ttv = pend.pop(0)
                        if ds not in vcur:
                            vcur[ds] = vload(gv, ds)
                        vstep(gv, ds, ttv, *vcur[ds])
                wcur = w2[:, 0:1024]
                self.tt("dve", wcur, wcur, w2[:, 1024:2048], ALU.add, [w2b], [w2b])
                for sl in range(2):
                    pf, pfb = upf[sl]
                    ga = GA[:, sl, :]
                    wa = GA[:, 2 + sl, :]
                    self.act(ga, pf[:, :], AF.Gelu_apprx_tanh, [pfb], [gab])
                    self.tt("dve", wa, ga, wcur[:, sl * 512:(sl + 1) * 512], ALU.mult, [gab, w2b], [gab])
                    pb, pbb = self.next_pb()
                    self.trs([(pb[:, c * 128:(c + 1) * 128], wa[:, c * 128:(c + 1) * 128]) for c in range(4)],
                             [gab], [pbb])
                    self.cp("act", WT_[:, 4 * sl:4 * sl + 4, tt * 128:(tt + 1) * 128],
                            pb[:, 0:512].rearrange("p (c t) -> p c t", c=4), [pbb], [wtb])
            while vfly:
                vadd()
        self._wslot = 0
        self.S.op("dve", lambda e: e.memset(yq2[:, 4096:4100], 0.0), [], [self.yqb, self.wbb[2]] + ebuf + w2buf + self.vhb)

    def final(self, j):
        I, Sc = self.I, self.Sc
        fb = self.fsb
        sm = self.smallb
        t0 = j * T
        self.load(self.FS(5), Sc["g2row"][:, :], fb[5], reads=[self.scrb["g2row"]])
        self.load(self.FS(6), I["fgR"][:, :], fb[6])
        for tt in range(4):
            x1 = self.FS(4)
            self.load(x1, Sc["x1_st"][t0 + tt * 128: t0 + (tt + 1) * 128, :], fb[4], reads=[self.scrb["x1_st"]])
            o = self.FS(tt)
            self.tt("dve", o, o, self.FS(5), ALU.mult, [fb[tt], fb[5]], [fb[tt]])
            self.tt("pool", o, o, x1, ALU.add, [fb[tt], fb[4]], [fb[tt]])
            ssq = self.SMALL[:, 8:9]
            rstd = self.SMALL[:, 9:10]
            self.act(self.XN[:, :], o, AF.Square, [fb[tt]], [self.xnb, sm], accum=ssq)
            self.act(rstd, ssq, AF.Ln, [sm], [sm], scale=1.0 / D, bias=self.EPSC[:, 0:1])
            self.act(rstd, rstd, AF.Exp, [sm], [sm], scale=-0.5)
            self.stt(o, o, rstd, self.FS(6), ALU.mult, ALU.mult, [fb[tt], fb[6], sm], [fb[tt]])
            self.store(self.OUT[t0 + tt * 128: t0 + (tt + 1) * 128, :], o, fb[tt], [fb[tt]], [self.outb])


def _wtiles(w, ncols):
    K_, C = w.shape
    t = w.reshape(KC, P, C // ncols, ncols).transpose(2, 1, 0, 3)
    return np.ascontiguousarray(t).reshape(C // ncols, P, KC * ncols)


def _fm(v, n):
    return np.ascontiguousarray(v.reshape(n, P).T)


def prepare_inputs(inp, NT):
    f = lambda a: np.asarray(a, dtype=np.float32)
    x, c, ctx, c_ctx = f(inp["x"]), f(inp["c"]), f(inp["ctx"]), f(inp["c_ctx"])
    w_mod, b_mod = f(inp["w_mod"])[0], f(inp["b_mod"])[0]
    shared = {}
    shared["w_mod_l"] = _wtiles(w_mod, 512)
    shared["bmodF"] = _fm(b_mod, 96)
    shared["bmodR"] = np.ascontiguousarray(np.broadcast_to(
        np.concatenate([b_mod[2 * D:3 * D], b_mod[5 * D:6 * D]])[None, :], (P, 2 * D)))
    shared["gF"] = np.ascontiguousarray(np.concatenate([_fm(f(inp["norm1_g"])[0], 16), _fm(f(inp["norm2_g"])[0], 16)], axis=1))
    shared["fgR"] = np.ascontiguousarray(np.broadcast_to(f(inp["final_g"])[None, :], (P, D)))
    shared["w_in_l"] = _wtiles(f(inp["w_in"])[0], 512)
    shared["w_out_l"] = _wtiles(f(inp["w_out"])[0], 512)
    shared["wq_l"] = _wtiles(f(inp["peer_wq"])[0], 512)
    shared["uT_l"] = _wtiles(np.ascontiguousarray(f(inp["peer_u"])[0].T), 512)
    v = f(inp["peer_v"])[0]
    shared["v_l"] = np.ascontiguousarray(v.reshape(16, 8, P, 4, 512).transpose(0, 3, 2, 1, 4)).reshape(64, P, 4096)
    cw, cbias = f(inp["conv_w"])[0], f(inp["conv_b"])[0]
    conv = np.concatenate([cw.reshape(4, 8, P), cbias.reshape(1, 8, P)], axis=0)
    shared["conv_l"] = np.ascontiguousarray(conv.transpose(2, 1, 0)).reshape(P, 40)
    wa, wx = f(inp["lru_wa"])[0], f(inp["lru_wx"])[0]
    lw = np.stack([wa, wx], axis=0)
    shared["lru_w"] = np.ascontiguousarray(lw.transpose(3, 0, 1, 2, 4)).reshape(P, 4096)
    lv = np.stack([f(inp["lru_ba"])[0], f(inp["lru_bx"])[0], f(inp["lru_lambda"])[0]], axis=0)
    shared["lru_v"] = np.ascontiguousarray(lv.reshape(3, 2, 8, P).transpose(3, 0, 1, 2)).reshape(P, 48)
    sw = f(inp["sgu_w"])[0]
    shared["wsT_l"] = np.ascontiguousarray(sw.transpose(2, 0, 1)).reshape(P, 1024)
    shared["bsR"] = np.ascontiguousarray(np.broadcast_to(f(inp["sgu_b"])[0].reshape(1, 1024), (P, 1024)))
    k = np.stack([f(inp["peer_k1"])[0], f(inp["peer_k2"])[0]], axis=0)
    shared["kT_l"] = np.ascontiguousarray(k.transpose(3, 0, 1, 2)).reshape(P, 2048)
    maps = []
    L = 4 * NT
    for core in range(8):
        b, q = core // 4, core % 4
        m = dict(shared)
        m["x_own"] = np.ascontiguousarray(x[b, q * NT:(q + 1) * NT])
        m["x_b"] = np.ascontiguousarray(x[b])
        halo = np.zeros((P, D), np.float32)
        if q > 0:
            halo[0:2] = x[b, q * NT - 2:q * NT]
        if q < 3:
            halo[2] = x[b, (q + 1) * NT]
        m["x_halo"] = halo
        m["ctx_b"] = np.ascontiguousarray(ctx[b])
        m["s_in"] = np.ascontiguousarray(np.stack([_fm(c[b], 16), _fm(c_ctx, 16)], axis=2)).reshape(P, 32)
        sel = np.zeros((P, 12), np.float32)
        sel[:, q] = 1.0
        sel[:, 4 + q] = 1.0
        sel[:, 8] = sel[:, 9] = 1.0 if q > 0 else 0.0
        sel[:, 10] = 1.0 if q < 3 else 0.0
        m["sel"] = sel
        maps.append(m)
    return maps


_NC_CACHE = {}


def run(inp, NT, dbg=None):
    key = (NT, dbg)
    if key not in _NC_CACHE:
        _NC_CACHE[key] = K(NT, dbg).build()
    nc = _NC_CACHE[key]
    maps = prepare_inputs(inp, NT)
    res = run_bass_kernel_spmd(nc, maps, core_ids=list(range(8)))
    B = 2
    out = np.zeros((B, 4 * NT, D), np.float32)
    for core in range(8):
        b, q = core // 4, core % 4
        out[b, q * NT:(q + 1) * NT] = res.results[core]["out"]
    return out, res


def kernel(**inputs):
    import os
    NT = np.asarray(inputs["x"]).shape[1] // 4
    if os.environ.get("KPROBE_NT"):
        NT2 = int(os.environ["KPROBE_NT"])
        inp = dict(inputs)
        inp["x"] = np.ascontiguousarray(np.asarray(inputs["x"])[:, :4 * NT2])
        o, _ = run(inp, NT2)
        out = np.zeros((2, 4 * NT, D), np.float32)
        out[:, :4 * NT2] = o
        return out
    out, _ = run(inputs, NT)
    return out
```
